# Optimizing a Trainium2 kernel written in Bass

```python
import jax
import jax.numpy as jnp
from jax import lax
import numpy as np

D_MODEL = 1024
BATCH = 16
SEQ = 4096
DEPTH = 4

CTX_LEN = 256
GRID_W = 64
HEAD_DIM = 64
BRANCH_W = D_MODEL // 2
N_BRANCH = 4
NA_HEADS = BRANCH_W // HEAD_DIM
NA_WIN_H = 8
NA_WIN_W = 16
ROPE_THETA = 10000.0
ROPE_AXIS_DIM = HEAD_DIM // 2
ROPE_FREQS = ROPE_AXIS_DIM // 2
FNET_GROUPS = 4
FNET_GROUP_W = BRANCH_W // FNET_GROUPS
GMLP_CHUNK = 128
GMLP_GROUPS = 8
GMLP_GROUP_W = BRANCH_W // GMLP_GROUPS
HGRN_HEADS = 8
HGRN_DK = BRANCH_W // HGRN_HEADS
HGRN_DV = BRANCH_W // HGRN_HEADS
HGRN_CHUNK = 64
EPS = 1e-6
F_FLOOR = 1e-30
NEG_INF = -1e30
F32 = jnp.float32

IN_NAMES = ('a_q', 'a_k', 'a_v', 'a_g', 'b_x', 'b_g', 'c_u', 'c_v', 'c_g',
            'd_q', 'd_f_fwd', 'd_f_bwd', 'd_i', 'd_g', 'gate_0', 'gate_1', 'gate_2', 'gate_3')
IN_SIZES = (BRANCH_W,) * 14 + (D_MODEL,) * N_BRANCH
IN_W = 14 * BRANCH_W + N_BRANCH * D_MODEL

kernel_name = 'hybrid_gated_branch_flow_block'


def in_proj(h, w_in_l, name):
    idx = IN_NAMES.index(name)
    start = sum(IN_SIZES[:idx])
    return h @ w_in_l[:, start:start + IN_SIZES[idx]]


def rmsnorm(x, g):
    xf = x.astype(F32)
    y = xf * lax.rsqrt(jnp.mean(xf * xf, axis=-1, keepdims=True) + EPS)
    return (y * g.astype(F32)).astype(x.dtype)


def heads(t, h):
    return t.reshape(t.shape[0], t.shape[1], h, t.shape[2] // h)


def axial_rope(n_tok):
    t = jnp.arange(n_tok, dtype=jnp.int32)
    pos = jnp.stack([t // GRID_W, t % GRID_W], axis=-1).astype(F32)
    inv = ROPE_THETA ** (-jnp.arange(ROPE_FREQS, dtype=F32) * 2.0 / ROPE_AXIS_DIM)
    ang = pos[:, :, None] * inv
    return jnp.cos(ang), jnp.sin(ang)


def rope2d(x, cos, sin):
    b, n, h, dh = x.shape
    xr = x.reshape(b, n, h, 2, 2, ROPE_FREQS).astype(F32)
    x1, x2 = xr[..., 0, :], xr[..., 1, :]
    c_, s_ = cos[:, None], sin[:, None]
    out = jnp.stack([x1 * c_ - x2 * s_, x2 * c_ + x1 * s_], axis=-2)
    return out.reshape(b, n, h, dh).astype(x.dtype)


def dense_attention(q, k, v):
    s = jnp.einsum('bhqd,bhkd->bhqk', q, k).astype(F32) * (q.shape[-1] ** -0.5)
    p = jax.nn.softmax(s, axis=-1).astype(v.dtype)
    return jnp.einsum('bhqk,bhkd->bhqd', p, v)


def neighbourhood_attention(q_rot, q_plain, k_rot, v, k_ctx, v_ctx, rpb, rows):
    b, n, h, dh = q_rot.shape
    kh = min(NA_WIN_H, rows)
    kw = NA_WIN_W
    scale = dh ** -0.5

    def to_grid(t):
        return t.reshape(b, rows, GRID_W, h, dh).transpose(0, 3, 1, 2, 4)

    qg_r, qg_p, kg, vg = to_grid(q_rot), to_grid(q_plain), to_grid(k_rot), to_grid(v)
    col = jnp.arange(GRID_W)
    col_start = jnp.clip(col - kw // 2, 0, GRID_W - kw)
    col_valid = (col[None, :] >= col_start[:, None]) & (col[None, :] < col_start[:, None] + kw)
    col_idx = jnp.clip(col[None, :] - col[:, None] + NA_WIN_W - 1, 0, 2 * NA_WIN_W - 2)
    rpb_f = rpb.astype(F32)
    band = kh * GRID_W

    def row_block(r):
        rs = jnp.clip(r - kh // 2, 0, rows - kh)
        q_r = lax.dynamic_index_in_dim(qg_r, r, axis=2, keepdims=False)
        q_p = lax.dynamic_index_in_dim(qg_p, r, axis=2, keepdims=False)
        k_band = lax.dynamic_slice_in_dim(kg, rs, kh, axis=2)
        v_band = lax.dynamic_slice_in_dim(vg, rs, kh, axis=2)
        row_idx = rs + jnp.arange(kh) - r + NA_WIN_H - 1
        bias = rpb_f[:, row_idx[None, :, None], col_idx[:, None, :]]
        bias = jnp.where(col_valid[:, None, :], bias, NEG_INF)
        s_band = jnp.einsum('bhqd,bhikd->bhqik', q_r, k_band).astype(F32) * scale + bias
        s_ctx = jnp.einsum('bhqd,bhld->bhql', q_p, k_ctx).astype(F32) * scale
        s = jnp.concatenate([s_band.reshape(b, h, GRID_W, band), s_ctx], axis=-1)
        p = jax.nn.softmax(s, axis=-1).astype(v.dtype)
        o = jnp.einsum('bhqik,bhikd->bhqd', p[..., :band].reshape(b, h, GRID_W, kh, GRID_W), v_band)
        return o + jnp.einsum('bhql,bhld->bhqd', p[..., band:], v_ctx)

    out = lax.map(row_block, jnp.arange(rows))
    return out.transpose(1, 0, 3, 2, 4).reshape(b, n, h * dh)


def na_branch(hx, hc, w, rpb, cos, sin, rows, with_ctx):
    q = heads(in_proj(hx, w, 'a_q'), NA_HEADS)
    k = heads(in_proj(hx, w, 'a_k'), NA_HEADS)
    v = heads(in_proj(hx, w, 'a_v'), NA_HEADS)
    kc = heads(in_proj(hc, w, 'a_k'), NA_HEADS).transpose(0, 2, 1, 3)
    vc = heads(in_proj(hc, w, 'a_v'), NA_HEADS).transpose(0, 2, 1, 3)
    o = neighbourhood_attention(rope2d(q, cos, sin), q, rope2d(k, cos, sin), v, kc, vc, rpb, rows)
    y_lat = o * jax.nn.silu(in_proj(hx, w, 'a_g'))
    if not with_ctx:
        return y_lat, None
    qc = heads(in_proj(hc, w, 'a_q'), NA_HEADS).transpose(0, 2, 1, 3)
    oc = dense_attention(qc, kc, vc).transpose(0, 2, 1, 3).reshape(hc.shape[0], hc.shape[1], BRANCH_W)
    return y_lat, oc * jax.nn.silu(in_proj(hc, w, 'a_g'))


def fourier_branch(h, w, w_f):
    b, n, _ = h.shape
    xb = in_proj(h, w, 'b_x')
    z = xb.reshape(b, n, FNET_GROUPS, FNET_GROUP_W).astype(F32)
    spec = jnp.fft.fft2(z, axes=(1, 3), norm='ortho').real
    y = jnp.einsum('bngc,gcd->bngd', spec, w_f.astype(F32)).reshape(b, n, BRANCH_W).astype(h.dtype)
    return y * jax.nn.silu(in_proj(h, w, 'b_g'))


def gmlp_branch(h, w, g_norm, w_s, b_s):
    b, n, _ = h.shape
    u = in_proj(h, w, 'c_u')
    vn = rmsnorm(in_proj(h, w, 'c_v'), g_norm)
    vn = vn.reshape(b, n // GMLP_CHUNK, GMLP_CHUNK, GMLP_GROUPS, GMLP_GROUP_W)
    mixed = jnp.einsum('gts,bksgc->bktgc', w_s, vn) + b_s.T[None, None, :, :, None]
    return u * mixed.reshape(b, n, BRANCH_W) * jax.nn.silu(in_proj(h, w, 'c_g'))


def forget_gate(z, lb):
    zf = z.astype(F32)
    f = lb + (1.0 - lb) * jax.nn.sigmoid(zf)
    logf = jnp.log(jnp.maximum(f, F_FLOOR))
    k = (1.0 - lb) * jax.nn.sigmoid(-zf)
    return heads(logf, HGRN_HEADS), heads(k, HGRN_HEADS)


def hgrn2_scan(k, logf, i, s0, q=None):
    b, n, h, _ = k.shape
    nc = n // HGRN_CHUNK

    def chunks(t):
        return t.reshape(b, nc, HGRN_CHUNK, h, t.shape[-1]).transpose(1, 0, 3, 2, 4)

    seen = jnp.tril(jnp.ones((HGRN_CHUNK, HGRN_CHUNK), dtype=bool))[:, :, None]

    def step(state, xs):
        kc, gc, ic = xs[0], xs[1], xs[2]
        a = jnp.cumsum(gc, axis=2)
        a_last = a[:, :, -1:, :]
        new_state = jnp.exp(a_last[:, :, 0, :])[..., None] * state + jnp.einsum(
            'bhsk,bhsv->bhkv', kc * jnp.exp(a_last - a), ic)
        if q is None:
            return new_state, None
        qc = xs[3]
        diff = jnp.where(seen, a[:, :, :, None, :] - a[:, :, None, :, :], 0.0)
        decay = jnp.where(seen, jnp.exp(diff), 0.0)
        scores = jnp.einsum('bhtk,bhsk,bhtsk->bhts', qc, kc, decay)
        o = jnp.einsum('bhts,bhsv->bhtv', scores, ic) + jnp.einsum('bhtk,bhkv->bhtv', qc * jnp.exp(a), state)
        return new_state, o

    if q is None:
        final, _ = lax.scan(step, s0, (chunks(k), chunks(logf), chunks(i)))
        return None, final
    final, o = lax.scan(step, s0, (chunks(k), chunks(logf), chunks(i), chunks(q)))
    return o.transpose(1, 0, 3, 2, 4).reshape(b, n, h, o.shape[-1]), final


def hgrn2_readout(o, gate, g_norm):
    b, n = o.shape[0], o.shape[1]
    on = o * lax.rsqrt(jnp.mean(o * o, axis=-1, keepdims=True) + EPS)
    y = (on.reshape(b, n, BRANCH_W) * g_norm.astype(F32)).astype(gate.dtype)
    return y * jax.nn.silu(gate)


def hgrn2_branch(hx, hc, w, lb, g_norm, with_ctx):
    def prep(h, with_q):
        q = heads(in_proj(h, w, 'd_q'), HGRN_HEADS).astype(F32) if with_q else None
        i = heads(in_proj(h, w, 'd_i'), HGRN_HEADS).astype(F32)
        lf_f, k_f = forget_gate(in_proj(h, w, 'd_f_fwd'), lb[0])
        lf_b, k_b = forget_gate(in_proj(h, w, 'd_f_bwd'), lb[1])
        return q, i, lf_f, k_f, lf_b, k_b

    def flip(t):
        return None if t is None else jnp.flip(t, axis=1)

    q, i, lf_f, k_f, lf_b, k_b = prep(hx, True)
    qc, ic, lfc_f, kc_f, lfc_b, kc_b = prep(hc, with_ctx)
    s0 = jnp.zeros((hx.shape[0], HGRN_HEADS, HGRN_DK, HGRN_DV), F32)
    oc_f, sc_f = hgrn2_scan(kc_f, lfc_f, ic, s0, qc)
    oc_b, sc_b = hgrn2_scan(flip(kc_b), flip(lfc_b), flip(ic), s0, flip(qc))
    o_f, _ = hgrn2_scan(k_f, lf_f, i, sc_f, q)
    o_b, _ = hgrn2_scan(flip(k_b), flip(lf_b), flip(i), sc_b, flip(q))
    y_lat = hgrn2_readout(o_f + flip(o_b), in_proj(hx, w, 'd_g'), g_norm)
    if not with_ctx:
        return y_lat, None
    return y_lat, hgrn2_readout(oc_f + flip(oc_b), in_proj(hc, w, 'd_g'), g_norm)


def merge_branches(ys, h, w, w_branch_l, w_out_l):
    merged = None
    for r in range(N_BRANCH):
        term = jax.nn.sigmoid(in_proj(h, w, 'gate_%d' % r)) * (ys[r] @ w_branch_l[r])
        merged = term if merged is None else merged + term
    return merged @ w_out_l


def setup_inputs(seed: int = 0) -> dict:
    key = jax.random.key(seed)
    ks = jax.random.split(key, 18)

    def nrm(k, shape, scale):
        return jax.random.normal(k, shape, F32) * scale

    return {
        'x': nrm(ks[0], (BATCH, SEQ, D_MODEL), 1.0),
        'c': nrm(ks[1], (BATCH, D_MODEL), 1.0),
        'ctx': nrm(ks[2], (BATCH, CTX_LEN, D_MODEL), 1.0),
        'c_ctx': nrm(ks[3], (D_MODEL,), 1.0),
        'w_ada': nrm(ks[4], (DEPTH, D_MODEL, 3 * D_MODEL), D_MODEL ** -0.5),
        'b_ada': nrm(ks[5], (DEPTH, 3 * D_MODEL), 0.02),
        'g_pre': 1.0 + nrm(ks[6], (DEPTH, D_MODEL), 0.02),
        'g_post': 1.0 + nrm(ks[7], (DEPTH, D_MODEL), 0.02),
        'w_in': nrm(ks[8], (DEPTH, D_MODEL, IN_W), D_MODEL ** -0.5),
        'na_rpb': nrm(ks[9], (DEPTH, NA_HEADS, 2 * NA_WIN_H - 1, 2 * NA_WIN_W - 1), 0.5),
        'fnet_w': nrm(ks[10], (DEPTH, FNET_GROUPS, FNET_GROUP_W, FNET_GROUP_W), FNET_GROUP_W ** -0.5),
        'gmlp_norm_g': 1.0 + nrm(ks[11], (DEPTH, BRANCH_W), 0.02),
        'gmlp_ws': nrm(ks[12], (DEPTH, GMLP_GROUPS, GMLP_CHUNK, GMLP_CHUNK), GMLP_CHUNK ** -0.5),
        'gmlp_bs': 1.0 + nrm(ks[13], (DEPTH, GMLP_GROUPS, GMLP_CHUNK), 0.02),
        'hgrn_lb_logits': nrm(ks[14], (DEPTH, 2, BRANCH_W), 0.5),
        'hgrn_norm_g': 1.0 + nrm(ks[15], (DEPTH, BRANCH_W), 0.02),
        'w_branch': nrm(ks[16], (DEPTH, N_BRANCH, BRANCH_W, D_MODEL), BRANCH_W ** -0.5),
        'w_out': nrm(ks[17], (DEPTH, D_MODEL, D_MODEL), D_MODEL ** -0.5),
    }


def reference(x, c, ctx, c_ctx, w_ada, b_ada, g_pre, g_post, w_in, na_rpb, fnet_w,
              gmlp_norm_g, gmlp_ws, gmlp_bs, hgrn_lb_logits, hgrn_norm_g, w_branch, w_out):
    n_tok = x.shape[1]
    rows = n_tok // GRID_W
    cos, sin = axial_rope(n_tok)
    lb_sm = jax.nn.softmax(hgrn_lb_logits.astype(F32), axis=0)
    lower_bounds = jnp.maximum(jnp.cumsum(lb_sm, axis=0) - lb_sm[0:1], 0.0)
    silu_c = jax.nn.silu(c)
    silu_cc = jax.nn.silu(c_ctx)
    for l in range(DEPTH):
        with_ctx = l < DEPTH - 1
        sh_x, sc_x, gt_x = jnp.split(silu_c @ w_ada[l] + b_ada[l], 3, axis=-1)
        sh_c, sc_c, gt_c = jnp.split(silu_cc @ w_ada[l] + b_ada[l], 3, axis=-1)
        hx = rmsnorm(x, g_pre[l]) * (1.0 + sc_x[:, None]) + sh_x[:, None]
        hc = rmsnorm(ctx, g_pre[l]) * (1.0 + sc_c) + sh_c
        w = w_in[l]
        ya, ya_c = na_branch(hx, hc, w, na_rpb[l], cos, sin, rows, with_ctx)
        yb = fourier_branch(hx, w, fnet_w[l])
        yc = gmlp_branch(hx, w, gmlp_norm_g[l], gmlp_ws[l], gmlp_bs[l])
        yd, yd_c = hgrn2_branch(hx, hc, w, lower_bounds[l], hgrn_norm_g[l], with_ctx)
        out_x = merge_branches((ya, yb, yc, yd), hx, w, w_branch[l], w_out[l])
        if with_ctx:
            yb_c = fourier_branch(hc, w, fnet_w[l])
            yc_c = gmlp_branch(hc, w, gmlp_norm_g[l], gmlp_ws[l], gmlp_bs[l])
            out_c = merge_branches((ya_c, yb_c, yc_c, yd_c), hc, w, w_branch[l], w_out[l])
            ctx = ctx + gt_c * rmsnorm(out_c, g_post[l])
        x = x + gt_x[:, None] * rmsnorm(out_x, g_post[l])
    return x
```

```python
import contextlib
import numpy as np
import ml_dtypes
import concourse.bass as bass
import concourse.mybir as mybir
from concourse.bass_utils import run_bass_kernel_spmd

F32 = mybir.dt.float32
BF16 = mybir.dt.bfloat16
I32 = mybir.dt.int32
AF = mybir.ActivationFunctionType
ALU = mybir.AluOpType

D = 1024
NTOK = 4096
LCTX = 256
NT = NTOK + LCTX
NCOL = 11264
DEPTH = 4
EPS = 1e-6
COL = dict(a_q=0, a_k=512, a_v=1024, a_g=1536, b_x=2048, b_g=2560, c_u=3072, c_v=3584, c_g=4096,
           d_q=4608, d_ff=5120, d_fb=5632, d_i=6144, d_g=6656, gate=7168)
ZW = 26
NEG = -30000.0

SEM_WINDOW = 16000
DMA_RING = 8


class Buf:
    __slots__ = ("name", "lw", "rd", "excl")

    def __init__(self, name="", excl=False):
        self.name = name
        self.lw = None
        self.rd = {}
        self.excl = excl


class DTrack:
    def __init__(self, ncols, unit=128):
        self.unit = unit
        self.b = [Buf() for _ in range((ncols + unit - 1) // unit)]

    def r(self, c0, c1):
        return self.b[c0 // self.unit:(c1 + self.unit - 1) // self.unit]


class _Rec:
    def __init__(self):
        self.call = None

    def __getattr__(self, name):
        def f(*a, **kw):
            self.call = (name, a, kw)
            return self
        return f


def _freeze(fn):
    r = _Rec()
    fn(r)
    name, a, kw = r.call
    return lambda e: getattr(e, name)(*a, **kw)


class Sched:
    ENGS = ("pe", "act", "dve", "pool", "sp")

    def __init__(self, nc):
        self.nc = nc
        self.ops = {e: [] for e in self.ENGS}
        self.cnt = {e: 0 for e in self.ENGS}
        self.dcnt = {e: 0 for e in self.ENGS}
        self.known = {e: {} for e in self.ENGS}
        self.pending = {e: False for e in self.ENGS}

    def _tokwaits(self, eng, toks):
        waits = {}
        for t in toks:
            if t[0] == 'e':
                if t[1] == eng and eng == 'pe':
                    continue
                key = ('e', t[1], (t[2] - 1) // SEM_WINDOW)
                val = (t[2] - 1) % SEM_WINDOW + 1
            else:
                key = ('d', t[1], t[2] % DMA_RING)
                val = 16 * (t[2] // DMA_RING + 1)
            if waits.get(key, 0) < val:
                waits[key] = val
        out = []
        kn = self.known[eng]
        for key, val in waits.items():
            if kn.get(key, 0) >= val:
                continue
            kn[key] = val
            out.append((key, val))
        return out

    def _deps(self, eng, reads, writes):
        toks = []
        for b in reads:
            if b.lw is not None:
                toks.append(b.lw)
            if b.excl:
                toks.extend(v for kk, v in b.rd.items() if kk != eng)
        for b in writes:
            if b.lw is not None:
                toks.append(b.lw)
            toks.extend(b.rd.values())
        return self._tokwaits(eng, toks)

    def op(self, eng, fn, reads=(), writes=(), inc=True):
        fn = _freeze(fn)
        waits = self._deps(eng, reads, writes)
        idx = self.cnt[eng] + 1
        tok = ('e', eng, idx)
        if inc:
            self.cnt[eng] = idx
            self.ops[eng].append((waits, fn, ('e', eng, (idx - 1) // SEM_WINDOW), 1))
            self.pending[eng] = False
        else:
            self.ops[eng].append((waits, fn, None, 0))
            self.pending[eng] = True
        for b in reads:
            b.rd[eng] = tok
        for b in writes:
            b.lw = tok
            b.rd = {}
        return tok

    def dma(self, q, out, in_, reads=(), writes=()):
        waits = self._deps(q, reads, writes)
        i = self.dcnt[q]
        self.dcnt[q] += 1
        if i >= DMA_RING:
            key = ('d', q, i % DMA_RING)
            val = 16 * (i // DMA_RING)
            kn = self.known[q]
            if kn.get(key, 0) < val:
                kn[key] = val
                waits.append((key, val))
        tok = ('d', q, i)
        fn = lambda e, out=out, in_=in_: e.dma_start(out=out, in_=in_)
        self.ops[q].append((waits, fn, ('d', q, i % DMA_RING), 16))
        qk = 'q' + q
        for b in reads:
            b.rd[qk] = tok
        for b in writes:
            b.lw = tok
            b.rd = {}
        return tok

    def barrier(self):
        toks = []
        for e in self.ENGS:
            assert not self.pending[e]
            if self.cnt[e] > 0:
                toks.append(('e', e, self.cnt[e]))
            n = self.dcnt[e]
            for i in range(max(0, n - DMA_RING), n):
                toks.append(('d', e, i))
        for e in self.ENGS:
            w = self._tokwaits(e, toks)
            if w:
                self.ops[e].append((w, None, None, 0))

    def emit(self):
        nc = self.nc
        self.barrier()
        sems = {}
        with contextlib.ExitStack() as st:
            def getsem(key):
                if key not in sems:
                    sems[key] = st.enter_context(nc.semaphore("s_%s_%s_%d" % key))
                return sems[key]
            for e in self.ENGS:
                for (waits, fn, inc, amt) in self.ops[e]:
                    for key, val in waits:
                        getsem(key)
                    if inc is not None:
                        getsem(inc)
            block = st.enter_context(nc.Block())
            handles = {"pe": block.tensor, "act": block.scalar, "dve": block.vector,
                       "pool": block.gpsimd, "sp": block.sync}
            for e in self.ENGS:
                ops = self.ops[e]
                if not ops:
                    continue

                def body(engine, ops=ops):
                    for (waits, fn, inc, amt) in ops:
                        for key, val in waits:
                            engine.wait_ge(sems[key], val)
                        if fn is not None:
                            ins = fn(engine)
                            if inc is not None:
                                ins.then_inc(sems[inc], amt)
                handles[e](body)


_UC = [0]
_DBG = {}


def _u(n):
    _UC[0] += 1
    return "%s_%d" % (n, _UC[0])


class K:
    pass


def _dram(nc, name, shape, dt, kind=None):
    if kind is None:
        return nc.dram_tensor(name, list(shape), dt).ap()
    return nc.dram_tensor(name, list(shape), dt, kind=kind).ap()


def build(nlayers=DEPTH, phases="PABCDM", dump=()):
    nc = bass.Bass("TRN2", target_bir_lowering=False)
    k = K()
    k.nc = nc
    S = Sched(nc)
    k.S = S
    IN = "ExternalInput"
    I = {}
    def inp(name, shape, dt=F32):
        I[name] = _dram(nc, name, shape, dt, IN)
        return I[name]
    inp("x", [2, NTOK, D]); inp("ctx", [2, LCTX, D]); inp("cT", [128, 8, 3])
    inp("w_ada", [nlayers, D, 3 * D]); inp("b_ada", [nlayers, 3 * D]); inp("b_adaT", [nlayers, 128, 24])
    inp("g_preT", [nlayers, 128, 8]); inp("g_post", [nlayers, D])
    inp("w_in", [nlayers, D, NCOL])
    inp("zb", [nlayers, 2, 128, 8, ZW * 64], BF16)
    inp("fnet_w", [nlayers, 512, 128]); inp("gmlp_g", [nlayers, 512]); inp("gmlp_wsT", [nlayers, 1024, 128])
    inp("gmlp_bsT", [nlayers, 128, 8]); inp("lbT", [128, 8, DEPTH]); inp("hgrn_gT", [nlayers, 128, 4])
    inp("w_branch", [nlayers, 2048, D]); inp("w_out", [nlayers, D, D])
    inp("ident", [128, 128], BF16); inp("rsign", [128, 128], BF16); inp("eye8", [128, 128], BF16)
    inp("cosT", [128, NTOK]); inp("sinT", [128, NTOK])
    inp("trif", [128, 128], I32); inp("trib", [128, 128], I32); inp("bones", [128, 128], BF16)
    inp("dft", [16, NTOK, 512], BF16); inp("dft256", [LCTX, 512], BF16)
    inp("cc128", [128, 128], BF16); inp("ssn128", [128, 128], BF16)
    OUT = _dram(nc, "out", [2, NTOK, D], F32, "ExternalOutput")
    def scr(name, shape, dt):
        return _dram(nc, name, shape, dt, "ExternalOutput" if name in dump else None)
    k.w_in_bf = scr("w_in_bf", [2, D, NCOL], BF16)
    k.w_br_bf = scr("w_br_bf", [2, 2048, D], BF16)
    k.w_out_bf = scr("w_out_bf", [2, D, D], BF16)
    k.fw_bf = scr("fw_bf", [2, 512, 128], BF16)
    k.ws_bf = scr("ws_bf", [2, 1024, 128], BF16)
    k.hT = scr("hT", [2, D, NT], BF16)
    k.yT = scr("yT", [2, 4, 512, NT], BF16)
    k.ofT = scr("ofT", [2, 512, NT], F32)
    k.ctxcur = scr("ctxcur", [2, LCTX, D], F32)
    k.I = I
    k.OUT = OUT
    k.t_w = [Buf() for _ in range(2)]
    k.t_hT = [DTrack(NT) for _ in range(2)]
    k.t_yT = [[DTrack(NT) for _ in range(4)] for _ in range(2)]
    k.t_ofT = [DTrack(NT) for _ in range(2)]
    k.t_x = [DTrack(NTOK) for _ in range(2)]
    k.t_ctx = [DTrack(LCTX) for _ in range(2)]

    with contextlib.ExitStack() as gst:
        k.gst = gst
        def gsb(name, shape, dt):
            return gst.enter_context(nc.sbuf_tensor(name, list(shape), dt))
        k.ps = [gst.enter_context(nc.psum_tensor("ps%d" % i, [128, 512], F32)) for i in range(8)]
        k.bps = [Buf("ps%d" % i, excl=True) for i in range(8)]
        k.ident = gsb("ident_sb", [128, 128], BF16)
        k.ones_bf = gsb("ones_bf", [128, 128], BF16)
        k.ones_f = gsb("ones_f", [128, 512], F32)
        k.scT = gsb("scT", [128, 8, 3], F32)
        k.modT = gsb("modT", [128, 24, 3], F32)
        k.gsT = gsb("gsT", [128, 8, 3], F32)
        k.gtg = gsb("gtg", [128, 3, D], F32)
        k.lb = gsb("lb", [128, 8, DEPTH], F32)
        k.oml = gsb("oml", [128, 8, DEPTH], F32)
        k.b_const = Buf(); k.b_scT = Buf(); k.b_mod = Buf(); k.b_gtg = Buf(); k.b_lb = Buf()
        S.dma('sp', k.ident[:], I["ident"], writes=[k.b_const])
        S.op('dve', lambda e: e.memset(k.ones_bf[:], 1.0), writes=[k.b_const])
        S.op('dve', lambda e: e.memset(k.ones_f[:], 1.0), writes=[k.b_const])
        prep_global(k)
        for l in range(nlayers):
            S.barrier()
            convert_weights(k, l)
            adaln(k, l)
            with_ctx = l < DEPTH - 1
            if "P" in phases:
                for s in range(2):
                    phase_pre(k, l, s)
            if "C" in phases:
                phase_c(k, l, with_ctx)
            if "B" in phases:
                phase_b(k, l, with_ctx)
            if "A" in phases:
                phase_a(k, l, with_ctx)
            if "D" in phases:
                phase_d(k, l, with_ctx)
            if "M" in phases:
                phase_m(k, l, with_ctx)
        S.emit()
    return nc


def prep_global(k):
    S, nc, I = k.S, k.nc, k.I
    with contextlib.ExitStack() as st:
        sb = lambda n, s, d: st.enter_context(nc.sbuf_tensor(_u(n), list(s), d))
        cT = sb("pg_cT", [128, 8, 3], F32)
        lg = sb("pg_lg", [128, 8, DEPTH], F32)
        ex = sb("pg_ex", [128, 8, DEPTH], F32)
        sm = sb("pg_sm", [128, 8], F32)
        b = Buf()
        S.dma('sp', cT[:], I["cT"], writes=[b])
        S.op('act', lambda e: e.activation(out=k.scT[:], in_=cT[:], func=AF.Silu), reads=[b], writes=[k.b_scT])
        b2 = Buf()
        S.dma('sp', lg[:], I["lbT"], writes=[b2])
        S.op('act', lambda e: e.activation(out=ex[:], in_=lg[:], func=AF.Exp), reads=[b2], writes=[b2])
        S.op('dve', lambda e: e.tensor_reduce(out=sm[:], in_=ex[:], axis=mybir.AxisListType.X, op=ALU.add), reads=[b2], writes=[b2])
        S.op('dve', lambda e: e.reciprocal(sm[:], sm[:]), reads=[b2], writes=[b2])
        S.op('dve', lambda e: e.tensor_tensor(out=ex[:], in0=ex[:], in1=sm[:].unsqueeze(2).broadcast_to([128, 8, DEPTH]), op=ALU.mult), reads=[b2], writes=[b2])
        S.op('dve', lambda e: e.memset(k.lb[:, :, 0:1], 0.0), reads=[b2], writes=[k.b_lb])
        for l in range(1, DEPTH):
            S.op('dve', lambda e, l=l: e.tensor_tensor(out=k.lb[:, :, l:l + 1], in0=k.lb[:, :, l - 1:l], in1=ex[:, :, l:l + 1], op=ALU.add), reads=[b2, k.b_lb], writes=[k.b_lb])
        S.op('dve', lambda e: e.tensor_scalar(out=k.lb[:], in0=k.lb[:], scalar1=0.0, scalar2=None, op0=ALU.max), reads=[k.b_lb], writes=[k.b_lb])
        S.op('dve', lambda e: e.tensor_scalar(out=k.oml[:], in0=k.lb[:], scalar1=-1.0, scalar2=1.0, op0=ALU.mult, op1=ALU.add), reads=[k.b_lb], writes=[k.b_lb])
        S.barrier()


def convert_weights(k, l):
    S, I = k.S, k.I
    p = l % 2
    w = [k.t_w[p]]
    for c0 in range(0, NCOL, 1408):
        S.dma('pool', k.w_in_bf[p, :, c0:c0 + 1408], I["w_in"][l, :, c0:c0 + 1408], writes=w)
    S.dma('pool', k.w_br_bf[p], I["w_branch"][l], writes=w)
    S.dma('pool', k.w_out_bf[p], I["w_out"][l], writes=w)
    S.dma('pool', k.fw_bf[p], I["fnet_w"][l], writes=w)
    S.dma('pool', k.ws_bf[p], I["gmlp_wsT"][l], writes=w)


def load_w(k, l, dst, c0, ncol, bdst):
    p = l % 2
    src = k.w_in_bf[p, :, c0:c0 + ncol].rearrange("(kc p) c -> p kc c", p=128)
    k.S.dma('sp', dst, src, reads=[k.t_w[p]], writes=[bdst])


def adaln(k, l):
    S, nc, I = k.S, k.nc, k.I
    with contextlib.ExitStack() as st:
        sb = lambda n, s, d: st.enter_context(nc.sbuf_tensor(_u(n), list(s), d))
        wa = [sb("ad_wa%d" % i, [128, 8, 512], F32) for i in range(2)]
        bwa = [Buf() for _ in range(2)]
        scbc = sb("ad_scbc", [128, 8, 3, 128], F32)
        bT = sb("ad_bT", [128, 24], F32)
        gpT = sb("ad_gpT", [128, 8], F32)
        brow = sb("ad_brow", [128, D], F32)
        grow = sb("ad_grow", [128, D], F32)
        tmp = sb("ad_tmp", [128, 8, 3], F32)
        b = Buf(); bsc = Buf()
        S.dma('sp', bT[:], I["b_adaT"][l], writes=[b])
        S.dma('sp', gpT[:], I["g_preT"][l], writes=[b])
        S.dma('sp', brow[:], I["b_ada"][l:l + 1, 2 * D:3 * D].partition_broadcast(128), writes=[b])
        S.dma('sp', grow[:], I["g_post"][l:l + 1, :].partition_broadcast(128), writes=[b])
        S.op('dve', lambda e: e.tensor_copy(scbc[:], k.scT[:].unsqueeze(3).broadcast_to([128, 8, 3, 128])), reads=[k.b_scT], writes=[bsc])
        pm = k.ps[0]
        for g in range(6):
            w = wa[g % 2]
            S.dma('sp', w[:], I["w_ada"][l, :, g * 512:(g + 1) * 512].rearrange("(kc p) c -> p kc c", p=128), writes=[bwa[g % 2]])
            for j in range(4):
                ch = 4 * g + j
                for kc in range(8):
                    S.op('pe', lambda e, w=w, kc=kc, j=j, ch=ch: e.matmul(pm[:, ch * 3:ch * 3 + 3], lhsT=w[:, kc, j * 128:(j + 1) * 128], rhs=k.scT[:, kc, :], start=(kc == 0), stop=(kc == 7)),
                         reads=[bwa[g % 2], k.b_scT], writes=[k.bps[0]], inc=(kc == 7))
            if g >= 4:
                half = g - 4
                for s in range(3):
                    pg = k.ps[1 + s]
                    for kc in range(8):
                        S.op('pe', lambda e, w=w, kc=kc, s=s, pg=pg: e.matmul(pg[:], lhsT=scbc[:, kc, s, :], rhs=w[:, kc, :], start=(kc == 0), stop=(kc == 7)),
                             reads=[bwa[g % 2], bsc], writes=[k.bps[1 + s]], inc=(kc == 7))
                    sl = slice(half * 512, (half + 1) * 512)
                    S.op('dve', lambda e, s=s, pg=pg, sl=sl: e.tensor_tensor(out=k.gtg[:, s, sl], in0=pg[:], in1=brow[:, sl], op=ALU.add), reads=[k.bps[1 + s], b], writes=[k.b_gtg])
                    S.op('dve', lambda e, s=s, sl=sl: e.tensor_tensor(out=k.gtg[:, s, sl], in0=k.gtg[:, s, sl], in1=grow[:, sl], op=ALU.mult), reads=[b, k.b_gtg], writes=[k.b_gtg])
        S.op('dve', lambda e: e.tensor_tensor(out=k.modT[:], in0=pm[:, 0:72].rearrange("p (c s) -> p c s", s=3), in1=bT[:].unsqueeze(2).broadcast_to([128, 24, 3]), op=ALU.add), reads=[k.bps[0], b], writes=[k.b_mod])
        S.op('dve', lambda e: e.tensor_scalar(out=tmp[:], in0=k.modT[:, 8:16, :], scalar1=1.0, scalar2=None, op0=ALU.add), reads=[k.b_mod], writes=[b])
        S.op('dve', lambda e: e.tensor_tensor(out=k.gsT[:], in0=tmp[:], in1=gpT[:].unsqueeze(2).broadcast_to([128, 8, 3]), op=ALU.mult), reads=[b], writes=[k.b_mod])
        S.barrier()


def rstd_from_ms(k, ms, bms):
    S = k.S
    S.op('act', lambda e: e.activation(out=ms, in_=ms, func=AF.Sqrt, bias=EPS, scale=1.0), reads=[bms], writes=[bms])
    S.op('dve', lambda e: e.reciprocal(ms, ms), reads=[bms], writes=[bms])


def phase_pre(k, l, s):
    S, nc, I = k.S, k.nc, k.I
    with contextlib.ExitStack() as st:
        sb = lambda n, sh, d: st.enter_context(nc.sbuf_tensor(_u(n), list(sh), d))
        xt = [sb("pr_x%d" % i, [128, D], F32) for i in range(3)]
        bx = [Buf() for _ in range(3)]
        xs = [sb("pr_xs%d" % i, [128, D], BF16) for i in range(2)]
        bxs = [Buf() for _ in range(2)]
        junk = sb("pr_junk", [128, D], F32)
        ms = [sb("pr_ms%d" % i, [128, 1], F32) for i in range(3)]
        bms = [Buf() for _ in range(3)]
        hb = [sb("pr_hb%d" % i, [128, 8, 512], BF16) for i in range(2)]
        bhb = [Buf() for _ in range(2)]
        bj = Buf()
        pT = [k.ps[i][:].bitcast(BF16) for i in range(4)]
        it = 0
        blocks = [(0, b0 * 512, 4) for b0 in range(8)] + [(1, NTOK, 2)]
        for bi, (isctx, c0, ntile) in enumerate(blocks):
            mi = 2 if isctx else s
            for t in range(ntile):
                xi = it % 3
                if isctx:
                    src = (I["ctx"] if l == 0 else k.ctxcur)[s, t * 128:(t + 1) * 128, :]
                    rb = k.t_ctx[s].r(t * 128, t * 128 + 128)
                else:
                    src = (I["x"] if l == 0 else k.OUT)[s, c0 + t * 128:c0 + (t + 1) * 128, :]
                    rb = k.t_x[s].r(c0 + t * 128, c0 + t * 128 + 128)
                S.dma('sp', xt[xi][:], src, reads=rb, writes=[bx[xi]])
                S.op('act', lambda e, xi=xi: e.activation(out=junk[:], in_=xt[xi][:], func=AF.Square, scale=1.0 / 32.0, accum_out=ms[xi][:]), reads=[bx[xi]], writes=[bj, bms[xi]])
                rstd_from_ms(k, ms[xi][:], bms[xi])
                si = it % 2
                S.op('pool', lambda e, xi=xi, si=si: e.tensor_scalar(out=xs[si][:], in0=xt[xi][:], scalar1=ms[xi][:, 0:1], scalar2=None, op0=ALU.mult), reads=[bx[xi], bms[xi]], writes=[bxs[si]])
                for kc in range(8):
                    dst = pT[kc // 2][:, (kc % 2) * 512 + t * 128:(kc % 2) * 512 + (t + 1) * 128]
                    S.op('pe', lambda e, dst=dst, si=si, kc=kc: e.transpose(dst, xs[si][:, kc * 128:(kc + 1) * 128], k.ident[:]), reads=[bxs[si], k.b_const], writes=[k.bps[kc // 2]], inc=(kc == 7))
                it += 1
            hi = bi % 2
            nt = ntile * 128
            for kc in range(8):
                src = pT[kc // 2][:, (kc % 2) * 512:(kc % 2) * 512 + nt]
                if kc % 2 == 0:
                    S.op('act', lambda e, src=src, kc=kc, hi=hi, nt=nt, mi=mi: e.activation(out=hb[hi][:, kc, :nt], in_=src, func=AF.Identity, bias=k.modT[:, kc, mi:mi + 1], scale=k.gsT[:, kc, mi:mi + 1]), reads=[k.bps[kc // 2], k.b_mod], writes=[bhb[hi]])
                else:
                    S.op('dve', lambda e, src=src, kc=kc, hi=hi, nt=nt, mi=mi: e.tensor_scalar(out=hb[hi][:, kc, :nt], in0=src, scalar1=k.gsT[:, kc, mi:mi + 1], scalar2=k.modT[:, kc, mi:mi + 1], op0=ALU.mult, op1=ALU.add), reads=[k.bps[kc // 2], k.b_mod], writes=[bhb[hi]])
            S.dma('sp', k.hT[s, :, c0:c0 + nt].rearrange("(kc p) t -> p kc t", p=128), hb[hi][:, :, :nt], reads=[bhb[hi]], writes=k.t_hT[s].r(c0, c0 + nt))
        S.barrier()


def load_h(k, s, dst, c0, n, bdst):
    k.S.dma('sp', dst[:, :, :n], k.hT[s, :, c0:c0 + n].rearrange("(kc p) t -> p kc t", p=128), reads=k.t_hT[s].r(c0, c0 + n), writes=[bdst])


def inproj_fm(k, pbank, n, w, bw, cw, h, bh, hc0=0):
    for kc in range(8):
        k.S.op('pe', lambda e, kc=kc: e.matmul(k.ps[pbank][:, :n], lhsT=w[:, kc, cw:cw + 128], rhs=h[:, kc, hc0:hc0 + n], start=(kc == 0), stop=(kc == 7)),
               reads=[bw, bh], writes=[k.bps[pbank]], inc=(kc == 7))


def inproj_tm(k, pbank, w, bw, cw, h, bh, t0):
    for kc in range(8):
        k.S.op('pe', lambda e, kc=kc: e.matmul(k.ps[pbank][:], lhsT=h[:, kc, t0:t0 + 128], rhs=w[:, kc, cw:cw + 512], start=(kc == 0), stop=(kc == 7)),
               reads=[bw, bh], writes=[k.bps[pbank]], inc=(kc == 7))


def seq_blocks(with_ctx_block=True):
    bl = [(b0 * 512, 4) for b0 in range(8)]
    if with_ctx_block:
        bl.append((NTOK, 2))
    return bl


def phase_c(k, l, with_ctx):
    S, nc, I = k.S, k.nc, k.I
    p = l % 2
    with contextlib.ExitStack() as st:
        sb = lambda n, sh, d: st.enter_context(nc.sbuf_tensor(_u(n), list(sh), d))
        w = sb("c_w", [128, 8, 1536], BF16); bw = Buf()
        wsT = sb("c_ws", [128, 8, 128], BF16)
        gn = sb("c_gn", [128, 512], F32)
        bsT = sb("c_bs", [128, 8], F32)
        bc = Buf()
        hb = [sb("c_hb%d" % i, [128, 8, 512], BF16) for i in range(2)]; bhb = [Buf() for _ in range(2)]
        junk = sb("c_junk", [128, 512], F32); bj = Buf()
        ms = sb("c_ms", [128, 1], F32); bms = Buf()
        vn = sb("c_vn", [128, 512], BF16); bvn = Buf()
        sg = sb("c_sg", [128, 512], F32); bsg = Buf()
        t1 = sb("c_t1", [128, 512], F32); bt1 = Buf()
        yt = sb("c_y", [128, 512], BF16); byt = Buf()
        yb = [sb("c_yb%d" % i, [128, 4, 512], BF16) for i in range(2)]; byb = [Buf() for _ in range(2)]
        load_w(k, l, w[:], COL["c_u"], 1536, bw)
        S.dma('sp', wsT[:], k.ws_bf[p].rearrange("(g s) t -> s g t", s=128), reads=[k.t_w[p]], writes=[bc])
        S.dma('sp', gn[:], I["gmlp_g"][l:l + 1, :].partition_broadcast(128), writes=[bc])
        S.dma('sp', bsT[:], I["gmlp_bsT"][l], writes=[bc])
        pT = [k.ps[4][:].bitcast(BF16), k.ps[5][:].bitcast(BF16)]
        bi = 0
        for s in range(2):
            blocks = seq_blocks(with_ctx)
            load_h(k, s, hb[bi % 2], blocks[0][0], blocks[0][1] * 128, bhb[bi % 2])
            for ib, (c0, ntile) in enumerate(blocks):
                h = hb[bi % 2]; bh = bhb[bi % 2]
                if ib + 1 < len(blocks):
                    load_h(k, s, hb[(bi + 1) % 2], blocks[ib + 1][0], blocks[ib + 1][1] * 128, bhb[(bi + 1) % 2])
                for t in range(ntile):
                    t0 = t * 128
                    inproj_tm(k, 0, w, bw, 512, h, bh, t0)
                    S.op('act', lambda e: e.activation(out=junk[:], in_=k.ps[0][:], func=AF.Square, scale=float(512 ** -0.5), accum_out=ms[:]), reads=[k.bps[0]], writes=[bj, bms])
                    rstd_from_ms(k, ms[:], bms)
                    S.op('dve', lambda e: e.scalar_tensor_tensor(out=vn[:], in0=k.ps[0][:], scalar=ms[:, 0:1], in1=gn[:], op0=ALU.mult, op1=ALU.mult), reads=[k.bps[0], bms, bc], writes=[bvn])
                    for g in range(8):
                        S.op('pe', lambda e, g=g: e.matmul(k.ps[1][:, g * 64:(g + 1) * 64], lhsT=wsT[:, g, :], rhs=vn[:, g * 64:(g + 1) * 64], start=True, stop=True), reads=[bc, bvn], writes=[k.bps[1]], inc=(g == 7))
                    inproj_tm(k, 2, w, bw, 0, h, bh, t0)
                    inproj_tm(k, 3, w, bw, 1024, h, bh, t0)
                    S.op('act', lambda e: e.activation(out=sg[:], in_=k.ps[3][:], func=AF.Silu), reads=[k.bps[3]], writes=[bsg])
                    S.op('dve', lambda e: e.tensor_tensor(out=t1[:].rearrange("p (g c) -> p g c", g=8), in0=k.ps[1][:].rearrange("p (g c) -> p g c", g=8), in1=bsT[:].unsqueeze(2).broadcast_to([128, 8, 64]), op=ALU.add), reads=[k.bps[1], bc], writes=[bt1])
                    S.op('dve', lambda e: e.tensor_tensor(out=t1[:], in0=k.ps[2][:], in1=t1[:], op=ALU.mult), reads=[k.bps[2], bt1], writes=[bt1])
                    S.op('pool', lambda e: e.tensor_tensor(out=yt[:], in0=t1[:], in1=sg[:], op=ALU.mult), reads=[bt1, bsg], writes=[byt])
                    for j in range(4):
                        dst = pT[j // 2][:, (j % 2) * 512 + t0:(j % 2) * 512 + t0 + 128]
                        S.op('pe', lambda e, dst=dst, j=j: e.transpose(dst, yt[:, j * 128:(j + 1) * 128], k.ident[:]), reads=[byt, k.b_const], writes=[k.bps[4 + j // 2]], inc=(j == 3))
                nt = ntile * 128
                yo = yb[bi % 2]
                for j in range(4):
                    src = pT[j // 2][:, (j % 2) * 512:(j % 2) * 512 + nt]
                    S.op('act', lambda e, src=src, j=j, yo=yo, nt=nt: e.copy(yo[:, j, :nt], src), reads=[k.bps[4 + j // 2]], writes=[byb[bi % 2]])
                S.dma('pool', k.yT[s, 2, :, c0:c0 + nt].rearrange("(j p) t -> p j t", p=128), yo[:, :, :nt], reads=[byb[bi % 2]], writes=k.t_yT[s][2].r(c0, c0 + nt))
                bi += 1
        S.barrier()


def phase_b(k, l, with_ctx):
    S, nc, I = k.S, k.nc, k.I
    p = l % 2
    with contextlib.ExitStack() as st:
        sb = lambda n, sh, d: st.enter_context(nc.sbuf_tensor(_u(n), list(sh), d))
        w = sb("b_w", [128, 8, 1024], BF16); bw = Buf()
        fw = sb("b_fw", [128, 4, 128], BF16)
        cc = sb("b_cc", [128, 128], BF16); ssn = sb("b_ss", [128, 128], BF16)
        m12 = sb("b_m12", [128, 2, 4, 128], BF16)
        bc = Buf(); bm = Buf()
        z = sb("b_z", [128, 32, 512], BF16); bz = Buf()
        hb = [sb("b_hb%d" % i, [128, 8, 512], BF16) for i in range(2)]; bhb = [Buf() for _ in range(2)]
        dft = [sb("b_dft%d" % i, [128, 32, 512], BF16) for i in range(2)]; bdft = [Buf() for _ in range(2)]
        Y = sb("b_Y", [128, 4, 512], BF16); bY = Buf()
        sg = sb("b_sg", [128, 4, 256], F32); bsg = Buf()
        yb = [sb("b_yb%d" % i, [128, 4, 256], BF16) for i in range(2)]; byb = [Buf() for _ in range(2)]
        load_w(k, l, w[:], COL["b_x"], 1024, bw)
        S.dma('sp', fw[:], k.fw_bf[p].rearrange("(g c) d -> c g d", c=128), reads=[k.t_w[p]], writes=[bc])
        S.dma('sp', cc[:], I["cc128"], writes=[bc])
        S.dma('sp', ssn[:], I["ssn128"], writes=[bc])
        for i, mat in enumerate((cc, ssn)):
            S.op('pe', lambda e, mat=mat, i=i: e.matmul(k.ps[i][:], lhsT=mat[:], rhs=fw[:].rearrange("c g d -> c (g d)"), start=True, stop=True), reads=[bc], writes=[k.bps[i]])
            S.op('act', lambda e, i=i: e.copy(m12[:, i, :, :].rearrange("c g d -> c (g d)"), k.ps[i][:]), reads=[k.bps[i]], writes=[bm])
        di = 0
        for s in range(2):
            for (tc0, ntile, nkb, isctx) in ([(0, 32, 16, False)] + ([(NTOK, 2, 1, True)] if with_ctx else [])):
                nblk = (ntile + 3) // 4
                for ib in range(nblk):
                    nt = min(4, ntile - ib * 4)
                    load_h(k, s, hb[ib % 2], tc0 + ib * 512, nt * 128, bhb[ib % 2])
                    for t in range(nt):
                        inproj_tm(k, 0, w, bw, 0, hb[ib % 2], bhb[ib % 2], t * 128)
                        S.op('act', lambda e, ib=ib, t=t: e.copy(z[:, ib * 4 + t, :], k.ps[0][:]), reads=[k.bps[0]], writes=[bz])
                for kb in range(nkb):
                    dt_ = dft[di % 2]; bd = bdft[di % 2]
                    if isctx:
                        S.dma('sp', dt_[:, :2, :], I["dft256"].rearrange("(t p) c -> p t c", p=128), writes=[bd])
                    else:
                        for hh in range(2):
                            S.dma('sp', dt_[:, hh * 16:(hh + 1) * 16, :], I["dft"][kb, hh * 2048:(hh + 1) * 2048, :].rearrange("(t p) c -> p t c", p=128), writes=[bd])
                    kc0 = tc0 + kb * 256
                    hi = (kb + 1) % 2
                    load_h(k, s, hb[hi], kc0, 256, bhb[hi])
                    for g in range(4):
                        for t in range(ntile):
                            S.op('pe', lambda e, g=g, t=t, dt_=dt_: e.matmul(k.ps[g][:], lhsT=z[:, t, g * 128:(g + 1) * 128], rhs=dt_[:, t, :], start=(t == 0), stop=(t == ntile - 1)),
                                 reads=[bz, bd], writes=[k.bps[g]], inc=(t == ntile - 1))
                        if g % 2 == 0:
                            S.op('act', lambda e, g=g: e.copy(Y[:, g, :], k.ps[g][:]), reads=[k.bps[g]], writes=[bY])
                        else:
                            S.op('dve', lambda e, g=g: e.tensor_copy(Y[:, g, :], k.ps[g][:]), reads=[k.bps[g]], writes=[bY])
                    for g in range(4):
                        pb_ = 4 + g // 2
                        dst = k.ps[pb_][:, (g % 2) * 256:(g % 2) * 256 + 256]
                        S.op('pe', lambda e, g=g, dst=dst: e.matmul(dst, lhsT=m12[:, 0, g, :], rhs=Y[:, g, 0:256], start=True, stop=False), reads=[bm, bY], writes=[k.bps[pb_]], inc=False)
                        S.op('pe', lambda e, g=g, dst=dst: e.matmul(dst, lhsT=m12[:, 1, g, :], rhs=Y[:, g, 256:512], start=False, stop=True), reads=[bm, bY], writes=[k.bps[pb_]])
                    for g in range(4):
                        pb_ = 6 + g // 2
                        dst = k.ps[pb_][:, (g % 2) * 256:(g % 2) * 256 + 256]
                        for kc in range(8):
                            S.op('pe', lambda e, g=g, dst=dst, kc=kc, hi=hi: e.matmul(dst, lhsT=w[:, kc, 512 + g * 128:512 + (g + 1) * 128], rhs=hb[hi][:, kc, 0:256], start=(kc == 0), stop=(kc == 7)),
                                 reads=[bw, bhb[hi]], writes=[k.bps[pb_]], inc=(kc == 7))
                    yo = yb[di % 2]
                    for gg in range(2):
                        S.op('act', lambda e, gg=gg: e.activation(out=sg[:, 2 * gg:2 * gg + 2, :].rearrange("p a t -> p (a t)"), in_=k.ps[6 + gg][:], func=AF.Silu), reads=[k.bps[6 + gg]], writes=[bsg])
                        S.op('dve', lambda e, gg=gg, yo=yo: e.tensor_tensor(out=yo[:, 2 * gg:2 * gg + 2, :].rearrange("p a t -> p (a t)"), in0=k.ps[4 + gg][:], in1=sg[:, 2 * gg:2 * gg + 2, :].rearrange("p a t -> p (a t)"), op=ALU.mult), reads=[k.bps[4 + gg], bsg], writes=[byb[di % 2]])
                    S.dma('pool', k.yT[s, 1, :, kc0:kc0 + 256].rearrange("(j p) t -> p j t", p=128), yo[:], reads=[byb[di % 2]], writes=k.t_yT[s][1].r(kc0, kc0 + 256))
                    di += 1
        S.barrier()


def na_qblocks(with_ctx):
    bl = [(0, 256, 0, list(range(0, 4)), 0, True), (60 * 64, 256, 0, list(range(28, 32)), 60, True),
          (4 * 64, 256, 1, list(range(0, 6)), 4, True), (56 * 64, 256, 1, list(range(26, 32)), 56, True)]
    for gq in range(1, 7):
        bl.append((gq * 512, 512, 1, list(range(4 * gq - 2, 4 * gq + 6)), 8 * gq, True))
    if with_ctx:
        bl.append((NTOK, 256, 1, [], 0, False))
    return bl


def phase_a(k, l, with_ctx):
    S, nc, I = k.S, k.nc, k.I
    with contextlib.ExitStack() as st:
        sb = lambda n, sh, d: st.enter_context(nc.sbuf_tensor(_u(n), list(sh), d))
        w = sb("a_w", [128, 8, 1024], BF16); bw = Buf()
        kT = sb("a_kT", [128, 4, NT], BF16); bkT = Buf()
        vt = sb("a_v", [128, 34, 512], BF16); bv = Buf()
        csb = [sb("a_cs%d" % i, [128, 2, 512], F32) for i in range(2)]; bcs = [Buf() for _ in range(2)]
        csi = [0]
        rs = sb("a_rs", [128, 128], BF16); eye8 = sb("a_e8", [128, 128], BF16)
        bc = Buf()
        zb = sb("a_zb", [128, 8, ZW * 64], BF16); bzb = Buf()
        hb = [sb("a_hb%d" % i, [128, 8, 512], BF16) for i in range(2)]; bhb = [Buf() for _ in range(2)]
        qp = sb("a_qp", [128, 4, 512], BF16); bqp = Buf()
        qr = sb("a_qr", [128, 4, 512], BF16); bqr = Buf()
        gs = sb("a_gs", [128, 4, 512], BF16); bgs = Buf()
        t1 = sb("a_t1", [128, 512], F32); bt1 = Buf()
        t2 = sb("a_t2", [128, 512], F32); bt2 = Buf()
        kp = sb("a_kp", [128, 512], BF16); bkp = Buf()
        es = [sb("a_es%d" % i, [128, 512], BF16) for i in range(4)]; bes = [Buf() for _ in range(4)]
        rd = sb("a_rd", [128, 512], F32); brd = Buf()
        yo = [sb("a_yo%d" % i, [128, 4, 512], BF16) for i in range(2)]; byo = [Buf() for _ in range(2)]
        S.dma('sp', rs[:], I["rsign"], writes=[bc]); S.dma('sp', eye8[:], I["eye8"], writes=[bc])

        def load_cs(c0, n):
            csi[0] += 1
            i = csi[0] % 2
            S.dma('sp', csb[i][:, 0, :n], I["cosT"][:, c0:c0 + n], writes=[bcs[i]])
            S.dma('sp', csb[i][:, 1, :n], I["sinT"][:, c0:c0 + n], writes=[bcs[i]])

        def rope(psrc, plain_bf, bplain, dst, c0, n, rbank):
            i = csi[0] % 2
            if _DBG.get('nors'):
                rbank = psrc
            else:
                S.op('pe', lambda e: e.matmul(k.ps[rbank][:, :n], lhsT=rs[:], rhs=plain_bf, start=True, stop=True), reads=[bc, bplain], writes=[k.bps[rbank]])
            S.op('dve', lambda e: e.tensor_tensor(out=t1[:, :n], in0=k.ps[psrc][:, :n], in1=csb[i][:, 0, :n], op=ALU.mult), reads=[k.bps[psrc], bcs[i], bplain], writes=[bt1])
            S.op('dve', lambda e: e.tensor_tensor(out=t2[:, :n], in0=k.ps[rbank][:, :n], in1=csb[i][:, 1, :n], op=ALU.mult), reads=[k.bps[rbank], bcs[i]], writes=[bt2])

        hi = 0
        for s in range(2):
            load_w(k, l, w[:], COL["a_k"], 1024, bw)
            for (c0, ntile) in seq_blocks(True):
                n = ntile * 128
                h = hb[hi % 2]; bh = bhb[hi % 2]; hi += 1
                load_h(k, s, h, c0, n, bh)
                if c0 < NTOK:
                    load_cs(c0, n)
                for cp in range(4):
                    inproj_fm(k, 0, n, w, bw, cp * 128, h, bh)
                    if c0 >= NTOK or _DBG.get('norope'):
                        S.op('act', lambda e, cp=cp: e.copy(kT[:, cp, c0:c0 + n], k.ps[0][:, :n]), reads=[k.bps[0]], writes=[bkT])
                    else:
                        S.op('act', lambda e: e.copy(kp[:, :n], k.ps[0][:, :n]), reads=[k.bps[0]], writes=[bkp])
                        rope(0, kp[:, :n], bkp, None, c0, n, 1)
                        if _DBG.get('nopool'):
                            S.op('dve', lambda e, cp=cp: e.tensor_tensor(out=kT[:, cp, c0:c0 + n], in0=t1[:, :n], in1=t2[:, :n], op=ALU.add), reads=[bt1, bt2], writes=[bkT])
                        else:
                            S.op('pool', lambda e, cp=cp: e.tensor_tensor(out=kT[:, cp, c0:c0 + n], in0=t1[:, :n], in1=t2[:, :n], op=ALU.add), reads=[bt1, bt2], writes=[bkT])
                for t in range(ntile):
                    inproj_tm(k, 2, w, bw, 512, h, bh, t * 128)
                    S.op('act', lambda e, t=t: e.copy(vt[:, c0 // 128 + t, :], k.ps[2][:]), reads=[k.bps[2]], writes=[bv])
            if _DBG.get('a1only'):
                continue
            load_w(k, l, w[:, :, 0:512], COL["a_q"], 512, bw)
            load_w(k, l, w[:, :, 512:1024], COL["a_g"], 512, bw)
            cur_kind = None
            for qi, (t0, nq, kind, chunks, q0, use_rope) in enumerate(na_qblocks(with_ctx)):
                if 'qsel' in _DBG and qi not in _DBG['qsel']:
                    continue
                if use_rope and kind != cur_kind:
                    S.dma('sp', zb[:], I["zb"][l, kind], writes=[bzb])
                    cur_kind = kind
                h = hb[hi % 2]; bh = bhb[hi % 2]; hi += 1
                load_h(k, s, h, t0, nq, bh)
                if use_rope:
                    load_cs(t0, nq)
                for cp in range(4):
                    inproj_fm(k, 0, nq, w, bw, cp * 128, h, bh)
                    S.op('act', lambda e, cp=cp: e.copy(qp[:, cp, :nq], k.ps[0][:, :nq]), reads=[k.bps[0]], writes=[bqp])
                    if use_rope:
                        rope(0, qp[:, cp, :nq], bqp, None, t0, nq, 1)
                        S.op('pool', lambda e, cp=cp: e.tensor_tensor(out=qr[:, cp, :nq], in0=t1[:, :nq], in1=t2[:, :nq], op=ALU.add), reads=[bt1, bt2], writes=[bqr])
                    inproj_fm(k, 2, nq, w, bw, 512 + cp * 128, h, bh)
                    S.op('act', lambda e, cp=cp: e.activation(out=gs[:, cp, :nq], in_=k.ps[2][:, :nq], func=AF.Silu), reads=[k.bps[2]], writes=[bgs])
                y = yo[qi % 2]
                ei = 0
                for cp in range(4):
                    klist = [(c, True) for c in chunks] + [(32, False), (33, False)]
                    for j in range(2):
                        pb = 64 * j
                        hd = 2 * cp + j
                        for ic, (c, band) in enumerate(klist):
                            sbk = 3 + (ei % 3)
                            e_t = es[ei % 4]; be = bes[ei % 4]; ei += 1
                            first = (ic == 0); last = (ic == len(klist) - 1)
                            if band:
                                woff = 14 - (2 * c - q0)
                                S.op('pe', lambda e, c=c, cp=cp, pb=pb, sbk=sbk: e.matmul(k.ps[sbk][:, :nq], lhsT=kT[pb:pb + 64, cp, c * 128:(c + 1) * 128], rhs=qr[pb:pb + 64, cp, :nq], start=True, stop=False),
                                     reads=[bkT, bqr], writes=[k.bps[sbk]], inc=False)
                                S.op('pe', lambda e, hd=hd, woff=woff, sbk=sbk: e.matmul(k.ps[sbk][:, :nq], lhsT=eye8[:], rhs=zb[:, hd, woff * 64:woff * 64 + nq], start=False, stop=True),
                                     reads=[bc, bzb], writes=[k.bps[sbk]])
                            else:
                                S.op('pe', lambda e, c=c, cp=cp, pb=pb, sbk=sbk: e.matmul(k.ps[sbk][:, :nq], lhsT=kT[pb:pb + 64, cp, c * 128:(c + 1) * 128], rhs=qp[pb:pb + 64, cp, :nq], start=True, stop=True),
                                     reads=[bkT, bqp], writes=[k.bps[sbk]])
                            S.op('act', lambda e, e_t=e_t, sbk=sbk: e.activation(out=e_t[:, :nq], in_=k.ps[sbk][:, :nq], func=AF.Exp, scale=0.125), reads=[k.bps[sbk]], writes=[be])
                            S.op('pe', lambda e, c=c, hd=hd, pb=pb, e_t=e_t, first=first, last=last: e.matmul(k.ps[6][pb:pb + 64, :nq], lhsT=vt[:, c, hd * 64:(hd + 1) * 64], rhs=e_t[:, :nq], start=first, stop=last),
                                 reads=[bv, be], writes=[k.bps[6]], inc=False)
                            S.op('pe', lambda e, pb=pb, e_t=e_t, first=first, last=last: e.matmul(k.ps[7][pb:pb + 64, :nq], lhsT=k.ones_bf[:, 0:64], rhs=e_t[:, :nq], start=first, stop=last),
                                 reads=[k.b_const, be], writes=[k.bps[7]])
                    S.op('act', lambda e: e.activation(out=rd[:, :nq], in_=k.ps[7][:, :nq], func=AF.Ln), reads=[k.bps[7]], writes=[brd])
                    S.op('act', lambda e: e.activation(out=rd[:, :nq], in_=rd[:, :nq], func=AF.Exp, scale=-1.0), reads=[brd], writes=[brd])
                    S.op('dve', lambda e: e.tensor_tensor(out=rd[:, :nq], in0=k.ps[6][:, :nq], in1=rd[:, :nq], op=ALU.mult), reads=[k.bps[6], brd], writes=[brd])
                    S.op('pool', lambda e, cp=cp, y=y: e.tensor_tensor(out=y[:, cp, :nq], in0=rd[:, :nq], in1=gs[:, cp, :nq], op=ALU.mult), reads=[brd, bgs], writes=[byo[qi % 2]])
                S.dma('pool', k.yT[s, 0, :, t0:t0 + nq].rearrange("(j p) t -> p j t", p=128), y[:, :, :nq], reads=[byo[qi % 2]], writes=k.t_yT[s][0].r(t0, t0 + nq))
        S.barrier()


def phase_d(k, l, with_ctx):
    S, nc, I = k.S, k.nc, k.I
    with contextlib.ExitStack() as st:
        sb = lambda n, sh, d: st.enter_context(nc.sbuf_tensor(_u(n), list(sh), d))
        w = sb("d_w", [128, 8, 2048], BF16); bw = Buf()
        gn = sb("d_gn", [128, 4], F32)
        trif = sb("d_trif", [128, 128], I32); trib = sb("d_trib", [128, 128], I32); bones = sb("d_bones", [128, 128], BF16)
        bc = Buf()
        hb = [sb("d_hb%d" % i, [128, 8, 512], BF16) for i in range(2)]; bhb = [Buf() for _ in range(2)]
        sg = sb("d_sg", [128, 512], F32); bsg = Buf()
        logf = sb("d_logf", [128, 512], F32); blf = Buf()
        Pp = sb("d_P", [128, 4, 516], F32); bP = Buf()
        kT = sb("d_kT", [128, 4, 512], BF16); bkT = Buf()
        qT = sb("d_qT", [128, 4, 512], BF16); bqT = Buf()
        itm = sb("d_itm", [128, 512], BF16); bitm = Buf()
        negr = sb("d_negr", [128, 4, 8], F32); bnr = Buf()
        dq = sb("d_dq", [128, 128], F32); bdq = Buf()
        eq = sb("d_eq", [128, 128], F32); beq = Buf()
        ek = [sb("d_ek%d" % i, [128, 4, 128], F32) for i in range(4)]; bek = [Buf() for _ in range(4)]
        qtl = sb("d_qtl", [128, 2, 4, 128], BF16); bqtl = Buf()
        ktl = sb("d_ktl", [128, 4, 4, 128], BF16); bktl = Buf()
        qh = sb("d_qh", [128, 4, 128], BF16); bqh = Buf()
        khT = sb("d_khT", [128, 4, 128], BF16); bkhT = Buf()
        khtm = sb("d_khtm", [128, 512], BF16); bkhtm = Buf()
        dec = sb("d_dec", [128, 4], F32); bdec = Buf()
        At = sb("d_At", [128, 8, 128], BF16); bAt = Buf()
        St = sb("d_S", [128, 2, 4, 64], F32); bS = [Buf(), Buf()]
        Sbf = sb("d_Sbf", [128, 4, 128], BF16); bSbf = Buf()
        ob = [sb("d_ob%d" % i, [128, 4, 512], F32) for i in range(2)]; bob = [Buf() for _ in range(2)]
        sq = sb("d_sq", [128, 512], BF16); bsq = Buf()
        rst = sb("d_rst", [128, 512], F32); brst = Buf()
        gl = sb("d_gl", [128, 512], F32); bgl = Buf()
        yb = [sb("d_yb%d" % i, [128, 4, 512], BF16) for i in range(2)]; byb = [Buf() for _ in range(2)]
        S.dma('sp', gn[:], I["hgrn_gT"][l], writes=[bc])
        S.dma('sp', trif[:], I["trif"], writes=[bc]); S.dma('sp', trib[:], I["trib"], writes=[bc]); S.dma('sp', bones[:], I["bones"], writes=[bc])
        S.op('dve', lambda e: e.memset(Pp[:], 0.0), writes=[bP])
        S.op('pool', lambda e: e.memset(qtl[:], 0.0), writes=[bqtl])
        S.op('pool', lambda e: e.memset(Sbf[:], 0.0), writes=[bSbf])
        ptr = k.ps[5][:].bitcast(BF16)
        hi = 0
        for s in range(2):
            for d in range(2):
                S.op('dve', lambda e, d=d: e.memset(St[:, d, :, :], 0.0), writes=[bS[d]])
            for (base, nblk_tiles) in ((NTOK, [2]), (0, [4] * 8)):
                isctx = base >= NTOK
                for d in range(2):
                    _DBG['dpass'] = _DBG.get('dpass', 0) + 1
                    if 'dmax' in _DBG and _DBG['dpass'] > _DBG['dmax']:
                        continue
                    sgn = 1.0 if d == 0 else -1.0
                    final = (d == 1)
                    load_w(k, l, w[:, :, 0:512], COL["d_q"], 512, bw)
                    load_w(k, l, w[:, :, 512:1024], COL["d_ff"] if d == 0 else COL["d_fb"], 512, bw)
                    load_w(k, l, w[:, :, 1024:1536], COL["d_i"], 512, bw)
                    if final:
                        load_w(k, l, w[:, :, 1536:2048], COL["d_g"], 512, bw)
                    for i in range(4):
                        S.op('pool', lambda e, i=i: e.memset(ek[i][:], 0.0), writes=[bek[i]])
                    S.op('pool', lambda e: e.memset(At[:], 0.0), writes=[bAt])
                    S.op('act', lambda e, d=d: e.copy(Sbf[0:64, :, 0:64], St[0:64, d, :, :]), reads=[bS[d]], writes=[bSbf]); S.op('act', lambda e, d=d: e.copy(Sbf[64:128, :, 64:128], St[64:128, d, :, :]), reads=[bS[d]], writes=[bSbf])
                    mask = trif if d == 0 else trib
                    blist = list(range(len(nblk_tiles)))
                    if d == 1:
                        blist = blist[::-1]
                    for ib in blist:
                        ntile = nblk_tiles[ib]
                        n = ntile * 128
                        c0 = base + ib * 512
                        h = hb[hi % 2]; bh = bhb[hi % 2]
                        o_b = ob[hi % 2]; bo = bob[hi % 2]; y_b = yb[hi % 2]; by_ = byb[hi % 2]
                        hi += 1
                        load_h(k, s, h, c0, n, bh)
                        if final:
                            S.dma('sp', o_b[:, :, :n], k.ofT[s, :, c0:c0 + n].rearrange("(j p) t -> p j t", p=128), reads=k.t_ofT[s].r(c0, c0 + n), writes=[bo])
                        for ft in range(4):
                            inproj_fm(k, 0, n, w, bw, 512 + ft * 128, h, bh)
                            S.op('act', lambda e: e.activation(out=sg[:, :n], in_=k.ps[0][:, :n], func=AF.Sigmoid), reads=[k.bps[0]], writes=[bsg])
                            S.op('dve', lambda e, ft=ft, d=d: e.tensor_scalar(out=sg[:, :n], in0=sg[:, :n], scalar1=k.oml[:, d * 4 + ft, l:l + 1], scalar2=k.lb[:, d * 4 + ft, l:l + 1], op0=ALU.mult, op1=ALU.add), reads=[bsg, k.b_lb], writes=[bsg])
                            S.op('dve', lambda e: e.tensor_scalar(out=sg[:, :n], in0=sg[:, :n], scalar1=1e-30, scalar2=None, op0=ALU.max), reads=[bsg], writes=[bsg])
                            S.op('act', lambda e: e.activation(out=logf[:, :n], in_=sg[:, :n], func=AF.Ln), reads=[bsg], writes=[blf])
                            S.op('dve', lambda e, ft=ft: e.tensor_scalar(out=kT[:, ft, :n], in0=sg[:, :n], scalar1=-1.0, scalar2=1.0, op0=ALU.mult, op1=ALU.add), reads=[bsg], writes=[bkT])
                            S.op('dve', lambda e, ft=ft: e.tensor_tensor_scan(out=Pp[:, ft, 1:1 + n], data0=k.ones_f[:, :n], data1=logf[:, :n], initial=0.0, op0=ALU.mult, op1=ALU.add), reads=[blf, k.b_const], writes=[bP])
                            inproj_fm(k, 1, n, w, bw, ft * 128, h, bh)
                            S.op('act', lambda e, ft=ft: e.copy(qT[:, ft, :n], k.ps[1][:, :n]), reads=[k.bps[1]], writes=[bqT])
                        tl = list(range(ntile))
                        if d == 1:
                            tl = tl[::-1]
                        for t in tl:
                            t0 = t * 128
                            xo = t0 + 1 if d == 0 else t0
                            if _DBG.get('dstage', 9) < 1:
                                continue
                            inproj_tm(k, 2, w, bw, 1024, h, bh, t0)
                            S.op('act', lambda e: e.copy(itm[:], k.ps[2][:]), reads=[k.bps[2]], writes=[bitm])
                            S.op('dve', lambda e, t0=t0: e.tensor_scalar(out=negr[:, :, 0:4], in0=Pp[:, :, t0 + 16:t0 + 113:32], scalar1=-1.0, scalar2=None, op0=ALU.mult), reads=[bP], writes=[bnr])
                            S.op('dve', lambda e, t0=t0: e.tensor_scalar(out=negr[:, :, 4:6], in0=Pp[:, :, t0:t0 + 129:128], scalar1=-1.0, scalar2=None, op0=ALU.mult), reads=[bP], writes=[bnr])
                            for ft in range(4):
                                X = Pp[:, ft, xo:xo + 128]
                                rpos = lambda i, ft=ft, t0=t0: Pp[:, ft, t0 + 16 + 32 * i:t0 + 17 + 32 * i]
                                rneg = lambda i, ft=ft: negr[:, ft, i:i + 1]
                                B0p = Pp[:, ft, t0:t0 + 1]; B1p = Pp[:, ft, t0 + 128:t0 + 129]
                                B0n = negr[:, ft, 4:5]; B1n = negr[:, ft, 5:6]
                                S.op('dve', lambda e, X=X, ft=ft, t0=t0: e.tensor_tensor(out=dq[:].rearrange("p (i c) -> p i c", i=4), in0=X.rearrange("p (i c) -> p i c", i=4), in1=Pp[:, ft, t0 + 16:t0 + 113:32].unsqueeze(2).broadcast_to([128, 4, 32]), op=ALU.subtract), reads=[bP], writes=[bdq])
                                S.op('act', lambda e: e.activation(out=eq[:], in_=dq[:], func=AF.Exp, scale=sgn), reads=[bdq], writes=[beq])
                                S.op('dve', lambda e, ft=ft, t0=t0: e.tensor_tensor(out=qtl[0:64, 0, ft, :], in0=eq[0:64, :], in1=qT[0:64, ft, t0:t0 + 128], op=ALU.mult), reads=[beq, bqT], writes=[bqtl]); S.op('dve', lambda e, ft=ft, t0=t0: e.tensor_tensor(out=qtl[64:128, 1, ft, :], in0=eq[64:128, :], in1=qT[64:128, ft, t0:t0 + 128], op=ALU.mult), reads=[beq, bqT], writes=[bqtl])
                                for i in range(4):
                                    lo, hi_ = (0, 32 * (i + 1)) if d == 0 else (32 * i, 128)
                                    bias = rpos(i) if d == 0 else rneg(i)
                                    S.op('act', lambda e, ft=ft, i=i, lo=lo, hi_=hi_, bias=bias, xo=xo: e.activation(out=ek[ft][:, i, lo:hi_], in_=Pp[:, ft, xo + lo:xo + hi_], func=AF.Exp, scale=-sgn, bias=bias), reads=[bP, bnr], writes=[bek[ft]])
                                S.op('dve', lambda e, ft=ft, t0=t0: e.tensor_tensor(out=ktl[:, ft, :, :], in0=ek[ft][:], in1=kT[:, ft, t0:t0 + 128].unsqueeze(1).broadcast_to([128, 4, 128]), op=ALU.mult), reads=[bek[ft], bkT], writes=[bktl])
                                bq_ = B0n if d == 0 else B1p
                                S.op('act', lambda e, X=X, bq_=bq_: e.activation(out=eq[:], in_=X, func=AF.Exp, scale=sgn, bias=bq_), reads=[bP, bnr], writes=[beq])
                                S.op('dve', lambda e, ft=ft, t0=t0: e.tensor_tensor(out=qh[:, ft, :], in0=eq[:], in1=qT[:, ft, t0:t0 + 128], op=ALU.mult), reads=[beq, bqT], writes=[bqh])
                                bk_ = B1p if d == 0 else B0n
                                S.op('act', lambda e, X=X, bk_=bk_: e.activation(out=dq[:], in_=X, func=AF.Exp, scale=-sgn, bias=bk_), reads=[bP, bnr], writes=[bdq])
                                S.op('dve', lambda e, ft=ft, t0=t0: e.tensor_tensor(out=khT[:, ft, :], in0=dq[:], in1=kT[:, ft, t0:t0 + 128], op=ALU.mult), reads=[bdq, bkT], writes=[bkhT])
                                if _DBG.get('dstage', 9) >= 2:
                                    S.op('pe', lambda e, ft=ft: e.transpose(ptr[:, ft * 128:(ft + 1) * 128], khT[:, ft, :], k.ident[:]), reads=[bkhT, k.b_const], writes=[k.bps[5]])
                                S.op('act', lambda e, ft=ft, B1p=B1p, B0n=B0n: e.activation(out=dec[:, ft:ft + 1], in_=B1p, func=AF.Exp, scale=1.0, bias=B0n), reads=[bP, bnr], writes=[bdec])
                            if _DBG.get('dstage', 9) < 2:
                                continue
                            S.op('dve', lambda e: e.tensor_copy(khtm[:], ptr[:, 0:512]), reads=[k.bps[5]], writes=[bkhtm])
                            if _DBG.get('dstage', 9) < 3:
                                continue
                            for hd in range(8):
                                cp, pb = hd // 2, 64 * (hd % 2)
                                sbk = 3 + hd // 4
                                for i in range(4):
                                    dst = k.ps[sbk][:, (hd % 4) * 128 + 32 * i:(hd % 4) * 128 + 32 * i + 32]
                                    S.op('pe', lambda e, dst=dst, cp=cp, hd=hd, i=i: e.matmul(dst, lhsT=ktl[:, cp, i, :], rhs=qtl[:, hd % 2, cp, 32 * i:32 * i + 32], start=True, stop=True),
                                         reads=[bktl, bqtl], writes=[k.bps[sbk]], inc=(hd % 4 == 3 and i == 3))
                            for half in range(0 if not _DBG.get('nomask') else 2, 2):
                                S.op('dve', lambda e, half=half: e.copy_predicated(out=At[:, 4 * half:4 * half + 4, :], mask=mask[:].unsqueeze(1).broadcast_to([128, 4, 128]), data=k.ps[3 + half][:].rearrange("p (h t) -> p h t", h=4)), reads=[k.bps[3 + half], bc, bAt], writes=[bAt])
                            if _DBG.get('dstage', 9) < 4:
                                continue
                            for cp in range(4):
                                for j in range(2):
                                    hd = 2 * cp + j
                                    S.op('pe', lambda e, hd=hd, j=j, cp=cp: e.matmul(k.ps[6][64 * j:64 * j + 64, cp * 128:(cp + 1) * 128], lhsT=itm[:, hd * 64:(hd + 1) * 64], rhs=At[:, hd, :], start=True, stop=False), reads=[bitm, bAt], writes=[k.bps[6]], inc=False)
                                S.op('pe', lambda e, cp=cp: e.matmul(k.ps[6][:, cp * 128:(cp + 1) * 128], lhsT=Sbf[:, cp, :], rhs=qh[:, cp, :], start=False, stop=True), reads=[bSbf, bqh], writes=[k.bps[6]], inc=(cp == 3))
                            if _DBG.get('dstage', 9) < 5:
                                continue
                            for cp in range(4):
                                S.op('pe', lambda e, cp=cp: e.matmul(k.ps[7][:, cp * 128:(cp + 1) * 128], lhsT=khtm[:, cp * 128:(cp + 1) * 128], rhs=itm[:, cp * 128:(cp + 1) * 128], start=True, stop=True), reads=[bkhtm, bitm], writes=[k.bps[7]], inc=(cp == 3))
                            for cp in range(4):
                                for j in range(2):
                                    pb = 64 * j
                                    S.op('dve', lambda e, cp=cp, pb=pb, j=j, d=d: e.scalar_tensor_tensor(out=St[pb:pb + 64, d, cp, :], in0=St[pb:pb + 64, d, cp, :], scalar=dec[pb:pb + 64, cp:cp + 1], in1=k.ps[7][pb:pb + 64, cp * 128 + 64 * j:cp * 128 + 64 * j + 64], op0=ALU.mult, op1=ALU.add),
                                         reads=[bS[d], bdec, k.bps[7]], writes=[bS[d]])
                            S.op('act', lambda e, d=d: e.copy(Sbf[0:64, :, 0:64], St[0:64, d, :, :]), reads=[bS[d]], writes=[bSbf]); S.op('act', lambda e, d=d: e.copy(Sbf[64:128, :, 64:128], St[64:128, d, :, :]), reads=[bS[d]], writes=[bSbf])
                            if _DBG.get('dstage', 9) < 6:
                                continue
                            if not final:
                                S.op('act', lambda e, t0=t0, o_b=o_b: e.copy(o_b[:, :, t0:t0 + 128], k.ps[6][:].rearrange("p (c t) -> p c t", c=4)), reads=[k.bps[6]], writes=[bo])
                            else:
                                S.op('dve', lambda e, t0=t0, o_b=o_b: e.tensor_tensor(out=o_b[:, :, t0:t0 + 128], in0=k.ps[6][:].rearrange("p (c t) -> p c t", c=4), in1=o_b[:, :, t0:t0 + 128], op=ALU.add), reads=[k.bps[6], bo], writes=[bo])
                        if not final:
                            S.dma('pool', k.ofT[s, :, c0:c0 + n].rearrange("(j p) t -> p j t", p=128), o_b[:, :, :n], reads=[bo], writes=k.t_ofT[s].r(c0, c0 + n))
                        elif (not isctx) or with_ctx:
                            for cp in range(4):
                                S.op('act', lambda e, cp=cp, o_b=o_b: e.activation(out=sq[:, :n], in_=o_b[:, cp, :n], func=AF.Square), reads=[bo], writes=[bsq])
                                S.op('pe', lambda e: e.matmul(k.ps[0][:, :n], lhsT=bones[:], rhs=sq[:, :n], start=True, stop=True), reads=[bc, bsq], writes=[k.bps[0]])
                                S.op('act', lambda e: e.activation(out=rst[:, :n], in_=k.ps[0][:, :n], func=AF.Ln, scale=1.0 / 64.0, bias=EPS), reads=[k.bps[0]], writes=[brst])
                                S.op('act', lambda e: e.activation(out=rst[:, :n], in_=rst[:, :n], func=AF.Exp, scale=-0.5), reads=[brst], writes=[brst])
                                inproj_fm(k, 1, n, w, bw, 1536 + cp * 128, h, bh)
                                S.op('act', lambda e: e.activation(out=gl[:, :n], in_=k.ps[1][:, :n], func=AF.Silu), reads=[k.bps[1]], writes=[bgl])
                                S.op('dve', lambda e, cp=cp, o_b=o_b: e.scalar_tensor_tensor(out=rst[:, :n], in0=o_b[:, cp, :n], scalar=gn[:, cp:cp + 1], in1=rst[:, :n], op0=ALU.mult, op1=ALU.mult), reads=[bo, bc, brst], writes=[brst])
                                S.op('pool', lambda e, cp=cp, y_b=y_b: e.tensor_tensor(out=y_b[:, cp, :n], in0=rst[:, :n], in1=gl[:, :n], op=ALU.mult), reads=[brst, bgl], writes=[by_])
                            S.dma('pool', k.yT[s, 3, :, c0:c0 + n].rearrange("(j p) t -> p j t", p=128), y_b[:, :, :n], reads=[by_], writes=k.t_yT[s][3].r(c0, c0 + n))
        S.barrier()


def phase_m(k, l, with_ctx):
    S, nc, I = k.S, k.nc, k.I
    p = l % 2
    with contextlib.ExitStack() as st:
        sb = lambda n, sh, d: st.enter_context(nc.sbuf_tensor(_u(n), list(sh), d))
        wg = sb("m_wg", [128, 8, 4096], BF16); bw = Buf()
        wbr = sb("m_wbr", [128, 16, D], BF16)
        wo = sb("m_wo", [128, 8, D], BF16)
        bwc = Buf()
        hb = [sb("m_hb%d" % i, [128, 8, 256], BF16) for i in range(2)]; bhb = [Buf() for _ in range(2)]
        yb = [sb("m_yb%d" % i, [128, 16, 256], BF16) for i in range(2)]; byb = [Buf() for _ in range(2)]
        sgt = [sb("m_sg%d" % i, [128, 256], F32) for i in range(2)]; bsg = [Buf() for _ in range(2)]
        tmp = [sb("m_tmp%d" % i, [128, 256], F32) for i in range(2)]; btmp = [Buf() for _ in range(2)]
        macc = sb("m_acc", [128, 256], F32); bacc = Buf()
        mT = sb("m_mT", [128, 8, 256], BF16); bmT = Buf()
        xt = [sb("m_x%d" % i, [128, D], F32) for i in range(2)]; bx = [Buf() for _ in range(2)]
        tt = sb("m_tt", [128, D], F32); btt = Buf()
        junk = sb("m_junk", [128, 512], F32); bj = Buf()
        ms = sb("m_ms", [128, 2], F32); bms = Buf()
        load_w(k, l, wg[:], COL["gate"], 4096, bw)
        S.dma('sp', wbr[:], k.w_br_bf[p].rearrange("(a p) c -> p a c", p=128), reads=[k.t_w[p]], writes=[bwc])
        S.dma('sp', wo[:], k.w_out_bf[p].rearrange("(a p) c -> p a c", p=128), reads=[k.t_w[p]], writes=[bwc])
        bi = 0
        xi = 0
        gi = 0
        for s in range(2):
            blocks = [(c0, False) for c0 in range(0, NTOK, 256)] + ([(NTOK, True)] if with_ctx else [])
            for (c0, isctx) in blocks:
                mi = 2 if isctx else s
                h = hb[bi % 2]; bh = bhb[bi % 2]; y = yb[bi % 2]; by_ = byb[bi % 2]
                bi += 1
                load_h(k, s, h, c0, 256, bh)
                for r in range(4):
                    S.dma('sp', y[:, 4 * r:4 * r + 4, :], k.yT[s, r, :, c0:c0 + 256].rearrange("(j p) t -> p j t", p=128), reads=k.t_yT[s][r].r(c0, c0 + 256), writes=[by_])
                for fc in range(8):
                    for r in range(4):
                        gb = gi % 2; gi += 1
                        for kc in range(8):
                            S.op('pe', lambda e, kc=kc, r=r, fc=fc, gb=gb: e.matmul(k.ps[gb][:, :256], lhsT=wg[:, kc, r * 1024 + fc * 128:r * 1024 + (fc + 1) * 128], rhs=h[:, kc, :], start=(kc == 0), stop=(kc == 7)),
                                 reads=[bw, bh], writes=[k.bps[gb]], inc=(kc == 7))
                        S.op('act', lambda e, gb=gb: e.activation(out=sgt[gb][:], in_=k.ps[gb][:, :256], func=AF.Sigmoid), reads=[k.bps[gb]], writes=[bsg[gb]])
                        for kc in range(4):
                            S.op('pe', lambda e, kc=kc, r=r, fc=fc, gb=gb: e.matmul(k.ps[2 + gb][:, :256], lhsT=wbr[:, 4 * r + kc, fc * 128:(fc + 1) * 128], rhs=y[:, 4 * r + kc, :], start=(kc == 0), stop=(kc == 3)),
                                 reads=[bwc, by_], writes=[k.bps[2 + gb]], inc=(kc == 3))
                        if r == 0:
                            S.op('dve', lambda e, gb=gb: e.tensor_tensor(out=macc[:], in0=k.ps[2 + gb][:, :256], in1=sgt[gb][:], op=ALU.mult), reads=[k.bps[2 + gb], bsg[gb]], writes=[bacc])
                        else:
                            S.op('dve', lambda e, gb=gb: e.tensor_tensor(out=tmp[gb][:], in0=k.ps[2 + gb][:, :256], in1=sgt[gb][:], op=ALU.mult), reads=[k.bps[2 + gb], bsg[gb]], writes=[btmp[gb]])
                            S.op('pool', lambda e, gb=gb: e.tensor_tensor(out=macc[:], in0=macc[:], in1=tmp[gb][:], op=ALU.add), reads=[bacc, btmp[gb]], writes=[bacc])
                    S.op('pool', lambda e, fc=fc: e.tensor_copy(mT[:, fc, :], macc[:]), reads=[bacc], writes=[bmT])
                for t in range(2):
                    tok0 = c0 + t * 128
                    x = xt[xi % 2]; bxx = bx[xi % 2]; xi += 1
                    if isctx:
                        src = (I["ctx"] if l == 0 else k.ctxcur)[s, t * 128:(t + 1) * 128, :]
                        dstd = k.ctxcur[s, t * 128:(t + 1) * 128, :]
                        tb = k.t_ctx[s].r(t * 128, t * 128 + 128)
                    else:
                        src = (I["x"] if l == 0 else k.OUT)[s, tok0:tok0 + 128, :]
                        dstd = k.OUT[s, tok0:tok0 + 128, :]
                        tb = k.t_x[s].r(tok0, tok0 + 128)
                    S.dma('sp', x[:], src, reads=tb, writes=[bxx])
                    for half in range(2):
                        for kc in range(8):
                            S.op('pe', lambda e, kc=kc, half=half, t=t: e.matmul(k.ps[4 + half][:], lhsT=mT[:, kc, t * 128:(t + 1) * 128], rhs=wo[:, kc, half * 512:(half + 1) * 512], start=(kc == 0), stop=(kc == 7)),
                                 reads=[bmT, bwc], writes=[k.bps[4 + half]], inc=(kc == 7))
                        S.op('act', lambda e, half=half: e.activation(out=junk[:], in_=k.ps[4 + half][:], func=AF.Square, scale=1.0 / 32.0, accum_out=ms[:, half:half + 1]), reads=[k.bps[4 + half]], writes=[bj, bms])
                    S.op('dve', lambda e: e.tensor_tensor(out=ms[:, 0:1], in0=ms[:, 0:1], in1=ms[:, 1:2], op=ALU.add), reads=[bms], writes=[bms])
                    rstd_from_ms(k, ms[:, 0:1], bms)
                    for half in range(2):
                        sl = slice(half * 512, (half + 1) * 512)
                        S.op('dve', lambda e, half=half, sl=sl, mi=mi: e.scalar_tensor_tensor(out=tt[:, sl], in0=k.ps[4 + half][:], scalar=ms[:, 0:1], in1=k.gtg[:, mi, sl], op0=ALU.mult, op1=ALU.mult), reads=[k.bps[4 + half], bms, k.b_gtg], writes=[btt])
                    S.op('pool', lambda e, x=x: e.tensor_tensor(out=x[:], in0=x[:], in1=tt[:], op=ALU.add), reads=[bxx, btt], writes=[bxx])
                    S.dma('pool', dstd, x[:], reads=[bxx], writes=tb)
        S.barrier()


_BF = ml_dtypes.bfloat16
_CONST = {}


def _constants():
    if _CONST:
        return _CONST
    c = {}
    c["ident"] = np.eye(128, dtype=np.float32).astype(_BF)
    c["eye8"] = (8.0 * np.eye(128, dtype=np.float32)).astype(_BF)
    rm = np.zeros((128, 128), np.float32)
    for dp in range(128):
        dd = dp % 64
        if (dd % 32) < 16:
            rm[dp, dp + 16] = -1.0
        else:
            rm[dp, dp - 16] = 1.0
    c["rsign"] = np.ascontiguousarray(rm.T).astype(_BF)
    t = np.arange(NTOK)
    pos = np.stack([t // 64, t % 64], 0).astype(np.float64)
    inv = 10000.0 ** (-np.arange(16, dtype=np.float64) * 2.0 / 32.0)
    d = np.arange(128) % 64
    ang = pos[d // 32, :] * inv[d % 16][:, None]
    c["cosT"] = np.cos(ang).astype(np.float32)
    c["sinT"] = np.sin(ang).astype(np.float32)
    s_, t_ = np.meshgrid(np.arange(128), np.arange(128), indexing="ij")
    c["trif"] = (s_ <= t_).astype(np.int32)
    c["trib"] = (s_ >= t_).astype(np.int32)
    c["bones"] = ((s_ // 64) == (t_ // 64)).astype(np.float32).astype(_BF)
    n = np.arange(NTOK, dtype=np.int64)
    m = (n[:, None] * n[None, :]) % NTOK
    sc = 1.0 / np.sqrt(NTOK * 128.0)
    angm = (2.0 * np.pi / NTOK) * m.astype(np.float32)
    cs = (np.cos(angm) * sc).astype(np.float32)
    sn = (np.sin(angm) * sc).astype(np.float32)
    dft = np.empty((16, NTOK, 512), dtype=_BF)
    for kb in range(16):
        dft[kb, :, 0:256] = cs[:, kb * 256:(kb + 1) * 256].astype(_BF)
        dft[kb, :, 256:512] = sn[:, kb * 256:(kb + 1) * 256].astype(_BF)
    c["dft"] = dft
    n2 = np.arange(LCTX, dtype=np.int64)
    a2 = (2.0 * np.pi / LCTX) * ((n2[:, None] * n2[None, :]) % LCTX)
    sc2 = 1.0 / np.sqrt(LCTX * 128.0)
    c["dft256"] = np.concatenate([np.cos(a2) * sc2, np.sin(a2) * sc2], 1).astype(np.float32).astype(_BF)
    n3 = np.arange(128, dtype=np.int64)
    a3 = (2.0 * np.pi / 128) * ((n3[:, None] * n3[None, :]) % 128)
    c["cc128"] = np.cos(a3).astype(np.float32).astype(_BF)
    c["ssn128"] = (-np.sin(a3)).astype(np.float32).astype(_BF)
    _CONST.update(c)
    return _CONST


def _zb_tables(rpb):
    L = rpb.shape[0]
    e = np.arange(2)[:, None, None, None]
    kc = np.arange(64)[None, :, None, None]
    w = np.arange(ZW)[None, None, :, None]
    qc = np.arange(64)[None, None, None, :]
    dr = 14 - w + e + 0 * kc + 0 * qc
    cs = np.clip(qc - 8, 0, 48)
    col_ok = (kc >= cs) & (kc < cs + 16)
    cidx = np.clip(kc - qc + 15, 0, 30) + 0 * dr
    out = np.empty((L, 2, 128, 8, ZW * 64), dtype=_BF)
    for kind in range(2):
        row_ok = (dr >= -7) & (dr <= 7)
        if kind == 1:
            row_ok = row_ok & (dr >= -4) & (dr < 4)
        ok = (row_ok & col_ok)
        ridx = np.clip(dr + 7, 0, 14)
        for l in range(L):
            g = rpb[l][:, ridx, cidx]
            g = np.where(ok[None], g, np.float32(NEG))
            out[l, kind] = g.transpose(1, 2, 0, 3, 4).reshape(128, 8, ZW * 64).astype(_BF)
    return out


def make_in_maps(inputs, nlayers=DEPTH, cores=range(8)):
    f = lambda a: np.ascontiguousarray(np.asarray(a, dtype=np.float32))
    c = _constants()
    L = nlayers
    shared = dict(c)
    shared["w_ada"] = f(inputs["w_ada"][:L])
    shared["b_ada"] = f(inputs["b_ada"][:L])
    shared["b_adaT"] = f(np.asarray(inputs["b_ada"][:L]).reshape(L, 24, 128).transpose(0, 2, 1))
    shared["g_preT"] = f(np.asarray(inputs["g_pre"][:L]).reshape(L, 8, 128).transpose(0, 2, 1))
    shared["g_post"] = f(inputs["g_post"][:L])
    shared["w_in"] = f(inputs["w_in"][:L])
    shared["zb"] = _zb_tables(np.asarray(inputs["na_rpb"][:L], dtype=np.float32))
    shared["fnet_w"] = f(np.asarray(inputs["fnet_w"][:L]).reshape(L, 512, 128))
    shared["gmlp_g"] = f(inputs["gmlp_norm_g"][:L])
    shared["gmlp_wsT"] = f(np.asarray(inputs["gmlp_ws"][:L]).transpose(0, 1, 3, 2).reshape(L, 1024, 128))
    shared["gmlp_bsT"] = f(np.asarray(inputs["gmlp_bs"][:L]).transpose(0, 2, 1))
    lg = np.asarray(inputs["hgrn_lb_logits"], dtype=np.float32)
    shared["lbT"] = f(lg.reshape(DEPTH, 2, 4, 128).transpose(3, 1, 2, 0).reshape(128, 8, DEPTH))
    shared["hgrn_gT"] = f(np.asarray(inputs["hgrn_norm_g"][:L]).reshape(L, 4, 128).transpose(0, 2, 1))
    shared["w_branch"] = f(np.asarray(inputs["w_branch"][:L]).reshape(L, 2048, D))
    shared["w_out"] = f(inputs["w_out"][:L])
    x = np.asarray(inputs["x"]); ctx = np.asarray(inputs["ctx"]); cc = np.asarray(inputs["c"]); c_ctx = np.asarray(inputs["c_ctx"])
    maps = []
    for ci in cores:
        m = dict(shared)
        m["x"] = f(x[2 * ci:2 * ci + 2])
        m["ctx"] = f(ctx[2 * ci:2 * ci + 2])
        c3 = np.stack([cc[2 * ci], cc[2 * ci + 1], c_ctx], 0)
        m["cT"] = f(c3.reshape(3, 8, 128).transpose(2, 1, 0))
        maps.append(m)
    return maps


_NC_CACHE = {}


def kernel(**inputs):
    if "nc" not in _NC_CACHE:
        _NC_CACHE["nc"] = build()
    nc = _NC_CACHE["nc"]
    maps = make_in_maps(inputs)
    res = run_bass_kernel_spmd(nc, maps, core_ids=list(range(8)))
    return np.concatenate([np.asarray(r["out"], dtype=np.float32) for r in res.results], axis=0)
```

```python
import contextlib
import numpy as np
import ml_dtypes
import concourse.bass as bass
import concourse.mybir as mybir
from concourse.bass_utils import run_bass_kernel_spmd

F32 = mybir.dt.float32
BF16 = mybir.dt.bfloat16
I32 = mybir.dt.int32
AF = mybir.ActivationFunctionType
ALU = mybir.AluOpType

D = 1024
NTOK = 4096
LCTX = 256
NT = NTOK + LCTX
NCOL = 11264
DEPTH = 4
EPS = 1e-6
COL = dict(a_q=0, a_k=512, a_v=1024, a_g=1536, b_x=2048, b_g=2560, c_u=3072, c_v=3584, c_g=4096,
           d_q=4608, d_ff=5120, d_fb=5632, d_i=6144, d_g=6656, gate=7168)
ZW = 26
NEG = -30000.0

SEM_WINDOW = 16000
DMA_RING = 8


class Buf:
    __slots__ = ("name", "lw", "rd", "excl")

    def __init__(self, name="", excl=False):
        self.name = name
        self.lw = None
        self.rd = {}
        self.excl = excl


class DTrack:
    def __init__(self, ncols, unit=128):
        self.unit = unit
        self.b = [Buf() for _ in range((ncols + unit - 1) // unit)]

    def r(self, c0, c1):
        return self.b[c0 // self.unit:(c1 + self.unit - 1) // self.unit]


class _Rec:
    def __init__(self):
        self.call = None

    def __getattr__(self, name):
        def f(*a, **kw):
            self.call = (name, a, kw)
            return self
        return f


def _freeze(fn):
    r = _Rec()
    fn(r)
    name, a, kw = r.call
    return lambda e: getattr(e, name)(*a, **kw)


class Sched:
    ENGS = ("pe", "act", "dve", "pool", "sp")

    def __init__(self, nc):
        self.nc = nc
        self.ops = {e: [] for e in self.ENGS}
        self.cnt = {e: 0 for e in self.ENGS}
        self.dcnt = {e: 0 for e in self.ENGS}
        self.known = {e: {} for e in self.ENGS}
        self.pending = {e: False for e in self.ENGS}

    def _tokwaits(self, eng, toks):
        waits = {}
        for t in toks:
            if t[0] == 'e':
                if t[1] == eng and eng == 'pe':
                    continue
                key = ('e', t[1], (t[2] - 1) // SEM_WINDOW)
                val = (t[2] - 1) % SEM_WINDOW + 1
            else:
                key = ('d', t[1], t[2] % DMA_RING)
                val = 16 * (t[2] // DMA_RING + 1)
            if waits.get(key, 0) < val:
                waits[key] = val
        out = []
        kn = self.known[eng]
        for key, val in waits.items():
            if kn.get(key, 0) >= val:
                continue
            kn[key] = val
            out.append((key, val))
        return out

    def _deps(self, eng, reads, writes):
        toks = []
        for b in reads:
            if b.lw is not None:
                toks.append(b.lw)
            if b.excl:
                toks.extend(v for kk, v in b.rd.items() if kk != eng)
        for b in writes:
            if b.lw is not None and not (b.lw[0] == 'e' and b.lw[1] == eng):
                toks.append(b.lw)
            toks.extend(v for v in b.rd.values() if not (v[0] == 'e' and v[1] == eng))
        return self._tokwaits(eng, toks)

    def op(self, eng, fn, reads=(), writes=(), inc=True):
        fn = _freeze(fn)
        waits = self._deps(eng, reads, writes)
        idx = self.cnt[eng] + 1
        tok = ('e', eng, idx)
        if inc:
            self.cnt[eng] = idx
            self.ops[eng].append((waits, fn, ('e', eng, (idx - 1) // SEM_WINDOW), 1))
            self.pending[eng] = False
        else:
            self.ops[eng].append((waits, fn, None, 0))
            self.pending[eng] = True
        for b in reads:
            b.rd[eng] = tok
        for b in writes:
            b.lw = tok
            b.rd = {}
        return tok

    def dma(self, q, out, in_, reads=(), writes=()):
        waits = self._deps(q, reads, writes)
        i = self.dcnt[q]
        self.dcnt[q] += 1
        if i >= DMA_RING:
            key = ('d', q, i % DMA_RING)
            val = 16 * (i // DMA_RING)
            kn = self.known[q]
            if kn.get(key, 0) < val:
                kn[key] = val
                waits.append((key, val))
        tok = ('d', q, i)
        fn = lambda e, out=out, in_=in_: e.dma_start(out=out, in_=in_)
        self.ops[q].append((waits, fn, ('d', q, i % DMA_RING), 16))
        qk = 'q' + q
        for b in reads:
            b.rd[qk] = tok
        for b in writes:
            b.lw = tok
            b.rd = {}
        return tok

    def barrier(self):
        toks = []
        for e in self.ENGS:
            assert not self.pending[e]
            if self.cnt[e] > 0:
                toks.append(('e', e, self.cnt[e]))
            n = self.dcnt[e]
            for i in range(max(0, n - DMA_RING), n):
                toks.append(('d', e, i))
        for e in self.ENGS:
            w = self._tokwaits(e, toks)
            if w:
                self.ops[e].append((w, None, None, 0))

    def emit(self):
        nc = self.nc
        self.barrier()
        sems = {}
        with contextlib.ExitStack() as st:
            def getsem(key):
                if key not in sems:
                    sems[key] = st.enter_context(nc.semaphore("s_%s_%s_%d" % key))
                return sems[key]
            for e in self.ENGS:
                for (waits, fn, inc, amt) in self.ops[e]:
                    for key, val in waits:
                        getsem(key)
                    if inc is not None:
                        getsem(inc)
            block = st.enter_context(nc.Block())
            handles = {"pe": block.tensor, "act": block.scalar, "dve": block.vector,
                       "pool": block.gpsimd, "sp": block.sync}
            for e in self.ENGS:
                ops = self.ops[e]
                if not ops:
                    continue

                def body(engine, ops=ops):
                    for (waits, fn, inc, amt) in ops:
                        for key, val in waits:
                            engine.wait_ge(sems[key], val)
                        if fn is not None:
                            ins = fn(engine)
                            if inc is not None:
                                ins.then_inc(sems[inc], amt)
                handles[e](body)


_UC = [0]
_DBG = {}


def _u(n):
    _UC[0] += 1
    return "%s_%d" % (n, _UC[0])


class K:
    pass


def _dram(nc, name, shape, dt, kind=None):
    if kind is None:
        return nc.dram_tensor(name, list(shape), dt).ap()
    return nc.dram_tensor(name, list(shape), dt, kind=kind).ap()


def build(nlayers=DEPTH, phases="PABCDM", dump=()):
    nc = bass.Bass("TRN2", target_bir_lowering=False)
    k = K()
    k.nc = nc
    S = Sched(nc)
    k.S = S
    IN = "ExternalInput"
    I = {}
    def inp(name, shape, dt=F32):
        I[name] = _dram(nc, name, shape, dt, IN)
        return I[name]
    inp("x", [2, NTOK, D]); inp("ctx", [2, LCTX, D]); inp("cT", [128, 8, 3])
    inp("w_ada", [nlayers, D, 3 * D]); inp("b_ada", [nlayers, 3 * D]); inp("b_adaT", [nlayers, 128, 24])
    inp("g_preT", [nlayers, 128, 8]); inp("g_post", [nlayers, D])
    inp("w_in", [nlayers, D, NCOL])
    inp("zb", [nlayers, 2, 128, 8, ZW * 64], BF16)
    inp("fnet_w", [nlayers, 512, 128]); inp("gmlp_g", [nlayers, 512]); inp("gmlp_wsT", [nlayers, 1024, 128])
    inp("gmlp_bsT", [nlayers, 128, 8]); inp("lbT", [128, 8, DEPTH]); inp("hgrn_gT", [nlayers, 128, 4])
    inp("w_branch", [nlayers, 2048, D]); inp("w_out", [nlayers, D, D])
    inp("ident", [128, 128], BF16); inp("rsign", [128, 128], BF16); inp("eye8", [128, 128], BF16)
    inp("cosT", [128, NTOK]); inp("sinT", [128, NTOK])
    inp("trif", [128, 128], I32); inp("trib", [128, 128], I32); inp("bones", [128, 128], BF16)
    inp("dft", [16, NTOK, 512], BF16); inp("dft256", [LCTX, 512], BF16)
    inp("cc128", [128, 128], BF16); inp("ssn128", [128, 128], BF16)
    OUT = _dram(nc, "out", [2, NTOK, D], F32, "ExternalOutput")
    def scr(name, shape, dt):
        return _dram(nc, name, shape, dt, "ExternalOutput" if name in dump else None)
    k.w_in_bf = scr("w_in_bf", [2, D, NCOL], BF16)
    k.w_br_bf = scr("w_br_bf", [2, 2048, D], BF16)
    k.w_out_bf = scr("w_out_bf", [2, D, D], BF16)
    k.fw_bf = scr("fw_bf", [2, 512, 128], BF16)
    k.ws_bf = scr("ws_bf", [2, 1024, 128], BF16)
    k.hT = scr("hT", [2, D, NT], BF16)
    k.yT = scr("yT", [2, 4, 512, NT], BF16)
    k.ofT = scr("ofT", [2, 512, NT], F32)
    k.ctxcur = scr("ctxcur", [2, LCTX, D], F32)
    k.I = I
    k.OUT = OUT
    k.t_w = [Buf() for _ in range(2)]
    k.t_hT = [DTrack(NT) for _ in range(2)]
    k.t_yT = [[DTrack(NT) for _ in range(4)] for _ in range(2)]
    k.t_ofT = [DTrack(NT) for _ in range(2)]
    k.t_x = [DTrack(NTOK) for _ in range(2)]
    k.t_ctx = [DTrack(LCTX) for _ in range(2)]

    with contextlib.ExitStack() as gst:
        k.gst = gst
        def gsb(name, shape, dt):
            return gst.enter_context(nc.sbuf_tensor(name, list(shape), dt))
        k.ps = [gst.enter_context(nc.psum_tensor("ps%d" % i, [128, 512], F32)) for i in range(8)]
        k.bps = [Buf("ps%d" % i, excl=True) for i in range(8)]
        k.ident = gsb("ident_sb", [128, 128], BF16)
        k.ones_bf = gsb("ones_bf", [128, 128], BF16)
        k.ones_f = gsb("ones_f", [128, 512], F32)
        k.scT = gsb("scT", [128, 8, 3], F32)
        k.modT = gsb("modT", [128, 24, 3], F32)
        k.gsT = gsb("gsT", [128, 8, 3], F32)
        k.gtg = gsb("gtg", [128, 3, D], F32)
        k.lb = gsb("lb", [128, 8, DEPTH], F32)
        k.oml = gsb("oml", [128, 8, DEPTH], F32)
        k.b_const = Buf(); k.b_scT = Buf(); k.b_mod = Buf(); k.b_gtg = Buf(); k.b_lb = Buf()
        S.dma('sp', k.ident[:], I["ident"], writes=[k.b_const])
        S.op('dve', lambda e: e.memset(k.ones_bf[:], 1.0), writes=[k.b_const])
        S.op('dve', lambda e: e.memset(k.ones_f[:], 1.0), writes=[k.b_const])
        prep_global(k)
        for l in range(nlayers):
            S.barrier()
            convert_weights(k, l)
            adaln(k, l)
            with_ctx = l < DEPTH - 1
            if "P" in phases:
                for s in range(2):
                    phase_pre(k, l, s)
            if "C" in phases:
                phase_c(k, l, with_ctx)
            if "B" in phases:
                phase_b(k, l, with_ctx)
            if "A" in phases:
                phase_a(k, l, with_ctx)
            if "D" in phases:
                phase_d(k, l, with_ctx)
            if "M" in phases:
                phase_m(k, l, with_ctx)
        S.emit()
    return nc


def prep_global(k):
    S, nc, I = k.S, k.nc, k.I
    with contextlib.ExitStack() as st:
        sb = lambda n, s, d: st.enter_context(nc.sbuf_tensor(_u(n), list(s), d))
        cT = sb("pg_cT", [128, 8, 3], F32)
        lg = sb("pg_lg", [128, 8, DEPTH], F32)
        ex = sb("pg_ex", [128, 8, DEPTH], F32)
        sm = sb("pg_sm", [128, 8], F32)
        b = Buf()
        S.dma('sp', cT[:], I["cT"], writes=[b])
        S.op('act', lambda e: e.activation(out=k.scT[:], in_=cT[:], func=AF.Silu), reads=[b], writes=[k.b_scT])
        b2 = Buf()
        S.dma('sp', lg[:], I["lbT"], writes=[b2])
        S.op('act', lambda e: e.activation(out=ex[:], in_=lg[:], func=AF.Exp), reads=[b2], writes=[b2])
        S.op('dve', lambda e: e.tensor_reduce(out=sm[:], in_=ex[:], axis=mybir.AxisListType.X, op=ALU.add), reads=[b2], writes=[b2])
        S.op('dve', lambda e: e.reciprocal(sm[:], sm[:]), reads=[b2], writes=[b2])
        S.op('dve', lambda e: e.tensor_tensor(out=ex[:], in0=ex[:], in1=sm[:].unsqueeze(2).broadcast_to([128, 8, DEPTH]), op=ALU.mult), reads=[b2], writes=[b2])
        S.op('dve', lambda e: e.memset(k.lb[:, :, 0:1], 0.0), reads=[b2], writes=[k.b_lb])
        for l in range(1, DEPTH):
            S.op('dve', lambda e, l=l: e.tensor_tensor(out=k.lb[:, :, l:l + 1], in0=k.lb[:, :, l - 1:l], in1=ex[:, :, l:l + 1], op=ALU.add), reads=[b2, k.b_lb], writes=[k.b_lb])
        S.op('dve', lambda e: e.tensor_scalar(out=k.lb[:], in0=k.lb[:], scalar1=0.0, scalar2=None, op0=ALU.max), reads=[k.b_lb], writes=[k.b_lb])
        S.op('dve', lambda e: e.tensor_scalar(out=k.oml[:], in0=k.lb[:], scalar1=-1.0, scalar2=1.0, op0=ALU.mult, op1=ALU.add), reads=[k.b_lb], writes=[k.b_lb])
        S.barrier()


def convert_weights(k, l):
    S, I = k.S, k.I
    p = l % 2
    w = [k.t_w[p]]
    for c0 in range(0, NCOL, 1408):
        S.dma('pool', k.w_in_bf[p, :, c0:c0 + 1408], I["w_in"][l, :, c0:c0 + 1408], writes=w)
    S.dma('pool', k.w_br_bf[p], I["w_branch"][l], writes=w)
    S.dma('pool', k.w_out_bf[p], I["w_out"][l], writes=w)
    S.dma('pool', k.fw_bf[p], I["fnet_w"][l], writes=w)
    S.dma('pool', k.ws_bf[p], I["gmlp_wsT"][l], writes=w)


def load_w(k, l, dst, c0, ncol, bdst):
    p = l % 2
    src = k.w_in_bf[p, :, c0:c0 + ncol].rearrange("(kc p) c -> p kc c", p=128)
    k.S.dma('sp', dst, src, reads=[k.t_w[p]], writes=[bdst])


def adaln(k, l):
    S, nc, I = k.S, k.nc, k.I
    with contextlib.ExitStack() as st:
        sb = lambda n, s, d: st.enter_context(nc.sbuf_tensor(_u(n), list(s), d))
        wa = [sb("ad_wa%d" % i, [128, 8, 512], F32) for i in range(2)]
        bwa = [Buf() for _ in range(2)]
        scbc = sb("ad_scbc", [128, 8, 3, 128], F32)
        bT = sb("ad_bT", [128, 24], F32)
        gpT = sb("ad_gpT", [128, 8], F32)
        brow = sb("ad_brow", [128, D], F32)
        grow = sb("ad_grow", [128, D], F32)
        tmp = sb("ad_tmp", [128, 8, 3], F32)
        b = Buf(); bsc = Buf()
        S.dma('sp', bT[:], I["b_adaT"][l], writes=[b])
        S.dma('sp', gpT[:], I["g_preT"][l], writes=[b])
        S.dma('sp', brow[:], I["b_ada"][l:l + 1, 2 * D:3 * D].partition_broadcast(128), writes=[b])
        S.dma('sp', grow[:], I["g_post"][l:l + 1, :].partition_broadcast(128), writes=[b])
        S.op('dve', lambda e: e.tensor_copy(scbc[:], k.scT[:].unsqueeze(3).broadcast_to([128, 8, 3, 128])), reads=[k.b_scT], writes=[bsc])
        pm = k.ps[0]
        for g in range(6):
            w = wa[g % 2]
            S.dma('sp', w[:], I["w_ada"][l, :, g * 512:(g + 1) * 512].rearrange("(kc p) c -> p kc c", p=128), writes=[bwa[g % 2]])
            for j in range(4):
                ch = 4 * g + j
                for kc in range(8):
                    S.op('pe', lambda e, w=w, kc=kc, j=j, ch=ch: e.matmul(pm[:, ch * 3:ch * 3 + 3], lhsT=w[:, kc, j * 128:(j + 1) * 128], rhs=k.scT[:, kc, :], start=(kc == 0), stop=(kc == 7)),
                         reads=[bwa[g % 2], k.b_scT], writes=[k.bps[0]], inc=(kc == 7))
            if g >= 4:
                half = g - 4
                for s in range(3):
                    pg = k.ps[1 + s]
                    for kc in range(8):
                        S.op('pe', lambda e, w=w, kc=kc, s=s, pg=pg: e.matmul(pg[:], lhsT=scbc[:, kc, s, :], rhs=w[:, kc, :], start=(kc == 0), stop=(kc == 7)),
                             reads=[bwa[g % 2], bsc], writes=[k.bps[1 + s]], inc=(kc == 7))
                    sl = slice(half * 512, (half + 1) * 512)
                    S.op('dve', lambda e, s=s, pg=pg, sl=sl: e.tensor_tensor(out=k.gtg[:, s, sl], in0=pg[:], in1=brow[:, sl], op=ALU.add), reads=[k.bps[1 + s], b], writes=[k.b_gtg])
                    S.op('dve', lambda e, s=s, sl=sl: e.tensor_tensor(out=k.gtg[:, s, sl], in0=k.gtg[:, s, sl], in1=grow[:, sl], op=ALU.mult), reads=[b, k.b_gtg], writes=[k.b_gtg])
        S.op('dve', lambda e: e.tensor_tensor(out=k.modT[:], in0=pm[:, 0:72].rearrange("p (c s) -> p c s", s=3), in1=bT[:].unsqueeze(2).broadcast_to([128, 24, 3]), op=ALU.add), reads=[k.bps[0], b], writes=[k.b_mod])
        S.op('dve', lambda e: e.tensor_scalar(out=tmp[:], in0=k.modT[:, 8:16, :], scalar1=1.0, scalar2=None, op0=ALU.add), reads=[k.b_mod], writes=[b])
        S.op('dve', lambda e: e.tensor_tensor(out=k.gsT[:], in0=tmp[:], in1=gpT[:].unsqueeze(2).broadcast_to([128, 8, 3]), op=ALU.mult), reads=[b], writes=[k.b_mod])
        S.barrier()


def rstd_from_ms(k, ms, bms):
    S = k.S
    S.op('act', lambda e: e.activation(out=ms, in_=ms, func=AF.Sqrt, bias=EPS, scale=1.0), reads=[bms], writes=[bms])
    S.op('dve', lambda e: e.reciprocal(ms, ms), reads=[bms], writes=[bms])


def phase_pre(k, l, s):
    S, nc, I = k.S, k.nc, k.I
    with contextlib.ExitStack() as st:
        sb = lambda n, sh, d: st.enter_context(nc.sbuf_tensor(_u(n), list(sh), d))
        xt = [sb("pr_x%d" % i, [128, D], F32) for i in range(6)]
        bx = [Buf() for _ in range(6)]
        xs = [sb("pr_xs%d" % i, [128, D], BF16) for i in range(2)]
        bxs = [Buf() for _ in range(2)]
        junk = sb("pr_junk", [128, D], F32)
        ms = [sb("pr_ms%d" % i, [128, 1], F32) for i in range(6)]
        bms = [Buf() for _ in range(6)]
        hb = [sb("pr_hb%d" % i, [128, 8, 512], BF16) for i in range(2)]
        bhb = [Buf() for _ in range(2)]
        bj = Buf()
        pT = [k.ps[i][:].bitcast(BF16) for i in range(4)]
        it = 0
        blocks = [(0, b0 * 512, 4) for b0 in range(8)] + [(1, NTOK, 2)]
        for bi, (isctx, c0, ntile) in enumerate(blocks):
            mi = 2 if isctx else s
            for t in range(ntile):
                xi = it % 6
                if isctx:
                    src = (I["ctx"] if l == 0 else k.ctxcur)[s, t * 128:(t + 1) * 128, :]
                    rb = k.t_ctx[s].r(t * 128, t * 128 + 128)
                else:
                    src = (I["x"] if l == 0 else k.OUT)[s, c0 + t * 128:c0 + (t + 1) * 128, :]
                    rb = k.t_x[s].r(c0 + t * 128, c0 + t * 128 + 128)
                S.dma('sp', xt[xi][:], src, reads=rb, writes=[bx[xi]])
                S.op('act', lambda e, xi=xi: e.activation(out=junk[:], in_=xt[xi][:], func=AF.Square, scale=1.0 / 32.0, accum_out=ms[xi][:]), reads=[bx[xi]], writes=[bj, bms[xi]])
                rstd_from_ms(k, ms[xi][:], bms[xi])
                si = it % 2
                S.op('pool', lambda e, xi=xi, si=si: e.tensor_scalar(out=xs[si][:], in0=xt[xi][:], scalar1=ms[xi][:, 0:1], scalar2=None, op0=ALU.mult), reads=[bx[xi], bms[xi]], writes=[bxs[si]])
                for kc in range(8):
                    dst = pT[kc // 2][:, (kc % 2) * 512 + t * 128:(kc % 2) * 512 + (t + 1) * 128]
                    S.op('pe', lambda e, dst=dst, si=si, kc=kc: e.transpose(dst, xs[si][:, kc * 128:(kc + 1) * 128], k.ident[:]), reads=[bxs[si], k.b_const], writes=[k.bps[kc // 2]], inc=(kc == 7))
                it += 1
            hi = bi % 2
            nt = ntile * 128
            for kc in range(8):
                src = pT[kc // 2][:, (kc % 2) * 512:(kc % 2) * 512 + nt]
                if kc % 2 == 0:
                    S.op('act', lambda e, src=src, kc=kc, hi=hi, nt=nt, mi=mi: e.activation(out=hb[hi][:, kc, :nt], in_=src, func=AF.Identity, bias=k.modT[:, kc, mi:mi + 1], scale=k.gsT[:, kc, mi:mi + 1]), reads=[k.bps[kc // 2], k.b_mod], writes=[bhb[hi]])
                else:
                    S.op('dve', lambda e, src=src, kc=kc, hi=hi, nt=nt, mi=mi: e.tensor_scalar(out=hb[hi][:, kc, :nt], in0=src, scalar1=k.gsT[:, kc, mi:mi + 1], scalar2=k.modT[:, kc, mi:mi + 1], op0=ALU.mult, op1=ALU.add), reads=[k.bps[kc // 2], k.b_mod], writes=[bhb[hi]])
            S.dma('sp', k.hT[s, :, c0:c0 + nt].rearrange("(kc p) t -> p kc t", p=128), hb[hi][:, :, :nt], reads=[bhb[hi]], writes=k.t_hT[s].r(c0, c0 + nt))
        S.barrier()


def load_h(k, s, dst, c0, n, bdst):
    k.S.dma('sp', dst[:, :, :n], k.hT[s, :, c0:c0 + n].rearrange("(kc p) t -> p kc t", p=128), reads=k.t_hT[s].r(c0, c0 + n), writes=[bdst])


def inproj_fm(k, pbank, n, w, bw, cw, h, bh, hc0=0):
    for kc in range(8):
        k.S.op('pe', lambda e, kc=kc: e.matmul(k.ps[pbank][:, :n], lhsT=w[:, kc, cw:cw + 128], rhs=h[:, kc, hc0:hc0 + n], start=(kc == 0), stop=(kc == 7)),
               reads=[bw, bh], writes=[k.bps[pbank]], inc=(kc == 7))


def inproj_tm(k, pbank, w, bw, cw, h, bh, t0):
    for kc in range(8):
        k.S.op('pe', lambda e, kc=kc: e.matmul(k.ps[pbank][:], lhsT=h[:, kc, t0:t0 + 128], rhs=w[:, kc, cw:cw + 512], start=(kc == 0), stop=(kc == 7)),
               reads=[bw, bh], writes=[k.bps[pbank]], inc=(kc == 7))


def seq_blocks(with_ctx_block=True):
    bl = [(b0 * 512, 4) for b0 in range(8)]
    if with_ctx_block:
        bl.append((NTOK, 2))
    return bl


def phase_c(k, l, with_ctx):
    S, nc, I = k.S, k.nc, k.I
    p = l % 2
    with contextlib.ExitStack() as st:
        sb = lambda n, sh, d: st.enter_context(nc.sbuf_tensor(_u(n), list(sh), d))
        w = sb("c_w", [128, 8, 1536], BF16); bw = Buf()
        wsT = sb("c_ws", [128, 8, 128], BF16)
        gn = sb("c_gn", [128, 512], F32)
        bsT = sb("c_bs", [128, 8], F32)
        bc = Buf()
        hb = [sb("c_hb%d" % i, [128, 8, 512], BF16) for i in range(2)]; bhb = [Buf() for _ in range(2)]
        junk = sb("c_junk", [128, 512], F32); bj = Buf()
        ms = sb("c_ms", [128, 1], F32); bms = Buf()
        vn = sb("c_vn", [128, 512], BF16); bvn = Buf()
        sg = sb("c_sg", [128, 512], F32); bsg = Buf()
        t1 = sb("c_t1", [128, 512], F32); bt1 = Buf()
        yt = sb("c_y", [128, 512], BF16); byt = Buf()
        yb = [sb("c_yb%d" % i, [128, 4, 512], BF16) for i in range(2)]; byb = [Buf() for _ in range(2)]
        load_w(k, l, w[:], COL["c_u"], 1536, bw)
        S.dma('sp', wsT[:], k.ws_bf[p].rearrange("(g s) t -> s g t", s=128), reads=[k.t_w[p]], writes=[bc])
        S.dma('sp', gn[:], I["gmlp_g"][l:l + 1, :].partition_broadcast(128), writes=[bc])
        S.dma('sp', bsT[:], I["gmlp_bsT"][l], writes=[bc])
        pT = [k.ps[4][:].bitcast(BF16), k.ps[5][:].bitcast(BF16)]
        bi = 0
        for s in range(2):
            blocks = seq_blocks(with_ctx)
            load_h(k, s, hb[bi % 2], blocks[0][0], blocks[0][1] * 128, bhb[bi % 2])
            for ib, (c0, ntile) in enumerate(blocks):
                h = hb[bi % 2]; bh = bhb[bi % 2]
                if ib + 1 < len(blocks):
                    load_h(k, s, hb[(bi + 1) % 2], blocks[ib + 1][0], blocks[ib + 1][1] * 128, bhb[(bi + 1) % 2])
                for t in range(ntile):
                    t0 = t * 128
                    inproj_tm(k, 0, w, bw, 512, h, bh, t0)
                    S.op('act', lambda e: e.activation(out=junk[:], in_=k.ps[0][:], func=AF.Square, scale=float(512 ** -0.5), accum_out=ms[:]), reads=[k.bps[0]], writes=[bj, bms])
                    rstd_from_ms(k, ms[:], bms)
                    S.op('dve', lambda e: e.scalar_tensor_tensor(out=vn[:], in0=k.ps[0][:], scalar=ms[:, 0:1], in1=gn[:], op0=ALU.mult, op1=ALU.mult), reads=[k.bps[0], bms, bc], writes=[bvn])
                    for g in range(8):
                        S.op('pe', lambda e, g=g: e.matmul(k.ps[1][:, g * 64:(g + 1) * 64], lhsT=wsT[:, g, :], rhs=vn[:, g * 64:(g + 1) * 64], start=True, stop=True), reads=[bc, bvn], writes=[k.bps[1]], inc=(g == 7))
                    inproj_tm(k, 2, w, bw, 0, h, bh, t0)
                    inproj_tm(k, 3, w, bw, 1024, h, bh, t0)
                    S.op('act', lambda e: e.activation(out=sg[:], in_=k.ps[3][:], func=AF.Silu), reads=[k.bps[3]], writes=[bsg])
                    S.op('dve', lambda e: e.tensor_tensor(out=t1[:].rearrange("p (g c) -> p g c", g=8), in0=k.ps[1][:].rearrange("p (g c) -> p g c", g=8), in1=bsT[:].unsqueeze(2).broadcast_to([128, 8, 64]), op=ALU.add), reads=[k.bps[1], bc], writes=[bt1])
                    S.op('dve', lambda e: e.tensor_tensor(out=t1[:], in0=k.ps[2][:], in1=t1[:], op=ALU.mult), reads=[k.bps[2], bt1], writes=[bt1])
                    S.op('pool', lambda e: e.tensor_tensor(out=yt[:], in0=t1[:], in1=sg[:], op=ALU.mult), reads=[bt1, bsg], writes=[byt])
                    for j in range(4):
                        dst = pT[j // 2][:, (j % 2) * 512 + t0:(j % 2) * 512 + t0 + 128]
                        S.op('pe', lambda e, dst=dst, j=j: e.transpose(dst, yt[:, j * 128:(j + 1) * 128], k.ident[:]), reads=[byt, k.b_const], writes=[k.bps[4 + j // 2]], inc=(j == 3))
                nt = ntile * 128
                yo = yb[bi % 2]
                for j in range(4):
                    src = pT[j // 2][:, (j % 2) * 512:(j % 2) * 512 + nt]
                    S.op('act', lambda e, src=src, j=j, yo=yo, nt=nt: e.copy(yo[:, j, :nt], src), reads=[k.bps[4 + j // 2]], writes=[byb[bi % 2]])
                S.dma('pool', k.yT[s, 2, :, c0:c0 + nt].rearrange("(j p) t -> p j t", p=128), yo[:, :, :nt], reads=[byb[bi % 2]], writes=k.t_yT[s][2].r(c0, c0 + nt))
                bi += 1
        S.barrier()


def phase_b(k, l, with_ctx):
    S, nc, I = k.S, k.nc, k.I
    p = l % 2
    with contextlib.ExitStack() as st:
        sb = lambda n, sh, d: st.enter_context(nc.sbuf_tensor(_u(n), list(sh), d))
        w = sb("b_w", [128, 8, 1024], BF16); bw = Buf()
        fw = sb("b_fw", [128, 4, 128], BF16)
        cc = sb("b_cc", [128, 128], BF16); ssn = sb("b_ss", [128, 128], BF16)
        m12 = sb("b_m12", [128, 2, 4, 128], BF16)
        bc = Buf(); bm = Buf()
        z = sb("b_z", [128, 32, 512], BF16); bz = Buf()
        hb = [sb("b_hb%d" % i, [128, 8, 512], BF16) for i in range(2)]; bhb = [Buf() for _ in range(2)]
        dft = [sb("b_dft%d" % i, [128, 32, 512], BF16) for i in range(2)]; bdft = [Buf() for _ in range(2)]
        Y = sb("b_Y", [128, 4, 512], BF16); bY = Buf()
        sg = sb("b_sg", [128, 4, 256], F32); bsg = Buf()
        yb = [sb("b_yb%d" % i, [128, 4, 256], BF16) for i in range(2)]; byb = [Buf() for _ in range(2)]
        load_w(k, l, w[:], COL["b_x"], 1024, bw)
        S.dma('sp', fw[:], k.fw_bf[p].rearrange("(g c) d -> c g d", c=128), reads=[k.t_w[p]], writes=[bc])
        S.dma('sp', cc[:], I["cc128"], writes=[bc])
        S.dma('sp', ssn[:], I["ssn128"], writes=[bc])
        for i, mat in enumerate((cc, ssn)):
            S.op('pe', lambda e, mat=mat, i=i: e.matmul(k.ps[i][:], lhsT=mat[:], rhs=fw[:].rearrange("c g d -> c (g d)"), start=True, stop=True), reads=[bc], writes=[k.bps[i]])
            S.op('act', lambda e, i=i: e.copy(m12[:, i, :, :].rearrange("c g d -> c (g d)"), k.ps[i][:]), reads=[k.bps[i]], writes=[bm])
        di = 0
        for s in range(2):
            for (tc0, ntile, nkb, isctx) in ([(0, 32, 16, False)] + ([(NTOK, 2, 1, True)] if with_ctx else [])):
                nblk = (ntile + 3) // 4
                for ib in range(nblk):
                    nt = min(4, ntile - ib * 4)
                    load_h(k, s, hb[ib % 2], tc0 + ib * 512, nt * 128, bhb[ib % 2])
                    for t in range(nt):
                        inproj_tm(k, 0, w, bw, 0, hb[ib % 2], bhb[ib % 2], t * 128)
                        S.op('act', lambda e, ib=ib, t=t: e.copy(z[:, ib * 4 + t, :], k.ps[0][:]), reads=[k.bps[0]], writes=[bz])
                for kb in range(nkb):
                    dt_ = dft[di % 2]; bd = bdft[di % 2]
                    if isctx:
                        S.dma('sp', dt_[:, :2, :], I["dft256"].rearrange("(t p) c -> p t c", p=128), writes=[bd])
                    else:
                        for hh in range(2):
                            S.dma('sp', dt_[:, hh * 16:(hh + 1) * 16, :], I["dft"][kb, hh * 2048:(hh + 1) * 2048, :].rearrange("(t p) c -> p t c", p=128), writes=[bd])
                    kc0 = tc0 + kb * 256
                    hi = (kb + 1) % 2
                    load_h(k, s, hb[hi], kc0, 256, bhb[hi])
                    for g in range(4):
                        for t in range(ntile):
                            S.op('pe', lambda e, g=g, t=t, dt_=dt_: e.matmul(k.ps[g][:], lhsT=z[:, t, g * 128:(g + 1) * 128], rhs=dt_[:, t, :], start=(t == 0), stop=(t == ntile - 1)),
                                 reads=[bz, bd], writes=[k.bps[g]], inc=(t == ntile - 1))
                        if g % 2 == 0:
                            S.op('act', lambda e, g=g: e.copy(Y[:, g, :], k.ps[g][:]), reads=[k.bps[g]], writes=[bY])
                        else:
                            S.op('dve', lambda e, g=g: e.tensor_copy(Y[:, g, :], k.ps[g][:]), reads=[k.bps[g]], writes=[bY])
                    for g in range(4):
                        pb_ = 4 + g // 2
                        dst = k.ps[pb_][:, (g % 2) * 256:(g % 2) * 256 + 256]
                        S.op('pe', lambda e, g=g, dst=dst: e.matmul(dst, lhsT=m12[:, 0, g, :], rhs=Y[:, g, 0:256], start=True, stop=False), reads=[bm, bY], writes=[k.bps[pb_]], inc=False)
                        S.op('pe', lambda e, g=g, dst=dst: e.matmul(dst, lhsT=m12[:, 1, g, :], rhs=Y[:, g, 256:512], start=False, stop=True), reads=[bm, bY], writes=[k.bps[pb_]])
                    for g in range(4):
                        pb_ = 6 + g // 2
                        dst = k.ps[pb_][:, (g % 2) * 256:(g % 2) * 256 + 256]
                        for kc in range(8):
                            S.op('pe', lambda e, g=g, dst=dst, kc=kc, hi=hi: e.matmul(dst, lhsT=w[:, kc, 512 + g * 128:512 + (g + 1) * 128], rhs=hb[hi][:, kc, 0:256], start=(kc == 0), stop=(kc == 7)),
                                 reads=[bw, bhb[hi]], writes=[k.bps[pb_]], inc=(kc == 7))
                    yo = yb[di % 2]
                    for gg in range(2):
                        S.op('act', lambda e, gg=gg: e.activation(out=sg[:, 2 * gg:2 * gg + 2, :].rearrange("p a t -> p (a t)"), in_=k.ps[6 + gg][:], func=AF.Silu), reads=[k.bps[6 + gg]], writes=[bsg])
                        S.op('dve', lambda e, gg=gg, yo=yo: e.tensor_tensor(out=yo[:, 2 * gg:2 * gg + 2, :].rearrange("p a t -> p (a t)"), in0=k.ps[4 + gg][:], in1=sg[:, 2 * gg:2 * gg + 2, :].rearrange("p a t -> p (a t)"), op=ALU.mult), reads=[k.bps[4 + gg], bsg], writes=[byb[di % 2]])
                    S.dma('pool', k.yT[s, 1, :, kc0:kc0 + 256].rearrange("(j p) t -> p j t", p=128), yo[:], reads=[byb[di % 2]], writes=k.t_yT[s][1].r(kc0, kc0 + 256))
                    di += 1
        S.barrier()


def na_qblocks(with_ctx):
    bl = [(0, 256, 0, list(range(0, 4)), 0, True), (60 * 64, 256, 0, list(range(28, 32)), 60, True),
          (4 * 64, 256, 1, list(range(0, 6)), 4, True), (56 * 64, 256, 1, list(range(26, 32)), 56, True)]
    for gq in range(1, 7):
        bl.append((gq * 512, 512, 1, list(range(4 * gq - 2, 4 * gq + 6)), 8 * gq, True))
    if with_ctx:
        bl.append((NTOK, 256, 1, [], 0, False))
    return bl


def phase_a(k, l, with_ctx):
    S, nc, I = k.S, k.nc, k.I
    with contextlib.ExitStack() as st:
        sb = lambda n, sh, d: st.enter_context(nc.sbuf_tensor(_u(n), list(sh), d))
        w = sb("a_w", [128, 8, 1024], BF16); bw = Buf()
        kT = sb("a_kT", [128, 4, NT], BF16); bkT = Buf()
        vt = sb("a_v", [128, 34, 512], BF16); bv = Buf()
        csb = [sb("a_cs%d" % i, [128, 2, 512], F32) for i in range(2)]; bcs = [Buf() for _ in range(2)]
        csi = [0]
        rs = sb("a_rs", [128, 128], BF16); eye8 = sb("a_e8", [128, 128], BF16)
        bc = Buf()
        zb = sb("a_zb", [128, 8, ZW * 64], BF16); bzb = Buf()
        hb = [sb("a_hb%d" % i, [128, 8, 512], BF16) for i in range(2)]; bhb = [Buf() for _ in range(2)]
        qp = sb("a_qp", [128, 4, 512], BF16); bqp = Buf()
        qr = sb("a_qr", [128, 4, 512], BF16); bqr = Buf()
        gs = sb("a_gs", [128, 4, 512], BF16); bgs = Buf()
        t1 = sb("a_t1", [128, 512], F32); bt1 = Buf()
        t2 = sb("a_t2", [128, 512], F32); bt2 = Buf()
        kp = sb("a_kp", [128, 512], BF16); bkp = Buf()
        es = [sb("a_es%d" % i, [128, 512], BF16) for i in range(4)]; bes = [Buf() for _ in range(4)]
        rd = sb("a_rd", [128, 512], F32); brd = Buf()
        yo = [sb("a_yo%d" % i, [128, 4, 512], BF16) for i in range(2)]; byo = [Buf() for _ in range(2)]
        S.dma('sp', rs[:], I["rsign"], writes=[bc]); S.dma('sp', eye8[:], I["eye8"], writes=[bc])

        def load_cs(c0, n):
            csi[0] += 1
            i = csi[0] % 2
            S.dma('sp', csb[i][:, 0, :n], I["cosT"][:, c0:c0 + n], writes=[bcs[i]])
            S.dma('sp', csb[i][:, 1, :n], I["sinT"][:, c0:c0 + n], writes=[bcs[i]])

        def rope(psrc, plain_bf, bplain, dst, c0, n, rbank):
            i = csi[0] % 2
            if _DBG.get('nors'):
                rbank = psrc
            else:
                S.op('pe', lambda e: e.matmul(k.ps[rbank][:, :n], lhsT=rs[:], rhs=plain_bf, start=True, stop=True), reads=[bc, bplain], writes=[k.bps[rbank]])
            S.op('dve', lambda e: e.tensor_tensor(out=t1[:, :n], in0=k.ps[psrc][:, :n], in1=csb[i][:, 0, :n], op=ALU.mult), reads=[k.bps[psrc], bcs[i], bplain], writes=[bt1])
            S.op('dve', lambda e: e.tensor_tensor(out=t2[:, :n], in0=k.ps[rbank][:, :n], in1=csb[i][:, 1, :n], op=ALU.mult), reads=[k.bps[rbank], bcs[i]], writes=[bt2])

        hi = 0
        for s in range(2):
            load_w(k, l, w[:], COL["a_k"], 1024, bw)
            for (c0, ntile) in seq_blocks(True):
                n = ntile * 128
                h = hb[hi % 2]; bh = bhb[hi % 2]; hi += 1
                load_h(k, s, h, c0, n, bh)
                if c0 < NTOK:
                    load_cs(c0, n)
                for cp in range(4):
                    inproj_fm(k, 0, n, w, bw, cp * 128, h, bh)
                    if c0 >= NTOK or _DBG.get('norope'):
                        S.op('act', lambda e, cp=cp: e.copy(kT[:, cp, c0:c0 + n], k.ps[0][:, :n]), reads=[k.bps[0]], writes=[bkT])
                    else:
                        S.op('act', lambda e: e.copy(kp[:, :n], k.ps[0][:, :n]), reads=[k.bps[0]], writes=[bkp])
                        rope(0, kp[:, :n], bkp, None, c0, n, 1)
                        if _DBG.get('nopool'):
                            S.op('dve', lambda e, cp=cp: e.tensor_tensor(out=kT[:, cp, c0:c0 + n], in0=t1[:, :n], in1=t2[:, :n], op=ALU.add), reads=[bt1, bt2], writes=[bkT])
                        else:
                            S.op('pool', lambda e, cp=cp: e.tensor_tensor(out=kT[:, cp, c0:c0 + n], in0=t1[:, :n], in1=t2[:, :n], op=ALU.add), reads=[bt1, bt2], writes=[bkT])
                for t in range(ntile):
                    inproj_tm(k, 2, w, bw, 512, h, bh, t * 128)
                    S.op('act', lambda e, t=t: e.copy(vt[:, c0 // 128 + t, :], k.ps[2][:]), reads=[k.bps[2]], writes=[bv])
            if _DBG.get('a1only'):
                continue
            load_w(k, l, w[:, :, 0:512], COL["a_q"], 512, bw)
            load_w(k, l, w[:, :, 512:1024], COL["a_g"], 512, bw)
            cur_kind = None
            for qi, (t0, nq, kind, chunks, q0, use_rope) in enumerate(na_qblocks(with_ctx)):
                if 'qsel' in _DBG and qi not in _DBG['qsel']:
                    continue
                if use_rope and kind != cur_kind:
                    S.dma('sp', zb[:], I["zb"][l, kind], writes=[bzb])
                    cur_kind = kind
                h = hb[hi % 2]; bh = bhb[hi % 2]; hi += 1
                load_h(k, s, h, t0, nq, bh)
                if use_rope:
                    load_cs(t0, nq)
                for cp in range(4):
                    inproj_fm(k, 0, nq, w, bw, cp * 128, h, bh)
                    S.op('act', lambda e, cp=cp: e.copy(qp[:, cp, :nq], k.ps[0][:, :nq]), reads=[k.bps[0]], writes=[bqp])
                    if use_rope:
                        rope(0, qp[:, cp, :nq], bqp, None, t0, nq, 1)
                        S.op('pool', lambda e, cp=cp: e.tensor_tensor(out=qr[:, cp, :nq], in0=t1[:, :nq], in1=t2[:, :nq], op=ALU.add), reads=[bt1, bt2], writes=[bqr])
                    inproj_fm(k, 2, nq, w, bw, 512 + cp * 128, h, bh)
                    S.op('act', lambda e, cp=cp: e.activation(out=gs[:, cp, :nq], in_=k.ps[2][:, :nq], func=AF.Silu), reads=[k.bps[2]], writes=[bgs])
                y = yo[qi % 2]
                for cp in range(4):
                    ob, db = (6, 7) if cp % 2 == 0 else (0, 1)
                    klist = [(c, True) for c in chunks] + [(32, False), (33, False)]
                    items = []
                    for j in range(2):
                        for ic, (c, band) in enumerate(klist):
                            items.append((j, c, band, ic == 0, ic == len(klist) - 1))

                    def stage1(idx, cp=cp):
                        j, c, band, first, last = items[idx]
                        pb = 64 * j; hd = 2 * cp + j
                        sbk = 3 + (idx % 3)
                        e_t = es[idx % 4]; be = bes[idx % 4]
                        if band:
                            woff = 14 - (2 * c - q0)
                            S.op('pe', lambda e: e.matmul(k.ps[sbk][:, :nq], lhsT=kT[pb:pb + 64, cp, c * 128:(c + 1) * 128], rhs=qr[pb:pb + 64, cp, :nq], start=True, stop=False),
                                 reads=[bkT, bqr], writes=[k.bps[sbk]], inc=False)
                            S.op('pe', lambda e: e.matmul(k.ps[sbk][:, :nq], lhsT=eye8[:], rhs=zb[:, hd, woff * 64:woff * 64 + nq], start=False, stop=True),
                                 reads=[bc, bzb], writes=[k.bps[sbk]])
                        else:
                            S.op('pe', lambda e: e.matmul(k.ps[sbk][:, :nq], lhsT=kT[pb:pb + 64, cp, c * 128:(c + 1) * 128], rhs=qp[pb:pb + 64, cp, :nq], start=True, stop=True),
                                 reads=[bkT, bqp], writes=[k.bps[sbk]])
                        S.op('act', lambda e: e.activation(out=e_t[:, :nq], in_=k.ps[sbk][:, :nq], func=AF.Exp, scale=0.125), reads=[k.bps[sbk]], writes=[be])

                    def stage2(idx, cp=cp, ob=ob, db=db):
                        j, c, band, first, last = items[idx]
                        pb = 64 * j; hd = 2 * cp + j
                        e_t = es[idx % 4]; be = bes[idx % 4]
                        S.op('pe', lambda e: e.matmul(k.ps[ob][pb:pb + 64, :nq], lhsT=vt[:, c, hd * 64:(hd + 1) * 64], rhs=e_t[:, :nq], start=first, stop=last),
                             reads=[bv, be], writes=[k.bps[ob]], inc=False)
                        S.op('pe', lambda e: e.matmul(k.ps[db][pb:pb + 64, :nq], lhsT=k.ones_bf[:, 0:64], rhs=e_t[:, :nq], start=first, stop=last),
                             reads=[k.b_const, be], writes=[k.bps[db]])

                    LOOK = 2
                    for idx in range(len(items) + LOOK):
                        if idx < len(items):
                            stage1(idx)
                        if idx >= LOOK:
                            stage2(idx - LOOK)
                    S.op('act', lambda e: e.activation(out=rd[:, :nq], in_=k.ps[db][:, :nq], func=AF.Ln), reads=[k.bps[db]], writes=[brd])
                    S.op('act', lambda e: e.activation(out=rd[:, :nq], in_=rd[:, :nq], func=AF.Exp, scale=-1.0), reads=[brd], writes=[brd])
                    S.op('dve', lambda e: e.tensor_tensor(out=rd[:, :nq], in0=k.ps[ob][:, :nq], in1=rd[:, :nq], op=ALU.mult), reads=[k.bps[ob], brd], writes=[brd])
                    S.op('pool', lambda e, cp=cp, y=y: e.tensor_tensor(out=y[:, cp, :nq], in0=rd[:, :nq], in1=gs[:, cp, :nq], op=ALU.mult), reads=[brd, bgs], writes=[byo[qi % 2]])
                S.dma('pool', k.yT[s, 0, :, t0:t0 + nq].rearrange("(j p) t -> p j t", p=128), y[:, :, :nq], reads=[byo[qi % 2]], writes=k.t_yT[s][0].r(t0, t0 + nq))
        S.barrier()


def phase_d(k, l, with_ctx):
    S, nc, I = k.S, k.nc, k.I
    with contextlib.ExitStack() as st:
        sb = lambda n, sh, d: st.enter_context(nc.sbuf_tensor(_u(n), list(sh), d))
        w = sb("d_w", [128, 8, 2048], BF16); bw = Buf()
        gn = sb("d_gn", [128, 4], F32)
        trif = sb("d_trif", [128, 128], I32); trib = sb("d_trib", [128, 128], I32); bones = sb("d_bones", [128, 128], BF16)
        bc = Buf()
        hb = [sb("d_hb%d" % i, [128, 8, 512], BF16) for i in range(2)]; bhb = [Buf() for _ in range(2)]
        Pp = sb("d_P", [128, 4, 516], F32); bP = Buf()
        kT = sb("d_kT", [128, 4, 512], BF16); bkT = Buf()
        qT = sb("d_qT", [128, 4, 512], BF16); bqT = Buf()
        itm2 = [sb("d_itm%d" % i, [128, 512], BF16) for i in range(2)]; bitm2 = [Buf() for _ in range(2)]
        negr2 = [sb("d_negr%d" % i, [128, 4, 8], F32) for i in range(2)]; bnr2 = [Buf() for _ in range(2)]
        pcnt = [0]
        dqa = [sb("d_dq%d" % i, [128, 128], F32) for i in range(8)]; bdqa = [Buf() for _ in range(8)]
        eqa = [sb("d_eq%d" % i, [128, 128], F32) for i in range(8)]; beqa = [Buf() for _ in range(8)]
        sga = [sb("d_sg%d" % i, [128, 512], F32) for i in range(2)]; bsga = [Buf() for _ in range(2)]
        lfa = [sb("d_lf%d" % i, [128, 512], F32) for i in range(2)]; blfa = [Buf() for _ in range(2)]
        ek = [sb("d_ek%d" % i, [128, 4, 128], F32) for i in range(4)]; bek = [Buf() for _ in range(4)]
        qtl2 = [sb("d_qtl%d" % i, [128, 2, 4, 128], BF16) for i in range(2)]; bqtl2 = [Buf() for _ in range(2)]
        ktl2 = [sb("d_ktl%d" % i, [128, 4, 4, 128], BF16) for i in range(2)]; bktl2 = [Buf() for _ in range(2)]
        qh2 = [sb("d_qh%d" % i, [128, 4, 128], BF16) for i in range(2)]; bqh2 = [Buf() for _ in range(2)]
        khT = sb("d_khT", [128, 4, 128], BF16); bkhT = Buf()
        khtm2 = [sb("d_khtm%d" % i, [128, 512], BF16) for i in range(2)]; bkhtm2 = [Buf() for _ in range(2)]
        dec2 = [sb("d_dec%d" % i, [128, 4], F32) for i in range(2)]; bdec2 = [Buf() for _ in range(2)]
        At = sb("d_At", [128, 8, 128], BF16); bAt = Buf()
        St = sb("d_S", [128, 2, 4, 64], F32); bS = [Buf(), Buf()]
        Sbf = sb("d_Sbf", [128, 4, 128], BF16); bSbf = Buf()
        ob = [sb("d_ob%d" % i, [128, 4, 512], F32) for i in range(2)]; bob = [Buf() for _ in range(2)]
        sq = sb("d_sq", [128, 512], BF16); bsq = Buf()
        rst = sb("d_rst", [128, 512], F32); brst = Buf()
        gl = sb("d_gl", [128, 512], F32); bgl = Buf()
        yb = [sb("d_yb%d" % i, [128, 4, 512], BF16) for i in range(2)]; byb = [Buf() for _ in range(2)]
        S.dma('sp', gn[:], I["hgrn_gT"][l], writes=[bc])
        S.dma('sp', trif[:], I["trif"], writes=[bc]); S.dma('sp', trib[:], I["trib"], writes=[bc]); S.dma('sp', bones[:], I["bones"], writes=[bc])
        S.op('dve', lambda e: e.memset(Pp[:], 0.0), writes=[bP])
        for i in range(2):
            S.op('pool', lambda e, i=i: e.memset(qtl2[i][:], 0.0), writes=[bqtl2[i]])
        S.op('pool', lambda e: e.memset(Sbf[:], 0.0), writes=[bSbf])
        ptr = k.ps[5][:].bitcast(BF16)
        hi = 0
        for s in range(2):
            for d in range(2):
                S.op('dve', lambda e, d=d: e.memset(St[:, d, :, :], 0.0), writes=[bS[d]])
            for (base, nblk_tiles) in ((NTOK, [2]), (0, [4] * 8)):
                isctx = base >= NTOK
                for d in range(2):
                    _DBG['dpass'] = _DBG.get('dpass', 0) + 1
                    if 'dmax' in _DBG and _DBG['dpass'] > _DBG['dmax']:
                        continue
                    sgn = 1.0 if d == 0 else -1.0
                    final = (d == 1)
                    load_w(k, l, w[:, :, 0:512], COL["d_q"], 512, bw)
                    load_w(k, l, w[:, :, 512:1024], COL["d_ff"] if d == 0 else COL["d_fb"], 512, bw)
                    load_w(k, l, w[:, :, 1024:1536], COL["d_i"], 512, bw)
                    if final:
                        load_w(k, l, w[:, :, 1536:2048], COL["d_g"], 512, bw)
                    for i in range(4):
                        S.op('pool', lambda e, i=i: e.memset(ek[i][:], 0.0), writes=[bek[i]])
                    S.op('pool', lambda e: e.memset(At[:], 0.0), writes=[bAt])
                    S.op('act', lambda e, d=d: e.copy(Sbf[0:64, :, 0:64], St[0:64, d, :, :]), reads=[bS[d]], writes=[bSbf]); S.op('act', lambda e, d=d: e.copy(Sbf[64:128, :, 64:128], St[64:128, d, :, :]), reads=[bS[d]], writes=[bSbf])
                    mask = trif if d == 0 else trib
                    blist = list(range(len(nblk_tiles)))
                    if d == 1:
                        blist = blist[::-1]
                    for ib in blist:
                        ntile = nblk_tiles[ib]
                        n = ntile * 128
                        c0 = base + ib * 512
                        h = hb[hi % 2]; bh = bhb[hi % 2]
                        o_b = ob[hi % 2]; bo = bob[hi % 2]; y_b = yb[hi % 2]; by_ = byb[hi % 2]
                        hi += 1
                        load_h(k, s, h, c0, n, bh)
                        if final:
                            S.dma('sp', o_b[:, :, :n], k.ofT[s, :, c0:c0 + n].rearrange("(j p) t -> p j t", p=128), reads=k.t_ofT[s].r(c0, c0 + n), writes=[bo])
                        for ft in range(4):
                            sg = sga[ft % 2]; bsg = bsga[ft % 2]; logf = lfa[ft % 2]; blf = blfa[ft % 2]
                            inproj_fm(k, 0, n, w, bw, 512 + ft * 128, h, bh)
                            S.op('act', lambda e: e.activation(out=sg[:, :n], in_=k.ps[0][:, :n], func=AF.Sigmoid), reads=[k.bps[0]], writes=[bsg])
                            S.op('dve', lambda e, ft=ft, d=d: e.tensor_scalar(out=sg[:, :n], in0=sg[:, :n], scalar1=k.oml[:, d * 4 + ft, l:l + 1], scalar2=k.lb[:, d * 4 + ft, l:l + 1], op0=ALU.mult, op1=ALU.add), reads=[bsg, k.b_lb], writes=[bsg])
                            S.op('dve', lambda e: e.tensor_scalar(out=sg[:, :n], in0=sg[:, :n], scalar1=1e-30, scalar2=None, op0=ALU.max), reads=[bsg], writes=[bsg])
                            S.op('act', lambda e: e.activation(out=logf[:, :n], in_=sg[:, :n], func=AF.Ln), reads=[bsg], writes=[blf])
                            S.op('dve', lambda e, ft=ft: e.tensor_scalar(out=kT[:, ft, :n], in0=sg[:, :n], scalar1=-1.0, scalar2=1.0, op0=ALU.mult, op1=ALU.add), reads=[bsg], writes=[bkT])
                            S.op('dve', lambda e, ft=ft: e.tensor_tensor_scan(out=Pp[:, ft, 1:1 + n], data0=k.ones_f[:, :n], data1=logf[:, :n], initial=0.0, op0=ALU.mult, op1=ALU.add), reads=[blf, k.b_const], writes=[bP])
                            inproj_fm(k, 1, n, w, bw, ft * 128, h, bh)
                            S.op('act', lambda e, ft=ft: e.copy(qT[:, ft, :n], k.ps[1][:, :n]), reads=[k.bps[1]], writes=[bqT])
                        tl = list(range(ntile))
                        if d == 1:
                            tl = tl[::-1]

                        def stageE(t, pi):
                            t0 = t * 128
                            xo = t0 + 1 if d == 0 else t0
                            itm = itm2[pi]; bitm = bitm2[pi]; qtl = qtl2[pi]; bqtl = bqtl2[pi]; ktl = ktl2[pi]; bktl = bktl2[pi]
                            qh = qh2[pi]; bqh = bqh2[pi]; khtm = khtm2[pi]; bkhtm = bkhtm2[pi]; dec = dec2[pi]; bdec = bdec2[pi]
                            negr = negr2[pi]; bnr = bnr2[pi]
                            inproj_tm(k, 2, w, bw, 1024, h, bh, t0)
                            S.op('act', lambda e: e.copy(itm[:], k.ps[2][:]), reads=[k.bps[2]], writes=[bitm])
                            S.op('dve', lambda e: e.tensor_scalar(out=negr[:, :, 0:4], in0=Pp[:, :, t0 + 16:t0 + 113:32], scalar1=-1.0, scalar2=None, op0=ALU.mult), reads=[bP], writes=[bnr])
                            S.op('dve', lambda e: e.tensor_scalar(out=negr[:, :, 4:6], in0=Pp[:, :, t0:t0 + 129:128], scalar1=-1.0, scalar2=None, op0=ALU.mult), reads=[bP], writes=[bnr])
                            def tiles(ft):
                                return (dqa[ft], bdqa[ft], eqa[ft], beqa[ft], dqa[4 + ft], bdqa[4 + ft], eqa[4 + ft], beqa[4 + ft],
                                        Pp[:, ft, xo:xo + 128], Pp[:, ft, t0:t0 + 1], Pp[:, ft, t0 + 128:t0 + 129], negr[:, ft, 4:5], negr[:, ft, 5:6])
                            for ft in range(4):
                                dq, bdq, eq, beq, dq2, bdq2, eq2, beq2, X, B0p, B1p, B0n, B1n = tiles(ft)
                                S.op('dve', lambda e: e.tensor_tensor(out=dq[:].rearrange("p (i c) -> p i c", i=4), in0=X.rearrange("p (i c) -> p i c", i=4), in1=Pp[:, ft, t0 + 16:t0 + 113:32].unsqueeze(2).broadcast_to([128, 4, 32]), op=ALU.subtract), reads=[bP], writes=[bdq])
                            for ft in range(4):
                                dq, bdq, eq, beq, dq2, bdq2, eq2, beq2, X, B0p, B1p, B0n, B1n = tiles(ft)
                                S.op('act', lambda e: e.activation(out=eq[:], in_=dq[:], func=AF.Exp, scale=sgn), reads=[bdq], writes=[beq])
                                for i in range(4):
                                    lo, hi_ = (0, 32 * (i + 1)) if d == 0 else (32 * i, 128)
                                    bias = Pp[:, ft, t0 + 16 + 32 * i:t0 + 17 + 32 * i] if d == 0 else negr[:, ft, i:i + 1]
                                    S.op('act', lambda e: e.activation(out=ek[ft][:, i, lo:hi_], in_=Pp[:, ft, xo + lo:xo + hi_], func=AF.Exp, scale=-sgn, bias=bias), reads=[bP, bnr], writes=[bek[ft]])
                                bq_ = B0n if d == 0 else B1p
                                S.op('act', lambda e: e.activation(out=eq2[:], in_=X, func=AF.Exp, scale=sgn, bias=bq_), reads=[bP, bnr], writes=[beq2])
                                bk_ = B1p if d == 0 else B0n
                                S.op('act', lambda e: e.activation(out=dq2[:], in_=X, func=AF.Exp, scale=-sgn, bias=bk_), reads=[bP, bnr], writes=[bdq2])
                                S.op('act', lambda e: e.activation(out=dec[:, ft:ft + 1], in_=B1p, func=AF.Exp, scale=1.0, bias=B0n), reads=[bP, bnr], writes=[bdec])
                            for ft in range(4):
                                dq, bdq, eq, beq, dq2, bdq2, eq2, beq2, X, B0p, B1p, B0n, B1n = tiles(ft)
                                S.op('dve', lambda e: e.tensor_tensor(out=khT[:, ft, :], in0=dq2[:], in1=kT[:, ft, t0:t0 + 128], op=ALU.mult), reads=[bdq2, bkT], writes=[bkhT])
                                S.op('pe', lambda e: e.transpose(ptr[:, ft * 128:(ft + 1) * 128], khT[:, ft, :], k.ident[:]), reads=[bkhT, k.b_const], writes=[k.bps[5]])
                                S.op('dve', lambda e: e.tensor_tensor(out=qtl[0:64, 0, ft, :], in0=eq[0:64, :], in1=qT[0:64, ft, t0:t0 + 128], op=ALU.mult), reads=[beq, bqT], writes=[bqtl])
                                S.op('dve', lambda e: e.tensor_tensor(out=qtl[64:128, 1, ft, :], in0=eq[64:128, :], in1=qT[64:128, ft, t0:t0 + 128], op=ALU.mult), reads=[beq, bqT], writes=[bqtl])
                                S.op('pool', lambda e: e.tensor_tensor(out=ktl[:, ft, :, :], in0=ek[ft][:], in1=kT[:, ft, t0:t0 + 128].unsqueeze(1).broadcast_to([128, 4, 128]), op=ALU.mult), reads=[bek[ft], bkT], writes=[bktl])
                                S.op('dve', lambda e: e.tensor_tensor(out=qh[:, ft, :], in0=eq2[:], in1=qT[:, ft, t0:t0 + 128], op=ALU.mult), reads=[beq2, bqT], writes=[bqh])
                            S.op('dve', lambda e: e.tensor_copy(khtm[:], ptr[:, 0:512]), reads=[k.bps[5]], writes=[bkhtm])

                        def stageF(t, pi):
                            t0 = t * 128
                            itm = itm2[pi]; bitm = bitm2[pi]; qtl = qtl2[pi]; bqtl = bqtl2[pi]; ktl = ktl2[pi]; bktl = bktl2[pi]
                            qh = qh2[pi]; bqh = bqh2[pi]; khtm = khtm2[pi]; bkhtm = bkhtm2[pi]; dec = dec2[pi]; bdec = bdec2[pi]
                            for hd in range(8):
                                cp = hd // 2
                                sbk = 3 + hd // 4
                                for i in range(4):
                                    dst = k.ps[sbk][:, (hd % 4) * 128 + 32 * i:(hd % 4) * 128 + 32 * i + 32]
                                    S.op('pe', lambda e: e.matmul(dst, lhsT=ktl[:, cp, i, :], rhs=qtl[:, hd % 2, cp, 32 * i:32 * i + 32], start=True, stop=True),
                                         reads=[bktl, bqtl], writes=[k.bps[sbk]], inc=(hd % 4 == 3 and i == 3))
                            for half in range(2):
                                S.op('dve', lambda e: e.copy_predicated(out=At[:, 4 * half:4 * half + 4, :], mask=mask[:].unsqueeze(1).broadcast_to([128, 4, 128]), data=k.ps[3 + half][:].rearrange("p (h t) -> p h t", h=4)), reads=[k.bps[3 + half], bc, bAt], writes=[bAt])
                            for cp in range(4):
                                for j in range(2):
                                    hd = 2 * cp + j
                                    S.op('pe', lambda e: e.matmul(k.ps[6][64 * j:64 * j + 64, cp * 128:(cp + 1) * 128], lhsT=itm[:, hd * 64:(hd + 1) * 64], rhs=At[:, hd, :], start=True, stop=False), reads=[bitm, bAt], writes=[k.bps[6]], inc=False)
                                S.op('pe', lambda e: e.matmul(k.ps[6][:, cp * 128:(cp + 1) * 128], lhsT=Sbf[:, cp, :], rhs=qh[:, cp, :], start=False, stop=True), reads=[bSbf, bqh], writes=[k.bps[6]], inc=(cp == 3))
                            for cp in range(4):
                                S.op('pe', lambda e: e.matmul(k.ps[7][:, cp * 128:(cp + 1) * 128], lhsT=khtm[:, cp * 128:(cp + 1) * 128], rhs=itm[:, cp * 128:(cp + 1) * 128], start=True, stop=True), reads=[bkhtm, bitm], writes=[k.bps[7]], inc=(cp == 3))
                            for cp in range(4):
                                for j in range(2):
                                    pb = 64 * j
                                    S.op('dve', lambda e: e.scalar_tensor_tensor(out=St[pb:pb + 64, d, cp, :], in0=St[pb:pb + 64, d, cp, :], scalar=dec[pb:pb + 64, cp:cp + 1], in1=k.ps[7][pb:pb + 64, cp * 128 + 64 * j:cp * 128 + 64 * j + 64], op0=ALU.mult, op1=ALU.add),
                                         reads=[bS[d], bdec, k.bps[7]], writes=[bS[d]])
                            S.op('act', lambda e: e.copy(Sbf[0:64, :, 0:64], St[0:64, d, :, :]), reads=[bS[d]], writes=[bSbf])
                            S.op('act', lambda e: e.copy(Sbf[64:128, :, 64:128], St[64:128, d, :, :]), reads=[bS[d]], writes=[bSbf])
                            if not final:
                                S.op('act', lambda e: e.copy(o_b[:, :, t0:t0 + 128], k.ps[6][:].rearrange("p (c t) -> p c t", c=4)), reads=[k.bps[6]], writes=[bo])
                            else:
                                S.op('dve', lambda e: e.tensor_tensor(out=o_b[:, :, t0:t0 + 128], in0=k.ps[6][:].rearrange("p (c t) -> p c t", c=4), in1=o_b[:, :, t0:t0 + 128], op=ALU.add), reads=[k.bps[6], bo], writes=[bo])

                        for it_, t in enumerate(tl):
                            if it_ == 0:
                                stageE(t, pcnt[0] % 2)
                            if it_ + 1 < len(tl):
                                stageE(tl[it_ + 1], (pcnt[0] + 1) % 2)
                            stageF(t, pcnt[0] % 2)
                            pcnt[0] += 1
                        if not final:
                            S.dma('pool', k.ofT[s, :, c0:c0 + n].rearrange("(j p) t -> p j t", p=128), o_b[:, :, :n], reads=[bo], writes=k.t_ofT[s].r(c0, c0 + n))
                        elif (not isctx) or with_ctx:
                            for cp in range(4):
                                S.op('act', lambda e, cp=cp, o_b=o_b: e.activation(out=sq[:, :n], in_=o_b[:, cp, :n], func=AF.Square), reads=[bo], writes=[bsq])
                                S.op('pe', lambda e: e.matmul(k.ps[0][:, :n], lhsT=bones[:], rhs=sq[:, :n], start=True, stop=True), reads=[bc, bsq], writes=[k.bps[0]])
                                S.op('act', lambda e: e.activation(out=rst[:, :n], in_=k.ps[0][:, :n], func=AF.Ln, scale=1.0 / 64.0, bias=EPS), reads=[k.bps[0]], writes=[brst])
                                S.op('act', lambda e: e.activation(out=rst[:, :n], in_=rst[:, :n], func=AF.Exp, scale=-0.5), reads=[brst], writes=[brst])
                                inproj_fm(k, 1, n, w, bw, 1536 + cp * 128, h, bh)
                                S.op('act', lambda e: e.activation(out=gl[:, :n], in_=k.ps[1][:, :n], func=AF.Silu), reads=[k.bps[1]], writes=[bgl])
                                S.op('dve', lambda e, cp=cp, o_b=o_b: e.scalar_tensor_tensor(out=rst[:, :n], in0=o_b[:, cp, :n], scalar=gn[:, cp:cp + 1], in1=rst[:, :n], op0=ALU.mult, op1=ALU.mult), reads=[bo, bc, brst], writes=[brst])
                                S.op('pool', lambda e, cp=cp, y_b=y_b: e.tensor_tensor(out=y_b[:, cp, :n], in0=rst[:, :n], in1=gl[:, :n], op=ALU.mult), reads=[brst, bgl], writes=[by_])
                            S.dma('pool', k.yT[s, 3, :, c0:c0 + n].rearrange("(j p) t -> p j t", p=128), y_b[:, :, :n], reads=[by_], writes=k.t_yT[s][3].r(c0, c0 + n))
        S.barrier()


def phase_m(k, l, with_ctx):
    S, nc, I = k.S, k.nc, k.I
    p = l % 2
    with contextlib.ExitStack() as st:
        sb = lambda n, sh, d: st.enter_context(nc.sbuf_tensor(_u(n), list(sh), d))
        wg = sb("m_wg", [128, 8, 4096], BF16); bw = Buf()
        wbr = sb("m_wbr", [128, 16, D], BF16)
        wo = sb("m_wo", [128, 8, D], BF16)
        bwc = Buf()
        hb = [sb("m_hb%d" % i, [128, 8, 256], BF16) for i in range(2)]; bhb = [Buf() for _ in range(2)]
        yb = [sb("m_yb%d" % i, [128, 16, 256], BF16) for i in range(2)]; byb = [Buf() for _ in range(2)]
        sgt = [sb("m_sg%d" % i, [128, 256], F32) for i in range(2)]; bsg = [Buf() for _ in range(2)]
        tmp = [sb("m_tmp%d" % i, [128, 256], F32) for i in range(2)]; btmp = [Buf() for _ in range(2)]
        macc = sb("m_acc", [128, 256], F32); bacc = Buf()
        mT = sb("m_mT", [128, 8, 256], BF16); bmT = Buf()
        xt = [sb("m_x%d" % i, [128, D], F32) for i in range(2)]; bx = [Buf() for _ in range(2)]
        tt = sb("m_tt", [128, D], F32); btt = Buf()
        junk = sb("m_junk", [128, 512], F32); bj = Buf()
        ms = sb("m_ms", [128, 2], F32); bms = Buf()
        load_w(k, l, wg[:], COL["gate"], 4096, bw)
        S.dma('sp', wbr[:], k.w_br_bf[p].rearrange("(a p) c -> p a c", p=128), reads=[k.t_w[p]], writes=[bwc])
        S.dma('sp', wo[:], k.w_out_bf[p].rearrange("(a p) c -> p a c", p=128), reads=[k.t_w[p]], writes=[bwc])
        bi = 0
        xi = 0
        gi = 0
        for s in range(2):
            blocks = [(c0, False) for c0 in range(0, NTOK, 256)] + ([(NTOK, True)] if with_ctx else [])
            for (c0, isctx) in blocks:
                mi = 2 if isctx else s
                h = hb[bi % 2]; bh = bhb[bi % 2]; y = yb[bi % 2]; by_ = byb[bi % 2]
                bi += 1
                load_h(k, s, h, c0, 256, bh)
                for r in range(4):
                    S.dma('sp', y[:, 4 * r:4 * r + 4, :], k.yT[s, r, :, c0:c0 + 256].rearrange("(j p) t -> p j t", p=128), reads=k.t_yT[s][r].r(c0, c0 + 256), writes=[by_])
                for fc in range(8):
                    for r in range(4):
                        gb = gi % 2; gi += 1
                        for kc in range(8):
                            S.op('pe', lambda e, kc=kc, r=r, fc=fc, gb=gb: e.matmul(k.ps[gb][:, :256], lhsT=wg[:, kc, r * 1024 + fc * 128:r * 1024 + (fc + 1) * 128], rhs=h[:, kc, :], start=(kc == 0), stop=(kc == 7)),
                                 reads=[bw, bh], writes=[k.bps[gb]], inc=(kc == 7))
                        S.op('act', lambda e, gb=gb: e.activation(out=sgt[gb][:], in_=k.ps[gb][:, :256], func=AF.Sigmoid), reads=[k.bps[gb]], writes=[bsg[gb]])
                        for kc in range(4):
                            S.op('pe', lambda e, kc=kc, r=r, fc=fc, gb=gb: e.matmul(k.ps[2 + gb][:, :256], lhsT=wbr[:, 4 * r + kc, fc * 128:(fc + 1) * 128], rhs=y[:, 4 * r + kc, :], start=(kc == 0), stop=(kc == 3)),
                                 reads=[bwc, by_], writes=[k.bps[2 + gb]], inc=(kc == 3))
                        if r == 0:
                            S.op('dve', lambda e, gb=gb: e.tensor_tensor(out=macc[:], in0=k.ps[2 + gb][:, :256], in1=sgt[gb][:], op=ALU.mult), reads=[k.bps[2 + gb], bsg[gb]], writes=[bacc])
                        else:
                            S.op('dve', lambda e, gb=gb: e.tensor_tensor(out=tmp[gb][:], in0=k.ps[2 + gb][:, :256], in1=sgt[gb][:], op=ALU.mult), reads=[k.bps[2 + gb], bsg[gb]], writes=[btmp[gb]])
                            S.op('pool', lambda e, gb=gb: e.tensor_tensor(out=macc[:], in0=macc[:], in1=tmp[gb][:], op=ALU.add), reads=[bacc, btmp[gb]], writes=[bacc])
                    S.op('pool', lambda e, fc=fc: e.tensor_copy(mT[:, fc, :], macc[:]), reads=[bacc], writes=[bmT])
                for t in range(2):
                    tok0 = c0 + t * 128
                    x = xt[xi % 2]; bxx = bx[xi % 2]; xi += 1
                    if isctx:
                        src = (I["ctx"] if l == 0 else k.ctxcur)[s, t * 128:(t + 1) * 128, :]
                        dstd = k.ctxcur[s, t * 128:(t + 1) * 128, :]
                        tb = k.t_ctx[s].r(t * 128, t * 128 + 128)
                    else:
                        src = (I["x"] if l == 0 else k.OUT)[s, tok0:tok0 + 128, :]
                        dstd = k.OUT[s, tok0:tok0 + 128, :]
                        tb = k.t_x[s].r(tok0, tok0 + 128)
                    S.dma('sp', x[:], src, reads=tb, writes=[bxx])
                    for half in range(2):
                        for kc in range(8):
                            S.op('pe', lambda e, kc=kc, half=half, t=t: e.matmul(k.ps[4 + half][:], lhsT=mT[:, kc, t * 128:(t + 1) * 128], rhs=wo[:, kc, half * 512:(half + 1) * 512], start=(kc == 0), stop=(kc == 7)),
                                 reads=[bmT, bwc], writes=[k.bps[4 + half]], inc=(kc == 7))
                        S.op('act', lambda e, half=half: e.activation(out=junk[:], in_=k.ps[4 + half][:], func=AF.Square, scale=1.0 / 32.0, accum_out=ms[:, half:half + 1]), reads=[k.bps[4 + half]], writes=[bj, bms])
                    S.op('dve', lambda e: e.tensor_tensor(out=ms[:, 0:1], in0=ms[:, 0:1], in1=ms[:, 1:2], op=ALU.add), reads=[bms], writes=[bms])
                    rstd_from_ms(k, ms[:, 0:1], bms)
                    for half in range(2):
                        sl = slice(half * 512, (half + 1) * 512)
                        S.op('dve', lambda e, half=half, sl=sl, mi=mi: e.scalar_tensor_tensor(out=tt[:, sl], in0=k.ps[4 + half][:], scalar=ms[:, 0:1], in1=k.gtg[:, mi, sl], op0=ALU.mult, op1=ALU.mult), reads=[k.bps[4 + half], bms, k.b_gtg], writes=[btt])
                    S.op('pool', lambda e, x=x: e.tensor_tensor(out=x[:], in0=x[:], in1=tt[:], op=ALU.add), reads=[bxx, btt], writes=[bxx])
                    S.dma('pool', dstd, x[:], reads=[bxx], writes=tb)
        S.barrier()


_BF = ml_dtypes.bfloat16
_CONST = {}


def _constants():
    if _CONST:
        return _CONST
    c = {}
    c["ident"] = np.eye(128, dtype=np.float32).astype(_BF)
    c["eye8"] = (8.0 * np.eye(128, dtype=np.float32)).astype(_BF)
    rm = np.zeros((128, 128), np.float32)
    for dp in range(128):
        dd = dp % 64
        if (dd % 32) < 16:
            rm[dp, dp + 16] = -1.0
        else:
            rm[dp, dp - 16] = 1.0
    c["rsign"] = np.ascontiguousarray(rm.T).astype(_BF)
    t = np.arange(NTOK)
    pos = np.stack([t // 64, t % 64], 0).astype(np.float64)
    inv = 10000.0 ** (-np.arange(16, dtype=np.float64) * 2.0 / 32.0)
    d = np.arange(128) % 64
    ang = pos[d // 32, :] * inv[d % 16][:, None]
    c["cosT"] = np.cos(ang).astype(np.float32)
    c["sinT"] = np.sin(ang).astype(np.float32)
    s_, t_ = np.meshgrid(np.arange(128), np.arange(128), indexing="ij")
    c["trif"] = (s_ <= t_).astype(np.int32)
    c["trib"] = (s_ >= t_).astype(np.int32)
    c["bones"] = ((s_ // 64) == (t_ // 64)).astype(np.float32).astype(_BF)
    n = np.arange(NTOK, dtype=np.int64)
    m = (n[:, None] * n[None, :]) % NTOK
    sc = 1.0 / np.sqrt(NTOK * 128.0)
    angm = (2.0 * np.pi / NTOK) * m.astype(np.float32)
    cs = (np.cos(angm) * sc).astype(np.float32)
    sn = (np.sin(angm) * sc).astype(np.float32)
    dft = np.empty((16, NTOK, 512), dtype=_BF)
    for kb in range(16):
        dft[kb, :, 0:256] = cs[:, kb * 256:(kb + 1) * 256].astype(_BF)
        dft[kb, :, 256:512] = sn[:, kb * 256:(kb + 1) * 256].astype(_BF)
    c["dft"] = dft
    n2 = np.arange(LCTX, dtype=np.int64)
    a2 = (2.0 * np.pi / LCTX) * ((n2[:, None] * n2[None, :]) % LCTX)
    sc2 = 1.0 / np.sqrt(LCTX * 128.0)
    c["dft256"] = np.concatenate([np.cos(a2) * sc2, np.sin(a2) * sc2], 1).astype(np.float32).astype(_BF)
    n3 = np.arange(128, dtype=np.int64)
    a3 = (2.0 * np.pi / 128) * ((n3[:, None] * n3[None, :]) % 128)
    c["cc128"] = np.cos(a3).astype(np.float32).astype(_BF)
    c["ssn128"] = (-np.sin(a3)).astype(np.float32).astype(_BF)
    _CONST.update(c)
    return _CONST


def _zb_tables(rpb):
    L = rpb.shape[0]
    e = np.arange(2)[:, None, None, None]
    kc = np.arange(64)[None, :, None, None]
    w = np.arange(ZW)[None, None, :, None]
    qc = np.arange(64)[None, None, None, :]
    dr = 14 - w + e + 0 * kc + 0 * qc
    cs = np.clip(qc - 8, 0, 48)
    col_ok = (kc >= cs) & (kc < cs + 16)
    cidx = np.clip(kc - qc + 15, 0, 30) + 0 * dr
    out = np.empty((L, 2, 128, 8, ZW * 64), dtype=_BF)
    for kind in range(2):
        row_ok = (dr >= -7) & (dr <= 7)
        if kind == 1:
            row_ok = row_ok & (dr >= -4) & (dr < 4)
        ok = (row_ok & col_ok)
        ridx = np.clip(dr + 7, 0, 14)
        for l in range(L):
            g = rpb[l][:, ridx, cidx]
            g = np.where(ok[None], g, np.float32(NEG))
            out[l, kind] = g.transpose(1, 2, 0, 3, 4).reshape(128, 8, ZW * 64).astype(_BF)
    return out


def make_in_maps(inputs, nlayers=DEPTH, cores=range(8)):
    f = lambda a: np.ascontiguousarray(np.asarray(a, dtype=np.float32))
    c = _constants()
    L = nlayers
    shared = dict(c)
    shared["w_ada"] = f(inputs["w_ada"][:L])
    shared["b_ada"] = f(inputs["b_ada"][:L])
    shared["b_adaT"] = f(np.asarray(inputs["b_ada"][:L]).reshape(L, 24, 128).transpose(0, 2, 1))
    shared["g_preT"] = f(np.asarray(inputs["g_pre"][:L]).reshape(L, 8, 128).transpose(0, 2, 1))
    shared["g_post"] = f(inputs["g_post"][:L])
    shared["w_in"] = f(inputs["w_in"][:L])
    shared["zb"] = _zb_tables(np.asarray(inputs["na_rpb"][:L], dtype=np.float32))
    shared["fnet_w"] = f(np.asarray(inputs["fnet_w"][:L]).reshape(L, 512, 128))
    shared["gmlp_g"] = f(inputs["gmlp_norm_g"][:L])
    shared["gmlp_wsT"] = f(np.asarray(inputs["gmlp_ws"][:L]).transpose(0, 1, 3, 2).reshape(L, 1024, 128))
    shared["gmlp_bsT"] = f(np.asarray(inputs["gmlp_bs"][:L]).transpose(0, 2, 1))
    lg = np.asarray(inputs["hgrn_lb_logits"], dtype=np.float32)
    shared["lbT"] = f(lg.reshape(DEPTH, 2, 4, 128).transpose(3, 1, 2, 0).reshape(128, 8, DEPTH))
    shared["hgrn_gT"] = f(np.asarray(inputs["hgrn_norm_g"][:L]).reshape(L, 4, 128).transpose(0, 2, 1))
    shared["w_branch"] = f(np.asarray(inputs["w_branch"][:L]).reshape(L, 2048, D))
    shared["w_out"] = f(inputs["w_out"][:L])
    x = np.asarray(inputs["x"]); ctx = np.asarray(inputs["ctx"]); cc = np.asarray(inputs["c"]); c_ctx = np.asarray(inputs["c_ctx"])
    maps = []
    for ci in cores:
        m = dict(shared)
        m["x"] = f(x[2 * ci:2 * ci + 2])
        m["ctx"] = f(ctx[2 * ci:2 * ci + 2])
        c3 = np.stack([cc[2 * ci], cc[2 * ci + 1], c_ctx], 0)
        m["cT"] = f(c3.reshape(3, 8, 128).transpose(2, 1, 0))
        maps.append(m)
    return maps


_NC_CACHE = {}


def kernel(**inputs):
    if "nc" not in _NC_CACHE:
        _NC_CACHE["nc"] = build()
    nc = _NC_CACHE["nc"]
    maps = make_in_maps(inputs)
    res = run_bass_kernel_spmd(nc, maps, core_ids=list(range(8)))
    return np.concatenate([np.asarray(r["out"], dtype=np.float32) for r in res.results], axis=0)
```

```python
import contextlib
import numpy as np
import ml_dtypes
import concourse.bass as bass
import concourse.mybir as mybir
from concourse.bass_utils import run_bass_kernel_spmd

F32 = mybir.dt.float32
BF16 = mybir.dt.bfloat16
I32 = mybir.dt.int32
AF = mybir.ActivationFunctionType
ALU = mybir.AluOpType

D = 1024
NTOK = 4096
LCTX = 256
NT = NTOK + LCTX
NCOL = 11264
DEPTH = 4
EPS = 1e-6
COL = dict(a_q=0, a_k=512, a_v=1024, a_g=1536, b_x=2048, b_g=2560, c_u=3072, c_v=3584, c_g=4096,
           d_q=4608, d_ff=5120, d_fb=5632, d_i=6144, d_g=6656, gate=7168)
ZW = 26
NEG = -30000.0

SEM_WINDOW = 16000
DMA_RING = 8


class Buf:
    __slots__ = ("name", "lw", "rd", "excl")

    def __init__(self, name="", excl=False):
        self.name = name
        self.lw = None
        self.rd = {}
        self.excl = excl


class DTrack:
    def __init__(self, ncols, unit=128):
        self.unit = unit
        self.b = [Buf() for _ in range((ncols + unit - 1) // unit)]

    def r(self, c0, c1):
        return self.b[c0 // self.unit:(c1 + self.unit - 1) // self.unit]


class _Rec:
    def __init__(self):
        self.call = None

    def __getattr__(self, name):
        def f(*a, **kw):
            self.call = (name, a, kw)
            return self
        return f


def _freeze(fn):
    r = _Rec()
    fn(r)
    name, a, kw = r.call
    return lambda e: getattr(e, name)(*a, **kw)


class Sched:
    ENGS = ("pe", "act", "dve", "pool", "sp")

    def __init__(self, nc):
        self.nc = nc
        self.ops = {e: [] for e in self.ENGS}
        self.cnt = {e: 0 for e in self.ENGS}
        self.dcnt = {e: 0 for e in self.ENGS}
        self.known = {e: {} for e in self.ENGS}
        self.pending = {e: False for e in self.ENGS}

    def _tokwaits(self, eng, toks):
        waits = {}
        for t in toks:
            if t[0] == 'e':
                if t[1] == eng and eng == 'pe':
                    continue
                key = ('e', t[1], (t[2] - 1) // SEM_WINDOW)
                val = (t[2] - 1) % SEM_WINDOW + 1
            else:
                key = ('d', t[1], t[2] % DMA_RING)
                val = 16 * (t[2] // DMA_RING + 1)
            if waits.get(key, 0) < val:
                waits[key] = val
        out = []
        kn = self.known[eng]
        for key, val in waits.items():
            if kn.get(key, 0) >= val:
                continue
            kn[key] = val
            out.append((key, val))
        return out

    def _deps(self, eng, reads, writes):
        toks = []
        for b in reads:
            if b.lw is not None:
                toks.append(b.lw)
            if b.excl:
                toks.extend(v for kk, v in b.rd.items() if kk != eng)
        for b in writes:
            if b.lw is not None and not (b.lw[0] == 'e' and b.lw[1] == eng):
                toks.append(b.lw)
            toks.extend(v for v in b.rd.values() if not (v[0] == 'e' and v[1] == eng))
        return self._tokwaits(eng, toks)

    def op(self, eng, fn, reads=(), writes=(), inc=True):
        fn = _freeze(fn)
        waits = self._deps(eng, reads, writes)
        idx = self.cnt[eng] + 1
        tok = ('e', eng, idx)
        if inc:
            self.cnt[eng] = idx
            self.ops[eng].append((waits, fn, ('e', eng, (idx - 1) // SEM_WINDOW), 1))
            self.pending[eng] = False
        else:
            self.ops[eng].append((waits, fn, None, 0))
            self.pending[eng] = True
        for b in reads:
            b.rd[eng] = tok
        for b in writes:
            b.lw = tok
            b.rd = {}
        return tok

    def dma(self, q, out, in_, reads=(), writes=()):
        waits = self._deps(q, reads, writes)
        i = self.dcnt[q]
        self.dcnt[q] += 1
        if i >= DMA_RING:
            key = ('d', q, i % DMA_RING)
            val = 16 * (i // DMA_RING)
            kn = self.known[q]
            if kn.get(key, 0) < val:
                kn[key] = val
                waits.append((key, val))
        tok = ('d', q, i)
        fn = lambda e, out=out, in_=in_: e.dma_start(out=out, in_=in_)
        self.ops[q].append((waits, fn, ('d', q, i % DMA_RING), 16))
        qk = 'q' + q
        for b in reads:
            b.rd[qk] = tok
        for b in writes:
            b.lw = tok
            b.rd = {}
        return tok

    def barrier(self):
        toks = []
        for e in self.ENGS:
            assert not self.pending[e]
            if self.cnt[e] > 0:
                toks.append(('e', e, self.cnt[e]))
            n = self.dcnt[e]
            for i in range(max(0, n - DMA_RING), n):
                toks.append(('d', e, i))
        for e in self.ENGS:
            w = self._tokwaits(e, toks)
            if w:
                self.ops[e].append((w, None, None, 0))

    def emit(self):
        nc = self.nc
        self.barrier()
        sems = {}
        with contextlib.ExitStack() as st:
            def getsem(key):
                if key not in sems:
                    sems[key] = st.enter_context(nc.semaphore("s_%s_%s_%d" % key))
                return sems[key]
            for e in self.ENGS:
                for (waits, fn, inc, amt) in self.ops[e]:
                    for key, val in waits:
                        getsem(key)
                    if inc is not None:
                        getsem(inc)
            block = st.enter_context(nc.Block())
            handles = {"pe": block.tensor, "act": block.scalar, "dve": block.vector,
                       "pool": block.gpsimd, "sp": block.sync}
            for e in self.ENGS:
                ops = self.ops[e]
                if not ops:
                    continue

                def body(engine, ops=ops):
                    for (waits, fn, inc, amt) in ops:
                        for key, val in waits:
                            engine.wait_ge(sems[key], val)
                        if fn is not None:
                            ins = fn(engine)
                            if inc is not None:
                                ins.then_inc(sems[inc], amt)
                handles[e](body)


_UC = [0]
_DBG = {}


def _u(n):
    _UC[0] += 1
    return "%s_%d" % (n, _UC[0])


class K:
    pass


def _dram(nc, name, shape, dt, kind=None):
    if kind is None:
        return nc.dram_tensor(name, list(shape), dt).ap()
    return nc.dram_tensor(name, list(shape), dt, kind=kind).ap()


def build(nlayers=DEPTH, phases="PABCDM", dump=()):
    nc = bass.Bass("TRN2", target_bir_lowering=False)
    k = K()
    k.nc = nc
    S = Sched(nc)
    k.S = S
    IN = "ExternalInput"
    I = {}
    def inp(name, shape, dt=F32):
        I[name] = _dram(nc, name, shape, dt, IN)
        return I[name]
    inp("x", [2, NTOK, D]); inp("ctx", [2, LCTX, D]); inp("cT", [128, 8, 3])
    inp("w_ada", [nlayers, D, 3 * D]); inp("b_ada", [nlayers, 3 * D]); inp("b_adaT", [nlayers, 128, 24])
    inp("g_preT", [nlayers, 128, 8]); inp("g_post", [nlayers, D])
    inp("w_in", [nlayers, D, NCOL])
    inp("zb", [nlayers, 2, 128, 8, ZW * 64], BF16)
    inp("fnet_w", [nlayers, 512, 128]); inp("gmlp_g", [nlayers, 512]); inp("gmlp_wsT", [nlayers, 1024, 128])
    inp("gmlp_bsT", [nlayers, 128, 8]); inp("lbT", [128, 8, DEPTH]); inp("hgrn_gT", [nlayers, 128, 4])
    inp("w_branch", [nlayers, 2048, D]); inp("w_out", [nlayers, D, D])
    inp("ident", [128, 128], BF16); inp("rsign", [128, 128], BF16); inp("eye8", [128, 128], BF16)
    inp("cosT", [128, NTOK]); inp("sinT", [128, NTOK])
    inp("trif", [128, 128], I32); inp("trib", [128, 128], I32); inp("bones", [128, 128], BF16)
    inp("dft", [16, NTOK, 512], BF16); inp("dft256", [LCTX, 512], BF16)
    inp("cc128", [128, 128], BF16); inp("ssn128", [128, 128], BF16)
    OUT = _dram(nc, "out", [2, NTOK, D], F32, "ExternalOutput")
    def scr(name, shape, dt):
        return _dram(nc, name, shape, dt, "ExternalOutput" if name in dump else None)
    k.w_in_bf = scr("w_in_bf", [2, D, NCOL], BF16)
    k.w_br_bf = scr("w_br_bf", [2, 2048, D], BF16)
    k.w_out_bf = scr("w_out_bf", [2, D, D], BF16)
    k.fw_bf = scr("fw_bf", [2, 512, 128], BF16)
    k.ws_bf = scr("ws_bf", [2, 1024, 128], BF16)
    k.hT = scr("hT", [2, D, NT], BF16)
    k.yT = scr("yT", [2, 4, 512, NT], BF16)
    k.ofT = scr("ofT", [2, 512, NT], F32)
    k.ctxcur = scr("ctxcur", [2, LCTX, D], F32)
    k.I = I
    k.OUT = OUT
    k.t_w = [Buf() for _ in range(2)]
    k.t_hT = [DTrack(NT) for _ in range(2)]
    k.t_yT = [[DTrack(NT) for _ in range(4)] for _ in range(2)]
    k.t_ofT = [DTrack(NT) for _ in range(2)]
    k.t_x = [DTrack(NTOK) for _ in range(2)]
    k.t_ctx = [DTrack(LCTX) for _ in range(2)]

    with contextlib.ExitStack() as gst:
        k.gst = gst
        def gsb(name, shape, dt):
            return gst.enter_context(nc.sbuf_tensor(name, list(shape), dt))
        k.ps = [gst.enter_context(nc.psum_tensor("ps%d" % i, [128, 512], F32)) for i in range(8)]
        k.bps = [Buf("ps%d" % i, excl=True) for i in range(8)]
        k.ident = gsb("ident_sb", [128, 128], BF16)
        k.ones_bf = gsb("ones_bf", [128, 128], BF16)
        k.ones_f = gsb("ones_f", [128, 512], F32)
        k.scT = gsb("scT", [128, 8, 3], F32)
        k.modT = gsb("modT", [128, 24, 3], F32)
        k.gsT = gsb("gsT", [128, 8, 3], F32)
        k.gtg = gsb("gtg", [128, 3, D], F32)
        k.lb = gsb("lb", [128, 8, DEPTH], F32)
        k.oml = gsb("oml", [128, 8, DEPTH], F32)
        k.b_const = Buf(); k.b_scT = Buf(); k.b_mod = Buf(); k.b_gtg = Buf(); k.b_lb = Buf()
        S.dma('sp', k.ident[:], I["ident"], writes=[k.b_const])
        S.op('dve', lambda e: e.memset(k.ones_bf[:], 1.0), writes=[k.b_const])
        S.op('dve', lambda e: e.memset(k.ones_f[:], 1.0), writes=[k.b_const])
        prep_global(k)
        for l in range(nlayers):
            S.barrier()
            convert_weights(k, l)
            adaln(k, l)
            with_ctx = l < DEPTH - 1
            if "P" in phases:
                for s in range(2):
                    phase_pre(k, l, s)
            if "C" in phases:
                phase_c(k, l, with_ctx)
            if "B" in phases:
                phase_b(k, l, with_ctx)
            if "A" in phases:
                phase_a(k, l, with_ctx)
            if "D" in phases:
                phase_d(k, l, with_ctx)
            if "M" in phases:
                phase_m(k, l, with_ctx)
        S.emit()
    return nc


def prep_global(k):
    S, nc, I = k.S, k.nc, k.I
    with contextlib.ExitStack() as st:
        sb = lambda n, s, d: st.enter_context(nc.sbuf_tensor(_u(n), list(s), d))
        cT = sb("pg_cT", [128, 8, 3], F32)
        lg = sb("pg_lg", [128, 8, DEPTH], F32)
        ex = sb("pg_ex", [128, 8, DEPTH], F32)
        sm = sb("pg_sm", [128, 8], F32)
        b = Buf()
        S.dma('sp', cT[:], I["cT"], writes=[b])
        S.op('act', lambda e: e.activation(out=k.scT[:], in_=cT[:], func=AF.Silu), reads=[b], writes=[k.b_scT])
        b2 = Buf()
        S.dma('sp', lg[:], I["lbT"], writes=[b2])
        S.op('act', lambda e: e.activation(out=ex[:], in_=lg[:], func=AF.Exp), reads=[b2], writes=[b2])
        S.op('dve', lambda e: e.tensor_reduce(out=sm[:], in_=ex[:], axis=mybir.AxisListType.X, op=ALU.add), reads=[b2], writes=[b2])
        S.op('dve', lambda e: e.reciprocal(sm[:], sm[:]), reads=[b2], writes=[b2])
        S.op('dve', lambda e: e.tensor_tensor(out=ex[:], in0=ex[:], in1=sm[:].unsqueeze(2).broadcast_to([128, 8, DEPTH]), op=ALU.mult), reads=[b2], writes=[b2])
        S.op('dve', lambda e: e.memset(k.lb[:, :, 0:1], 0.0), reads=[b2], writes=[k.b_lb])
        for l in range(1, DEPTH):
            S.op('dve', lambda e, l=l: e.tensor_tensor(out=k.lb[:, :, l:l + 1], in0=k.lb[:, :, l - 1:l], in1=ex[:, :, l:l + 1], op=ALU.add), reads=[b2, k.b_lb], writes=[k.b_lb])
        S.op('dve', lambda e: e.tensor_scalar(out=k.lb[:], in0=k.lb[:], scalar1=0.0, scalar2=None, op0=ALU.max), reads=[k.b_lb], writes=[k.b_lb])
        S.op('dve', lambda e: e.tensor_scalar(out=k.oml[:], in0=k.lb[:], scalar1=-1.0, scalar2=1.0, op0=ALU.mult, op1=ALU.add), reads=[k.b_lb], writes=[k.b_lb])
        S.barrier()


def convert_weights(k, l):
    S, I = k.S, k.I
    p = l % 2
    w = [k.t_w[p]]
    for c0 in range(0, NCOL, 1408):
        S.dma('pool', k.w_in_bf[p, :, c0:c0 + 1408], I["w_in"][l, :, c0:c0 + 1408], writes=w)
    S.dma('pool', k.w_br_bf[p], I["w_branch"][l], writes=w)
    S.dma('pool', k.w_out_bf[p], I["w_out"][l], writes=w)
    S.dma('pool', k.fw_bf[p], I["fnet_w"][l], writes=w)
    S.dma('pool', k.ws_bf[p], I["gmlp_wsT"][l], writes=w)


def load_w(k, l, dst, c0, ncol, bdst):
    p = l % 2
    src = k.w_in_bf[p, :, c0:c0 + ncol].rearrange("(kc p) c -> p kc c", p=128)
    k.S.dma('sp', dst, src, reads=[k.t_w[p]], writes=[bdst])


def adaln(k, l):
    S, nc, I = k.S, k.nc, k.I
    with contextlib.ExitStack() as st:
        sb = lambda n, s, d: st.enter_context(nc.sbuf_tensor(_u(n), list(s), d))
        wa = [sb("ad_wa%d" % i, [128, 8, 512], F32) for i in range(2)]
        bwa = [Buf() for _ in range(2)]
        scbc = sb("ad_scbc", [128, 8, 3, 128], F32)
        bT = sb("ad_bT", [128, 24], F32)
        gpT = sb("ad_gpT", [128, 8], F32)
        brow = sb("ad_brow", [128, D], F32)
        grow = sb("ad_grow", [128, D], F32)
        tmp = sb("ad_tmp", [128, 8, 3], F32)
        b = Buf(); bsc = Buf()
        S.dma('sp', bT[:], I["b_adaT"][l], writes=[b])
        S.dma('sp', gpT[:], I["g_preT"][l], writes=[b])
        S.dma('sp', brow[:], I["b_ada"][l:l + 1, 2 * D:3 * D].partition_broadcast(128), writes=[b])
        S.dma('sp', grow[:], I["g_post"][l:l + 1, :].partition_broadcast(128), writes=[b])
        S.op('dve', lambda e: e.tensor_copy(scbc[:], k.scT[:].unsqueeze(3).broadcast_to([128, 8, 3, 128])), reads=[k.b_scT], writes=[bsc])
        pm = k.ps[0]
        for g in range(6):
            w = wa[g % 2]
            S.dma('sp', w[:], I["w_ada"][l, :, g * 512:(g + 1) * 512].rearrange("(kc p) c -> p kc c", p=128), writes=[bwa[g % 2]])
            for j in range(4):
                ch = 4 * g + j
                for kc in range(8):
                    S.op('pe', lambda e, w=w, kc=kc, j=j, ch=ch: e.matmul(pm[:, ch * 3:ch * 3 + 3], lhsT=w[:, kc, j * 128:(j + 1) * 128], rhs=k.scT[:, kc, :], start=(kc == 0), stop=(kc == 7)),
                         reads=[bwa[g % 2], k.b_scT], writes=[k.bps[0]], inc=(kc == 7))
            if g >= 4:
                half = g - 4
                for s in range(3):
                    pg = k.ps[1 + s]
                    for kc in range(8):
                        S.op('pe', lambda e, w=w, kc=kc, s=s, pg=pg: e.matmul(pg[:], lhsT=scbc[:, kc, s, :], rhs=w[:, kc, :], start=(kc == 0), stop=(kc == 7)),
                             reads=[bwa[g % 2], bsc], writes=[k.bps[1 + s]], inc=(kc == 7))
                    sl = slice(half * 512, (half + 1) * 512)
                    S.op('dve', lambda e, s=s, pg=pg, sl=sl: e.tensor_tensor(out=k.gtg[:, s, sl], in0=pg[:], in1=brow[:, sl], op=ALU.add), reads=[k.bps[1 + s], b], writes=[k.b_gtg])
                    S.op('dve', lambda e, s=s, sl=sl: e.tensor_tensor(out=k.gtg[:, s, sl], in0=k.gtg[:, s, sl], in1=grow[:, sl], op=ALU.mult), reads=[b, k.b_gtg], writes=[k.b_gtg])
        S.op('dve', lambda e: e.tensor_tensor(out=k.modT[:], in0=pm[:, 0:72].rearrange("p (c s) -> p c s", s=3), in1=bT[:].unsqueeze(2).broadcast_to([128, 24, 3]), op=ALU.add), reads=[k.bps[0], b], writes=[k.b_mod])
        S.op('dve', lambda e: e.tensor_scalar(out=tmp[:], in0=k.modT[:, 8:16, :], scalar1=1.0, scalar2=None, op0=ALU.add), reads=[k.b_mod], writes=[b])
        S.op('dve', lambda e: e.tensor_tensor(out=k.gsT[:], in0=tmp[:], in1=gpT[:].unsqueeze(2).broadcast_to([128, 8, 3]), op=ALU.mult), reads=[b], writes=[k.b_mod])
        S.barrier()


def rstd_from_ms(k, ms, bms):
    S = k.S
    S.op('act', lambda e: e.activation(out=ms, in_=ms, func=AF.Sqrt, bias=EPS, scale=1.0), reads=[bms], writes=[bms])
    S.op('dve', lambda e: e.reciprocal(ms, ms), reads=[bms], writes=[bms])


def phase_pre(k, l, s):
    S, nc, I = k.S, k.nc, k.I
    with contextlib.ExitStack() as st:
        sb = lambda n, sh, d: st.enter_context(nc.sbuf_tensor(_u(n), list(sh), d))
        xt = [sb("pr_x%d" % i, [128, D], F32) for i in range(6)]
        bx = [Buf() for _ in range(6)]
        xs = [sb("pr_xs%d" % i, [128, D], BF16) for i in range(2)]
        bxs = [Buf() for _ in range(2)]
        junk = sb("pr_junk", [128, D], F32)
        ms = [sb("pr_ms%d" % i, [128, 1], F32) for i in range(6)]
        bms = [Buf() for _ in range(6)]
        hb = [sb("pr_hb%d" % i, [128, 8, 512], BF16) for i in range(2)]
        bhb = [Buf() for _ in range(2)]
        bj = Buf()
        pT = [k.ps[i][:].bitcast(BF16) for i in range(4)]
        it = 0
        blocks = [(0, b0 * 512, 4) for b0 in range(8)] + [(1, NTOK, 2)]
        for bi, (isctx, c0, ntile) in enumerate(blocks):
            mi = 2 if isctx else s
            for t in range(ntile):
                xi = it % 6
                if isctx:
                    src = (I["ctx"] if l == 0 else k.ctxcur)[s, t * 128:(t + 1) * 128, :]
                    rb = k.t_ctx[s].r(t * 128, t * 128 + 128)
                else:
                    src = (I["x"] if l == 0 else k.OUT)[s, c0 + t * 128:c0 + (t + 1) * 128, :]
                    rb = k.t_x[s].r(c0 + t * 128, c0 + t * 128 + 128)
                S.dma('sp', xt[xi][:], src, reads=rb, writes=[bx[xi]])
                S.op('act', lambda e, xi=xi: e.activation(out=junk[:], in_=xt[xi][:], func=AF.Square, scale=1.0 / 32.0, accum_out=ms[xi][:]), reads=[bx[xi]], writes=[bj, bms[xi]])
                rstd_from_ms(k, ms[xi][:], bms[xi])
                si = it % 2
                S.op('pool', lambda e, xi=xi, si=si: e.tensor_scalar(out=xs[si][:], in0=xt[xi][:], scalar1=ms[xi][:, 0:1], scalar2=None, op0=ALU.mult), reads=[bx[xi], bms[xi]], writes=[bxs[si]])
                for kc in range(8):
                    dst = pT[kc // 2][:, (kc % 2) * 512 + t * 128:(kc % 2) * 512 + (t + 1) * 128]
                    S.op('pe', lambda e, dst=dst, si=si, kc=kc: e.transpose(dst, xs[si][:, kc * 128:(kc + 1) * 128], k.ident[:]), reads=[bxs[si], k.b_const], writes=[k.bps[kc // 2]], inc=(kc == 7))
                it += 1
            hi = bi % 2
            nt = ntile * 128
            for kc in range(8):
                src = pT[kc // 2][:, (kc % 2) * 512:(kc % 2) * 512 + nt]
                if kc % 2 == 0:
                    S.op('act', lambda e, src=src, kc=kc, hi=hi, nt=nt, mi=mi: e.activation(out=hb[hi][:, kc, :nt], in_=src, func=AF.Identity, bias=k.modT[:, kc, mi:mi + 1], scale=k.gsT[:, kc, mi:mi + 1]), reads=[k.bps[kc // 2], k.b_mod], writes=[bhb[hi]])
                else:
                    S.op('dve', lambda e, src=src, kc=kc, hi=hi, nt=nt, mi=mi: e.tensor_scalar(out=hb[hi][:, kc, :nt], in0=src, scalar1=k.gsT[:, kc, mi:mi + 1], scalar2=k.modT[:, kc, mi:mi + 1], op0=ALU.mult, op1=ALU.add), reads=[k.bps[kc // 2], k.b_mod], writes=[bhb[hi]])
            S.dma('sp', k.hT[s, :, c0:c0 + nt].rearrange("(kc p) t -> p kc t", p=128), hb[hi][:, :, :nt], reads=[bhb[hi]], writes=k.t_hT[s].r(c0, c0 + nt))
        S.barrier()


def load_h(k, s, dst, c0, n, bdst):
    k.S.dma('sp', dst[:, :, :n], k.hT[s, :, c0:c0 + n].rearrange("(kc p) t -> p kc t", p=128), reads=k.t_hT[s].r(c0, c0 + n), writes=[bdst])


def inproj_fm(k, pbank, n, w, bw, cw, h, bh, hc0=0):
    for kc in range(8):
        k.S.op('pe', lambda e, kc=kc: e.matmul(k.ps[pbank][:, :n], lhsT=w[:, kc, cw:cw + 128], rhs=h[:, kc, hc0:hc0 + n], start=(kc == 0), stop=(kc == 7)),
               reads=[bw, bh], writes=[k.bps[pbank]], inc=(kc == 7))


def inproj_tm(k, pbank, w, bw, cw, h, bh, t0):
    for kc in range(8):
        k.S.op('pe', lambda e, kc=kc: e.matmul(k.ps[pbank][:], lhsT=h[:, kc, t0:t0 + 128], rhs=w[:, kc, cw:cw + 512], start=(kc == 0), stop=(kc == 7)),
               reads=[bw, bh], writes=[k.bps[pbank]], inc=(kc == 7))


def seq_blocks(with_ctx_block=True):
    bl = [(b0 * 512, 4) for b0 in range(8)]
    if with_ctx_block:
        bl.append((NTOK, 2))
    return bl


def phase_c(k, l, with_ctx):
    S, nc, I = k.S, k.nc, k.I
    p = l % 2
    with contextlib.ExitStack() as st:
        sb = lambda n, sh, d: st.enter_context(nc.sbuf_tensor(_u(n), list(sh), d))
        w = sb("c_w", [128, 8, 1536], BF16); bw = Buf()
        wsT = sb("c_ws", [128, 8, 128], BF16)
        gn = sb("c_gn", [128, 512], F32)
        bsT = sb("c_bs", [128, 8], F32)
        bc = Buf()
        hb = [sb("c_hb%d" % i, [128, 8, 512], BF16) for i in range(2)]; bhb = [Buf() for _ in range(2)]
        junk = sb("c_junk", [128, 512], F32); bj = Buf()
        ms = sb("c_ms", [128, 1], F32); bms = Buf()
        vn = sb("c_vn", [128, 512], BF16); bvn = Buf()
        sg = sb("c_sg", [128, 512], F32); bsg = Buf()
        t1 = sb("c_t1", [128, 512], F32); bt1 = Buf()
        yt = sb("c_y", [128, 512], BF16); byt = Buf()
        yb = [sb("c_yb%d" % i, [128, 4, 512], BF16) for i in range(2)]; byb = [Buf() for _ in range(2)]
        load_w(k, l, w[:], COL["c_u"], 1536, bw)
        S.dma('sp', wsT[:], k.ws_bf[p].rearrange("(g s) t -> s g t", s=128), reads=[k.t_w[p]], writes=[bc])
        S.dma('sp', gn[:], I["gmlp_g"][l:l + 1, :].partition_broadcast(128), writes=[bc])
        S.dma('sp', bsT[:], I["gmlp_bsT"][l], writes=[bc])
        pT = [k.ps[4][:].bitcast(BF16), k.ps[5][:].bitcast(BF16)]
        bi = 0
        for s in range(2):
            blocks = seq_blocks(with_ctx)
            load_h(k, s, hb[bi % 2], blocks[0][0], blocks[0][1] * 128, bhb[bi % 2])
            for ib, (c0, ntile) in enumerate(blocks):
                h = hb[bi % 2]; bh = bhb[bi % 2]
                if ib + 1 < len(blocks):
                    load_h(k, s, hb[(bi + 1) % 2], blocks[ib + 1][0], blocks[ib + 1][1] * 128, bhb[(bi + 1) % 2])
                for t in range(ntile):
                    t0 = t * 128
                    inproj_tm(k, 0, w, bw, 512, h, bh, t0)
                    S.op('act', lambda e: e.activation(out=junk[:], in_=k.ps[0][:], func=AF.Square, scale=float(512 ** -0.5), accum_out=ms[:]), reads=[k.bps[0]], writes=[bj, bms])
                    rstd_from_ms(k, ms[:], bms)
                    S.op('dve', lambda e: e.scalar_tensor_tensor(out=vn[:], in0=k.ps[0][:], scalar=ms[:, 0:1], in1=gn[:], op0=ALU.mult, op1=ALU.mult), reads=[k.bps[0], bms, bc], writes=[bvn])
                    for g in range(8):
                        S.op('pe', lambda e, g=g: e.matmul(k.ps[1][:, g * 64:(g + 1) * 64], lhsT=wsT[:, g, :], rhs=vn[:, g * 64:(g + 1) * 64], start=True, stop=True), reads=[bc, bvn], writes=[k.bps[1]], inc=(g == 7))
                    inproj_tm(k, 2, w, bw, 0, h, bh, t0)
                    inproj_tm(k, 3, w, bw, 1024, h, bh, t0)
                    S.op('act', lambda e: e.activation(out=sg[:], in_=k.ps[3][:], func=AF.Silu), reads=[k.bps[3]], writes=[bsg])
                    S.op('dve', lambda e: e.tensor_tensor(out=t1[:].rearrange("p (g c) -> p g c", g=8), in0=k.ps[1][:].rearrange("p (g c) -> p g c", g=8), in1=bsT[:].unsqueeze(2).broadcast_to([128, 8, 64]), op=ALU.add), reads=[k.bps[1], bc], writes=[bt1])
                    S.op('dve', lambda e: e.tensor_tensor(out=t1[:], in0=k.ps[2][:], in1=t1[:], op=ALU.mult), reads=[k.bps[2], bt1], writes=[bt1])
                    S.op('pool', lambda e: e.tensor_tensor(out=yt[:], in0=t1[:], in1=sg[:], op=ALU.mult), reads=[bt1, bsg], writes=[byt])
                    for j in range(4):
                        dst = pT[j // 2][:, (j % 2) * 512 + t0:(j % 2) * 512 + t0 + 128]
                        S.op('pe', lambda e, dst=dst, j=j: e.transpose(dst, yt[:, j * 128:(j + 1) * 128], k.ident[:]), reads=[byt, k.b_const], writes=[k.bps[4 + j // 2]], inc=(j == 3))
                nt = ntile * 128
                yo = yb[bi % 2]
                for j in range(4):
                    src = pT[j // 2][:, (j % 2) * 512:(j % 2) * 512 + nt]
                    S.op('act', lambda e, src=src, j=j, yo=yo, nt=nt: e.copy(yo[:, j, :nt], src), reads=[k.bps[4 + j // 2]], writes=[byb[bi % 2]])
                S.dma('pool', k.yT[s, 2, :, c0:c0 + nt].rearrange("(j p) t -> p j t", p=128), yo[:, :, :nt], reads=[byb[bi % 2]], writes=k.t_yT[s][2].r(c0, c0 + nt))
                bi += 1
        S.barrier()


def phase_b(k, l, with_ctx):
    S, nc, I = k.S, k.nc, k.I
    p = l % 2
    with contextlib.ExitStack() as st:
        sb = lambda n, sh, d: st.enter_context(nc.sbuf_tensor(_u(n), list(sh), d))
        w = sb("b_w", [128, 8, 1024], BF16); bw = Buf()
        fw = sb("b_fw", [128, 4, 128], BF16)
        cc = sb("b_cc", [128, 128], BF16); ssn = sb("b_ss", [128, 128], BF16)
        m12 = sb("b_m12", [128, 2, 4, 128], BF16)
        bc = Buf(); bm = Buf()
        z = sb("b_z", [128, 32, 512], BF16); bz = Buf()
        hb = [sb("b_hb%d" % i, [128, 8, 512], BF16) for i in range(2)]; bhb = [Buf() for _ in range(2)]
        dft = [sb("b_dft%d" % i, [128, 32, 512], BF16) for i in range(2)]; bdft = [Buf() for _ in range(2)]
        Y = sb("b_Y", [128, 4, 512], BF16); bY = Buf()
        sg = sb("b_sg", [128, 4, 256], F32); bsg = Buf()
        yb = [sb("b_yb%d" % i, [128, 4, 256], BF16) for i in range(2)]; byb = [Buf() for _ in range(2)]
        load_w(k, l, w[:], COL["b_x"], 1024, bw)
        S.dma('sp', fw[:], k.fw_bf[p].rearrange("(g c) d -> c g d", c=128), reads=[k.t_w[p]], writes=[bc])
        S.dma('sp', cc[:], I["cc128"], writes=[bc])
        S.dma('sp', ssn[:], I["ssn128"], writes=[bc])
        for i, mat in enumerate((cc, ssn)):
            S.op('pe', lambda e, mat=mat, i=i: e.matmul(k.ps[i][:], lhsT=mat[:], rhs=fw[:].rearrange("c g d -> c (g d)"), start=True, stop=True), reads=[bc], writes=[k.bps[i]])
            S.op('act', lambda e, i=i: e.copy(m12[:, i, :, :].rearrange("c g d -> c (g d)"), k.ps[i][:]), reads=[k.bps[i]], writes=[bm])
        di = 0
        for s in range(2):
            for (tc0, ntile, nkb, isctx) in ([(0, 32, 16, False)] + ([(NTOK, 2, 1, True)] if with_ctx else [])):
                nblk = (ntile + 3) // 4
                for ib in range(nblk):
                    nt = min(4, ntile - ib * 4)
                    load_h(k, s, hb[ib % 2], tc0 + ib * 512, nt * 128, bhb[ib % 2])
                    for t in range(nt):
                        inproj_tm(k, 0, w, bw, 0, hb[ib % 2], bhb[ib % 2], t * 128)
                        S.op('act', lambda e, ib=ib, t=t: e.copy(z[:, ib * 4 + t, :], k.ps[0][:]), reads=[k.bps[0]], writes=[bz])
                for kb in range(nkb):
                    dt_ = dft[di % 2]; bd = bdft[di % 2]
                    if isctx:
                        S.dma('sp', dt_[:, :2, :], I["dft256"].rearrange("(t p) c -> p t c", p=128), writes=[bd])
                    else:
                        for hh in range(2):
                            S.dma('sp', dt_[:, hh * 16:(hh + 1) * 16, :], I["dft"][kb, hh * 2048:(hh + 1) * 2048, :].rearrange("(t p) c -> p t c", p=128), writes=[bd])
                    kc0 = tc0 + kb * 256
                    hi = (kb + 1) % 2
                    load_h(k, s, hb[hi], kc0, 256, bhb[hi])
                    for g in range(4):
                        for t in range(ntile):
                            S.op('pe', lambda e, g=g, t=t, dt_=dt_: e.matmul(k.ps[g][:], lhsT=z[:, t, g * 128:(g + 1) * 128], rhs=dt_[:, t, :], start=(t == 0), stop=(t == ntile - 1)),
                                 reads=[bz, bd], writes=[k.bps[g]], inc=(t == ntile - 1))
                        if g % 2 == 0:
                            S.op('act', lambda e, g=g: e.copy(Y[:, g, :], k.ps[g][:]), reads=[k.bps[g]], writes=[bY])
                        else:
                            S.op('dve', lambda e, g=g: e.tensor_copy(Y[:, g, :], k.ps[g][:]), reads=[k.bps[g]], writes=[bY])
                    for g in range(4):
                        pb_ = 4 + g // 2
                        dst = k.ps[pb_][:, (g % 2) * 256:(g % 2) * 256 + 256]
                        S.op('pe', lambda e, g=g, dst=dst: e.matmul(dst, lhsT=m12[:, 0, g, :], rhs=Y[:, g, 0:256], start=True, stop=False), reads=[bm, bY], writes=[k.bps[pb_]], inc=False)
                        S.op('pe', lambda e, g=g, dst=dst: e.matmul(dst, lhsT=m12[:, 1, g, :], rhs=Y[:, g, 256:512], start=False, stop=True), reads=[bm, bY], writes=[k.bps[pb_]])
                    for g in range(4):
                        pb_ = 6 + g // 2
                        dst = k.ps[pb_][:, (g % 2) * 256:(g % 2) * 256 + 256]
                        for kc in range(8):
                            S.op('pe', lambda e, g=g, dst=dst, kc=kc, hi=hi: e.matmul(dst, lhsT=w[:, kc, 512 + g * 128:512 + (g + 1) * 128], rhs=hb[hi][:, kc, 0:256], start=(kc == 0), stop=(kc == 7)),
                                 reads=[bw, bhb[hi]], writes=[k.bps[pb_]], inc=(kc == 7))
                    yo = yb[di % 2]
                    for gg in range(2):
                        S.op('act', lambda e, gg=gg: e.activation(out=sg[:, 2 * gg:2 * gg + 2, :].rearrange("p a t -> p (a t)"), in_=k.ps[6 + gg][:], func=AF.Silu), reads=[k.bps[6 + gg]], writes=[bsg])
                        S.op('dve', lambda e, gg=gg, yo=yo: e.tensor_tensor(out=yo[:, 2 * gg:2 * gg + 2, :].rearrange("p a t -> p (a t)"), in0=k.ps[4 + gg][:], in1=sg[:, 2 * gg:2 * gg + 2, :].rearrange("p a t -> p (a t)"), op=ALU.mult), reads=[k.bps[4 + gg], bsg], writes=[byb[di % 2]])
                    S.dma('pool', k.yT[s, 1, :, kc0:kc0 + 256].rearrange("(j p) t -> p j t", p=128), yo[:], reads=[byb[di % 2]], writes=k.t_yT[s][1].r(kc0, kc0 + 256))
                    di += 1
        S.barrier()


def na_qblocks(with_ctx):
    bl = [(0, 256, 0, list(range(0, 4)), 0, True), (60 * 64, 256, 0, list(range(28, 32)), 60, True),
          (4 * 64, 256, 1, list(range(0, 6)), 4, True), (56 * 64, 256, 1, list(range(26, 32)), 56, True)]
    for gq in range(1, 7):
        bl.append((gq * 512, 512, 1, list(range(4 * gq - 2, 4 * gq + 6)), 8 * gq, True))
    if with_ctx:
        bl.append((NTOK, 256, 1, [], 0, False))
    return bl


def phase_a(k, l, with_ctx):
    S, nc, I = k.S, k.nc, k.I
    with contextlib.ExitStack() as st:
        sb = lambda n, sh, d: st.enter_context(nc.sbuf_tensor(_u(n), list(sh), d))
        w = sb("a_w", [128, 8, 1024], BF16); bw = Buf()
        kT = sb("a_kT", [128, 4, NT], BF16); bkT = Buf()
        vt = sb("a_v", [128, 34, 512], BF16); bv = Buf()
        csb = [sb("a_cs%d" % i, [128, 2, 512], F32) for i in range(2)]; bcs = [Buf() for _ in range(2)]
        csi = [0]
        rs = sb("a_rs", [128, 128], BF16); eye8 = sb("a_e8", [128, 128], BF16)
        bc = Buf()
        zb = sb("a_zb", [128, 8, ZW * 64], BF16); bzb = Buf()
        hb = [sb("a_hb%d" % i, [128, 8, 512], BF16) for i in range(2)]; bhb = [Buf() for _ in range(2)]
        qp = sb("a_qp", [128, 4, 512], BF16); bqp = Buf()
        qp2 = sb("a_qp2", [128, 2, 4, 512], BF16); bqp2 = Buf()
        qr2 = sb("a_qr2", [128, 2, 4, 512], BF16); bqr = Buf()
        gs = sb("a_gs", [128, 4, 512], BF16); bgs = Buf()
        t1 = sb("a_t1", [128, 512], F32); bt1 = Buf()
        t2 = sb("a_t2", [128, 512], F32); bt2 = Buf()
        kp = sb("a_kp", [128, 512], BF16); bkp = Buf()
        es = [sb("a_es%d" % i, [128, 512], BF16) for i in range(4)]; bes = [Buf() for _ in range(4)]
        rd = sb("a_rd", [128, 512], F32); brd = Buf()
        yo = [sb("a_yo%d" % i, [128, 4, 512], BF16) for i in range(2)]; byo = [Buf() for _ in range(2)]
        S.dma('sp', rs[:], I["rsign"], writes=[bc]); S.dma('sp', eye8[:], I["eye8"], writes=[bc])
        S.op('pool', lambda e: e.memset(qp2[:], 0.0), writes=[bqp2])
        S.op('pool', lambda e: e.memset(qr2[:], 0.0), writes=[bqr])

        def load_cs(c0, n):
            csi[0] += 1
            i = csi[0] % 2
            S.dma('sp', csb[i][:, 0, :n], I["cosT"][:, c0:c0 + n], writes=[bcs[i]])
            S.dma('sp', csb[i][:, 1, :n], I["sinT"][:, c0:c0 + n], writes=[bcs[i]])

        def rope(psrc, plain_bf, bplain, dst, c0, n, rbank):
            i = csi[0] % 2
            if _DBG.get('nors'):
                rbank = psrc
            else:
                S.op('pe', lambda e: e.matmul(k.ps[rbank][:, :n], lhsT=rs[:], rhs=plain_bf, start=True, stop=True), reads=[bc, bplain], writes=[k.bps[rbank]])
            S.op('dve', lambda e: e.tensor_tensor(out=t1[:, :n], in0=k.ps[psrc][:, :n], in1=csb[i][:, 0, :n], op=ALU.mult), reads=[k.bps[psrc], bcs[i], bplain], writes=[bt1])
            S.op('dve', lambda e: e.tensor_tensor(out=t2[:, :n], in0=k.ps[rbank][:, :n], in1=csb[i][:, 1, :n], op=ALU.mult), reads=[k.bps[rbank], bcs[i]], writes=[bt2])

        hi = 0
        for s in range(2):
            load_w(k, l, w[:], COL["a_k"], 1024, bw)
            for (c0, ntile) in seq_blocks(True):
                n = ntile * 128
                h = hb[hi % 2]; bh = bhb[hi % 2]; hi += 1
                load_h(k, s, h, c0, n, bh)
                if c0 < NTOK:
                    load_cs(c0, n)
                for cp in range(4):
                    inproj_fm(k, 0, n, w, bw, cp * 128, h, bh)
                    if c0 >= NTOK or _DBG.get('norope'):
                        S.op('act', lambda e, cp=cp: e.copy(kT[:, cp, c0:c0 + n], k.ps[0][:, :n]), reads=[k.bps[0]], writes=[bkT])
                    else:
                        S.op('act', lambda e: e.copy(kp[:, :n], k.ps[0][:, :n]), reads=[k.bps[0]], writes=[bkp])
                        rope(0, kp[:, :n], bkp, None, c0, n, 1)
                        if _DBG.get('nopool'):
                            S.op('dve', lambda e, cp=cp: e.tensor_tensor(out=kT[:, cp, c0:c0 + n], in0=t1[:, :n], in1=t2[:, :n], op=ALU.add), reads=[bt1, bt2], writes=[bkT])
                        else:
                            S.op('pool', lambda e, cp=cp: e.tensor_tensor(out=kT[:, cp, c0:c0 + n], in0=t1[:, :n], in1=t2[:, :n], op=ALU.add), reads=[bt1, bt2], writes=[bkT])
                for t in range(ntile):
                    inproj_tm(k, 2, w, bw, 512, h, bh, t * 128)
                    S.op('act', lambda e, t=t: e.copy(vt[:, c0 // 128 + t, :], k.ps[2][:]), reads=[k.bps[2]], writes=[bv])
            if _DBG.get('a1only'):
                continue
            load_w(k, l, w[:, :, 0:512], COL["a_q"], 512, bw)
            load_w(k, l, w[:, :, 512:1024], COL["a_g"], 512, bw)
            cur_kind = None
            for qi, (t0, nq, kind, chunks, q0, use_rope) in enumerate(na_qblocks(with_ctx)):
                if 'qsel' in _DBG and qi not in _DBG['qsel']:
                    continue
                if use_rope and kind != cur_kind:
                    S.dma('sp', zb[:], I["zb"][l, kind], writes=[bzb])
                    cur_kind = kind
                h = hb[hi % 2]; bh = bhb[hi % 2]; hi += 1
                load_h(k, s, h, t0, nq, bh)
                if use_rope:
                    load_cs(t0, nq)
                for cp in range(4):
                    inproj_fm(k, 0, nq, w, bw, cp * 128, h, bh)
                    S.op('act', lambda e, cp=cp: e.copy(qp[:, cp, :nq], k.ps[0][:, :nq]), reads=[k.bps[0]], writes=[bqp])
                    for j in range(2):
                        S.op('act', lambda e, cp=cp, j=j: e.copy(qp2[64 * j:64 * j + 64, j, cp, :nq], k.ps[0][64 * j:64 * j + 64, :nq]), reads=[k.bps[0]], writes=[bqp2])
                    if use_rope:
                        rope(0, qp[:, cp, :nq], bqp, None, t0, nq, 1)
                        for j in range(2):
                            S.op('pool', lambda e, cp=cp, j=j: e.tensor_tensor(out=qr2[64 * j:64 * j + 64, j, cp, :nq], in0=t1[64 * j:64 * j + 64, :nq], in1=t2[64 * j:64 * j + 64, :nq], op=ALU.add), reads=[bt1, bt2], writes=[bqr])
                    inproj_fm(k, 2, nq, w, bw, 512 + cp * 128, h, bh)
                    S.op('act', lambda e, cp=cp: e.activation(out=gs[:, cp, :nq], in_=k.ps[2][:, :nq], func=AF.Silu), reads=[k.bps[2]], writes=[bgs])
                y = yo[qi % 2]
                for cp in range(4):
                    ob, db = (6, 7) if cp % 2 == 0 else (0, 1)
                    klist = [(c, True) for c in chunks] + [(32, False), (33, False)]
                    items = []
                    for j in range(2):
                        for ic, (c, band) in enumerate(klist):
                            items.append((j, c, band, ic == 0, ic == len(klist) - 1))

                    def stage1(idx, cp=cp):
                        j, c, band, first, last = items[idx]
                        pb = 64 * j; hd = 2 * cp + j
                        sbk = 3 + (idx % 3)
                        e_t = es[idx % 4]; be = bes[idx % 4]
                        if band:
                            woff = 14 - (2 * c - q0)
                            S.op('pe', lambda e: e.matmul(k.ps[sbk][:, :nq], lhsT=kT[:, cp, c * 128:(c + 1) * 128], rhs=qr2[:, j, cp, :nq], start=True, stop=False),
                                 reads=[bkT, bqr], writes=[k.bps[sbk]], inc=False)
                            S.op('pe', lambda e: e.matmul(k.ps[sbk][:, :nq], lhsT=eye8[:], rhs=zb[:, hd, woff * 64:woff * 64 + nq], start=False, stop=True),
                                 reads=[bc, bzb], writes=[k.bps[sbk]])
                        else:
                            S.op('pe', lambda e: e.matmul(k.ps[sbk][:, :nq], lhsT=kT[:, cp, c * 128:(c + 1) * 128], rhs=qp2[:, j, cp, :nq], start=True, stop=True),
                                 reads=[bkT, bqp2], writes=[k.bps[sbk]])
                        S.op('act', lambda e: e.activation(out=e_t[:, :nq], in_=k.ps[sbk][:, :nq], func=AF.Exp, scale=0.125), reads=[k.bps[sbk]], writes=[be])

                    def stage2(idx, cp=cp, ob=ob, db=db):
                        j, c, band, first, last = items[idx]
                        pb = 64 * j; hd = 2 * cp + j
                        e_t = es[idx % 4]; be = bes[idx % 4]
                        S.op('pe', lambda e: e.matmul(k.ps[ob][pb:pb + 64, :nq], lhsT=vt[:, c, hd * 64:(hd + 1) * 64], rhs=e_t[:, :nq], start=first, stop=last),
                             reads=[bv, be], writes=[k.bps[ob]], inc=False)
                        S.op('pe', lambda e: e.matmul(k.ps[db][pb:pb + 64, :nq], lhsT=k.ones_bf[:, 0:64], rhs=e_t[:, :nq], start=first, stop=last),
                             reads=[k.b_const, be], writes=[k.bps[db]])

                    LOOK = 2
                    for idx in range(len(items) + LOOK):
                        if idx < len(items):
                            stage1(idx)
                        if idx >= LOOK:
                            stage2(idx - LOOK)
                    S.op('act', lambda e: e.activation(out=rd[:, :nq], in_=k.ps[db][:, :nq], func=AF.Ln), reads=[k.bps[db]], writes=[brd])
                    S.op('act', lambda e: e.activation(out=rd[:, :nq], in_=rd[:, :nq], func=AF.Exp, scale=-1.0), reads=[brd], writes=[brd])
                    S.op('dve', lambda e: e.tensor_tensor(out=rd[:, :nq], in0=k.ps[ob][:, :nq], in1=rd[:, :nq], op=ALU.mult), reads=[k.bps[ob], brd], writes=[brd])
                    S.op('pool', lambda e, cp=cp, y=y: e.tensor_tensor(out=y[:, cp, :nq], in0=rd[:, :nq], in1=gs[:, cp, :nq], op=ALU.mult), reads=[brd, bgs], writes=[byo[qi % 2]])
                S.dma('pool', k.yT[s, 0, :, t0:t0 + nq].rearrange("(j p) t -> p j t", p=128), y[:, :, :nq], reads=[byo[qi % 2]], writes=k.t_yT[s][0].r(t0, t0 + nq))
        S.barrier()


def phase_d(k, l, with_ctx):
    S, nc, I = k.S, k.nc, k.I
    with contextlib.ExitStack() as st:
        sb = lambda n, sh, d: st.enter_context(nc.sbuf_tensor(_u(n), list(sh), d))
        w = sb("d_w", [128, 8, 2048], BF16); bw = Buf()
        gn = sb("d_gn", [128, 4], F32)
        trif = sb("d_trif", [128, 128], I32); trib = sb("d_trib", [128, 128], I32); bones = sb("d_bones", [128, 128], BF16)
        bc = Buf()
        hb = [sb("d_hb%d" % i, [128, 8, 512], BF16) for i in range(2)]; bhb = [Buf() for _ in range(2)]
        Pp = sb("d_P", [128, 4, 516], F32); bP = Buf()
        kT = sb("d_kT", [128, 4, 512], BF16); bkT = Buf()
        qT = sb("d_qT", [128, 4, 512], BF16); bqT = Buf()
        itm2 = [sb("d_itm%d" % i, [128, 512], BF16) for i in range(2)]; bitm2 = [Buf() for _ in range(2)]
        negr2 = [sb("d_negr%d" % i, [128, 4, 8], F32) for i in range(2)]; bnr2 = [Buf() for _ in range(2)]
        pcnt = [0]
        dqa = [sb("d_dq%d" % i, [128, 128], F32) for i in range(8)]; bdqa = [Buf() for _ in range(8)]
        eqa = [sb("d_eq%d" % i, [128, 128], F32) for i in range(8)]; beqa = [Buf() for _ in range(8)]
        sga = [sb("d_sg%d" % i, [128, 512], F32) for i in range(2)]; bsga = [Buf() for _ in range(2)]
        lfa = [sb("d_lf%d" % i, [128, 512], F32) for i in range(2)]; blfa = [Buf() for _ in range(2)]
        ek = [sb("d_ek%d" % i, [128, 4, 128], F32) for i in range(4)]; bek = [Buf() for _ in range(4)]
        qtl2 = [sb("d_qtl%d" % i, [128, 2, 4, 128], BF16) for i in range(2)]; bqtl2 = [Buf() for _ in range(2)]
        ktl2 = [sb("d_ktl%d" % i, [128, 4, 4, 128], BF16) for i in range(2)]; bktl2 = [Buf() for _ in range(2)]
        qh2 = [sb("d_qh%d" % i, [128, 4, 128], BF16) for i in range(2)]; bqh2 = [Buf() for _ in range(2)]
        khT = sb("d_khT", [128, 4, 128], BF16); bkhT = Buf()
        khtm2 = [sb("d_khtm%d" % i, [128, 512], BF16) for i in range(2)]; bkhtm2 = [Buf() for _ in range(2)]
        dec2 = [sb("d_dec%d" % i, [128, 4], F32) for i in range(2)]; bdec2 = [Buf() for _ in range(2)]
        At = sb("d_At", [128, 8, 128], BF16); bAt = Buf()
        St = sb("d_S", [128, 2, 4, 64], F32); bS = [Buf(), Buf()]
        Sbf = sb("d_Sbf", [128, 4, 128], BF16); bSbf = Buf()
        ob = [sb("d_ob%d" % i, [128, 4, 512], F32) for i in range(2)]; bob = [Buf() for _ in range(2)]
        sq = sb("d_sq", [128, 512], BF16); bsq = Buf()
        rst = sb("d_rst", [128, 512], F32); brst = Buf()
        gl = sb("d_gl", [128, 512], F32); bgl = Buf()
        yb = [sb("d_yb%d" % i, [128, 4, 512], BF16) for i in range(2)]; byb = [Buf() for _ in range(2)]
        S.dma('sp', gn[:], I["hgrn_gT"][l], writes=[bc])
        S.dma('sp', trif[:], I["trif"], writes=[bc]); S.dma('sp', trib[:], I["trib"], writes=[bc]); S.dma('sp', bones[:], I["bones"], writes=[bc])
        S.op('dve', lambda e: e.memset(Pp[:], 0.0), writes=[bP])
        for i in range(2):
            S.op('pool', lambda e, i=i: e.memset(qtl2[i][:], 0.0), writes=[bqtl2[i]])
        S.op('pool', lambda e: e.memset(Sbf[:], 0.0), writes=[bSbf])
        ptr = k.ps[5][:].bitcast(BF16)
        hi = 0
        for s in range(2):
            for d in range(2):
                S.op('dve', lambda e, d=d: e.memset(St[:, d, :, :], 0.0), writes=[bS[d]])
            for (base, nblk_tiles) in ((NTOK, [2]), (0, [4] * 8)):
                isctx = base >= NTOK
                for d in range(2):
                    _DBG['dpass'] = _DBG.get('dpass', 0) + 1
                    if 'dmax' in _DBG and _DBG['dpass'] > _DBG['dmax']:
                        continue
                    sgn = 1.0 if d == 0 else -1.0
                    final = (d == 1)
                    load_w(k, l, w[:, :, 0:512], COL["d_q"], 512, bw)
                    load_w(k, l, w[:, :, 512:1024], COL["d_ff"] if d == 0 else COL["d_fb"], 512, bw)
                    load_w(k, l, w[:, :, 1024:1536], COL["d_i"], 512, bw)
                    if final:
                        load_w(k, l, w[:, :, 1536:2048], COL["d_g"], 512, bw)
                    for i in range(4):
                        S.op('pool', lambda e, i=i: e.memset(ek[i][:], 0.0), writes=[bek[i]])
                    S.op('pool', lambda e: e.memset(At[:], 0.0), writes=[bAt])
                    S.op('act', lambda e, d=d: e.copy(Sbf[0:64, :, 0:64], St[0:64, d, :, :]), reads=[bS[d]], writes=[bSbf]); S.op('act', lambda e, d=d: e.copy(Sbf[64:128, :, 64:128], St[64:128, d, :, :]), reads=[bS[d]], writes=[bSbf])
                    mask = trif if d == 0 else trib
                    blist = list(range(len(nblk_tiles)))
                    if d == 1:
                        blist = blist[::-1]
                    for ib in blist:
                        ntile = nblk_tiles[ib]
                        n = ntile * 128
                        c0 = base + ib * 512
                        h = hb[hi % 2]; bh = bhb[hi % 2]
                        o_b = ob[hi % 2]; bo = bob[hi % 2]; y_b = yb[hi % 2]; by_ = byb[hi % 2]
                        hi += 1
                        load_h(k, s, h, c0, n, bh)
                        if final:
                            S.dma('sp', o_b[:, :, :n], k.ofT[s, :, c0:c0 + n].rearrange("(j p) t -> p j t", p=128), reads=k.t_ofT[s].r(c0, c0 + n), writes=[bo])
                        for ft in range(4):
                            sg = sga[ft % 2]; bsg = bsga[ft % 2]; logf = lfa[ft % 2]; blf = blfa[ft % 2]
                            inproj_fm(k, 0, n, w, bw, 512 + ft * 128, h, bh)
                            S.op('act', lambda e: e.activation(out=sg[:, :n], in_=k.ps[0][:, :n], func=AF.Sigmoid), reads=[k.bps[0]], writes=[bsg])
                            S.op('dve', lambda e, ft=ft, d=d: e.tensor_scalar(out=sg[:, :n], in0=sg[:, :n], scalar1=k.oml[:, d * 4 + ft, l:l + 1], scalar2=k.lb[:, d * 4 + ft, l:l + 1], op0=ALU.mult, op1=ALU.add), reads=[bsg, k.b_lb], writes=[bsg])
                            S.op('dve', lambda e: e.tensor_scalar(out=sg[:, :n], in0=sg[:, :n], scalar1=1e-30, scalar2=None, op0=ALU.max), reads=[bsg], writes=[bsg])
                            S.op('act', lambda e: e.activation(out=logf[:, :n], in_=sg[:, :n], func=AF.Ln), reads=[bsg], writes=[blf])
                            S.op('dve', lambda e, ft=ft: e.tensor_scalar(out=kT[:, ft, :n], in0=sg[:, :n], scalar1=-1.0, scalar2=1.0, op0=ALU.mult, op1=ALU.add), reads=[bsg], writes=[bkT])
                            S.op('dve', lambda e, ft=ft: e.tensor_tensor_scan(out=Pp[:, ft, 1:1 + n], data0=k.ones_f[:, :n], data1=logf[:, :n], initial=0.0, op0=ALU.mult, op1=ALU.add), reads=[blf, k.b_const], writes=[bP])
                            inproj_fm(k, 1, n, w, bw, ft * 128, h, bh)
                            S.op('act', lambda e, ft=ft: e.copy(qT[:, ft, :n], k.ps[1][:, :n]), reads=[k.bps[1]], writes=[bqT])
                        tl = list(range(ntile))
                        if d == 1:
                            tl = tl[::-1]

                        def stageE(t, pi):
                            t0 = t * 128
                            xo = t0 + 1 if d == 0 else t0
                            itm = itm2[pi]; bitm = bitm2[pi]; qtl = qtl2[pi]; bqtl = bqtl2[pi]; ktl = ktl2[pi]; bktl = bktl2[pi]
                            qh = qh2[pi]; bqh = bqh2[pi]; khtm = khtm2[pi]; bkhtm = bkhtm2[pi]; dec = dec2[pi]; bdec = bdec2[pi]
                            negr = negr2[pi]; bnr = bnr2[pi]
                            inproj_tm(k, 2, w, bw, 1024, h, bh, t0)
                            S.op('act', lambda e: e.copy(itm[:], k.ps[2][:]), reads=[k.bps[2]], writes=[bitm])
                            S.op('dve', lambda e: e.tensor_scalar(out=negr[:, :, 0:4], in0=Pp[:, :, t0 + 16:t0 + 113:32], scalar1=-1.0, scalar2=None, op0=ALU.mult), reads=[bP], writes=[bnr])
                            S.op('dve', lambda e: e.tensor_scalar(out=negr[:, :, 4:6], in0=Pp[:, :, t0:t0 + 129:128], scalar1=-1.0, scalar2=None, op0=ALU.mult), reads=[bP], writes=[bnr])
                            def tiles(ft):
                                return (dqa[ft], bdqa[ft], eqa[ft], beqa[ft], dqa[4 + ft], bdqa[4 + ft], eqa[4 + ft], beqa[4 + ft],
                                        Pp[:, ft, xo:xo + 128], Pp[:, ft, t0:t0 + 1], Pp[:, ft, t0 + 128:t0 + 129], negr[:, ft, 4:5], negr[:, ft, 5:6])
                            for ft in range(4):
                                dq, bdq, eq, beq, dq2, bdq2, eq2, beq2, X, B0p, B1p, B0n, B1n = tiles(ft)
                                S.op('dve', lambda e: e.tensor_tensor(out=dq[:].rearrange("p (i c) -> p i c", i=4), in0=X.rearrange("p (i c) -> p i c", i=4), in1=Pp[:, ft, t0 + 16:t0 + 113:32].unsqueeze(2).broadcast_to([128, 4, 32]), op=ALU.subtract), reads=[bP], writes=[bdq])
                            for ft in range(4):
                                dq, bdq, eq, beq, dq2, bdq2, eq2, beq2, X, B0p, B1p, B0n, B1n = tiles(ft)
                                S.op('act', lambda e: e.activation(out=eq[:], in_=dq[:], func=AF.Exp, scale=sgn), reads=[bdq], writes=[beq])
                                for i in range(4):
                                    lo, hi_ = (0, 32 * (i + 1)) if d == 0 else (32 * i, 128)
                                    bias = Pp[:, ft, t0 + 16 + 32 * i:t0 + 17 + 32 * i] if d == 0 else negr[:, ft, i:i + 1]
                                    S.op('act', lambda e: e.activation(out=ek[ft][:, i, lo:hi_], in_=Pp[:, ft, xo + lo:xo + hi_], func=AF.Exp, scale=-sgn, bias=bias), reads=[bP, bnr], writes=[bek[ft]])
                                bq_ = B0n if d == 0 else B1p
                                S.op('act', lambda e: e.activation(out=eq2[:], in_=X, func=AF.Exp, scale=sgn, bias=bq_), reads=[bP, bnr], writes=[beq2])
                                bk_ = B1p if d == 0 else B0n
                                S.op('act', lambda e: e.activation(out=dq2[:], in_=X, func=AF.Exp, scale=-sgn, bias=bk_), reads=[bP, bnr], writes=[bdq2])
                                S.op('act', lambda e: e.activation(out=dec[:, ft:ft + 1], in_=B1p, func=AF.Exp, scale=1.0, bias=B0n), reads=[bP, bnr], writes=[bdec])
                            for ft in range(4):
                                dq, bdq, eq, beq, dq2, bdq2, eq2, beq2, X, B0p, B1p, B0n, B1n = tiles(ft)
                                S.op('dve', lambda e: e.tensor_tensor(out=khT[:, ft, :], in0=dq2[:], in1=kT[:, ft, t0:t0 + 128], op=ALU.mult), reads=[bdq2, bkT], writes=[bkhT])
                                S.op('pe', lambda e: e.transpose(ptr[:, ft * 128:(ft + 1) * 128], khT[:, ft, :], k.ident[:]), reads=[bkhT, k.b_const], writes=[k.bps[5]])
                                S.op('dve', lambda e: e.tensor_tensor(out=qtl[0:64, 0, ft, :], in0=eq[0:64, :], in1=qT[0:64, ft, t0:t0 + 128], op=ALU.mult), reads=[beq, bqT], writes=[bqtl])
                                S.op('dve', lambda e: e.tensor_tensor(out=qtl[64:128, 1, ft, :], in0=eq[64:128, :], in1=qT[64:128, ft, t0:t0 + 128], op=ALU.mult), reads=[beq, bqT], writes=[bqtl])
                                S.op('pool', lambda e: e.tensor_tensor(out=ktl[:, ft, :, :], in0=ek[ft][:], in1=kT[:, ft, t0:t0 + 128].unsqueeze(1).broadcast_to([128, 4, 128]), op=ALU.mult), reads=[bek[ft], bkT], writes=[bktl])
                                S.op('dve', lambda e: e.tensor_tensor(out=qh[:, ft, :], in0=eq2[:], in1=qT[:, ft, t0:t0 + 128], op=ALU.mult), reads=[beq2, bqT], writes=[bqh])
                            S.op('dve', lambda e: e.tensor_copy(khtm[:], ptr[:, 0:512]), reads=[k.bps[5]], writes=[bkhtm])

                        def stageF(t, pi):
                            t0 = t * 128
                            itm = itm2[pi]; bitm = bitm2[pi]; qtl = qtl2[pi]; bqtl = bqtl2[pi]; ktl = ktl2[pi]; bktl = bktl2[pi]
                            qh = qh2[pi]; bqh = bqh2[pi]; khtm = khtm2[pi]; bkhtm = bkhtm2[pi]; dec = dec2[pi]; bdec = bdec2[pi]
                            for hd in range(8):
                                cp = hd // 2
                                sbk = 3 + hd // 4
                                for i in range(4):
                                    dst = k.ps[sbk][:, (hd % 4) * 128 + 32 * i:(hd % 4) * 128 + 32 * i + 32]
                                    S.op('pe', lambda e: e.matmul(dst, lhsT=ktl[:, cp, i, :], rhs=qtl[:, hd % 2, cp, 32 * i:32 * i + 32], start=True, stop=True),
                                         reads=[bktl, bqtl], writes=[k.bps[sbk]], inc=(hd % 4 == 3 and i == 3))
                            for half in range(2):
                                S.op('dve', lambda e: e.copy_predicated(out=At[:, 4 * half:4 * half + 4, :], mask=mask[:].unsqueeze(1).broadcast_to([128, 4, 128]), data=k.ps[3 + half][:].rearrange("p (h t) -> p h t", h=4)), reads=[k.bps[3 + half], bc, bAt], writes=[bAt])
                            for cp in range(4):
                                for j in range(2):
                                    hd = 2 * cp + j
                                    S.op('pe', lambda e: e.matmul(k.ps[6][64 * j:64 * j + 64, cp * 128:(cp + 1) * 128], lhsT=itm[:, hd * 64:(hd + 1) * 64], rhs=At[:, hd, :], start=True, stop=False), reads=[bitm, bAt], writes=[k.bps[6]], inc=False)
                                S.op('pe', lambda e: e.matmul(k.ps[6][:, cp * 128:(cp + 1) * 128], lhsT=Sbf[:, cp, :], rhs=qh[:, cp, :], start=False, stop=True), reads=[bSbf, bqh], writes=[k.bps[6]], inc=(cp == 3))
                            for cp in range(4):
                                S.op('pe', lambda e: e.matmul(k.ps[7][:, cp * 128:(cp + 1) * 128], lhsT=khtm[:, cp * 128:(cp + 1) * 128], rhs=itm[:, cp * 128:(cp + 1) * 128], start=True, stop=True), reads=[bkhtm, bitm], writes=[k.bps[7]], inc=(cp == 3))
                            for cp in range(4):
                                for j in range(2):
                                    pb = 64 * j
                                    S.op('dve', lambda e: e.scalar_tensor_tensor(out=St[pb:pb + 64, d, cp, :], in0=St[pb:pb + 64, d, cp, :], scalar=dec[pb:pb + 64, cp:cp + 1], in1=k.ps[7][pb:pb + 64, cp * 128 + 64 * j:cp * 128 + 64 * j + 64], op0=ALU.mult, op1=ALU.add),
                                         reads=[bS[d], bdec, k.bps[7]], writes=[bS[d]])
                            S.op('act', lambda e: e.copy(Sbf[0:64, :, 0:64], St[0:64, d, :, :]), reads=[bS[d]], writes=[bSbf])
                            S.op('act', lambda e: e.copy(Sbf[64:128, :, 64:128], St[64:128, d, :, :]), reads=[bS[d]], writes=[bSbf])
                            if not final:
                                S.op('act', lambda e: e.copy(o_b[:, :, t0:t0 + 128], k.ps[6][:].rearrange("p (c t) -> p c t", c=4)), reads=[k.bps[6]], writes=[bo])
                            else:
                                S.op('dve', lambda e: e.tensor_tensor(out=o_b[:, :, t0:t0 + 128], in0=k.ps[6][:].rearrange("p (c t) -> p c t", c=4), in1=o_b[:, :, t0:t0 + 128], op=ALU.add), reads=[k.bps[6], bo], writes=[bo])

                        for it_, t in enumerate(tl):
                            if it_ == 0:
                                stageE(t, pcnt[0] % 2)
                            if it_ + 1 < len(tl):
                                stageE(tl[it_ + 1], (pcnt[0] + 1) % 2)
                            stageF(t, pcnt[0] % 2)
                            pcnt[0] += 1
                        if not final:
                            S.dma('pool', k.ofT[s, :, c0:c0 + n].rearrange("(j p) t -> p j t", p=128), o_b[:, :, :n], reads=[bo], writes=k.t_ofT[s].r(c0, c0 + n))
                        elif (not isctx) or with_ctx:
                            for cp in range(4):
                                S.op('act', lambda e, cp=cp, o_b=o_b: e.activation(out=sq[:, :n], in_=o_b[:, cp, :n], func=AF.Square), reads=[bo], writes=[bsq])
                                S.op('pe', lambda e: e.matmul(k.ps[0][:, :n], lhsT=bones[:], rhs=sq[:, :n], start=True, stop=True), reads=[bc, bsq], writes=[k.bps[0]])
                                S.op('act', lambda e: e.activation(out=rst[:, :n], in_=k.ps[0][:, :n], func=AF.Ln, scale=1.0 / 64.0, bias=EPS), reads=[k.bps[0]], writes=[brst])
                                S.op('act', lambda e: e.activation(out=rst[:, :n], in_=rst[:, :n], func=AF.Exp, scale=-0.5), reads=[brst], writes=[brst])
                                inproj_fm(k, 1, n, w, bw, 1536 + cp * 128, h, bh)
                                S.op('act', lambda e: e.activation(out=gl[:, :n], in_=k.ps[1][:, :n], func=AF.Silu), reads=[k.bps[1]], writes=[bgl])
                                S.op('dve', lambda e, cp=cp, o_b=o_b: e.scalar_tensor_tensor(out=rst[:, :n], in0=o_b[:, cp, :n], scalar=gn[:, cp:cp + 1], in1=rst[:, :n], op0=ALU.mult, op1=ALU.mult), reads=[bo, bc, brst], writes=[brst])
                                S.op('pool', lambda e, cp=cp, y_b=y_b: e.tensor_tensor(out=y_b[:, cp, :n], in0=rst[:, :n], in1=gl[:, :n], op=ALU.mult), reads=[brst, bgl], writes=[by_])
                            S.dma('pool', k.yT[s, 3, :, c0:c0 + n].rearrange("(j p) t -> p j t", p=128), y_b[:, :, :n], reads=[by_], writes=k.t_yT[s][3].r(c0, c0 + n))
        S.barrier()


def phase_m(k, l, with_ctx):
    S, nc, I = k.S, k.nc, k.I
    p = l % 2
    with contextlib.ExitStack() as st:
        sb = lambda n, sh, d: st.enter_context(nc.sbuf_tensor(_u(n), list(sh), d))
        wg = sb("m_wg", [128, 8, 4096], BF16); bw = Buf()
        wbr = sb("m_wbr", [128, 16, D], BF16)
        wo = sb("m_wo", [128, 8, D], BF16)
        bwc = Buf()
        hb = [sb("m_hb%d" % i, [128, 8, 256], BF16) for i in range(2)]; bhb = [Buf() for _ in range(2)]
        yb = [sb("m_yb%d" % i, [128, 16, 256], BF16) for i in range(2)]; byb = [Buf() for _ in range(2)]
        sgt = [sb("m_sg%d" % i, [128, 256], F32) for i in range(2)]; bsg = [Buf() for _ in range(2)]
        tmp = [sb("m_tmp%d" % i, [128, 256], F32) for i in range(2)]; btmp = [Buf() for _ in range(2)]
        macc = sb("m_acc", [128, 256], F32); bacc = Buf()
        mT = sb("m_mT", [128, 8, 256], BF16); bmT = Buf()
        xt = [sb("m_x%d" % i, [128, D], F32) for i in range(2)]; bx = [Buf() for _ in range(2)]
        tt = sb("m_tt", [128, D], F32); btt = Buf()
        junk = sb("m_junk", [128, 512], F32); bj = Buf()
        ms = sb("m_ms", [128, 2], F32); bms = Buf()
        load_w(k, l, wg[:], COL["gate"], 4096, bw)
        S.dma('sp', wbr[:], k.w_br_bf[p].rearrange("(a p) c -> p a c", p=128), reads=[k.t_w[p]], writes=[bwc])
        S.dma('sp', wo[:], k.w_out_bf[p].rearrange("(a p) c -> p a c", p=128), reads=[k.t_w[p]], writes=[bwc])
        bi = 0
        xi = 0
        gi = 0
        for s in range(2):
            blocks = [(c0, False) for c0 in range(0, NTOK, 256)] + ([(NTOK, True)] if with_ctx else [])
            for (c0, isctx) in blocks:
                mi = 2 if isctx else s
                h = hb[bi % 2]; bh = bhb[bi % 2]; y = yb[bi % 2]; by_ = byb[bi % 2]
                bi += 1
                load_h(k, s, h, c0, 256, bh)
                for r in range(4):
                    S.dma('sp', y[:, 4 * r:4 * r + 4, :], k.yT[s, r, :, c0:c0 + 256].rearrange("(j p) t -> p j t", p=128), reads=k.t_yT[s][r].r(c0, c0 + 256), writes=[by_])
                for fc in range(8):
                    for r in range(4):
                        gb = gi % 2; gi += 1
                        for kc in range(8):
                            S.op('pe', lambda e, kc=kc, r=r, fc=fc, gb=gb: e.matmul(k.ps[gb][:, :256], lhsT=wg[:, kc, r * 1024 + fc * 128:r * 1024 + (fc + 1) * 128], rhs=h[:, kc, :], start=(kc == 0), stop=(kc == 7)),
                                 reads=[bw, bh], writes=[k.bps[gb]], inc=(kc == 7))
                        S.op('act', lambda e, gb=gb: e.activation(out=sgt[gb][:], in_=k.ps[gb][:, :256], func=AF.Sigmoid), reads=[k.bps[gb]], writes=[bsg[gb]])
                        for kc in range(4):
                            S.op('pe', lambda e, kc=kc, r=r, fc=fc, gb=gb: e.matmul(k.ps[2 + gb][:, :256], lhsT=wbr[:, 4 * r + kc, fc * 128:(fc + 1) * 128], rhs=y[:, 4 * r + kc, :], start=(kc == 0), stop=(kc == 3)),
                                 reads=[bwc, by_], writes=[k.bps[2 + gb]], inc=(kc == 3))
                        if r == 0:
                            S.op('dve', lambda e, gb=gb: e.tensor_tensor(out=macc[:], in0=k.ps[2 + gb][:, :256], in1=sgt[gb][:], op=ALU.mult), reads=[k.bps[2 + gb], bsg[gb]], writes=[bacc])
                        else:
                            S.op('dve', lambda e, gb=gb: e.tensor_tensor(out=tmp[gb][:], in0=k.ps[2 + gb][:, :256], in1=sgt[gb][:], op=ALU.mult), reads=[k.bps[2 + gb], bsg[gb]], writes=[btmp[gb]])
                            S.op('pool', lambda e, gb=gb: e.tensor_tensor(out=macc[:], in0=macc[:], in1=tmp[gb][:], op=ALU.add), reads=[bacc, btmp[gb]], writes=[bacc])
                    S.op('pool', lambda e, fc=fc: e.tensor_copy(mT[:, fc, :], macc[:]), reads=[bacc], writes=[bmT])
                for t in range(2):
                    tok0 = c0 + t * 128
                    x = xt[xi % 2]; bxx = bx[xi % 2]; xi += 1
                    if isctx:
                        src = (I["ctx"] if l == 0 else k.ctxcur)[s, t * 128:(t + 1) * 128, :]
                        dstd = k.ctxcur[s, t * 128:(t + 1) * 128, :]
                        tb = k.t_ctx[s].r(t * 128, t * 128 + 128)
                    else:
                        src = (I["x"] if l == 0 else k.OUT)[s, tok0:tok0 + 128, :]
                        dstd = k.OUT[s, tok0:tok0 + 128, :]
                        tb = k.t_x[s].r(tok0, tok0 + 128)
                    S.dma('sp', x[:], src, reads=tb, writes=[bxx])
                    for half in range(2):
                        for kc in range(8):
                            S.op('pe', lambda e, kc=kc, half=half, t=t: e.matmul(k.ps[4 + half][:], lhsT=mT[:, kc, t * 128:(t + 1) * 128], rhs=wo[:, kc, half * 512:(half + 1) * 512], start=(kc == 0), stop=(kc == 7)),
                                 reads=[bmT, bwc], writes=[k.bps[4 + half]], inc=(kc == 7))
                        S.op('act', lambda e, half=half: e.activation(out=junk[:], in_=k.ps[4 + half][:], func=AF.Square, scale=1.0 / 32.0, accum_out=ms[:, half:half + 1]), reads=[k.bps[4 + half]], writes=[bj, bms])
                    S.op('dve', lambda e: e.tensor_tensor(out=ms[:, 0:1], in0=ms[:, 0:1], in1=ms[:, 1:2], op=ALU.add), reads=[bms], writes=[bms])
                    rstd_from_ms(k, ms[:, 0:1], bms)
                    for half in range(2):
                        sl = slice(half * 512, (half + 1) * 512)
                        S.op('dve', lambda e, half=half, sl=sl, mi=mi: e.scalar_tensor_tensor(out=tt[:, sl], in0=k.ps[4 + half][:], scalar=ms[:, 0:1], in1=k.gtg[:, mi, sl], op0=ALU.mult, op1=ALU.mult), reads=[k.bps[4 + half], bms, k.b_gtg], writes=[btt])
                    S.op('pool', lambda e, x=x: e.tensor_tensor(out=x[:], in0=x[:], in1=tt[:], op=ALU.add), reads=[bxx, btt], writes=[bxx])
                    S.dma('pool', dstd, x[:], reads=[bxx], writes=tb)
        S.barrier()


_BF = ml_dtypes.bfloat16
_CONST = {}


def _constants():
    if _CONST:
        return _CONST
    c = {}
    c["ident"] = np.eye(128, dtype=np.float32).astype(_BF)
    c["eye8"] = (8.0 * np.eye(128, dtype=np.float32)).astype(_BF)
    rm = np.zeros((128, 128), np.float32)
    for dp in range(128):
        dd = dp % 64
        if (dd % 32) < 16:
            rm[dp, dp + 16] = -1.0
        else:
            rm[dp, dp - 16] = 1.0
    c["rsign"] = np.ascontiguousarray(rm.T).astype(_BF)
    t = np.arange(NTOK)
    pos = np.stack([t // 64, t % 64], 0).astype(np.float64)
    inv = 10000.0 ** (-np.arange(16, dtype=np.float64) * 2.0 / 32.0)
    d = np.arange(128) % 64
    ang = pos[d // 32, :] * inv[d % 16][:, None]
    c["cosT"] = np.cos(ang).astype(np.float32)
    c["sinT"] = np.sin(ang).astype(np.float32)
    s_, t_ = np.meshgrid(np.arange(128), np.arange(128), indexing="ij")
    c["trif"] = (s_ <= t_).astype(np.int32)
    c["trib"] = (s_ >= t_).astype(np.int32)
    c["bones"] = ((s_ // 64) == (t_ // 64)).astype(np.float32).astype(_BF)
    n = np.arange(NTOK, dtype=np.int64)
    m = (n[:, None] * n[None, :]) % NTOK
    sc = 1.0 / np.sqrt(NTOK * 128.0)
    angm = (2.0 * np.pi / NTOK) * m.astype(np.float32)
    cs = (np.cos(angm) * sc).astype(np.float32)
    sn = (np.sin(angm) * sc).astype(np.float32)
    dft = np.empty((16, NTOK, 512), dtype=_BF)
    for kb in range(16):
        dft[kb, :, 0:256] = cs[:, kb * 256:(kb + 1) * 256].astype(_BF)
        dft[kb, :, 256:512] = sn[:, kb * 256:(kb + 1) * 256].astype(_BF)
    c["dft"] = dft
    n2 = np.arange(LCTX, dtype=np.int64)
    a2 = (2.0 * np.pi / LCTX) * ((n2[:, None] * n2[None, :]) % LCTX)
    sc2 = 1.0 / np.sqrt(LCTX * 128.0)
    c["dft256"] = np.concatenate([np.cos(a2) * sc2, np.sin(a2) * sc2], 1).astype(np.float32).astype(_BF)
    n3 = np.arange(128, dtype=np.int64)
    a3 = (2.0 * np.pi / 128) * ((n3[:, None] * n3[None, :]) % 128)
    c["cc128"] = np.cos(a3).astype(np.float32).astype(_BF)
    c["ssn128"] = (-np.sin(a3)).astype(np.float32).astype(_BF)
    _CONST.update(c)
    return _CONST


def _zb_tables(rpb):
    L = rpb.shape[0]
    e = np.arange(2)[:, None, None, None]
    kc = np.arange(64)[None, :, None, None]
    w = np.arange(ZW)[None, None, :, None]
    qc = np.arange(64)[None, None, None, :]
    dr = 14 - w + e + 0 * kc + 0 * qc
    cs = np.clip(qc - 8, 0, 48)
    col_ok = (kc >= cs) & (kc < cs + 16)
    cidx = np.clip(kc - qc + 15, 0, 30) + 0 * dr
    out = np.empty((L, 2, 128, 8, ZW * 64), dtype=_BF)
    for kind in range(2):
        row_ok = (dr >= -7) & (dr <= 7)
        if kind == 1:
            row_ok = row_ok & (dr >= -4) & (dr < 4)
        ok = (row_ok & col_ok)
        ridx = np.clip(dr + 7, 0, 14)
        for l in range(L):
            g = rpb[l][:, ridx, cidx]
            g = np.where(ok[None], g, np.float32(NEG))
            out[l, kind] = g.transpose(1, 2, 0, 3, 4).reshape(128, 8, ZW * 64).astype(_BF)
    return out


def make_in_maps(inputs, nlayers=DEPTH, cores=range(8)):
    f = lambda a: np.ascontiguousarray(np.asarray(a, dtype=np.float32))
    c = _constants()
    L = nlayers
    shared = dict(c)
    shared["w_ada"] = f(inputs["w_ada"][:L])
    shared["b_ada"] = f(inputs["b_ada"][:L])
    shared["b_adaT"] = f(np.asarray(inputs["b_ada"][:L]).reshape(L, 24, 128).transpose(0, 2, 1))
    shared["g_preT"] = f(np.asarray(inputs["g_pre"][:L]).reshape(L, 8, 128).transpose(0, 2, 1))
    shared["g_post"] = f(inputs["g_post"][:L])
    shared["w_in"] = f(inputs["w_in"][:L])
    shared["zb"] = _zb_tables(np.asarray(inputs["na_rpb"][:L], dtype=np.float32))
    shared["fnet_w"] = f(np.asarray(inputs["fnet_w"][:L]).reshape(L, 512, 128))
    shared["gmlp_g"] = f(inputs["gmlp_norm_g"][:L])
    shared["gmlp_wsT"] = f(np.asarray(inputs["gmlp_ws"][:L]).transpose(0, 1, 3, 2).reshape(L, 1024, 128))
    shared["gmlp_bsT"] = f(np.asarray(inputs["gmlp_bs"][:L]).transpose(0, 2, 1))
    lg = np.asarray(inputs["hgrn_lb_logits"], dtype=np.float32)
    shared["lbT"] = f(lg.reshape(DEPTH, 2, 4, 128).transpose(3, 1, 2, 0).reshape(128, 8, DEPTH))
    shared["hgrn_gT"] = f(np.asarray(inputs["hgrn_norm_g"][:L]).reshape(L, 4, 128).transpose(0, 2, 1))
    shared["w_branch"] = f(np.asarray(inputs["w_branch"][:L]).reshape(L, 2048, D))
    shared["w_out"] = f(inputs["w_out"][:L])
    x = np.asarray(inputs["x"]); ctx = np.asarray(inputs["ctx"]); cc = np.asarray(inputs["c"]); c_ctx = np.asarray(inputs["c_ctx"])
    maps = []
    for ci in cores:
        m = dict(shared)
        m["x"] = f(x[2 * ci:2 * ci + 2])
        m["ctx"] = f(ctx[2 * ci:2 * ci + 2])
        c3 = np.stack([cc[2 * ci], cc[2 * ci + 1], c_ctx], 0)
        m["cT"] = f(c3.reshape(3, 8, 128).transpose(2, 1, 0))
        maps.append(m)
    return maps


_NC_CACHE = {}


def kernel(**inputs):
    if "nc" not in _NC_CACHE:
        _NC_CACHE["nc"] = build()
    nc = _NC_CACHE["nc"]
    maps = make_in_maps(inputs)
    res = run_bass_kernel_spmd(nc, maps, core_ids=list(range(8)))
    return np.concatenate([np.asarray(r["out"], dtype=np.float32) for r in res.results], axis=0)
```

```python
import contextlib
import numpy as np
import ml_dtypes
import concourse.bass as bass
import concourse.mybir as mybir
from concourse.bass_utils import run_bass_kernel_spmd

F32 = mybir.dt.float32
BF16 = mybir.dt.bfloat16
I32 = mybir.dt.int32
AF = mybir.ActivationFunctionType
ALU = mybir.AluOpType

D = 1024
NTOK = 4096
LCTX = 256
NT = NTOK + LCTX
NCOL = 11264
DEPTH = 4
EPS = 1e-6
COL = dict(a_q=0, a_k=512, a_v=1024, a_g=1536, b_x=2048, b_g=2560, c_u=3072, c_v=3584, c_g=4096,
           d_q=4608, d_ff=5120, d_fb=5632, d_i=6144, d_g=6656, gate=7168)
ZW = 26
NEG = -30000.0

SEM_WINDOW = 16000
DMA_RING = 8


class Buf:
    __slots__ = ("name", "lw", "rd", "excl")

    def __init__(self, name="", excl=False):
        self.name = name
        self.lw = None
        self.rd = {}
        self.excl = excl


class DTrack:
    def __init__(self, ncols, unit=128):
        self.unit = unit
        self.b = [Buf() for _ in range((ncols + unit - 1) // unit)]

    def r(self, c0, c1):
        return self.b[c0 // self.unit:(c1 + self.unit - 1) // self.unit]


class _Rec:
    def __init__(self):
        self.call = None

    def __getattr__(self, name):
        def f(*a, **kw):
            self.call = (name, a, kw)
            return self
        return f


def _freeze(fn):
    r = _Rec()
    fn(r)
    name, a, kw = r.call
    return lambda e: getattr(e, name)(*a, **kw)


class Sched:
    ENGS = ("pe", "act", "dve", "pool", "sp")

    def __init__(self, nc):
        self.nc = nc
        self.ops = {e: [] for e in self.ENGS}
        self.cnt = {e: 0 for e in self.ENGS}
        self.dcnt = {e: 0 for e in self.ENGS}
        self.known = {e: {} for e in self.ENGS}
        self.pending = {e: False for e in self.ENGS}

    def _tokwaits(self, eng, toks):
        waits = {}
        for t in toks:
            if t[0] == 'e':
                if t[1] == eng and eng == 'pe':
                    continue
                key = ('e', t[1], (t[2] - 1) // SEM_WINDOW)
                val = (t[2] - 1) % SEM_WINDOW + 1
            else:
                key = ('d', t[1], t[2] % DMA_RING)
                val = 16 * (t[2] // DMA_RING + 1)
            if waits.get(key, 0) < val:
                waits[key] = val
        out = []
        kn = self.known[eng]
        for key, val in waits.items():
            if kn.get(key, 0) >= val:
                continue
            kn[key] = val
            out.append((key, val))
        return out

    def _deps(self, eng, reads, writes):
        toks = []
        for b in reads:
            if b.lw is not None:
                toks.append(b.lw)
            if b.excl:
                toks.extend(v for kk, v in b.rd.items() if kk != eng)
        for b in writes:
            if b.lw is not None and not (b.lw[0] == 'e' and b.lw[1] == eng):
                toks.append(b.lw)
            toks.extend(v for v in b.rd.values() if not (v[0] == 'e' and v[1] == eng))
        return self._tokwaits(eng, toks)

    def op(self, eng, fn, reads=(), writes=(), inc=True):
        fn = _freeze(fn)
        waits = self._deps(eng, reads, writes)
        idx = self.cnt[eng] + 1
        tok = ('e', eng, idx)
        if inc:
            self.cnt[eng] = idx
            self.ops[eng].append((waits, fn, ('e', eng, (idx - 1) // SEM_WINDOW), 1))
            self.pending[eng] = False
        else:
            self.ops[eng].append((waits, fn, None, 0))
            self.pending[eng] = True
        for b in reads:
            b.rd[eng] = tok
        for b in writes:
            b.lw = tok
            b.rd = {}
        return tok

    def dma(self, q, out, in_, reads=(), writes=()):
        waits = self._deps(q, reads, writes)
        i = self.dcnt[q]
        self.dcnt[q] += 1
        if i >= DMA_RING:
            key = ('d', q, i % DMA_RING)
            val = 16 * (i // DMA_RING)
            kn = self.known[q]
            if kn.get(key, 0) < val:
                kn[key] = val
                waits.append((key, val))
        tok = ('d', q, i)
        fn = lambda e, out=out, in_=in_: e.dma_start(out=out, in_=in_)
        self.ops[q].append((waits, fn, ('d', q, i % DMA_RING), 16))
        qk = 'q' + q
        for b in reads:
            b.rd[qk] = tok
        for b in writes:
            b.lw = tok
            b.rd = {}
        return tok

    def barrier(self):
        toks = []
        for e in self.ENGS:
            assert not self.pending[e]
            if self.cnt[e] > 0:
                toks.append(('e', e, self.cnt[e]))
            n = self.dcnt[e]
            for i in range(max(0, n - DMA_RING), n):
                toks.append(('d', e, i))
        for e in self.ENGS:
            w = self._tokwaits(e, toks)
            if w:
                self.ops[e].append((w, None, None, 0))

    def emit(self):
        nc = self.nc
        self.barrier()
        sems = {}
        with contextlib.ExitStack() as st:
            def getsem(key):
                if key not in sems:
                    sems[key] = st.enter_context(nc.semaphore("s_%s_%s_%d" % key))
                return sems[key]
            for e in self.ENGS:
                for (waits, fn, inc, amt) in self.ops[e]:
                    for key, val in waits:
                        getsem(key)
                    if inc is not None:
                        getsem(inc)
            block = st.enter_context(nc.Block())
            handles = {"pe": block.tensor, "act": block.scalar, "dve": block.vector,
                       "pool": block.gpsimd, "sp": block.sync}
            for e in self.ENGS:
                ops = self.ops[e]
                if not ops:
                    continue

                def body(engine, ops=ops):
                    for (waits, fn, inc, amt) in ops:
                        for key, val in waits:
                            engine.wait_ge(sems[key], val)
                        if fn is not None:
                            ins = fn(engine)
                            if inc is not None:
                                ins.then_inc(sems[inc], amt)
                handles[e](body)


_UC = [0]
_DBG = {}


def _u(n):
    _UC[0] += 1
    return "%s_%d" % (n, _UC[0])


class K:
    pass


def _dram(nc, name, shape, dt, kind=None):
    if kind is None:
        return nc.dram_tensor(name, list(shape), dt).ap()
    return nc.dram_tensor(name, list(shape), dt, kind=kind).ap()


def build(nlayers=DEPTH, phases="PABCDM", dump=()):
    nc = bass.Bass("TRN2", target_bir_lowering=False)
    k = K()
    k.nc = nc
    S = Sched(nc)
    k.S = S
    IN = "ExternalInput"
    I = {}
    def inp(name, shape, dt=F32):
        I[name] = _dram(nc, name, shape, dt, IN)
        return I[name]
    inp("x", [2, NTOK, D]); inp("ctx", [2, LCTX, D]); inp("cT", [128, 8, 3])
    inp("w_ada", [nlayers, D, 3 * D]); inp("b_ada", [nlayers, 3 * D]); inp("b_adaT", [nlayers, 128, 24])
    inp("g_preT", [nlayers, 128, 8]); inp("g_post", [nlayers, D])
    inp("w_in", [nlayers, D, NCOL])
    inp("zb", [nlayers, 2, 128, 8, ZW * 64], BF16)
    inp("fnet_w", [nlayers, 512, 128]); inp("gmlp_g", [nlayers, 512]); inp("gmlp_wsT", [nlayers, 1024, 128])
    inp("gmlp_bsT", [nlayers, 128, 8]); inp("lbT", [128, 8, DEPTH]); inp("hgrn_gT", [nlayers, 128, 4])
    inp("w_branch", [nlayers, 2048, D]); inp("w_out", [nlayers, D, D])
    inp("ident", [128, 128], BF16); inp("rsign", [128, 128], BF16); inp("eye8", [128, 128], BF16)
    inp("cosT", [128, NTOK]); inp("sinT", [128, NTOK])
    inp("trif", [128, 128], I32); inp("trib", [128, 128], I32); inp("bones", [128, 128], BF16)
    inp("dft", [16, NTOK, 512], BF16); inp("dft256", [LCTX, 512], BF16)
    inp("cc128", [128, 128], BF16); inp("ssn128", [128, 128], BF16)
    OUT = _dram(nc, "out", [2, NTOK, D], F32, "ExternalOutput")
    def scr(name, shape, dt):
        return _dram(nc, name, shape, dt, "ExternalOutput" if name in dump else None)
    k.w_in_bf = scr("w_in_bf", [2, D, NCOL], BF16)
    k.w_br_bf = scr("w_br_bf", [2, 2048, D], BF16)
    k.w_out_bf = scr("w_out_bf", [2, D, D], BF16)
    k.fw_bf = scr("fw_bf", [2, 512, 128], BF16)
    k.ws_bf = scr("ws_bf", [2, 1024, 128], BF16)
    k.hT = scr("hT", [2, D, NT], BF16)
    k.yT = scr("yT", [2, 4, 512, NT], BF16)
    k.ofT = scr("ofT", [2, 512, NT], F32)
    k.ctxcur = scr("ctxcur", [2, LCTX, D], F32)
    k.I = I
    k.OUT = OUT
    k.t_w = [Buf() for _ in range(2)]
    k.t_hT = [DTrack(NT) for _ in range(2)]
    k.t_yT = [[DTrack(NT) for _ in range(4)] for _ in range(2)]
    k.t_ofT = [DTrack(NT) for _ in range(2)]
    k.t_x = [DTrack(NTOK) for _ in range(2)]
    k.t_ctx = [DTrack(LCTX) for _ in range(2)]

    with contextlib.ExitStack() as gst:
        k.gst = gst
        def gsb(name, shape, dt):
            return gst.enter_context(nc.sbuf_tensor(name, list(shape), dt))
        k.ps = [gst.enter_context(nc.psum_tensor("ps%d" % i, [128, 512], F32)) for i in range(8)]
        k.bps = [Buf("ps%d" % i, excl=True) for i in range(8)]
        k.ident = gsb("ident_sb", [128, 128], BF16)
        k.ones_bf = gsb("ones_bf", [128, 128], BF16)
        k.ones_f = gsb("ones_f", [128, 512], F32)
        k.scT = gsb("scT", [128, 8, 3], F32)
        k.modT = gsb("modT", [128, 24, 3], F32)
        k.gsT = gsb("gsT", [128, 8, 3], F32)
        k.gtg = gsb("gtg", [128, 3, D], F32)
        k.lb = gsb("lb", [128, 8, DEPTH], F32)
        k.oml = gsb("oml", [128, 8, DEPTH], F32)
        k.b_const = Buf(); k.b_scT = Buf(); k.b_mod = Buf(); k.b_gtg = Buf(); k.b_lb = Buf()
        S.dma('sp', k.ident[:], I["ident"], writes=[k.b_const])
        S.op('dve', lambda e: e.memset(k.ones_bf[:], 1.0), writes=[k.b_const])
        S.op('dve', lambda e: e.memset(k.ones_f[:], 1.0), writes=[k.b_const])
        prep_global(k)
        for l in range(nlayers):
            S.barrier()
            convert_weights(k, l)
            adaln(k, l)
            with_ctx = l < DEPTH - 1
            if "P" in phases:
                for s in range(2):
                    phase_pre(k, l, s)
            if "C" in phases:
                phase_c(k, l, with_ctx)
            if "B" in phases:
                phase_b(k, l, with_ctx)
            if "A" in phases:
                phase_a(k, l, with_ctx)
            if "D" in phases:
                phase_d(k, l, with_ctx)
            if "M" in phases:
                phase_m(k, l, with_ctx)
        S.emit()
    return nc


def prep_global(k):
    S, nc, I = k.S, k.nc, k.I
    with contextlib.ExitStack() as st:
        sb = lambda n, s, d: st.enter_context(nc.sbuf_tensor(_u(n), list(s), d))
        cT = sb("pg_cT", [128, 8, 3], F32)
        lg = sb("pg_lg", [128, 8, DEPTH], F32)
        ex = sb("pg_ex", [128, 8, DEPTH], F32)
        sm = sb("pg_sm", [128, 8], F32)
        b = Buf()
        S.dma('sp', cT[:], I["cT"], writes=[b])
        S.op('act', lambda e: e.activation(out=k.scT[:], in_=cT[:], func=AF.Silu), reads=[b], writes=[k.b_scT])
        b2 = Buf()
        S.dma('sp', lg[:], I["lbT"], writes=[b2])
        S.op('act', lambda e: e.activation(out=ex[:], in_=lg[:], func=AF.Exp), reads=[b2], writes=[b2])
        S.op('dve', lambda e: e.tensor_reduce(out=sm[:], in_=ex[:], axis=mybir.AxisListType.X, op=ALU.add), reads=[b2], writes=[b2])
        S.op('dve', lambda e: e.reciprocal(sm[:], sm[:]), reads=[b2], writes=[b2])
        S.op('dve', lambda e: e.tensor_tensor(out=ex[:], in0=ex[:], in1=sm[:].unsqueeze(2).broadcast_to([128, 8, DEPTH]), op=ALU.mult), reads=[b2], writes=[b2])
        S.op('dve', lambda e: e.memset(k.lb[:, :, 0:1], 0.0), reads=[b2], writes=[k.b_lb])
        for l in range(1, DEPTH):
            S.op('dve', lambda e, l=l: e.tensor_tensor(out=k.lb[:, :, l:l + 1], in0=k.lb[:, :, l - 1:l], in1=ex[:, :, l:l + 1], op=ALU.add), reads=[b2, k.b_lb], writes=[k.b_lb])
        S.op('dve', lambda e: e.tensor_scalar(out=k.lb[:], in0=k.lb[:], scalar1=0.0, scalar2=None, op0=ALU.max), reads=[k.b_lb], writes=[k.b_lb])
        S.op('dve', lambda e: e.tensor_scalar(out=k.oml[:], in0=k.lb[:], scalar1=-1.0, scalar2=1.0, op0=ALU.mult, op1=ALU.add), reads=[k.b_lb], writes=[k.b_lb])
        S.barrier()


def convert_weights(k, l):
    S, I = k.S, k.I
    p = l % 2
    w = [k.t_w[p]]
    for c0 in range(0, NCOL, 1408):
        S.dma('pool', k.w_in_bf[p, :, c0:c0 + 1408], I["w_in"][l, :, c0:c0 + 1408], writes=w)
    S.dma('pool', k.w_br_bf[p], I["w_branch"][l], writes=w)
    S.dma('pool', k.w_out_bf[p], I["w_out"][l], writes=w)
    S.dma('pool', k.fw_bf[p], I["fnet_w"][l], writes=w)
    S.dma('pool', k.ws_bf[p], I["gmlp_wsT"][l], writes=w)


def load_w(k, l, dst, c0, ncol, bdst):
    p = l % 2
    src = k.w_in_bf[p, :, c0:c0 + ncol].rearrange("(kc p) c -> p kc c", p=128)
    k.S.dma('sp', dst, src, reads=[k.t_w[p]], writes=[bdst])


def adaln(k, l):
    S, nc, I = k.S, k.nc, k.I
    with contextlib.ExitStack() as st:
        sb = lambda n, s, d: st.enter_context(nc.sbuf_tensor(_u(n), list(s), d))
        wa = [sb("ad_wa%d" % i, [128, 8, 512], F32) for i in range(2)]
        bwa = [Buf() for _ in range(2)]
        scbc = sb("ad_scbc", [128, 8, 3, 128], F32)
        bT = sb("ad_bT", [128, 24], F32)
        gpT = sb("ad_gpT", [128, 8], F32)
        brow = sb("ad_brow", [128, D], F32)
        grow = sb("ad_grow", [128, D], F32)
        tmp = sb("ad_tmp", [128, 8, 3], F32)
        b = Buf(); bsc = Buf()
        S.dma('sp', bT[:], I["b_adaT"][l], writes=[b])
        S.dma('sp', gpT[:], I["g_preT"][l], writes=[b])
        S.dma('sp', brow[:], I["b_ada"][l:l + 1, 2 * D:3 * D].partition_broadcast(128), writes=[b])
        S.dma('sp', grow[:], I["g_post"][l:l + 1, :].partition_broadcast(128), writes=[b])
        S.op('dve', lambda e: e.tensor_copy(scbc[:], k.scT[:].unsqueeze(3).broadcast_to([128, 8, 3, 128])), reads=[k.b_scT], writes=[bsc])
        pm = k.ps[0]
        for g in range(6):
            w = wa[g % 2]
            S.dma('sp', w[:], I["w_ada"][l, :, g * 512:(g + 1) * 512].rearrange("(kc p) c -> p kc c", p=128), writes=[bwa[g % 2]])
            for j in range(4):
                ch = 4 * g + j
                for kc in range(8):
                    S.op('pe', lambda e, w=w, kc=kc, j=j, ch=ch: e.matmul(pm[:, ch * 3:ch * 3 + 3], lhsT=w[:, kc, j * 128:(j + 1) * 128], rhs=k.scT[:, kc, :], start=(kc == 0), stop=(kc == 7)),
                         reads=[bwa[g % 2], k.b_scT], writes=[k.bps[0]], inc=(kc == 7))
            if g >= 4:
                half = g - 4
                for s in range(3):
                    pg = k.ps[1 + s]
                    for kc in range(8):
                        S.op('pe', lambda e, w=w, kc=kc, s=s, pg=pg: e.matmul(pg[:], lhsT=scbc[:, kc, s, :], rhs=w[:, kc, :], start=(kc == 0), stop=(kc == 7)),
                             reads=[bwa[g % 2], bsc], writes=[k.bps[1 + s]], inc=(kc == 7))
                    sl = slice(half * 512, (half + 1) * 512)
                    S.op('dve', lambda e, s=s, pg=pg, sl=sl: e.tensor_tensor(out=k.gtg[:, s, sl], in0=pg[:], in1=brow[:, sl], op=ALU.add), reads=[k.bps[1 + s], b], writes=[k.b_gtg])
                    S.op('dve', lambda e, s=s, sl=sl: e.tensor_tensor(out=k.gtg[:, s, sl], in0=k.gtg[:, s, sl], in1=grow[:, sl], op=ALU.mult), reads=[b, k.b_gtg], writes=[k.b_gtg])
        S.op('dve', lambda e: e.tensor_tensor(out=k.modT[:], in0=pm[:, 0:72].rearrange("p (c s) -> p c s", s=3), in1=bT[:].unsqueeze(2).broadcast_to([128, 24, 3]), op=ALU.add), reads=[k.bps[0], b], writes=[k.b_mod])
        S.op('dve', lambda e: e.tensor_scalar(out=tmp[:], in0=k.modT[:, 8:16, :], scalar1=1.0, scalar2=None, op0=ALU.add), reads=[k.b_mod], writes=[b])
        S.op('dve', lambda e: e.tensor_tensor(out=k.gsT[:], in0=tmp[:], in1=gpT[:].unsqueeze(2).broadcast_to([128, 8, 3]), op=ALU.mult), reads=[b], writes=[k.b_mod])
        S.barrier()


def rstd_from_ms(k, ms, bms):
    S = k.S
    S.op('act', lambda e: e.activation(out=ms, in_=ms, func=AF.Sqrt, bias=EPS, scale=1.0), reads=[bms], writes=[bms])
    S.op('dve', lambda e: e.reciprocal(ms, ms), reads=[bms], writes=[bms])


def phase_pre(k, l, s):
    S, nc, I = k.S, k.nc, k.I
    with contextlib.ExitStack() as st:
        sb = lambda n, sh, d: st.enter_context(nc.sbuf_tensor(_u(n), list(sh), d))
        xt = [sb("pr_x%d" % i, [128, D], F32) for i in range(8)]
        bx = [Buf() for _ in range(8)]
        xs = [sb("pr_xs%d" % i, [128, D], BF16) for i in range(2)]
        bxs = [Buf() for _ in range(2)]
        junk = sb("pr_junk", [128, D], F32)
        ms = [sb("pr_ms%d" % i, [128, 4], F32) for i in range(2)]
        bms = [Buf() for _ in range(2)]
        hb = [sb("pr_hb%d" % i, [128, 8, 512], BF16) for i in range(2)]
        bhb = [Buf() for _ in range(2)]
        bj = Buf()
        pT = [k.ps[i][:].bitcast(BF16) for i in range(4)]
        it = 0
        blocks = [(0, b0 * 512, 4) for b0 in range(8)] + [(1, NTOK, 2)]
        for bi, (isctx, c0, ntile) in enumerate(blocks):
            mi = 2 if isctx else s
            xis = []
            for t in range(ntile):
                xi = it % 8
                xis.append(xi)
                if isctx:
                    src = (I["ctx"] if l == 0 else k.ctxcur)[s, t * 128:(t + 1) * 128, :]
                    rb = k.t_ctx[s].r(t * 128, t * 128 + 128)
                else:
                    src = (I["x"] if l == 0 else k.OUT)[s, c0 + t * 128:c0 + (t + 1) * 128, :]
                    rb = k.t_x[s].r(c0 + t * 128, c0 + t * 128 + 128)
                S.dma('sp', xt[xi][:], src, reads=rb, writes=[bx[xi]])
                S.op('act', lambda e: e.activation(out=junk[:], in_=xt[xi][:], func=AF.Square, scale=1.0 / 32.0, accum_out=ms[bi % 2][:, t:t + 1]), reads=[bx[xi]], writes=[bj, bms[bi % 2]])
                it += 1
            rstd_from_ms(k, ms[bi % 2][:, :ntile], bms[bi % 2])
            for t in range(ntile):
                xi = xis[t]
                si = (bi * 4 + t) % 2
                if t % 2 == 0:
                    S.op('act', lambda e: e.activation(out=xs[si][:], in_=xt[xi][:], func=AF.Copy, scale=ms[bi % 2][:, t:t + 1]), reads=[bx[xi], bms[bi % 2]], writes=[bxs[si]])
                else:
                    S.op('dve', lambda e: e.tensor_scalar(out=xs[si][:], in0=xt[xi][:], scalar1=ms[bi % 2][:, t:t + 1], scalar2=None, op0=ALU.mult), reads=[bx[xi], bms[bi % 2]], writes=[bxs[si]])
                for kc in range(8):
                    dst = pT[kc // 2][:, (kc % 2) * 512 + t * 128:(kc % 2) * 512 + (t + 1) * 128]
                    S.op('pe', lambda e: e.transpose(dst, xs[si][:, kc * 128:(kc + 1) * 128], k.ident[:]), reads=[bxs[si], k.b_const], writes=[k.bps[kc // 2]], inc=(kc == 7))
            hi = bi % 2
            nt = ntile * 128
            for kc in range(8):
                src = pT[kc // 2][:, (kc % 2) * 512:(kc % 2) * 512 + nt]
                if kc % 2 == 0:
                    S.op('act', lambda e, src=src, kc=kc, hi=hi, nt=nt, mi=mi: e.activation(out=hb[hi][:, kc, :nt], in_=src, func=AF.Identity, bias=k.modT[:, kc, mi:mi + 1], scale=k.gsT[:, kc, mi:mi + 1]), reads=[k.bps[kc // 2], k.b_mod], writes=[bhb[hi]])
                else:
                    S.op('dve', lambda e, src=src, kc=kc, hi=hi, nt=nt, mi=mi: e.tensor_scalar(out=hb[hi][:, kc, :nt], in0=src, scalar1=k.gsT[:, kc, mi:mi + 1], scalar2=k.modT[:, kc, mi:mi + 1], op0=ALU.mult, op1=ALU.add), reads=[k.bps[kc // 2], k.b_mod], writes=[bhb[hi]])
            S.dma('sp', k.hT[s, :, c0:c0 + nt].rearrange("(kc p) t -> p kc t", p=128), hb[hi][:, :, :nt], reads=[bhb[hi]], writes=k.t_hT[s].r(c0, c0 + nt))
        S.barrier()


def load_h(k, s, dst, c0, n, bdst):
    k.S.dma('sp', dst[:, :, :n], k.hT[s, :, c0:c0 + n].rearrange("(kc p) t -> p kc t", p=128), reads=k.t_hT[s].r(c0, c0 + n), writes=[bdst])


def inproj_fm(k, pbank, n, w, bw, cw, h, bh, hc0=0):
    for kc in range(8):
        k.S.op('pe', lambda e, kc=kc: e.matmul(k.ps[pbank][:, :n], lhsT=w[:, kc, cw:cw + 128], rhs=h[:, kc, hc0:hc0 + n], start=(kc == 0), stop=(kc == 7)),
               reads=[bw, bh], writes=[k.bps[pbank]], inc=(kc == 7))


def inproj_tm(k, pbank, w, bw, cw, h, bh, t0):
    for kc in range(8):
        k.S.op('pe', lambda e, kc=kc: e.matmul(k.ps[pbank][:], lhsT=h[:, kc, t0:t0 + 128], rhs=w[:, kc, cw:cw + 512], start=(kc == 0), stop=(kc == 7)),
               reads=[bw, bh], writes=[k.bps[pbank]], inc=(kc == 7))


def seq_blocks(with_ctx_block=True):
    bl = [(b0 * 512, 4) for b0 in range(8)]
    if with_ctx_block:
        bl.append((NTOK, 2))
    return bl


def phase_c(k, l, with_ctx):
    S, nc, I = k.S, k.nc, k.I
    p = l % 2
    with contextlib.ExitStack() as st:
        sb = lambda n, sh, d: st.enter_context(nc.sbuf_tensor(_u(n), list(sh), d))
        w = sb("c_w", [128, 8, 1536], BF16); bw = Buf()
        wsT = sb("c_ws", [128, 8, 128], BF16)
        gn = sb("c_gn", [128, 512], F32)
        bsT = sb("c_bs", [128, 8], F32)
        bc = Buf()
        hb = [sb("c_hb%d" % i, [128, 8, 512], BF16) for i in range(2)]; bhb = [Buf() for _ in range(2)]
        junk = sb("c_junk", [128, 512], F32); bj = Buf()
        ms = sb("c_ms", [128, 1], F32); bms = Buf()
        vn = sb("c_vn", [128, 512], BF16); bvn = Buf()
        sg = sb("c_sg", [128, 512], F32); bsg = Buf()
        t1 = sb("c_t1", [128, 512], F32); bt1 = Buf()
        yt = sb("c_y", [128, 512], BF16); byt = Buf()
        yb = [sb("c_yb%d" % i, [128, 4, 512], BF16) for i in range(2)]; byb = [Buf() for _ in range(2)]
        load_w(k, l, w[:], COL["c_u"], 1536, bw)
        S.dma('sp', wsT[:], k.ws_bf[p].rearrange("(g s) t -> s g t", s=128), reads=[k.t_w[p]], writes=[bc])
        S.dma('sp', gn[:], I["gmlp_g"][l:l + 1, :].partition_broadcast(128), writes=[bc])
        S.dma('sp', bsT[:], I["gmlp_bsT"][l], writes=[bc])
        pT = [k.ps[4][:].bitcast(BF16), k.ps[5][:].bitcast(BF16)]
        bi = 0
        for s in range(2):
            blocks = seq_blocks(with_ctx)
            load_h(k, s, hb[bi % 2], blocks[0][0], blocks[0][1] * 128, bhb[bi % 2])
            for ib, (c0, ntile) in enumerate(blocks):
                h = hb[bi % 2]; bh = bhb[bi % 2]
                if ib + 1 < len(blocks):
                    load_h(k, s, hb[(bi + 1) % 2], blocks[ib + 1][0], blocks[ib + 1][1] * 128, bhb[(bi + 1) % 2])
                for t in range(ntile):
                    t0 = t * 128
                    inproj_tm(k, 0, w, bw, 512, h, bh, t0)
                    S.op('act', lambda e: e.activation(out=junk[:], in_=k.ps[0][:], func=AF.Square, scale=float(512 ** -0.5), accum_out=ms[:]), reads=[k.bps[0]], writes=[bj, bms])
                    rstd_from_ms(k, ms[:], bms)
                    S.op('dve', lambda e: e.scalar_tensor_tensor(out=vn[:], in0=k.ps[0][:], scalar=ms[:, 0:1], in1=gn[:], op0=ALU.mult, op1=ALU.mult), reads=[k.bps[0], bms, bc], writes=[bvn])
                    for g in range(8):
                        S.op('pe', lambda e, g=g: e.matmul(k.ps[1][:, g * 64:(g + 1) * 64], lhsT=wsT[:, g, :], rhs=vn[:, g * 64:(g + 1) * 64], start=True, stop=True), reads=[bc, bvn], writes=[k.bps[1]], inc=(g == 7))
                    inproj_tm(k, 2, w, bw, 0, h, bh, t0)
                    inproj_tm(k, 3, w, bw, 1024, h, bh, t0)
                    S.op('act', lambda e: e.activation(out=sg[:], in_=k.ps[3][:], func=AF.Silu), reads=[k.bps[3]], writes=[bsg])
                    S.op('dve', lambda e: e.tensor_tensor(out=t1[:].rearrange("p (g c) -> p g c", g=8), in0=k.ps[1][:].rearrange("p (g c) -> p g c", g=8), in1=bsT[:].unsqueeze(2).broadcast_to([128, 8, 64]), op=ALU.add), reads=[k.bps[1], bc], writes=[bt1])
                    S.op('dve', lambda e: e.tensor_tensor(out=t1[:], in0=k.ps[2][:], in1=t1[:], op=ALU.mult), reads=[k.bps[2], bt1], writes=[bt1])
                    S.op('dve', lambda e: e.tensor_tensor(out=yt[:], in0=t1[:], in1=sg[:], op=ALU.mult), reads=[bt1, bsg], writes=[byt])
                    for j in range(4):
                        dst = pT[j // 2][:, (j % 2) * 512 + t0:(j % 2) * 512 + t0 + 128]
                        S.op('pe', lambda e, dst=dst, j=j: e.transpose(dst, yt[:, j * 128:(j + 1) * 128], k.ident[:]), reads=[byt, k.b_const], writes=[k.bps[4 + j // 2]], inc=(j == 3))
                nt = ntile * 128
                yo = yb[bi % 2]
                for j in range(4):
                    src = pT[j // 2][:, (j % 2) * 512:(j % 2) * 512 + nt]
                    S.op('act', lambda e, src=src, j=j, yo=yo, nt=nt: e.copy(yo[:, j, :nt], src), reads=[k.bps[4 + j // 2]], writes=[byb[bi % 2]])
                S.dma('pool', k.yT[s, 2, :, c0:c0 + nt].rearrange("(j p) t -> p j t", p=128), yo[:, :, :nt], reads=[byb[bi % 2]], writes=k.t_yT[s][2].r(c0, c0 + nt))
                bi += 1
        S.barrier()


def phase_b(k, l, with_ctx):
    S, nc, I = k.S, k.nc, k.I
    p = l % 2
    with contextlib.ExitStack() as st:
        sb = lambda n, sh, d: st.enter_context(nc.sbuf_tensor(_u(n), list(sh), d))
        w = sb("b_w", [128, 8, 1024], BF16); bw = Buf()
        fw = sb("b_fw", [128, 4, 128], BF16)
        cc = sb("b_cc", [128, 128], BF16); ssn = sb("b_ss", [128, 128], BF16)
        m12 = sb("b_m12", [128, 2, 4, 128], BF16)
        bc = Buf(); bm = Buf()
        z = sb("b_z", [128, 32, 512], BF16); bz = Buf()
        hb = [sb("b_hb%d" % i, [128, 8, 512], BF16) for i in range(2)]; bhb = [Buf() for _ in range(2)]
        dft = [sb("b_dft%d" % i, [128, 32, 512], BF16) for i in range(2)]; bdft = [Buf() for _ in range(2)]
        Y = sb("b_Y", [128, 4, 512], BF16); bY = Buf()
        sg = sb("b_sg", [128, 4, 256], F32); bsg = Buf()
        yb = [sb("b_yb%d" % i, [128, 4, 256], BF16) for i in range(2)]; byb = [Buf() for _ in range(2)]
        load_w(k, l, w[:], COL["b_x"], 1024, bw)
        S.dma('sp', fw[:], k.fw_bf[p].rearrange("(g c) d -> c g d", c=128), reads=[k.t_w[p]], writes=[bc])
        S.dma('sp', cc[:], I["cc128"], writes=[bc])
        S.dma('sp', ssn[:], I["ssn128"], writes=[bc])
        for i, mat in enumerate((cc, ssn)):
            S.op('pe', lambda e, mat=mat, i=i: e.matmul(k.ps[i][:], lhsT=mat[:], rhs=fw[:].rearrange("c g d -> c (g d)"), start=True, stop=True), reads=[bc], writes=[k.bps[i]])
            S.op('act', lambda e, i=i: e.copy(m12[:, i, :, :].rearrange("c g d -> c (g d)"), k.ps[i][:]), reads=[k.bps[i]], writes=[bm])
        di = 0
        for s in range(2):
            for (tc0, ntile, nkb, isctx) in ([(0, 32, 16, False)] + ([(NTOK, 2, 1, True)] if with_ctx else [])):
                nblk = (ntile + 3) // 4
                for ib in range(nblk):
                    nt = min(4, ntile - ib * 4)
                    load_h(k, s, hb[ib % 2], tc0 + ib * 512, nt * 128, bhb[ib % 2])
                    for t in range(nt):
                        inproj_tm(k, 0, w, bw, 0, hb[ib % 2], bhb[ib % 2], t * 128)
                        S.op('act', lambda e, ib=ib, t=t: e.copy(z[:, ib * 4 + t, :], k.ps[0][:]), reads=[k.bps[0]], writes=[bz])
                for kb in range(nkb):
                    dt_ = dft[di % 2]; bd = bdft[di % 2]
                    if isctx:
                        S.dma('sp', dt_[:, :2, :], I["dft256"].rearrange("(t p) c -> p t c", p=128), writes=[bd])
                    else:
                        for hh in range(2):
                            S.dma('sp', dt_[:, hh * 16:(hh + 1) * 16, :], I["dft"][kb, hh * 2048:(hh + 1) * 2048, :].rearrange("(t p) c -> p t c", p=128), writes=[bd])
                    kc0 = tc0 + kb * 256
                    hi = (kb + 1) % 2
                    load_h(k, s, hb[hi], kc0, 256, bhb[hi])
                    for g in range(4):
                        for t in range(ntile):
                            S.op('pe', lambda e, g=g, t=t, dt_=dt_: e.matmul(k.ps[g][:], lhsT=z[:, t, g * 128:(g + 1) * 128], rhs=dt_[:, t, :], start=(t == 0), stop=(t == ntile - 1)),
                                 reads=[bz, bd], writes=[k.bps[g]], inc=(t == ntile - 1))
                        if g % 2 == 0:
                            S.op('act', lambda e, g=g: e.copy(Y[:, g, :], k.ps[g][:]), reads=[k.bps[g]], writes=[bY])
                        else:
                            S.op('dve', lambda e, g=g: e.tensor_copy(Y[:, g, :], k.ps[g][:]), reads=[k.bps[g]], writes=[bY])
                    for g in range(4):
                        pb_ = 4 + g // 2
                        dst = k.ps[pb_][:, (g % 2) * 256:(g % 2) * 256 + 256]
                        S.op('pe', lambda e, g=g, dst=dst: e.matmul(dst, lhsT=m12[:, 0, g, :], rhs=Y[:, g, 0:256], start=True, stop=False), reads=[bm, bY], writes=[k.bps[pb_]], inc=False)
                        S.op('pe', lambda e, g=g, dst=dst: e.matmul(dst, lhsT=m12[:, 1, g, :], rhs=Y[:, g, 256:512], start=False, stop=True), reads=[bm, bY], writes=[k.bps[pb_]])
                    for g in range(4):
                        pb_ = 6 + g // 2
                        dst = k.ps[pb_][:, (g % 2) * 256:(g % 2) * 256 + 256]
                        for kc in range(8):
                            S.op('pe', lambda e, g=g, dst=dst, kc=kc, hi=hi: e.matmul(dst, lhsT=w[:, kc, 512 + g * 128:512 + (g + 1) * 128], rhs=hb[hi][:, kc, 0:256], start=(kc == 0), stop=(kc == 7)),
                                 reads=[bw, bhb[hi]], writes=[k.bps[pb_]], inc=(kc == 7))
                    yo = yb[di % 2]
                    for gg in range(2):
                        S.op('act', lambda e, gg=gg: e.activation(out=sg[:, 2 * gg:2 * gg + 2, :].rearrange("p a t -> p (a t)"), in_=k.ps[6 + gg][:], func=AF.Silu), reads=[k.bps[6 + gg]], writes=[bsg])
                        S.op('dve', lambda e, gg=gg, yo=yo: e.tensor_tensor(out=yo[:, 2 * gg:2 * gg + 2, :].rearrange("p a t -> p (a t)"), in0=k.ps[4 + gg][:], in1=sg[:, 2 * gg:2 * gg + 2, :].rearrange("p a t -> p (a t)"), op=ALU.mult), reads=[k.bps[4 + gg], bsg], writes=[byb[di % 2]])
                    S.dma('pool', k.yT[s, 1, :, kc0:kc0 + 256].rearrange("(j p) t -> p j t", p=128), yo[:], reads=[byb[di % 2]], writes=k.t_yT[s][1].r(kc0, kc0 + 256))
                    di += 1
        S.barrier()


def na_qblocks(with_ctx):
    bl = [(0, 256, 0, list(range(0, 4)), 0, True), (60 * 64, 256, 0, list(range(28, 32)), 60, True),
          (4 * 64, 256, 1, list(range(0, 6)), 4, True), (56 * 64, 256, 1, list(range(26, 32)), 56, True)]
    for gq in range(1, 7):
        bl.append((gq * 512, 512, 1, list(range(4 * gq - 2, 4 * gq + 6)), 8 * gq, True))
    if with_ctx:
        bl.append((NTOK, 256, 1, [], 0, False))
    return bl


def phase_a(k, l, with_ctx):
    S, nc, I = k.S, k.nc, k.I
    with contextlib.ExitStack() as st:
        sb = lambda n, sh, d: st.enter_context(nc.sbuf_tensor(_u(n), list(sh), d))
        w = sb("a_w", [128, 8, 1024], BF16); bw = Buf()
        kT = sb("a_kT", [128, 4, NT], BF16); bkT = Buf()
        vt = sb("a_v", [128, 34, 512], BF16); bv = Buf()
        csb = [sb("a_cs%d" % i, [128, 2, 512], F32) for i in range(2)]; bcs = [Buf() for _ in range(2)]
        csi = [0]
        rs = sb("a_rs", [128, 128], BF16); eye8 = sb("a_e8", [128, 128], BF16)
        bc = Buf()
        zb = sb("a_zb", [128, 8, ZW * 64], BF16); bzb = Buf()
        hb = [sb("a_hb%d" % i, [128, 8, 512], BF16) for i in range(2)]; bhb = [Buf() for _ in range(2)]
        qp = sb("a_qp", [128, 4, 512], BF16); bqp = Buf()
        qp2 = sb("a_qp2", [128, 2, 4, 512], BF16); bqp2 = Buf()
        qr2 = sb("a_qr2", [128, 2, 4, 512], BF16); bqr = Buf()
        gs = sb("a_gs", [128, 4, 512], BF16); bgs = Buf()
        t1 = sb("a_t1", [128, 512], F32); bt1 = Buf()
        t2 = sb("a_t2", [128, 512], F32); bt2 = Buf()
        kp = sb("a_kp", [128, 512], BF16); bkp = Buf()
        es = [sb("a_es%d" % i, [128, 512], BF16) for i in range(4)]; bes = [Buf() for _ in range(4)]
        rd = sb("a_rd", [128, 512], F32); brd = Buf()
        yo = [sb("a_yo%d" % i, [128, 4, 512], BF16) for i in range(2)]; byo = [Buf() for _ in range(2)]
        S.dma('sp', rs[:], I["rsign"], writes=[bc]); S.dma('sp', eye8[:], I["eye8"], writes=[bc])
        S.op('pool', lambda e: e.memset(qp2[:], 0.0), writes=[bqp2])
        S.op('pool', lambda e: e.memset(qr2[:], 0.0), writes=[bqr])

        def load_cs(c0, n):
            csi[0] += 1
            i = csi[0] % 2
            S.dma('sp', csb[i][:, 0, :n], I["cosT"][:, c0:c0 + n], writes=[bcs[i]])
            S.dma('sp', csb[i][:, 1, :n], I["sinT"][:, c0:c0 + n], writes=[bcs[i]])

        def rope(psrc, plain_bf, bplain, dst, c0, n, rbank):
            i = csi[0] % 2
            if _DBG.get('nors'):
                rbank = psrc
            else:
                S.op('pe', lambda e: e.matmul(k.ps[rbank][:, :n], lhsT=rs[:], rhs=plain_bf, start=True, stop=True), reads=[bc, bplain], writes=[k.bps[rbank]])
            S.op('dve', lambda e: e.tensor_tensor(out=t1[:, :n], in0=k.ps[psrc][:, :n], in1=csb[i][:, 0, :n], op=ALU.mult), reads=[k.bps[psrc], bcs[i], bplain], writes=[bt1])
            S.op('dve', lambda e: e.tensor_tensor(out=t2[:, :n], in0=k.ps[rbank][:, :n], in1=csb[i][:, 1, :n], op=ALU.mult), reads=[k.bps[rbank], bcs[i]], writes=[bt2])

        hi = 0
        for s in range(2):
            load_w(k, l, w[:], COL["a_k"], 1024, bw)
            for (c0, ntile) in seq_blocks(True):
                n = ntile * 128
                h = hb[hi % 2]; bh = bhb[hi % 2]; hi += 1
                load_h(k, s, h, c0, n, bh)
                if c0 < NTOK:
                    load_cs(c0, n)
                for cp in range(4):
                    inproj_fm(k, 0, n, w, bw, cp * 128, h, bh)
                    if c0 >= NTOK or _DBG.get('norope'):
                        S.op('act', lambda e, cp=cp: e.copy(kT[:, cp, c0:c0 + n], k.ps[0][:, :n]), reads=[k.bps[0]], writes=[bkT])
                    else:
                        S.op('act', lambda e: e.copy(kp[:, :n], k.ps[0][:, :n]), reads=[k.bps[0]], writes=[bkp])
                        rope(0, kp[:, :n], bkp, None, c0, n, 1)
                        if _DBG.get('nopool'):
                            S.op('dve', lambda e, cp=cp: e.tensor_tensor(out=kT[:, cp, c0:c0 + n], in0=t1[:, :n], in1=t2[:, :n], op=ALU.add), reads=[bt1, bt2], writes=[bkT])
                        else:
                            S.op('dve', lambda e, cp=cp: e.tensor_tensor(out=kT[:, cp, c0:c0 + n], in0=t1[:, :n], in1=t2[:, :n], op=ALU.add), reads=[bt1, bt2], writes=[bkT])
                for t in range(ntile):
                    inproj_tm(k, 2, w, bw, 512, h, bh, t * 128)
                    S.op('act', lambda e, t=t: e.copy(vt[:, c0 // 128 + t, :], k.ps[2][:]), reads=[k.bps[2]], writes=[bv])
            if _DBG.get('a1only'):
                continue
            load_w(k, l, w[:, :, 0:512], COL["a_q"], 512, bw)
            load_w(k, l, w[:, :, 512:1024], COL["a_g"], 512, bw)
            cur_kind = None
            for qi, (t0, nq, kind, chunks, q0, use_rope) in enumerate(na_qblocks(with_ctx)):
                if 'qsel' in _DBG and qi not in _DBG['qsel']:
                    continue
                if use_rope and kind != cur_kind:
                    S.dma('sp', zb[:], I["zb"][l, kind], writes=[bzb])
                    cur_kind = kind
                h = hb[hi % 2]; bh = bhb[hi % 2]; hi += 1
                load_h(k, s, h, t0, nq, bh)
                if use_rope:
                    load_cs(t0, nq)
                for cp in range(4):
                    inproj_fm(k, 0, nq, w, bw, cp * 128, h, bh)
                    S.op('act', lambda e, cp=cp: e.copy(qp[:, cp, :nq], k.ps[0][:, :nq]), reads=[k.bps[0]], writes=[bqp])
                    for j in range(2):
                        S.op('act', lambda e, cp=cp, j=j: e.copy(qp2[64 * j:64 * j + 64, j, cp, :nq], k.ps[0][64 * j:64 * j + 64, :nq]), reads=[k.bps[0]], writes=[bqp2])
                    if use_rope:
                        rope(0, qp[:, cp, :nq], bqp, None, t0, nq, 1)
                        for j in range(2):
                            S.op('dve', lambda e, cp=cp, j=j: e.tensor_tensor(out=qr2[64 * j:64 * j + 64, j, cp, :nq], in0=t1[64 * j:64 * j + 64, :nq], in1=t2[64 * j:64 * j + 64, :nq], op=ALU.add), reads=[bt1, bt2], writes=[bqr])
                    inproj_fm(k, 2, nq, w, bw, 512 + cp * 128, h, bh)
                    S.op('act', lambda e, cp=cp: e.activation(out=gs[:, cp, :nq], in_=k.ps[2][:, :nq], func=AF.Silu), reads=[k.bps[2]], writes=[bgs])
                y = yo[qi % 2]
                for cp in range(4):
                    ob, db = (6, 7) if cp % 2 == 0 else (0, 1)
                    klist = [(c, True) for c in chunks] + [(32, False), (33, False)]
                    items = []
                    for j in range(2):
                        for ic, (c, band) in enumerate(klist):
                            items.append((j, c, band, ic == 0, ic == len(klist) - 1))

                    def stage1(idx, cp=cp):
                        j, c, band, first, last = items[idx]
                        pb = 64 * j; hd = 2 * cp + j
                        sbk = 3 + (idx % 3)
                        e_t = es[idx % 4]; be = bes[idx % 4]
                        if band:
                            woff = 14 - (2 * c - q0)
                            S.op('pe', lambda e: e.matmul(k.ps[sbk][:, :nq], lhsT=kT[:, cp, c * 128:(c + 1) * 128], rhs=qr2[:, j, cp, :nq], start=True, stop=False),
                                 reads=[bkT, bqr], writes=[k.bps[sbk]], inc=False)
                            S.op('pe', lambda e: e.matmul(k.ps[sbk][:, :nq], lhsT=eye8[:], rhs=zb[:, hd, woff * 64:woff * 64 + nq], start=False, stop=True),
                                 reads=[bc, bzb], writes=[k.bps[sbk]])
                        else:
                            S.op('pe', lambda e: e.matmul(k.ps[sbk][:, :nq], lhsT=kT[:, cp, c * 128:(c + 1) * 128], rhs=qp2[:, j, cp, :nq], start=True, stop=True),
                                 reads=[bkT, bqp2], writes=[k.bps[sbk]])
                        S.op('act', lambda e: e.activation(out=e_t[:, :nq], in_=k.ps[sbk][:, :nq], func=AF.Exp, scale=0.125), reads=[k.bps[sbk]], writes=[be])

                    def stage2(idx, cp=cp, ob=ob, db=db):
                        j, c, band, first, last = items[idx]
                        pb = 64 * j; hd = 2 * cp + j
                        e_t = es[idx % 4]; be = bes[idx % 4]
                        S.op('pe', lambda e: e.matmul(k.ps[ob][pb:pb + 64, :nq], lhsT=vt[:, c, hd * 64:(hd + 1) * 64], rhs=e_t[:, :nq], start=first, stop=last),
                             reads=[bv, be], writes=[k.bps[ob]], inc=False)
                        S.op('pe', lambda e: e.matmul(k.ps[db][pb:pb + 64, :nq], lhsT=k.ones_bf[:, 0:64], rhs=e_t[:, :nq], start=first, stop=last),
                             reads=[k.b_const, be], writes=[k.bps[db]])

                    LOOK = 2
                    for idx in range(len(items) + LOOK):
                        if idx < len(items):
                            stage1(idx)
                        if idx >= LOOK:
                            stage2(idx - LOOK)
                    S.op('act', lambda e: e.activation(out=rd[:, :nq], in_=k.ps[db][:, :nq], func=AF.Ln), reads=[k.bps[db]], writes=[brd])
                    S.op('act', lambda e: e.activation(out=rd[:, :nq], in_=rd[:, :nq], func=AF.Exp, scale=-1.0), reads=[brd], writes=[brd])
                    S.op('dve', lambda e: e.tensor_tensor(out=rd[:, :nq], in0=k.ps[ob][:, :nq], in1=rd[:, :nq], op=ALU.mult), reads=[k.bps[ob], brd], writes=[brd])
                    S.op('dve', lambda e, cp=cp, y=y: e.tensor_tensor(out=y[:, cp, :nq], in0=rd[:, :nq], in1=gs[:, cp, :nq], op=ALU.mult), reads=[brd, bgs], writes=[byo[qi % 2]])
                S.dma('pool', k.yT[s, 0, :, t0:t0 + nq].rearrange("(j p) t -> p j t", p=128), y[:, :, :nq], reads=[byo[qi % 2]], writes=k.t_yT[s][0].r(t0, t0 + nq))
        S.barrier()


def phase_d(k, l, with_ctx):
    S, nc, I = k.S, k.nc, k.I
    with contextlib.ExitStack() as st:
        sb = lambda n, sh, d: st.enter_context(nc.sbuf_tensor(_u(n), list(sh), d))
        w = sb("d_w", [128, 8, 2048], BF16); bw = Buf()
        gn = sb("d_gn", [128, 4], F32)
        trif = sb("d_trif", [128, 128], I32); trib = sb("d_trib", [128, 128], I32); bones = sb("d_bones", [128, 128], BF16)
        bc = Buf()
        hb = [sb("d_hb%d" % i, [128, 8, 512], BF16) for i in range(2)]; bhb = [Buf() for _ in range(2)]
        Pp = sb("d_P", [128, 4, 516], F32); bP = Buf()
        kT = sb("d_kT", [128, 4, 512], BF16); bkT = Buf()
        qT = sb("d_qT", [128, 4, 512], BF16); bqT = Buf()
        itm2 = [sb("d_itm%d" % i, [128, 512], BF16) for i in range(2)]; bitm2 = [Buf() for _ in range(2)]
        negr2 = [sb("d_negr%d" % i, [128, 4, 8], F32) for i in range(2)]; bnr2 = [Buf() for _ in range(2)]
        pcnt = [0]
        dqa = [sb("d_dq%d" % i, [128, 128], F32) for i in range(8)]; bdqa = [Buf() for _ in range(8)]
        eqa = [sb("d_eq%d" % i, [128, 128], F32) for i in range(8)]; beqa = [Buf() for _ in range(8)]
        sga = [sb("d_sg%d" % i, [128, 512], F32) for i in range(4)]; bsga = [Buf() for _ in range(4)]
        lfa = [sb("d_lf%d" % i, [128, 512], F32) for i in range(4)]; blfa = [Buf() for _ in range(4)]
        ek = [sb("d_ek%d" % i, [128, 4, 128], F32) for i in range(4)]; bek = [Buf() for _ in range(4)]
        qtl2 = [sb("d_qtl%d" % i, [128, 2, 4, 128], BF16) for i in range(2)]; bqtl2 = [Buf() for _ in range(2)]
        ktl2 = [sb("d_ktl%d" % i, [128, 4, 4, 128], BF16) for i in range(2)]; bktl2 = [Buf() for _ in range(2)]
        qh2 = [sb("d_qh%d" % i, [128, 4, 128], BF16) for i in range(2)]; bqh2 = [Buf() for _ in range(2)]
        khT = sb("d_khT", [128, 4, 128], BF16); bkhT = Buf()
        khtm2 = [sb("d_khtm%d" % i, [128, 512], BF16) for i in range(2)]; bkhtm2 = [Buf() for _ in range(2)]
        dec2 = [sb("d_dec%d" % i, [128, 4], F32) for i in range(2)]; bdec2 = [Buf() for _ in range(2)]
        At = sb("d_At", [128, 8, 128], BF16); bAt = Buf()
        St = sb("d_S", [128, 2, 4, 64], F32); bS = [Buf(), Buf()]
        Sbf = sb("d_Sbf", [128, 4, 128], BF16); bSbf = Buf()
        ob = [sb("d_ob%d" % i, [128, 4, 512], F32) for i in range(2)]; bob = [Buf() for _ in range(2)]
        sq = sb("d_sq", [128, 512], BF16); bsq = Buf()
        rst = sb("d_rst", [128, 512], F32); brst = Buf()
        gl = sb("d_gl", [128, 512], F32); bgl = Buf()
        yb = [sb("d_yb%d" % i, [128, 4, 512], BF16) for i in range(2)]; byb = [Buf() for _ in range(2)]
        S.dma('sp', gn[:], I["hgrn_gT"][l], writes=[bc])
        S.dma('sp', trif[:], I["trif"], writes=[bc]); S.dma('sp', trib[:], I["trib"], writes=[bc]); S.dma('sp', bones[:], I["bones"], writes=[bc])
        S.op('dve', lambda e: e.memset(Pp[:], 0.0), writes=[bP])
        for i in range(2):
            S.op('pool', lambda e, i=i: e.memset(qtl2[i][:], 0.0), writes=[bqtl2[i]])
        S.op('pool', lambda e: e.memset(Sbf[:], 0.0), writes=[bSbf])
        ptr = k.ps[5][:].bitcast(BF16)
        hi = 0
        for s in range(2):
            for d in range(2):
                S.op('dve', lambda e, d=d: e.memset(St[:, d, :, :], 0.0), writes=[bS[d]])
            for (base, nblk_tiles) in ((NTOK, [2]), (0, [4] * 8)):
                isctx = base >= NTOK
                for d in range(2):
                    _DBG['dpass'] = _DBG.get('dpass', 0) + 1
                    if 'dmax' in _DBG and _DBG['dpass'] > _DBG['dmax']:
                        continue
                    sgn = 1.0 if d == 0 else -1.0
                    final = (d == 1)
                    load_w(k, l, w[:, :, 0:512], COL["d_q"], 512, bw)
                    load_w(k, l, w[:, :, 512:1024], COL["d_ff"] if d == 0 else COL["d_fb"], 512, bw)
                    load_w(k, l, w[:, :, 1024:1536], COL["d_i"], 512, bw)
                    if final:
                        load_w(k, l, w[:, :, 1536:2048], COL["d_g"], 512, bw)
                    for i in range(4):
                        S.op('pool', lambda e, i=i: e.memset(ek[i][:], 0.0), writes=[bek[i]])
                    S.op('pool', lambda e: e.memset(At[:], 0.0), writes=[bAt])
                    S.op('act', lambda e, d=d: e.copy(Sbf[0:64, :, 0:64], St[0:64, d, :, :]), reads=[bS[d]], writes=[bSbf]); S.op('act', lambda e, d=d: e.copy(Sbf[64:128, :, 64:128], St[64:128, d, :, :]), reads=[bS[d]], writes=[bSbf])
                    mask = trif if d == 0 else trib
                    blist = list(range(len(nblk_tiles)))
                    if d == 1:
                        blist = blist[::-1]
                    for ib in blist:
                        ntile = nblk_tiles[ib]
                        n = ntile * 128
                        c0 = base + ib * 512
                        h = hb[hi % 2]; bh = bhb[hi % 2]
                        o_b = ob[hi % 2]; bo = bob[hi % 2]; y_b = yb[hi % 2]; by_ = byb[hi % 2]
                        hi += 1
                        load_h(k, s, h, c0, n, bh)
                        if final:
                            S.dma('sp', o_b[:, :, :n], k.ofT[s, :, c0:c0 + n].rearrange("(j p) t -> p j t", p=128), reads=k.t_ofT[s].r(c0, c0 + n), writes=[bo])
                        for ft in range(4):
                            zb_ = ft % 2
                            inproj_fm(k, zb_, n, w, bw, 512 + ft * 128, h, bh)
                            S.op('act', lambda e: e.activation(out=sga[ft][:, :n], in_=k.ps[zb_][:, :n], func=AF.Sigmoid), reads=[k.bps[zb_]], writes=[bsga[ft]])
                            S.op('dve', lambda e: e.tensor_scalar(out=sga[ft][:, :n], in0=sga[ft][:, :n], scalar1=k.oml[:, d * 4 + ft, l:l + 1], scalar2=k.lb[:, d * 4 + ft, l:l + 1], op0=ALU.mult, op1=ALU.add), reads=[bsga[ft], k.b_lb], writes=[bsga[ft]])
                            S.op('dve', lambda e: e.tensor_scalar(out=sga[ft][:, :n], in0=sga[ft][:, :n], scalar1=1e-30, scalar2=None, op0=ALU.max), reads=[bsga[ft]], writes=[bsga[ft]])
                        for ft in range(4):
                            S.op('act', lambda e: e.activation(out=lfa[ft][:, :n], in_=sga[ft][:, :n], func=AF.Ln), reads=[bsga[ft]], writes=[blfa[ft]])
                            S.op('dve', lambda e: e.tensor_scalar(out=kT[:, ft, :n], in0=sga[ft][:, :n], scalar1=-1.0, scalar2=1.0, op0=ALU.mult, op1=ALU.add), reads=[bsga[ft]], writes=[bkT])
                            S.op('dve', lambda e: e.tensor_tensor_scan(out=Pp[:, ft, 1:1 + n], data0=k.ones_f[:, :n], data1=lfa[ft][:, :n], initial=0.0, op0=ALU.mult, op1=ALU.add), reads=[blfa[ft], k.b_const], writes=[bP])
                        for ft in range(4):
                            zb_ = ft % 2
                            inproj_fm(k, zb_, n, w, bw, ft * 128, h, bh)
                            S.op('act', lambda e: e.copy(qT[:, ft, :n], k.ps[zb_][:, :n]), reads=[k.bps[zb_]], writes=[bqT])
                        tl = list(range(ntile))
                        if d == 1:
                            tl = tl[::-1]

                        def stageE(t, pi):
                            t0 = t * 128
                            xo = t0 + 1 if d == 0 else t0
                            itm = itm2[pi]; bitm = bitm2[pi]; qtl = qtl2[pi]; bqtl = bqtl2[pi]; ktl = ktl2[pi]; bktl = bktl2[pi]
                            qh = qh2[pi]; bqh = bqh2[pi]; khtm = khtm2[pi]; bkhtm = bkhtm2[pi]; dec = dec2[pi]; bdec = bdec2[pi]
                            negr = negr2[pi]; bnr = bnr2[pi]
                            inproj_tm(k, 2, w, bw, 1024, h, bh, t0)
                            S.op('act', lambda e: e.copy(itm[:], k.ps[2][:]), reads=[k.bps[2]], writes=[bitm])
                            S.op('dve', lambda e: e.tensor_scalar(out=negr[:, :, 0:4], in0=Pp[:, :, t0 + 16:t0 + 113:32], scalar1=-1.0, scalar2=None, op0=ALU.mult), reads=[bP], writes=[bnr])
                            S.op('dve', lambda e: e.tensor_scalar(out=negr[:, :, 4:6], in0=Pp[:, :, t0:t0 + 129:128], scalar1=-1.0, scalar2=None, op0=ALU.mult), reads=[bP], writes=[bnr])
                            def tiles(ft):
                                return (dqa[ft], bdqa[ft], eqa[ft], beqa[ft], dqa[4 + ft], bdqa[4 + ft], eqa[4 + ft], beqa[4 + ft],
                                        Pp[:, ft, xo:xo + 128], Pp[:, ft, t0:t0 + 1], Pp[:, ft, t0 + 128:t0 + 129], negr[:, ft, 4:5], negr[:, ft, 5:6])
                            for ft in range(4):
                                dq, bdq, eq, beq, dq2, bdq2, eq2, beq2, X, B0p, B1p, B0n, B1n = tiles(ft)
                                S.op('dve', lambda e: e.tensor_tensor(out=dq[:].rearrange("p (i c) -> p i c", i=4), in0=X.rearrange("p (i c) -> p i c", i=4), in1=Pp[:, ft, t0 + 16:t0 + 113:32].unsqueeze(2).broadcast_to([128, 4, 32]), op=ALU.subtract), reads=[bP], writes=[bdq])
                            for ft in range(4):
                                dq, bdq, eq, beq, dq2, bdq2, eq2, beq2, X, B0p, B1p, B0n, B1n = tiles(ft)
                                S.op('act', lambda e: e.activation(out=eq[:], in_=dq[:], func=AF.Exp, scale=sgn), reads=[bdq], writes=[beq])
                                for i in range(4):
                                    lo, hi_ = (0, 32 * (i + 1)) if d == 0 else (32 * i, 128)
                                    bias = Pp[:, ft, t0 + 16 + 32 * i:t0 + 17 + 32 * i] if d == 0 else negr[:, ft, i:i + 1]
                                    S.op('act', lambda e: e.activation(out=ek[ft][:, i, lo:hi_], in_=Pp[:, ft, xo + lo:xo + hi_], func=AF.Exp, scale=-sgn, bias=bias), reads=[bP, bnr], writes=[bek[ft]])
                                bq_ = B0n if d == 0 else B1p
                                S.op('act', lambda e: e.activation(out=eq2[:], in_=X, func=AF.Exp, scale=sgn, bias=bq_), reads=[bP, bnr], writes=[beq2])
                                bk_ = B1p if d == 0 else B0n
                                S.op('act', lambda e: e.activation(out=dq2[:], in_=X, func=AF.Exp, scale=-sgn, bias=bk_), reads=[bP, bnr], writes=[bdq2])
                                S.op('act', lambda e: e.activation(out=dec[:, ft:ft + 1], in_=B1p, func=AF.Exp, scale=1.0, bias=B0n), reads=[bP, bnr], writes=[bdec])
                            for ft in range(4):
                                dq, bdq, eq, beq, dq2, bdq2, eq2, beq2, X, B0p, B1p, B0n, B1n = tiles(ft)
                                S.op('dve', lambda e: e.tensor_tensor(out=khT[:, ft, :], in0=dq2[:], in1=kT[:, ft, t0:t0 + 128], op=ALU.mult), reads=[bdq2, bkT], writes=[bkhT])
                                S.op('pe', lambda e: e.transpose(ptr[:, ft * 128:(ft + 1) * 128], khT[:, ft, :], k.ident[:]), reads=[bkhT, k.b_const], writes=[k.bps[5]])
                                S.op('dve', lambda e: e.tensor_tensor(out=qtl[0:64, 0, ft, :], in0=eq[0:64, :], in1=qT[0:64, ft, t0:t0 + 128], op=ALU.mult), reads=[beq, bqT], writes=[bqtl])
                                S.op('dve', lambda e: e.tensor_tensor(out=qtl[64:128, 1, ft, :], in0=eq[64:128, :], in1=qT[64:128, ft, t0:t0 + 128], op=ALU.mult), reads=[beq, bqT], writes=[bqtl])
                                S.op('dve', lambda e: e.tensor_tensor(out=ktl[:, ft, :, :], in0=ek[ft][:], in1=kT[:, ft, t0:t0 + 128].unsqueeze(1).broadcast_to([128, 4, 128]), op=ALU.mult), reads=[bek[ft], bkT], writes=[bktl])
                                S.op('dve', lambda e: e.tensor_tensor(out=qh[:, ft, :], in0=eq2[:], in1=qT[:, ft, t0:t0 + 128], op=ALU.mult), reads=[beq2, bqT], writes=[bqh])
                            S.op('dve', lambda e: e.tensor_copy(khtm[:], ptr[:, 0:512]), reads=[k.bps[5]], writes=[bkhtm])

                        def stageF(t, pi):
                            t0 = t * 128
                            itm = itm2[pi]; bitm = bitm2[pi]; qtl = qtl2[pi]; bqtl = bqtl2[pi]; ktl = ktl2[pi]; bktl = bktl2[pi]
                            qh = qh2[pi]; bqh = bqh2[pi]; khtm = khtm2[pi]; bkhtm = bkhtm2[pi]; dec = dec2[pi]; bdec = bdec2[pi]
                            for hd in range(8):
                                cp = hd // 2
                                sbk = 3 + hd // 4
                                for i in range(4):
                                    dst = k.ps[sbk][:, (hd % 4) * 128 + 32 * i:(hd % 4) * 128 + 32 * i + 32]
                                    S.op('pe', lambda e: e.matmul(dst, lhsT=ktl[:, cp, i, :], rhs=qtl[:, hd % 2, cp, 32 * i:32 * i + 32], start=True, stop=True),
                                         reads=[bktl, bqtl], writes=[k.bps[sbk]], inc=(hd % 4 == 3 and i == 3))
                            for half in range(2):
                                S.op('dve', lambda e: e.copy_predicated(out=At[:, 4 * half:4 * half + 4, :], mask=mask[:].unsqueeze(1).broadcast_to([128, 4, 128]), data=k.ps[3 + half][:].rearrange("p (h t) -> p h t", h=4)), reads=[k.bps[3 + half], bc, bAt], writes=[bAt])
                            for cp in range(4):
                                for j in range(2):
                                    hd = 2 * cp + j
                                    S.op('pe', lambda e: e.matmul(k.ps[6][64 * j:64 * j + 64, cp * 128:(cp + 1) * 128], lhsT=itm[:, hd * 64:(hd + 1) * 64], rhs=At[:, hd, :], start=True, stop=False), reads=[bitm, bAt], writes=[k.bps[6]], inc=False)
                                S.op('pe', lambda e: e.matmul(k.ps[6][:, cp * 128:(cp + 1) * 128], lhsT=Sbf[:, cp, :], rhs=qh[:, cp, :], start=False, stop=True), reads=[bSbf, bqh], writes=[k.bps[6]], inc=(cp == 3))
                            for cp in range(4):
                                S.op('pe', lambda e: e.matmul(k.ps[7][:, cp * 128:(cp + 1) * 128], lhsT=khtm[:, cp * 128:(cp + 1) * 128], rhs=itm[:, cp * 128:(cp + 1) * 128], start=True, stop=True), reads=[bkhtm, bitm], writes=[k.bps[7]], inc=(cp == 3))
                            for cp in range(4):
                                for j in range(2):
                                    pb = 64 * j
                                    S.op('dve', lambda e: e.scalar_tensor_tensor(out=St[pb:pb + 64, d, cp, :], in0=St[pb:pb + 64, d, cp, :], scalar=dec[pb:pb + 64, cp:cp + 1], in1=k.ps[7][pb:pb + 64, cp * 128 + 64 * j:cp * 128 + 64 * j + 64], op0=ALU.mult, op1=ALU.add),
                                         reads=[bS[d], bdec, k.bps[7]], writes=[bS[d]])
                            S.op('act', lambda e: e.copy(Sbf[0:64, :, 0:64], St[0:64, d, :, :]), reads=[bS[d]], writes=[bSbf])
                            S.op('act', lambda e: e.copy(Sbf[64:128, :, 64:128], St[64:128, d, :, :]), reads=[bS[d]], writes=[bSbf])
                            if not final:
                                S.op('act', lambda e: e.copy(o_b[:, :, t0:t0 + 128], k.ps[6][:].rearrange("p (c t) -> p c t", c=4)), reads=[k.bps[6]], writes=[bo])
                            else:
                                S.op('dve', lambda e: e.tensor_tensor(out=o_b[:, :, t0:t0 + 128], in0=k.ps[6][:].rearrange("p (c t) -> p c t", c=4), in1=o_b[:, :, t0:t0 + 128], op=ALU.add), reads=[k.bps[6], bo], writes=[bo])

                        for it_, t in enumerate(tl):
                            if it_ == 0:
                                stageE(t, pcnt[0] % 2)
                            if it_ + 1 < len(tl):
                                stageE(tl[it_ + 1], (pcnt[0] + 1) % 2)
                            stageF(t, pcnt[0] % 2)
                            pcnt[0] += 1
                        if not final:
                            S.dma('pool', k.ofT[s, :, c0:c0 + n].rearrange("(j p) t -> p j t", p=128), o_b[:, :, :n], reads=[bo], writes=k.t_ofT[s].r(c0, c0 + n))
                        elif (not isctx) or with_ctx:
                            for cp in range(4):
                                S.op('act', lambda e, cp=cp, o_b=o_b: e.activation(out=sq[:, :n], in_=o_b[:, cp, :n], func=AF.Square), reads=[bo], writes=[bsq])
                                S.op('pe', lambda e: e.matmul(k.ps[0][:, :n], lhsT=bones[:], rhs=sq[:, :n], start=True, stop=True), reads=[bc, bsq], writes=[k.bps[0]])
                                S.op('act', lambda e: e.activation(out=rst[:, :n], in_=k.ps[0][:, :n], func=AF.Ln, scale=1.0 / 64.0, bias=EPS), reads=[k.bps[0]], writes=[brst])
                                S.op('act', lambda e: e.activation(out=rst[:, :n], in_=rst[:, :n], func=AF.Exp, scale=-0.5), reads=[brst], writes=[brst])
                                inproj_fm(k, 1, n, w, bw, 1536 + cp * 128, h, bh)
                                S.op('act', lambda e: e.activation(out=gl[:, :n], in_=k.ps[1][:, :n], func=AF.Silu), reads=[k.bps[1]], writes=[bgl])
                                S.op('dve', lambda e, cp=cp, o_b=o_b: e.scalar_tensor_tensor(out=rst[:, :n], in0=o_b[:, cp, :n], scalar=gn[:, cp:cp + 1], in1=rst[:, :n], op0=ALU.mult, op1=ALU.mult), reads=[bo, bc, brst], writes=[brst])
                                S.op('dve', lambda e, cp=cp, y_b=y_b: e.tensor_tensor(out=y_b[:, cp, :n], in0=rst[:, :n], in1=gl[:, :n], op=ALU.mult), reads=[brst, bgl], writes=[by_])
                            S.dma('pool', k.yT[s, 3, :, c0:c0 + n].rearrange("(j p) t -> p j t", p=128), y_b[:, :, :n], reads=[by_], writes=k.t_yT[s][3].r(c0, c0 + n))
        S.barrier()


def phase_m(k, l, with_ctx):
    S, nc, I = k.S, k.nc, k.I
    p = l % 2
    with contextlib.ExitStack() as st:
        sb = lambda n, sh, d: st.enter_context(nc.sbuf_tensor(_u(n), list(sh), d))
        wg = sb("m_wg", [128, 8, 4096], BF16); bw = Buf()
        wbr = sb("m_wbr", [128, 16, D], BF16)
        wo = sb("m_wo", [128, 8, D], BF16)
        bwc = Buf()
        hb = [sb("m_hb%d" % i, [128, 8, 256], BF16) for i in range(2)]; bhb = [Buf() for _ in range(2)]
        yb = [sb("m_yb%d" % i, [128, 16, 256], BF16) for i in range(2)]; byb = [Buf() for _ in range(2)]
        sgt = [sb("m_sg%d" % i, [128, 256], F32) for i in range(2)]; bsg = [Buf() for _ in range(2)]
        tmp = [sb("m_tmp%d" % i, [128, 256], F32) for i in range(2)]; btmp = [Buf() for _ in range(2)]
        macc = sb("m_acc", [128, 256], F32); bacc = Buf()
        mT = sb("m_mT", [128, 8, 256], BF16); bmT = Buf()
        xt = [sb("m_x%d" % i, [128, D], F32) for i in range(2)]; bx = [Buf() for _ in range(2)]
        tt = sb("m_tt", [128, D], F32); btt = Buf()
        junk = sb("m_junk", [128, 512], F32); bj = Buf()
        ms = sb("m_ms", [128, 2], F32); bms = Buf()
        load_w(k, l, wg[:], COL["gate"], 4096, bw)
        S.dma('sp', wbr[:], k.w_br_bf[p].rearrange("(a p) c -> p a c", p=128), reads=[k.t_w[p]], writes=[bwc])
        S.dma('sp', wo[:], k.w_out_bf[p].rearrange("(a p) c -> p a c", p=128), reads=[k.t_w[p]], writes=[bwc])
        bi = 0
        xi = 0
        gi = 0
        for s in range(2):
            blocks = [(c0, False) for c0 in range(0, NTOK, 256)] + ([(NTOK, True)] if with_ctx else [])
            for (c0, isctx) in blocks:
                mi = 2 if isctx else s
                h = hb[bi % 2]; bh = bhb[bi % 2]; y = yb[bi % 2]; by_ = byb[bi % 2]
                bi += 1
                load_h(k, s, h, c0, 256, bh)
                for r in range(4):
                    S.dma('sp', y[:, 4 * r:4 * r + 4, :], k.yT[s, r, :, c0:c0 + 256].rearrange("(j p) t -> p j t", p=128), reads=k.t_yT[s][r].r(c0, c0 + 256), writes=[by_])
                for fc in range(8):
                    for r in range(4):
                        gb = gi % 2; gi += 1
                        for kc in range(8):
                            S.op('pe', lambda e, kc=kc, r=r, fc=fc, gb=gb: e.matmul(k.ps[gb][:, :256], lhsT=wg[:, kc, r * 1024 + fc * 128:r * 1024 + (fc + 1) * 128], rhs=h[:, kc, :], start=(kc == 0), stop=(kc == 7)),
                                 reads=[bw, bh], writes=[k.bps[gb]], inc=(kc == 7))
                        S.op('act', lambda e, gb=gb: e.activation(out=sgt[gb][:], in_=k.ps[gb][:, :256], func=AF.Sigmoid), reads=[k.bps[gb]], writes=[bsg[gb]])
                        for kc in range(4):
                            S.op('pe', lambda e, kc=kc, r=r, fc=fc, gb=gb: e.matmul(k.ps[2 + gb][:, :256], lhsT=wbr[:, 4 * r + kc, fc * 128:(fc + 1) * 128], rhs=y[:, 4 * r + kc, :], start=(kc == 0), stop=(kc == 3)),
                                 reads=[bwc, by_], writes=[k.bps[2 + gb]], inc=(kc == 3))
                        if r == 0:
                            S.op('dve', lambda e, gb=gb: e.tensor_tensor(out=macc[:], in0=k.ps[2 + gb][:, :256], in1=sgt[gb][:], op=ALU.mult), reads=[k.bps[2 + gb], bsg[gb]], writes=[bacc])
                        else:
                            S.op('dve', lambda e, gb=gb: e.tensor_tensor(out=tmp[gb][:], in0=k.ps[2 + gb][:, :256], in1=sgt[gb][:], op=ALU.mult), reads=[k.bps[2 + gb], bsg[gb]], writes=[btmp[gb]])
                            S.op('dve', lambda e, gb=gb: e.tensor_tensor(out=macc[:], in0=macc[:], in1=tmp[gb][:], op=ALU.add), reads=[bacc, btmp[gb]], writes=[bacc])
                    S.op('act', lambda e, fc=fc: e.copy(mT[:, fc, :], macc[:]), reads=[bacc], writes=[bmT])
                for t in range(2):
                    tok0 = c0 + t * 128
                    x = xt[xi % 2]; bxx = bx[xi % 2]; xi += 1
                    if isctx:
                        src = (I["ctx"] if l == 0 else k.ctxcur)[s, t * 128:(t + 1) * 128, :]
                        dstd = k.ctxcur[s, t * 128:(t + 1) * 128, :]
                        tb = k.t_ctx[s].r(t * 128, t * 128 + 128)
                    else:
                        src = (I["x"] if l == 0 else k.OUT)[s, tok0:tok0 + 128, :]
                        dstd = k.OUT[s, tok0:tok0 + 128, :]
                        tb = k.t_x[s].r(tok0, tok0 + 128)
                    S.dma('sp', x[:], src, reads=tb, writes=[bxx])
                    for half in range(2):
                        for kc in range(8):
                            S.op('pe', lambda e, kc=kc, half=half, t=t: e.matmul(k.ps[4 + half][:], lhsT=mT[:, kc, t * 128:(t + 1) * 128], rhs=wo[:, kc, half * 512:(half + 1) * 512], start=(kc == 0), stop=(kc == 7)),
                                 reads=[bmT, bwc], writes=[k.bps[4 + half]], inc=(kc == 7))
                        S.op('act', lambda e, half=half: e.activation(out=junk[:], in_=k.ps[4 + half][:], func=AF.Square, scale=1.0 / 32.0, accum_out=ms[:, half:half + 1]), reads=[k.bps[4 + half]], writes=[bj, bms])
                    S.op('dve', lambda e: e.tensor_tensor(out=ms[:, 0:1], in0=ms[:, 0:1], in1=ms[:, 1:2], op=ALU.add), reads=[bms], writes=[bms])
                    rstd_from_ms(k, ms[:, 0:1], bms)
                    for half in range(2):
                        sl = slice(half * 512, (half + 1) * 512)
                        S.op('dve', lambda e, half=half, sl=sl, mi=mi: e.scalar_tensor_tensor(out=tt[:, sl], in0=k.ps[4 + half][:], scalar=ms[:, 0:1], in1=k.gtg[:, mi, sl], op0=ALU.mult, op1=ALU.mult), reads=[k.bps[4 + half], bms, k.b_gtg], writes=[btt])
                    S.op('dve', lambda e, x=x: e.tensor_tensor(out=x[:], in0=x[:], in1=tt[:], op=ALU.add), reads=[bxx, btt], writes=[bxx])
                    S.dma('pool', dstd, x[:], reads=[bxx], writes=tb)
        S.barrier()


_BF = ml_dtypes.bfloat16
_CONST = {}


def _constants():
    if _CONST:
        return _CONST
    c = {}
    c["ident"] = np.eye(128, dtype=np.float32).astype(_BF)
    c["eye8"] = (8.0 * np.eye(128, dtype=np.float32)).astype(_BF)
    rm = np.zeros((128, 128), np.float32)
    for dp in range(128):
        dd = dp % 64
        if (dd % 32) < 16:
            rm[dp, dp + 16] = -1.0
        else:
            rm[dp, dp - 16] = 1.0
    c["rsign"] = np.ascontiguousarray(rm.T).astype(_BF)
    t = np.arange(NTOK)
    pos = np.stack([t // 64, t % 64], 0).astype(np.float64)
    inv = 10000.0 ** (-np.arange(16, dtype=np.float64) * 2.0 / 32.0)
    d = np.arange(128) % 64
    ang = pos[d // 32, :] * inv[d % 16][:, None]
    c["cosT"] = np.cos(ang).astype(np.float32)
    c["sinT"] = np.sin(ang).astype(np.float32)
    s_, t_ = np.meshgrid(np.arange(128), np.arange(128), indexing="ij")
    c["trif"] = (s_ <= t_).astype(np.int32)
    c["trib"] = (s_ >= t_).astype(np.int32)
    c["bones"] = ((s_ // 64) == (t_ // 64)).astype(np.float32).astype(_BF)
    n = np.arange(NTOK, dtype=np.int64)
    m = (n[:, None] * n[None, :]) % NTOK
    sc = 1.0 / np.sqrt(NTOK * 128.0)
    angm = (2.0 * np.pi / NTOK) * m.astype(np.float32)
    cs = (np.cos(angm) * sc).astype(np.float32)
    sn = (np.sin(angm) * sc).astype(np.float32)
    dft = np.empty((16, NTOK, 512), dtype=_BF)
    for kb in range(16):
        dft[kb, :, 0:256] = cs[:, kb * 256:(kb + 1) * 256].astype(_BF)
        dft[kb, :, 256:512] = sn[:, kb * 256:(kb + 1) * 256].astype(_BF)
    c["dft"] = dft
    n2 = np.arange(LCTX, dtype=np.int64)
    a2 = (2.0 * np.pi / LCTX) * ((n2[:, None] * n2[None, :]) % LCTX)
    sc2 = 1.0 / np.sqrt(LCTX * 128.0)
    c["dft256"] = np.concatenate([np.cos(a2) * sc2, np.sin(a2) * sc2], 1).astype(np.float32).astype(_BF)
    n3 = np.arange(128, dtype=np.int64)
    a3 = (2.0 * np.pi / 128) * ((n3[:, None] * n3[None, :]) % 128)
    c["cc128"] = np.cos(a3).astype(np.float32).astype(_BF)
    c["ssn128"] = (-np.sin(a3)).astype(np.float32).astype(_BF)
    _CONST.update(c)
    return _CONST


def _zb_tables(rpb):
    L = rpb.shape[0]
    e = np.arange(2)[:, None, None, None]
    kc = np.arange(64)[None, :, None, None]
    w = np.arange(ZW)[None, None, :, None]
    qc = np.arange(64)[None, None, None, :]
    dr = 14 - w + e + 0 * kc + 0 * qc
    cs = np.clip(qc - 8, 0, 48)
    col_ok = (kc >= cs) & (kc < cs + 16)
    cidx = np.clip(kc - qc + 15, 0, 30) + 0 * dr
    out = np.empty((L, 2, 128, 8, ZW * 64), dtype=_BF)
    for kind in range(2):
        row_ok = (dr >= -7) & (dr <= 7)
        if kind == 1:
            row_ok = row_ok & (dr >= -4) & (dr < 4)
        ok = (row_ok & col_ok)
        ridx = np.clip(dr + 7, 0, 14)
        for l in range(L):
            g = rpb[l][:, ridx, cidx]
            g = np.where(ok[None], g, np.float32(NEG))
            out[l, kind] = g.transpose(1, 2, 0, 3, 4).reshape(128, 8, ZW * 64).astype(_BF)
    return out


def make_in_maps(inputs, nlayers=DEPTH, cores=range(8)):
    f = lambda a: np.ascontiguousarray(np.asarray(a, dtype=np.float32))
    c = _constants()
    L = nlayers
    shared = dict(c)
    shared["w_ada"] = f(inputs["w_ada"][:L])
    shared["b_ada"] = f(inputs["b_ada"][:L])
    shared["b_adaT"] = f(np.asarray(inputs["b_ada"][:L]).reshape(L, 24, 128).transpose(0, 2, 1))
    shared["g_preT"] = f(np.asarray(inputs["g_pre"][:L]).reshape(L, 8, 128).transpose(0, 2, 1))
    shared["g_post"] = f(inputs["g_post"][:L])
    shared["w_in"] = f(inputs["w_in"][:L])
    shared["zb"] = _zb_tables(np.asarray(inputs["na_rpb"][:L], dtype=np.float32))
    shared["fnet_w"] = f(np.asarray(inputs["fnet_w"][:L]).reshape(L, 512, 128))
    shared["gmlp_g"] = f(inputs["gmlp_norm_g"][:L])
    shared["gmlp_wsT"] = f(np.asarray(inputs["gmlp_ws"][:L]).transpose(0, 1, 3, 2).reshape(L, 1024, 128))
    shared["gmlp_bsT"] = f(np.asarray(inputs["gmlp_bs"][:L]).transpose(0, 2, 1))
    lg = np.asarray(inputs["hgrn_lb_logits"], dtype=np.float32)
    shared["lbT"] = f(lg.reshape(DEPTH, 2, 4, 128).transpose(3, 1, 2, 0).reshape(128, 8, DEPTH))
    shared["hgrn_gT"] = f(np.asarray(inputs["hgrn_norm_g"][:L]).reshape(L, 4, 128).transpose(0, 2, 1))
    shared["w_branch"] = f(np.asarray(inputs["w_branch"][:L]).reshape(L, 2048, D))
    shared["w_out"] = f(inputs["w_out"][:L])
    x = np.asarray(inputs["x"]); ctx = np.asarray(inputs["ctx"]); cc = np.asarray(inputs["c"]); c_ctx = np.asarray(inputs["c_ctx"])
    maps = []
    for ci in cores:
        m = dict(shared)
        m["x"] = f(x[2 * ci:2 * ci + 2])
        m["ctx"] = f(ctx[2 * ci:2 * ci + 2])
        c3 = np.stack([cc[2 * ci], cc[2 * ci + 1], c_ctx], 0)
        m["cT"] = f(c3.reshape(3, 8, 128).transpose(2, 1, 0))
        maps.append(m)
    return maps


_NC_CACHE = {}


def kernel(**inputs):
    if "nc" not in _NC_CACHE:
        _NC_CACHE["nc"] = build()
    nc = _NC_CACHE["nc"]
    maps = make_in_maps(inputs)
    res = run_bass_kernel_spmd(nc, maps, core_ids=list(range(8)))
    return np.concatenate([np.asarray(r["out"], dtype=np.float32) for r in res.results], axis=0)
```

```python
import contextlib
import numpy as np
import ml_dtypes
import concourse.bass as bass
import concourse.mybir as mybir
from concourse.bass_utils import run_bass_kernel_spmd

F32 = mybir.dt.float32
BF16 = mybir.dt.bfloat16
I32 = mybir.dt.int32
AF = mybir.ActivationFunctionType
ALU = mybir.AluOpType

D = 1024
NTOK = 4096
LCTX = 256
NT = NTOK + LCTX
NCOL = 11264
DEPTH = 4
EPS = 1e-6
COL = dict(a_q=0, a_k=512, a_v=1024, a_g=1536, b_x=2048, b_g=2560, c_u=3072, c_v=3584, c_g=4096,
           d_q=4608, d_ff=5120, d_fb=5632, d_i=6144, d_g=6656, gate=7168)
ZW = 26
NEG = -30000.0

SEM_WINDOW = 16000
DMA_RING = 8


class Buf:
    __slots__ = ("name", "lw", "rd", "excl")

    def __init__(self, name="", excl=False):
        self.name = name
        self.lw = None
        self.rd = {}
        self.excl = excl


class DTrack:
    def __init__(self, ncols, unit=128):
        self.unit = unit
        self.b = [Buf() for _ in range((ncols + unit - 1) // unit)]

    def r(self, c0, c1):
        return self.b[c0 // self.unit:(c1 + self.unit - 1) // self.unit]


class _Rec:
    def __init__(self):
        self.call = None

    def __getattr__(self, name):
        def f(*a, **kw):
            self.call = (name, a, kw)
            return self
        return f


def _freeze(fn):
    r = _Rec()
    fn(r)
    name, a, kw = r.call
    return lambda e: getattr(e, name)(*a, **kw)


class Sched:
    ENGS = ("pe", "act", "dve", "pool", "sp")

    def __init__(self, nc):
        self.nc = nc
        self.ops = {e: [] for e in self.ENGS}
        self.cnt = {e: 0 for e in self.ENGS}
        self.dcnt = {e: 0 for e in self.ENGS}
        self.known = {e: {} for e in self.ENGS}
        self.pending = {e: False for e in self.ENGS}

    def _tokwaits(self, eng, toks):
        waits = {}
        for t in toks:
            if t[0] == 'e':
                if t[1] == eng and eng == 'pe':
                    continue
                key = ('e', t[1], (t[2] - 1) // SEM_WINDOW)
                val = (t[2] - 1) % SEM_WINDOW + 1
            else:
                key = ('d', t[1], t[2] % DMA_RING)
                val = 16 * (t[2] // DMA_RING + 1)
            if waits.get(key, 0) < val:
                waits[key] = val
        out = []
        kn = self.known[eng]
        for key, val in waits.items():
            if kn.get(key, 0) >= val:
                continue
            kn[key] = val
            out.append((key, val))
        return out

    def _deps(self, eng, reads, writes):
        toks = []
        for b in reads:
            if b.lw is not None:
                toks.append(b.lw)
            if b.excl:
                toks.extend(v for kk, v in b.rd.items() if kk != eng)
        for b in writes:
            if b.lw is not None and not (b.lw[0] == 'e' and b.lw[1] == eng):
                toks.append(b.lw)
            toks.extend(v for v in b.rd.values() if not (v[0] == 'e' and v[1] == eng))
        return self._tokwaits(eng, toks)

    def op(self, eng, fn, reads=(), writes=(), inc=True):
        fn = _freeze(fn)
        waits = self._deps(eng, reads, writes)
        idx = self.cnt[eng] + 1
        tok = ('e', eng, idx)
        if inc:
            self.cnt[eng] = idx
            self.ops[eng].append((waits, fn, ('e', eng, (idx - 1) // SEM_WINDOW), 1))
            self.pending[eng] = False
        else:
            self.ops[eng].append((waits, fn, None, 0))
            self.pending[eng] = True
        for b in reads:
            b.rd[eng] = tok
        for b in writes:
            b.lw = tok
            b.rd = {}
        return tok

    def dma(self, q, out, in_, reads=(), writes=()):
        waits = self._deps(q, reads, writes)
        i = self.dcnt[q]
        self.dcnt[q] += 1
        if i >= DMA_RING:
            key = ('d', q, i % DMA_RING)
            val = 16 * (i // DMA_RING)
            kn = self.known[q]
            if kn.get(key, 0) < val:
                kn[key] = val
                waits.append((key, val))
        tok = ('d', q, i)
        fn = lambda e, out=out, in_=in_: e.dma_start(out=out, in_=in_)
        self.ops[q].append((waits, fn, ('d', q, i % DMA_RING), 16))
        qk = 'q' + q
        for b in reads:
            b.rd[qk] = tok
        for b in writes:
            b.lw = tok
            b.rd = {}
        return tok

    def barrier(self):
        toks = []
        for e in self.ENGS:
            assert not self.pending[e]
            if self.cnt[e] > 0:
                toks.append(('e', e, self.cnt[e]))
            n = self.dcnt[e]
            for i in range(max(0, n - DMA_RING), n):
                toks.append(('d', e, i))
        for e in self.ENGS:
            w = self._tokwaits(e, toks)
            if w:
                self.ops[e].append((w, None, None, 0))

    def emit(self):
        nc = self.nc
        self.barrier()
        sems = {}
        with contextlib.ExitStack() as st:
            def getsem(key):
                if key not in sems:
                    sems[key] = st.enter_context(nc.semaphore("s_%s_%s_%d" % key))
                return sems[key]
            for e in self.ENGS:
                for (waits, fn, inc, amt) in self.ops[e]:
                    for key, val in waits:
                        getsem(key)
                    if inc is not None:
                        getsem(inc)
            block = st.enter_context(nc.Block())
            handles = {"pe": block.tensor, "act": block.scalar, "dve": block.vector,
                       "pool": block.gpsimd, "sp": block.sync}
            for e in self.ENGS:
                ops = self.ops[e]
                if not ops:
                    continue

                def body(engine, ops=ops):
                    for (waits, fn, inc, amt) in ops:
                        for key, val in waits:
                            engine.wait_ge(sems[key], val)
                        if fn is not None:
                            ins = fn(engine)
                            if inc is not None:
                                ins.then_inc(sems[inc], amt)
                handles[e](body)


_UC = [0]
_DBG = {}


def _u(n):
    _UC[0] += 1
    return "%s_%d" % (n, _UC[0])


class K:
    pass


def _dram(nc, name, shape, dt, kind=None):
    if kind is None:
        return nc.dram_tensor(name, list(shape), dt).ap()
    return nc.dram_tensor(name, list(shape), dt, kind=kind).ap()


def build(nlayers=DEPTH, phases="PABCDM", dump=()):
    nc = bass.Bass("TRN2", target_bir_lowering=False)
    k = K()
    k.nc = nc
    S = Sched(nc)
    k.S = S
    IN = "ExternalInput"
    I = {}
    def inp(name, shape, dt=F32):
        I[name] = _dram(nc, name, shape, dt, IN)
        return I[name]
    inp("x", [2, NTOK, D]); inp("ctx", [2, LCTX, D]); inp("cT", [128, 8, 3])
    inp("w_ada", [nlayers, D, 3 * D]); inp("b_ada", [nlayers, 3 * D]); inp("b_adaT", [nlayers, 128, 24])
    inp("g_preT", [nlayers, 128, 8]); inp("g_post", [nlayers, D])
    inp("w_in", [nlayers, D, NCOL])
    inp("zb", [nlayers, 2, 128, 8, ZW * 64], BF16)
    inp("fnet_w", [nlayers, 512, 128]); inp("gmlp_g", [nlayers, 512]); inp("gmlp_wsT", [nlayers, 1024, 128])
    inp("gmlp_bsT", [nlayers, 128, 8]); inp("lbT", [128, 8, DEPTH]); inp("hgrn_gT", [nlayers, 128, 4])
    inp("w_branch", [nlayers, 2048, D]); inp("w_out", [nlayers, D, D])
    inp("ident", [128, 128], BF16); inp("rsign", [128, 128], BF16); inp("eye8", [128, 128], BF16)
    inp("cosT", [128, NTOK]); inp("sinT", [128, NTOK])
    inp("trif", [128, 128], I32); inp("trib", [128, 128], I32); inp("bones", [128, 128], BF16)
    inp("dft", [16, NTOK, 512], BF16); inp("dft256", [LCTX, 512], BF16)
    inp("cc128", [128, 128], BF16); inp("ssn128", [128, 128], BF16)
    OUT = _dram(nc, "out", [2, NTOK, D], F32, "ExternalOutput")
    def scr(name, shape, dt):
        return _dram(nc, name, shape, dt, "ExternalOutput" if name in dump else None)
    k.w_in_bf = scr("w_in_bf", [2, D, NCOL], BF16)
    k.w_br_bf = scr("w_br_bf", [2, 2048, D], BF16)
    k.w_out_bf = scr("w_out_bf", [2, D, D], BF16)
    k.fw_bf = scr("fw_bf", [2, 512, 128], BF16)
    k.ws_bf = scr("ws_bf", [2, 1024, 128], BF16)
    k.hT = scr("hT", [2, D, NT], BF16)
    k.yT = scr("yT", [2, 4, 512, NT], BF16)
    k.ofT = scr("ofT", [2, 512, NT], F32)
    k.ctxcur = scr("ctxcur", [2, LCTX, D], F32)
    k.I = I
    k.OUT = OUT
    k.t_w = [Buf() for _ in range(2)]
    k.t_hT = [DTrack(NT) for _ in range(2)]
    k.t_yT = [[DTrack(NT) for _ in range(4)] for _ in range(2)]
    k.t_ofT = [DTrack(NT) for _ in range(2)]
    k.t_x = [DTrack(NTOK) for _ in range(2)]
    k.t_ctx = [DTrack(LCTX) for _ in range(2)]

    with contextlib.ExitStack() as gst:
        k.gst = gst
        def gsb(name, shape, dt):
            return gst.enter_context(nc.sbuf_tensor(name, list(shape), dt))
        k.ps = [gst.enter_context(nc.psum_tensor("ps%d" % i, [128, 512], F32)) for i in range(8)]
        k.bps = [Buf("ps%d" % i, excl=True) for i in range(8)]
        k.ident = gsb("ident_sb", [128, 128], BF16)
        k.ones_bf = gsb("ones_bf", [128, 128], BF16)
        k.ones_f = gsb("ones_f", [128, 512], F32)
        k.scT = gsb("scT", [128, 8, 3], F32)
        k.modT = gsb("modT", [128, 24, 3], F32)
        k.gsT = gsb("gsT", [128, 8, 3], F32)
        k.gtg = gsb("gtg", [128, 3, D], F32)
        k.lb = gsb("lb", [128, 8, DEPTH], F32)
        k.oml = gsb("oml", [128, 8, DEPTH], F32)
        k.b_const = Buf(); k.b_scT = Buf(); k.b_mod = Buf(); k.b_gtg = Buf(); k.b_lb = Buf()
        S.dma('sp', k.ident[:], I["ident"], writes=[k.b_const])
        S.op('dve', lambda e: e.memset(k.ones_bf[:], 1.0), writes=[k.b_const])
        S.op('dve', lambda e: e.memset(k.ones_f[:], 1.0), writes=[k.b_const])
        prep_global(k)
        for l in range(nlayers):
            S.barrier()
            convert_weights(k, l)
            adaln(k, l)
            with_ctx = l < DEPTH - 1
            if "P" in phases:
                for s in range(2):
                    phase_pre(k, l, s)
            if "C" in phases:
                phase_c(k, l, with_ctx)
            if "B" in phases:
                phase_b(k, l, with_ctx)
            if "A" in phases:
                phase_a(k, l, with_ctx)
            if "D" in phases:
                phase_d(k, l, with_ctx)
            if "M" in phases:
                phase_m(k, l, with_ctx)
        S.emit()
    return nc


def prep_global(k):
    S, nc, I = k.S, k.nc, k.I
    with contextlib.ExitStack() as st:
        sb = lambda n, s, d: st.enter_context(nc.sbuf_tensor(_u(n), list(s), d))
        cT = sb("pg_cT", [128, 8, 3], F32)
        lg = sb("pg_lg", [128, 8, DEPTH], F32)
        ex = sb("pg_ex", [128, 8, DEPTH], F32)
        sm = sb("pg_sm", [128, 8], F32)
        b = Buf()
        S.dma('sp', cT[:], I["cT"], writes=[b])
        S.op('act', lambda e: e.activation(out=k.scT[:], in_=cT[:], func=AF.Silu), reads=[b], writes=[k.b_scT])
        b2 = Buf()
        S.dma('sp', lg[:], I["lbT"], writes=[b2])
        S.op('act', lambda e: e.activation(out=ex[:], in_=lg[:], func=AF.Exp), reads=[b2], writes=[b2])
        S.op('dve', lambda e: e.tensor_reduce(out=sm[:], in_=ex[:], axis=mybir.AxisListType.X, op=ALU.add), reads=[b2], writes=[b2])
        S.op('dve', lambda e: e.reciprocal(sm[:], sm[:]), reads=[b2], writes=[b2])
        S.op('dve', lambda e: e.tensor_tensor(out=ex[:], in0=ex[:], in1=sm[:].unsqueeze(2).broadcast_to([128, 8, DEPTH]), op=ALU.mult), reads=[b2], writes=[b2])
        S.op('dve', lambda e: e.memset(k.lb[:, :, 0:1], 0.0), reads=[b2], writes=[k.b_lb])
        for l in range(1, DEPTH):
            S.op('dve', lambda e, l=l: e.tensor_tensor(out=k.lb[:, :, l:l + 1], in0=k.lb[:, :, l - 1:l], in1=ex[:, :, l:l + 1], op=ALU.add), reads=[b2, k.b_lb], writes=[k.b_lb])
        S.op('dve', lambda e: e.tensor_scalar(out=k.lb[:], in0=k.lb[:], scalar1=0.0, scalar2=None, op0=ALU.max), reads=[k.b_lb], writes=[k.b_lb])
        S.op('dve', lambda e: e.tensor_scalar(out=k.oml[:], in0=k.lb[:], scalar1=-1.0, scalar2=1.0, op0=ALU.mult, op1=ALU.add), reads=[k.b_lb], writes=[k.b_lb])
        S.barrier()


def convert_weights(k, l):
    S, I = k.S, k.I
    p = l % 2
    w = [k.t_w[p]]
    for c0 in range(0, NCOL, 1408):
        S.dma('pool', k.w_in_bf[p, :, c0:c0 + 1408], I["w_in"][l, :, c0:c0 + 1408], writes=w)
    S.dma('pool', k.w_br_bf[p], I["w_branch"][l], writes=w)
    S.dma('pool', k.w_out_bf[p], I["w_out"][l], writes=w)
    S.dma('pool', k.fw_bf[p], I["fnet_w"][l], writes=w)
    S.dma('pool', k.ws_bf[p], I["gmlp_wsT"][l], writes=w)


def load_w(k, l, dst, c0, ncol, bdst):
    p = l % 2
    src = k.w_in_bf[p, :, c0:c0 + ncol].rearrange("(kc p) c -> p kc c", p=128)
    k.S.dma('sp', dst, src, reads=[k.t_w[p]], writes=[bdst])


def adaln(k, l):
    S, nc, I = k.S, k.nc, k.I
    with contextlib.ExitStack() as st:
        sb = lambda n, s, d: st.enter_context(nc.sbuf_tensor(_u(n), list(s), d))
        wa = [sb("ad_wa%d" % i, [128, 8, 512], F32) for i in range(2)]
        bwa = [Buf() for _ in range(2)]
        scbc = sb("ad_scbc", [128, 8, 3, 128], F32)
        bT = sb("ad_bT", [128, 24], F32)
        gpT = sb("ad_gpT", [128, 8], F32)
        brow = sb("ad_brow", [128, D], F32)
        grow = sb("ad_grow", [128, D], F32)
        tmp = sb("ad_tmp", [128, 8, 3], F32)
        b = Buf(); bsc = Buf()
        S.dma('sp', bT[:], I["b_adaT"][l], writes=[b])
        S.dma('sp', gpT[:], I["g_preT"][l], writes=[b])
        S.dma('sp', brow[:], I["b_ada"][l:l + 1, 2 * D:3 * D].partition_broadcast(128), writes=[b])
        S.dma('sp', grow[:], I["g_post"][l:l + 1, :].partition_broadcast(128), writes=[b])
        S.op('dve', lambda e: e.tensor_copy(scbc[:], k.scT[:].unsqueeze(3).broadcast_to([128, 8, 3, 128])), reads=[k.b_scT], writes=[bsc])
        pm = k.ps[0]
        for g in range(6):
            w = wa[g % 2]
            S.dma('sp', w[:], I["w_ada"][l, :, g * 512:(g + 1) * 512].rearrange("(kc p) c -> p kc c", p=128), writes=[bwa[g % 2]])
            for j in range(4):
                ch = 4 * g + j
                for kc in range(8):
                    S.op('pe', lambda e, w=w, kc=kc, j=j, ch=ch: e.matmul(pm[:, ch * 3:ch * 3 + 3], lhsT=w[:, kc, j * 128:(j + 1) * 128], rhs=k.scT[:, kc, :], start=(kc == 0), stop=(kc == 7)),
                         reads=[bwa[g % 2], k.b_scT], writes=[k.bps[0]], inc=(kc == 7))
            if g >= 4:
                half = g - 4
                for s in range(3):
                    pg = k.ps[1 + s]
                    for kc in range(8):
                        S.op('pe', lambda e, w=w, kc=kc, s=s, pg=pg: e.matmul(pg[:], lhsT=scbc[:, kc, s, :], rhs=w[:, kc, :], start=(kc == 0), stop=(kc == 7)),
                             reads=[bwa[g % 2], bsc], writes=[k.bps[1 + s]], inc=(kc == 7))
                    sl = slice(half * 512, (half + 1) * 512)
                    S.op('dve', lambda e, s=s, pg=pg, sl=sl: e.tensor_tensor(out=k.gtg[:, s, sl], in0=pg[:], in1=brow[:, sl], op=ALU.add), reads=[k.bps[1 + s], b], writes=[k.b_gtg])
                    S.op('dve', lambda e, s=s, sl=sl: e.tensor_tensor(out=k.gtg[:, s, sl], in0=k.gtg[:, s, sl], in1=grow[:, sl], op=ALU.mult), reads=[b, k.b_gtg], writes=[k.b_gtg])
        S.op('dve', lambda e: e.tensor_tensor(out=k.modT[:], in0=pm[:, 0:72].rearrange("p (c s) -> p c s", s=3), in1=bT[:].unsqueeze(2).broadcast_to([128, 24, 3]), op=ALU.add), reads=[k.bps[0], b], writes=[k.b_mod])
        S.op('dve', lambda e: e.tensor_scalar(out=tmp[:], in0=k.modT[:, 8:16, :], scalar1=1.0, scalar2=None, op0=ALU.add), reads=[k.b_mod], writes=[b])
        S.op('dve', lambda e: e.tensor_tensor(out=k.gsT[:], in0=tmp[:], in1=gpT[:].unsqueeze(2).broadcast_to([128, 8, 3]), op=ALU.mult), reads=[b], writes=[k.b_mod])
        S.barrier()


def rstd_from_ms(k, ms, bms):
    S = k.S
    S.op('act', lambda e: e.activation(out=ms, in_=ms, func=AF.Sqrt, bias=EPS, scale=1.0), reads=[bms], writes=[bms])
    S.op('dve', lambda e: e.reciprocal(ms, ms), reads=[bms], writes=[bms])


def phase_pre(k, l, s):
    S, nc, I = k.S, k.nc, k.I
    with contextlib.ExitStack() as st:
        sb = lambda n, sh, d: st.enter_context(nc.sbuf_tensor(_u(n), list(sh), d))
        xt = [sb("pr_x%d" % i, [128, D], F32) for i in range(8)]
        bx = [Buf() for _ in range(8)]
        xs = [sb("pr_xs%d" % i, [128, D], BF16) for i in range(2)]
        bxs = [Buf() for _ in range(2)]
        junk = sb("pr_junk", [128, D], F32)
        ms = [sb("pr_ms%d" % i, [128, 4], F32) for i in range(2)]
        bms = [Buf() for _ in range(2)]
        hb = [sb("pr_hb%d" % i, [128, 8, 512], BF16) for i in range(2)]
        bhb = [Buf() for _ in range(2)]
        bj = Buf()
        pT = [k.ps[i][:].bitcast(BF16) for i in range(4)]
        it = 0
        blocks = [(0, b0 * 512, 4) for b0 in range(8)] + [(1, NTOK, 2)]
        for bi, (isctx, c0, ntile) in enumerate(blocks):
            mi = 2 if isctx else s
            xis = []
            for t in range(ntile):
                xi = it % 8
                xis.append(xi)
                if isctx:
                    src = (I["ctx"] if l == 0 else k.ctxcur)[s, t * 128:(t + 1) * 128, :]
                    rb = k.t_ctx[s].r(t * 128, t * 128 + 128)
                else:
                    src = (I["x"] if l == 0 else k.OUT)[s, c0 + t * 128:c0 + (t + 1) * 128, :]
                    rb = k.t_x[s].r(c0 + t * 128, c0 + t * 128 + 128)
                S.dma('sp', xt[xi][:], src, reads=rb, writes=[bx[xi]])
                S.op('act', lambda e: e.activation(out=junk[:], in_=xt[xi][:], func=AF.Square, scale=1.0 / 32.0, accum_out=ms[bi % 2][:, t:t + 1]), reads=[bx[xi]], writes=[bj, bms[bi % 2]])
                it += 1
            rstd_from_ms(k, ms[bi % 2][:, :ntile], bms[bi % 2])
            for t in range(ntile):
                xi = xis[t]
                si = (bi * 4 + t) % 2
                if t % 2 == 0:
                    S.op('act', lambda e: e.activation(out=xs[si][:], in_=xt[xi][:], func=AF.Copy, scale=ms[bi % 2][:, t:t + 1]), reads=[bx[xi], bms[bi % 2]], writes=[bxs[si]])
                else:
                    S.op('dve', lambda e: e.tensor_scalar(out=xs[si][:], in0=xt[xi][:], scalar1=ms[bi % 2][:, t:t + 1], scalar2=None, op0=ALU.mult), reads=[bx[xi], bms[bi % 2]], writes=[bxs[si]])
                for kc in range(8):
                    dst = pT[kc // 2][:, (kc % 2) * 512 + t * 128:(kc % 2) * 512 + (t + 1) * 128]
                    S.op('pe', lambda e: e.transpose(dst, xs[si][:, kc * 128:(kc + 1) * 128], k.ident[:]), reads=[bxs[si], k.b_const], writes=[k.bps[kc // 2]], inc=(kc == 7))
            hi = bi % 2
            nt = ntile * 128
            for kc in range(8):
                src = pT[kc // 2][:, (kc % 2) * 512:(kc % 2) * 512 + nt]
                if kc % 2 == 0:
                    S.op('act', lambda e, src=src, kc=kc, hi=hi, nt=nt, mi=mi: e.activation(out=hb[hi][:, kc, :nt], in_=src, func=AF.Identity, bias=k.modT[:, kc, mi:mi + 1], scale=k.gsT[:, kc, mi:mi + 1]), reads=[k.bps[kc // 2], k.b_mod], writes=[bhb[hi]])
                else:
                    S.op('dve', lambda e, src=src, kc=kc, hi=hi, nt=nt, mi=mi: e.tensor_scalar(out=hb[hi][:, kc, :nt], in0=src, scalar1=k.gsT[:, kc, mi:mi + 1], scalar2=k.modT[:, kc, mi:mi + 1], op0=ALU.mult, op1=ALU.add), reads=[k.bps[kc // 2], k.b_mod], writes=[bhb[hi]])
            S.dma('sp', k.hT[s, :, c0:c0 + nt].rearrange("(kc p) t -> p kc t", p=128), hb[hi][:, :, :nt], reads=[bhb[hi]], writes=k.t_hT[s].r(c0, c0 + nt))
        S.barrier()


def load_h(k, s, dst, c0, n, bdst):
    k.S.dma('sp', dst[:, :, :n], k.hT[s, :, c0:c0 + n].rearrange("(kc p) t -> p kc t", p=128), reads=k.t_hT[s].r(c0, c0 + n), writes=[bdst])


def inproj_fm(k, pbank, n, w, bw, cw, h, bh, hc0=0):
    for kc in range(8):
        k.S.op('pe', lambda e, kc=kc: e.matmul(k.ps[pbank][:, :n], lhsT=w[:, kc, cw:cw + 128], rhs=h[:, kc, hc0:hc0 + n], start=(kc == 0), stop=(kc == 7)),
               reads=[bw, bh], writes=[k.bps[pbank]], inc=(kc == 7))


def inproj_tm(k, pbank, w, bw, cw, h, bh, t0):
    for kc in range(8):
        k.S.op('pe', lambda e, kc=kc: e.matmul(k.ps[pbank][:], lhsT=h[:, kc, t0:t0 + 128], rhs=w[:, kc, cw:cw + 512], start=(kc == 0), stop=(kc == 7)),
               reads=[bw, bh], writes=[k.bps[pbank]], inc=(kc == 7))


def seq_blocks(with_ctx_block=True):
    bl = [(b0 * 512, 4) for b0 in range(8)]
    if with_ctx_block:
        bl.append((NTOK, 2))
    return bl


def phase_c(k, l, with_ctx):
    S, nc, I = k.S, k.nc, k.I
    p = l % 2
    with contextlib.ExitStack() as st:
        sb = lambda n, sh, d: st.enter_context(nc.sbuf_tensor(_u(n), list(sh), d))
        w = sb("c_w", [128, 8, 1536], BF16); bw = Buf()
        wsT = sb("c_ws", [128, 8, 128], BF16)
        gn = sb("c_gn", [128, 512], F32)
        bsT = sb("c_bs", [128, 8], F32)
        bc = Buf()
        hb = [sb("c_hb%d" % i, [128, 8, 512], BF16) for i in range(2)]; bhb = [Buf() for _ in range(2)]
        junk = sb("c_junk", [128, 512], F32); bj = Buf()
        ms = sb("c_ms", [128, 1], F32); bms = Buf()
        vn = sb("c_vn", [128, 512], BF16); bvn = Buf()
        sg = sb("c_sg", [128, 512], F32); bsg = Buf()
        t1 = sb("c_t1", [128, 512], F32); bt1 = Buf()
        yt = sb("c_y", [128, 512], BF16); byt = Buf()
        yb = [sb("c_yb%d" % i, [128, 4, 512], BF16) for i in range(2)]; byb = [Buf() for _ in range(2)]
        load_w(k, l, w[:], COL["c_u"], 1536, bw)
        S.dma('sp', wsT[:], k.ws_bf[p].rearrange("(g s) t -> s g t", s=128), reads=[k.t_w[p]], writes=[bc])
        S.dma('sp', gn[:], I["gmlp_g"][l:l + 1, :].partition_broadcast(128), writes=[bc])
        S.dma('sp', bsT[:], I["gmlp_bsT"][l], writes=[bc])
        pT = [k.ps[4][:].bitcast(BF16), k.ps[5][:].bitcast(BF16)]
        bi = 0
        for s in range(2):
            blocks = seq_blocks(with_ctx)
            load_h(k, s, hb[bi % 2], blocks[0][0], blocks[0][1] * 128, bhb[bi % 2])
            for ib, (c0, ntile) in enumerate(blocks):
                h = hb[bi % 2]; bh = bhb[bi % 2]
                if ib + 1 < len(blocks):
                    load_h(k, s, hb[(bi + 1) % 2], blocks[ib + 1][0], blocks[ib + 1][1] * 128, bhb[(bi + 1) % 2])
                for t in range(ntile):
                    t0 = t * 128
                    inproj_tm(k, 0, w, bw, 512, h, bh, t0)
                    S.op('act', lambda e: e.activation(out=junk[:], in_=k.ps[0][:], func=AF.Square, scale=float(512 ** -0.5), accum_out=ms[:]), reads=[k.bps[0]], writes=[bj, bms])
                    rstd_from_ms(k, ms[:], bms)
                    S.op('dve', lambda e: e.scalar_tensor_tensor(out=vn[:], in0=k.ps[0][:], scalar=ms[:, 0:1], in1=gn[:], op0=ALU.mult, op1=ALU.mult), reads=[k.bps[0], bms, bc], writes=[bvn])
                    inproj_tm(k, 2, w, bw, 0, h, bh, t0)
                    inproj_tm(k, 3, w, bw, 1024, h, bh, t0)
                    for g in range(8):
                        S.op('pe', lambda e, g=g: e.matmul(k.ps[1][:, g * 64:(g + 1) * 64], lhsT=wsT[:, g, :], rhs=vn[:, g * 64:(g + 1) * 64], start=True, stop=True), reads=[bc, bvn], writes=[k.bps[1]], inc=(g == 7))
                    S.op('act', lambda e: e.activation(out=sg[:], in_=k.ps[3][:], func=AF.Silu), reads=[k.bps[3]], writes=[bsg])
                    S.op('dve', lambda e: e.tensor_tensor(out=t1[:].rearrange("p (g c) -> p g c", g=8), in0=k.ps[1][:].rearrange("p (g c) -> p g c", g=8), in1=bsT[:].unsqueeze(2).broadcast_to([128, 8, 64]), op=ALU.add), reads=[k.bps[1], bc], writes=[bt1])
                    S.op('dve', lambda e: e.tensor_tensor(out=t1[:], in0=k.ps[2][:], in1=t1[:], op=ALU.mult), reads=[k.bps[2], bt1], writes=[bt1])
                    S.op('dve', lambda e: e.tensor_tensor(out=yt[:], in0=t1[:], in1=sg[:], op=ALU.mult), reads=[bt1, bsg], writes=[byt])
                    for j in range(4):
                        dst = pT[j // 2][:, (j % 2) * 512 + t0:(j % 2) * 512 + t0 + 128]
                        S.op('pe', lambda e, dst=dst, j=j: e.transpose(dst, yt[:, j * 128:(j + 1) * 128], k.ident[:]), reads=[byt, k.b_const], writes=[k.bps[4 + j // 2]], inc=(j == 3))
                nt = ntile * 128
                yo = yb[bi % 2]
                for j in range(4):
                    src = pT[j // 2][:, (j % 2) * 512:(j % 2) * 512 + nt]
                    S.op('act', lambda e, src=src, j=j, yo=yo, nt=nt: e.copy(yo[:, j, :nt], src), reads=[k.bps[4 + j // 2]], writes=[byb[bi % 2]])
                S.dma('pool', k.yT[s, 2, :, c0:c0 + nt].rearrange("(j p) t -> p j t", p=128), yo[:, :, :nt], reads=[byb[bi % 2]], writes=k.t_yT[s][2].r(c0, c0 + nt))
                bi += 1
        S.barrier()


def phase_b(k, l, with_ctx):
    S, nc, I = k.S, k.nc, k.I
    p = l % 2
    with contextlib.ExitStack() as st:
        sb = lambda n, sh, d: st.enter_context(nc.sbuf_tensor(_u(n), list(sh), d))
        w = sb("b_w", [128, 8, 1024], BF16); bw = Buf()
        fw = sb("b_fw", [128, 4, 128], BF16)
        cc = sb("b_cc", [128, 128], BF16); ssn = sb("b_ss", [128, 128], BF16)
        m12 = sb("b_m12", [128, 2, 4, 128], BF16)
        bc = Buf(); bm = Buf()
        z = sb("b_z", [128, 32, 512], BF16); bz = Buf()
        hb = [sb("b_hb%d" % i, [128, 8, 512], BF16) for i in range(2)]; bhb = [Buf() for _ in range(2)]
        dft = [sb("b_dft%d" % i, [128, 32, 512], BF16) for i in range(2)]; bdft = [Buf() for _ in range(2)]
        Y = sb("b_Y", [128, 4, 512], BF16); bY = Buf()
        sg = sb("b_sg", [128, 4, 256], F32); bsg = Buf()
        yb = [sb("b_yb%d" % i, [128, 4, 256], BF16) for i in range(2)]; byb = [Buf() for _ in range(2)]
        load_w(k, l, w[:], COL["b_x"], 1024, bw)
        S.dma('sp', fw[:], k.fw_bf[p].rearrange("(g c) d -> c g d", c=128), reads=[k.t_w[p]], writes=[bc])
        S.dma('sp', cc[:], I["cc128"], writes=[bc])
        S.dma('sp', ssn[:], I["ssn128"], writes=[bc])
        for i, mat in enumerate((cc, ssn)):
            S.op('pe', lambda e, mat=mat, i=i: e.matmul(k.ps[i][:], lhsT=mat[:], rhs=fw[:].rearrange("c g d -> c (g d)"), start=True, stop=True), reads=[bc], writes=[k.bps[i]])
            S.op('act', lambda e, i=i: e.copy(m12[:, i, :, :].rearrange("c g d -> c (g d)"), k.ps[i][:]), reads=[k.bps[i]], writes=[bm])
        di = 0
        for s in range(2):
            for (tc0, ntile, nkb, isctx) in ([(0, 32, 16, False)] + ([(NTOK, 2, 1, True)] if with_ctx else [])):
                nblk = (ntile + 3) // 4
                for ib in range(nblk):
                    nt = min(4, ntile - ib * 4)
                    load_h(k, s, hb[ib % 2], tc0 + ib * 512, nt * 128, bhb[ib % 2])
                    for t in range(nt):
                        inproj_tm(k, 0, w, bw, 0, hb[ib % 2], bhb[ib % 2], t * 128)
                        S.op('act', lambda e, ib=ib, t=t: e.copy(z[:, ib * 4 + t, :], k.ps[0][:]), reads=[k.bps[0]], writes=[bz])
                for kb in range(nkb):
                    dt_ = dft[di % 2]; bd = bdft[di % 2]
                    if isctx:
                        S.dma('sp', dt_[:, :2, :], I["dft256"].rearrange("(t p) c -> p t c", p=128), writes=[bd])
                    else:
                        for hh in range(2):
                            S.dma('sp', dt_[:, hh * 16:(hh + 1) * 16, :], I["dft"][kb, hh * 2048:(hh + 1) * 2048, :].rearrange("(t p) c -> p t c", p=128), writes=[bd])
                    kc0 = tc0 + kb * 256
                    hi = (kb + 1) % 2
                    load_h(k, s, hb[hi], kc0, 256, bhb[hi])
                    for g in range(4):
                        for t in range(ntile):
                            S.op('pe', lambda e, g=g, t=t, dt_=dt_: e.matmul(k.ps[g][:], lhsT=z[:, t, g * 128:(g + 1) * 128], rhs=dt_[:, t, :], start=(t == 0), stop=(t == ntile - 1)),
                                 reads=[bz, bd], writes=[k.bps[g]], inc=(t == ntile - 1))
                        if g % 2 == 0:
                            S.op('act', lambda e, g=g: e.copy(Y[:, g, :], k.ps[g][:]), reads=[k.bps[g]], writes=[bY])
                        else:
                            S.op('dve', lambda e, g=g: e.tensor_copy(Y[:, g, :], k.ps[g][:]), reads=[k.bps[g]], writes=[bY])
                    for g in range(4):
                        pb_ = 4 + g // 2
                        dst = k.ps[pb_][:, (g % 2) * 256:(g % 2) * 256 + 256]
                        S.op('pe', lambda e, g=g, dst=dst: e.matmul(dst, lhsT=m12[:, 0, g, :], rhs=Y[:, g, 0:256], start=True, stop=False), reads=[bm, bY], writes=[k.bps[pb_]], inc=False)
                        S.op('pe', lambda e, g=g, dst=dst: e.matmul(dst, lhsT=m12[:, 1, g, :], rhs=Y[:, g, 256:512], start=False, stop=True), reads=[bm, bY], writes=[k.bps[pb_]])
                    for g in range(4):
                        pb_ = 6 + g // 2
                        dst = k.ps[pb_][:, (g % 2) * 256:(g % 2) * 256 + 256]
                        for kc in range(8):
                            S.op('pe', lambda e, g=g, dst=dst, kc=kc, hi=hi: e.matmul(dst, lhsT=w[:, kc, 512 + g * 128:512 + (g + 1) * 128], rhs=hb[hi][:, kc, 0:256], start=(kc == 0), stop=(kc == 7)),
                                 reads=[bw, bhb[hi]], writes=[k.bps[pb_]], inc=(kc == 7))
                    yo = yb[di % 2]
                    for gg in range(2):
                        S.op('act', lambda e, gg=gg: e.activation(out=sg[:, 2 * gg:2 * gg + 2, :].rearrange("p a t -> p (a t)"), in_=k.ps[6 + gg][:], func=AF.Silu), reads=[k.bps[6 + gg]], writes=[bsg])
                        S.op('dve', lambda e, gg=gg, yo=yo: e.tensor_tensor(out=yo[:, 2 * gg:2 * gg + 2, :].rearrange("p a t -> p (a t)"), in0=k.ps[4 + gg][:], in1=sg[:, 2 * gg:2 * gg + 2, :].rearrange("p a t -> p (a t)"), op=ALU.mult), reads=[k.bps[4 + gg], bsg], writes=[byb[di % 2]])
                    S.dma('pool', k.yT[s, 1, :, kc0:kc0 + 256].rearrange("(j p) t -> p j t", p=128), yo[:], reads=[byb[di % 2]], writes=k.t_yT[s][1].r(kc0, kc0 + 256))
                    di += 1
        S.barrier()


def na_qblocks(with_ctx):
    bl = [(0, 256, 0, list(range(0, 4)), 0, True), (60 * 64, 256, 0, list(range(28, 32)), 60, True),
          (4 * 64, 256, 1, list(range(0, 6)), 4, True), (56 * 64, 256, 1, list(range(26, 32)), 56, True)]
    for gq in range(1, 7):
        bl.append((gq * 512, 512, 1, list(range(4 * gq - 2, 4 * gq + 6)), 8 * gq, True))
    if with_ctx:
        bl.append((NTOK, 256, 1, [], 0, False))
    return bl


def phase_a(k, l, with_ctx):
    S, nc, I = k.S, k.nc, k.I
    with contextlib.ExitStack() as st:
        sb = lambda n, sh, d: st.enter_context(nc.sbuf_tensor(_u(n), list(sh), d))
        w = sb("a_w", [128, 8, 1024], BF16); bw = Buf()
        kT = sb("a_kT", [128, 4, NT], BF16); bkT = Buf()
        vt = sb("a_v", [128, 34, 512], BF16); bv = Buf()
        csb = [sb("a_cs%d" % i, [128, 2, 512], F32) for i in range(2)]; bcs = [Buf() for _ in range(2)]
        csi = [0]
        rs = sb("a_rs", [128, 128], BF16); eye8 = sb("a_e8", [128, 128], BF16)
        bc = Buf()
        zb = sb("a_zb", [128, 8, ZW * 64], BF16); bzb = Buf()
        hb = [sb("a_hb%d" % i, [128, 8, 512], BF16) for i in range(2)]; bhb = [Buf() for _ in range(2)]
        qp = sb("a_qp", [128, 4, 512], BF16); bqp = Buf()
        qp2 = sb("a_qp2", [128, 2, 4, 512], BF16); bqp2 = Buf()
        qr2 = sb("a_qr2", [128, 2, 4, 512], BF16); bqr = Buf()
        gs = sb("a_gs", [128, 4, 512], BF16); bgs = Buf()
        t1 = sb("a_t1", [128, 512], F32); bt1 = Buf()
        t2 = sb("a_t2", [128, 512], F32); bt2 = Buf()
        kp = sb("a_kp", [128, 512], BF16); bkp = Buf()
        es = [sb("a_es%d" % i, [128, 512], BF16) for i in range(4)]; bes = [Buf() for _ in range(4)]
        rd = sb("a_rd", [128, 512], F32); brd = Buf()
        yo = [sb("a_yo%d" % i, [128, 4, 512], BF16) for i in range(2)]; byo = [Buf() for _ in range(2)]
        S.dma('sp', rs[:], I["rsign"], writes=[bc]); S.dma('sp', eye8[:], I["eye8"], writes=[bc])
        S.op('pool', lambda e: e.memset(qp2[:], 0.0), writes=[bqp2])
        S.op('pool', lambda e: e.memset(qr2[:], 0.0), writes=[bqr])

        def load_cs(c0, n):
            csi[0] += 1
            i = csi[0] % 2
            S.dma('sp', csb[i][:, 0, :n], I["cosT"][:, c0:c0 + n], writes=[bcs[i]])
            S.dma('sp', csb[i][:, 1, :n], I["sinT"][:, c0:c0 + n], writes=[bcs[i]])

        def rope(psrc, plain_bf, bplain, dst, c0, n, rbank):
            i = csi[0] % 2
            if _DBG.get('nors'):
                rbank = psrc
            else:
                S.op('pe', lambda e: e.matmul(k.ps[rbank][:, :n], lhsT=rs[:], rhs=plain_bf, start=True, stop=True), reads=[bc, bplain], writes=[k.bps[rbank]])
            S.op('dve', lambda e: e.tensor_tensor(out=t1[:, :n], in0=k.ps[psrc][:, :n], in1=csb[i][:, 0, :n], op=ALU.mult), reads=[k.bps[psrc], bcs[i], bplain], writes=[bt1])
            S.op('dve', lambda e: e.tensor_tensor(out=t2[:, :n], in0=k.ps[rbank][:, :n], in1=csb[i][:, 1, :n], op=ALU.mult), reads=[k.bps[rbank], bcs[i]], writes=[bt2])

        hi = 0
        for s in range(2):
            load_w(k, l, w[:], COL["a_k"], 1024, bw)
            for (c0, ntile) in seq_blocks(True):
                n = ntile * 128
                h = hb[hi % 2]; bh = bhb[hi % 2]; hi += 1
                load_h(k, s, h, c0, n, bh)
                if c0 < NTOK:
                    load_cs(c0, n)
                for cp in range(4):
                    inproj_fm(k, 0, n, w, bw, cp * 128, h, bh)
                    if c0 >= NTOK or _DBG.get('norope'):
                        S.op('act', lambda e, cp=cp: e.copy(kT[:, cp, c0:c0 + n], k.ps[0][:, :n]), reads=[k.bps[0]], writes=[bkT])
                    else:
                        S.op('act', lambda e: e.copy(kp[:, :n], k.ps[0][:, :n]), reads=[k.bps[0]], writes=[bkp])
                        rope(0, kp[:, :n], bkp, None, c0, n, 1)
                        if _DBG.get('nopool'):
                            S.op('dve', lambda e, cp=cp: e.tensor_tensor(out=kT[:, cp, c0:c0 + n], in0=t1[:, :n], in1=t2[:, :n], op=ALU.add), reads=[bt1, bt2], writes=[bkT])
                        else:
                            S.op('dve', lambda e, cp=cp: e.tensor_tensor(out=kT[:, cp, c0:c0 + n], in0=t1[:, :n], in1=t2[:, :n], op=ALU.add), reads=[bt1, bt2], writes=[bkT])
                for t in range(ntile):
                    inproj_tm(k, 2, w, bw, 512, h, bh, t * 128)
                    S.op('act', lambda e, t=t: e.copy(vt[:, c0 // 128 + t, :], k.ps[2][:]), reads=[k.bps[2]], writes=[bv])
            if _DBG.get('a1only'):
                continue
            load_w(k, l, w[:, :, 0:512], COL["a_q"], 512, bw)
            load_w(k, l, w[:, :, 512:1024], COL["a_g"], 512, bw)
            cur_kind = None
            for qi, (t0, nq, kind, chunks, q0, use_rope) in enumerate(na_qblocks(with_ctx)):
                if 'qsel' in _DBG and qi not in _DBG['qsel']:
                    continue
                if use_rope and kind != cur_kind:
                    S.dma('sp', zb[:], I["zb"][l, kind], writes=[bzb])
                    cur_kind = kind
                h = hb[hi % 2]; bh = bhb[hi % 2]; hi += 1
                load_h(k, s, h, t0, nq, bh)
                if use_rope:
                    load_cs(t0, nq)
                for cp in range(4):
                    inproj_fm(k, 0, nq, w, bw, cp * 128, h, bh)
                    S.op('act', lambda e, cp=cp: e.copy(qp[:, cp, :nq], k.ps[0][:, :nq]), reads=[k.bps[0]], writes=[bqp])
                    for j in range(2):
                        S.op('act', lambda e, cp=cp, j=j: e.copy(qp2[64 * j:64 * j + 64, j, cp, :nq], k.ps[0][64 * j:64 * j + 64, :nq]), reads=[k.bps[0]], writes=[bqp2])
                    if use_rope:
                        rope(0, qp[:, cp, :nq], bqp, None, t0, nq, 1)
                        for j in range(2):
                            S.op('dve', lambda e, cp=cp, j=j: e.tensor_tensor(out=qr2[64 * j:64 * j + 64, j, cp, :nq], in0=t1[64 * j:64 * j + 64, :nq], in1=t2[64 * j:64 * j + 64, :nq], op=ALU.add), reads=[bt1, bt2], writes=[bqr])
                    inproj_fm(k, 2, nq, w, bw, 512 + cp * 128, h, bh)
                    S.op('act', lambda e, cp=cp: e.activation(out=gs[:, cp, :nq], in_=k.ps[2][:, :nq], func=AF.Silu), reads=[k.bps[2]], writes=[bgs])
                y = yo[qi % 2]
                for cp in range(4):
                    ob, db = (6, 7) if cp % 2 == 0 else (0, 1)
                    klist = [(c, True) for c in chunks] + [(32, False), (33, False)]
                    items = []
                    for j in range(2):
                        for ic, (c, band) in enumerate(klist):
                            items.append((j, c, band, ic == 0, ic == len(klist) - 1))

                    def stage1(idx, cp=cp):
                        j, c, band, first, last = items[idx]
                        pb = 64 * j; hd = 2 * cp + j
                        sbk = 3 + (idx % 3)
                        e_t = es[idx % 4]; be = bes[idx % 4]
                        if band:
                            woff = 14 - (2 * c - q0)
                            S.op('pe', lambda e: e.matmul(k.ps[sbk][:, :nq], lhsT=kT[:, cp, c * 128:(c + 1) * 128], rhs=qr2[:, j, cp, :nq], start=True, stop=False),
                                 reads=[bkT, bqr], writes=[k.bps[sbk]], inc=False)
                            S.op('pe', lambda e: e.matmul(k.ps[sbk][:, :nq], lhsT=eye8[:], rhs=zb[:, hd, woff * 64:woff * 64 + nq], start=False, stop=True),
                                 reads=[bc, bzb], writes=[k.bps[sbk]])
                        else:
                            S.op('pe', lambda e: e.matmul(k.ps[sbk][:, :nq], lhsT=kT[:, cp, c * 128:(c + 1) * 128], rhs=qp2[:, j, cp, :nq], start=True, stop=True),
                                 reads=[bkT, bqp2], writes=[k.bps[sbk]])
                        S.op('act', lambda e: e.activation(out=e_t[:, :nq], in_=k.ps[sbk][:, :nq], func=AF.Exp, scale=0.125), reads=[k.bps[sbk]], writes=[be])

                    def stage2(idx, cp=cp, ob=ob, db=db):
                        j, c, band, first, last = items[idx]
                        pb = 64 * j; hd = 2 * cp + j
                        e_t = es[idx % 4]; be = bes[idx % 4]
                        S.op('pe', lambda e: e.matmul(k.ps[ob][pb:pb + 64, :nq], lhsT=vt[:, c, hd * 64:(hd + 1) * 64], rhs=e_t[:, :nq], start=first, stop=last),
                             reads=[bv, be], writes=[k.bps[ob]], inc=False)
                        S.op('pe', lambda e: e.matmul(k.ps[db][pb:pb + 64, :nq], lhsT=k.ones_bf[:, 0:64], rhs=e_t[:, :nq], start=first, stop=last),
                             reads=[k.b_const, be], writes=[k.bps[db]])

                    LOOK = 2
                    for idx in range(len(items) + LOOK):
                        if idx < len(items):
                            stage1(idx)
                        if idx >= LOOK:
                            stage2(idx - LOOK)
                    S.op('act', lambda e: e.activation(out=rd[:, :nq], in_=k.ps[db][:, :nq], func=AF.Ln), reads=[k.bps[db]], writes=[brd])
                    S.op('act', lambda e: e.activation(out=rd[:, :nq], in_=rd[:, :nq], func=AF.Exp, scale=-1.0), reads=[brd], writes=[brd])
                    S.op('dve', lambda e: e.tensor_tensor(out=rd[:, :nq], in0=k.ps[ob][:, :nq], in1=rd[:, :nq], op=ALU.mult), reads=[k.bps[ob], brd], writes=[brd])
                    S.op('dve', lambda e, cp=cp, y=y: e.tensor_tensor(out=y[:, cp, :nq], in0=rd[:, :nq], in1=gs[:, cp, :nq], op=ALU.mult), reads=[brd, bgs], writes=[byo[qi % 2]])
                S.dma('pool', k.yT[s, 0, :, t0:t0 + nq].rearrange("(j p) t -> p j t", p=128), y[:, :, :nq], reads=[byo[qi % 2]], writes=k.t_yT[s][0].r(t0, t0 + nq))
        S.barrier()


def phase_d(k, l, with_ctx):
    S, nc, I = k.S, k.nc, k.I
    with contextlib.ExitStack() as st:
        sb = lambda n, sh, d: st.enter_context(nc.sbuf_tensor(_u(n), list(sh), d))
        w = sb("d_w", [128, 8, 2048], BF16); bw = Buf()
        gn = sb("d_gn", [128, 4], F32)
        trif = sb("d_trif", [128, 128], I32); trib = sb("d_trib", [128, 128], I32); bones = sb("d_bones", [128, 128], BF16)
        bc = Buf()
        hb = [sb("d_hb%d" % i, [128, 8, 512], BF16) for i in range(2)]; bhb = [Buf() for _ in range(2)]
        Pp = sb("d_P", [128, 4, 516], F32); bP = Buf()
        kT = sb("d_kT", [128, 4, 512], BF16); bkT = Buf()
        qT = sb("d_qT", [128, 4, 512], BF16); bqT = Buf()
        itm2 = [sb("d_itm%d" % i, [128, 512], BF16) for i in range(2)]; bitm2 = [Buf() for _ in range(2)]
        negr2 = [sb("d_negr%d" % i, [128, 4, 8], F32) for i in range(2)]; bnr2 = [Buf() for _ in range(2)]
        pcnt = [0]
        dqa = [sb("d_dq%d" % i, [128, 128], F32) for i in range(8)]; bdqa = [Buf() for _ in range(8)]
        eqa = [sb("d_eq%d" % i, [128, 128], F32) for i in range(8)]; beqa = [Buf() for _ in range(8)]
        sga = [sb("d_sg%d" % i, [128, 512], F32) for i in range(4)]; bsga = [Buf() for _ in range(4)]
        lfa = [sb("d_lf%d" % i, [128, 512], F32) for i in range(4)]; blfa = [Buf() for _ in range(4)]
        ek = [sb("d_ek%d" % i, [128, 4, 128], F32) for i in range(4)]; bek = [Buf() for _ in range(4)]
        qtl2 = [sb("d_qtl%d" % i, [128, 2, 4, 128], BF16) for i in range(2)]; bqtl2 = [Buf() for _ in range(2)]
        ktl2 = [sb("d_ktl%d" % i, [128, 4, 4, 128], BF16) for i in range(2)]; bktl2 = [Buf() for _ in range(2)]
        qh2 = [sb("d_qh%d" % i, [128, 4, 128], BF16) for i in range(2)]; bqh2 = [Buf() for _ in range(2)]
        khT = sb("d_khT", [128, 4, 128], BF16); bkhT = Buf()
        khtm2 = [sb("d_khtm%d" % i, [128, 512], BF16) for i in range(2)]; bkhtm2 = [Buf() for _ in range(2)]
        dec2 = [sb("d_dec%d" % i, [128, 4], F32) for i in range(2)]; bdec2 = [Buf() for _ in range(2)]
        At = sb("d_At", [128, 8, 128], BF16); bAt = Buf()
        St = sb("d_S", [128, 2, 4, 64], F32); bS = [Buf(), Buf()]
        Sbf = sb("d_Sbf", [128, 4, 128], BF16); bSbf = Buf()
        ob = [sb("d_ob%d" % i, [128, 4, 512], F32) for i in range(2)]; bob = [Buf() for _ in range(2)]
        sq = sb("d_sq", [128, 512], BF16); bsq = Buf()
        rst = sb("d_rst", [128, 512], F32); brst = Buf()
        gl = sb("d_gl", [128, 512], F32); bgl = Buf()
        yb = [sb("d_yb%d" % i, [128, 4, 512], BF16) for i in range(2)]; byb = [Buf() for _ in range(2)]
        S.dma('sp', gn[:], I["hgrn_gT"][l], writes=[bc])
        S.dma('sp', trif[:], I["trif"], writes=[bc]); S.dma('sp', trib[:], I["trib"], writes=[bc]); S.dma('sp', bones[:], I["bones"], writes=[bc])
        S.op('dve', lambda e: e.memset(Pp[:], 0.0), writes=[bP])
        for i in range(2):
            S.op('pool', lambda e, i=i: e.memset(qtl2[i][:], 0.0), writes=[bqtl2[i]])
        S.op('pool', lambda e: e.memset(Sbf[:], 0.0), writes=[bSbf])
        ptr = k.ps[5][:].bitcast(BF16)
        hi = 0
        for s in range(2):
            for d in range(2):
                S.op('dve', lambda e, d=d: e.memset(St[:, d, :, :], 0.0), writes=[bS[d]])
            for (base, nblk_tiles) in ((NTOK, [2]), (0, [4] * 8)):
                isctx = base >= NTOK
                for d in range(2):
                    _DBG['dpass'] = _DBG.get('dpass', 0) + 1
                    if 'dmax' in _DBG and _DBG['dpass'] > _DBG['dmax']:
                        continue
                    sgn = 1.0 if d == 0 else -1.0
                    final = (d == 1)
                    load_w(k, l, w[:, :, 0:512], COL["d_q"], 512, bw)
                    load_w(k, l, w[:, :, 512:1024], COL["d_ff"] if d == 0 else COL["d_fb"], 512, bw)
                    load_w(k, l, w[:, :, 1024:1536], COL["d_i"], 512, bw)
                    if final:
                        load_w(k, l, w[:, :, 1536:2048], COL["d_g"], 512, bw)
                    for i in range(4):
                        S.op('pool', lambda e, i=i: e.memset(ek[i][:], 0.0), writes=[bek[i]])
                    S.op('pool', lambda e: e.memset(At[:], 0.0), writes=[bAt])
                    S.op('act', lambda e, d=d: e.copy(Sbf[0:64, :, 0:64], St[0:64, d, :, :]), reads=[bS[d]], writes=[bSbf]); S.op('act', lambda e, d=d: e.copy(Sbf[64:128, :, 64:128], St[64:128, d, :, :]), reads=[bS[d]], writes=[bSbf])
                    mask = trif if d == 0 else trib
                    blist = list(range(len(nblk_tiles)))
                    if d == 1:
                        blist = blist[::-1]
                    for ib in blist:
                        ntile = nblk_tiles[ib]
                        n = ntile * 128
                        c0 = base + ib * 512
                        h = hb[hi % 2]; bh = bhb[hi % 2]
                        o_b = ob[hi % 2]; bo = bob[hi % 2]; y_b = yb[hi % 2]; by_ = byb[hi % 2]
                        hi += 1
                        load_h(k, s, h, c0, n, bh)
                        if final:
                            S.dma('sp', o_b[:, :, :n], k.ofT[s, :, c0:c0 + n].rearrange("(j p) t -> p j t", p=128), reads=k.t_ofT[s].r(c0, c0 + n), writes=[bo])
                        for ft in range(4):
                            zb_ = ft % 2
                            inproj_fm(k, zb_, n, w, bw, 512 + ft * 128, h, bh)
                            S.op('act', lambda e: e.activation(out=sga[ft][:, :n], in_=k.ps[zb_][:, :n], func=AF.Sigmoid), reads=[k.bps[zb_]], writes=[bsga[ft]])
                            S.op('dve', lambda e: e.tensor_scalar(out=sga[ft][:, :n], in0=sga[ft][:, :n], scalar1=k.oml[:, d * 4 + ft, l:l + 1], scalar2=k.lb[:, d * 4 + ft, l:l + 1], op0=ALU.mult, op1=ALU.add), reads=[bsga[ft], k.b_lb], writes=[bsga[ft]])
                            S.op('dve', lambda e: e.tensor_scalar(out=sga[ft][:, :n], in0=sga[ft][:, :n], scalar1=1e-30, scalar2=None, op0=ALU.max), reads=[bsga[ft]], writes=[bsga[ft]])
                        for ft in range(4):
                            S.op('act', lambda e: e.activation(out=lfa[ft][:, :n], in_=sga[ft][:, :n], func=AF.Ln), reads=[bsga[ft]], writes=[blfa[ft]])
                            S.op('dve', lambda e: e.tensor_scalar(out=kT[:, ft, :n], in0=sga[ft][:, :n], scalar1=-1.0, scalar2=1.0, op0=ALU.mult, op1=ALU.add), reads=[bsga[ft]], writes=[bkT])
                            S.op('dve', lambda e: e.tensor_tensor_scan(out=Pp[:, ft, 1:1 + n], data0=k.ones_f[:, :n], data1=lfa[ft][:, :n], initial=0.0, op0=ALU.mult, op1=ALU.add), reads=[blfa[ft], k.b_const], writes=[bP])
                        for ft in range(4):
                            zb_ = ft % 2
                            inproj_fm(k, zb_, n, w, bw, ft * 128, h, bh)
                            S.op('act', lambda e: e.copy(qT[:, ft, :n], k.ps[zb_][:, :n]), reads=[k.bps[zb_]], writes=[bqT])
                        tl = list(range(ntile))
                        if d == 1:
                            tl = tl[::-1]

                        def stageE(t, pi, part):
                            t0 = t * 128
                            xo = t0 + 1 if d == 0 else t0
                            itm = itm2[pi]; bitm = bitm2[pi]; qtl = qtl2[pi]; bqtl = bqtl2[pi]; ktl = ktl2[pi]; bktl = bktl2[pi]
                            qh = qh2[pi]; bqh = bqh2[pi]; khtm = khtm2[pi]; bkhtm = bkhtm2[pi]; dec = dec2[pi]; bdec = bdec2[pi]
                            negr = negr2[pi]; bnr = bnr2[pi]
                            if part == 1:
                              inproj_tm(k, 2, w, bw, 1024, h, bh, t0)
                              S.op('act', lambda e: e.copy(itm[:], k.ps[2][:]), reads=[k.bps[2]], writes=[bitm])
                            if part == 1:
                              S.op('dve', lambda e: e.tensor_scalar(out=negr[:, :, 0:4], in0=Pp[:, :, t0 + 16:t0 + 113:32], scalar1=-1.0, scalar2=None, op0=ALU.mult), reads=[bP], writes=[bnr])
                              S.op('dve', lambda e: e.tensor_scalar(out=negr[:, :, 4:6], in0=Pp[:, :, t0:t0 + 129:128], scalar1=-1.0, scalar2=None, op0=ALU.mult), reads=[bP], writes=[bnr])
                            def tiles(ft):
                                return (dqa[ft], bdqa[ft], eqa[ft], beqa[ft], dqa[4 + ft], bdqa[4 + ft], eqa[4 + ft], beqa[4 + ft],
                                        Pp[:, ft, xo:xo + 128], Pp[:, ft, t0:t0 + 1], Pp[:, ft, t0 + 128:t0 + 129], negr[:, ft, 4:5], negr[:, ft, 5:6])
                            for ft in (range(4) if part == 1 else []):
                                dq, bdq, eq, beq, dq2, bdq2, eq2, beq2, X, B0p, B1p, B0n, B1n = tiles(ft)
                                S.op('dve', lambda e: e.tensor_tensor(out=dq[:].rearrange("p (i c) -> p i c", i=4), in0=X.rearrange("p (i c) -> p i c", i=4), in1=Pp[:, ft, t0 + 16:t0 + 113:32].unsqueeze(2).broadcast_to([128, 4, 32]), op=ALU.subtract), reads=[bP], writes=[bdq])
                            for ft in (range(4) if part == 1 else []):
                                dq, bdq, eq, beq, dq2, bdq2, eq2, beq2, X, B0p, B1p, B0n, B1n = tiles(ft)
                                S.op('act', lambda e: e.activation(out=eq[:], in_=dq[:], func=AF.Exp, scale=sgn), reads=[bdq], writes=[beq])
                                for i in range(4):
                                    lo, hi_ = (0, 32 * (i + 1)) if d == 0 else (32 * i, 128)
                                    bias = Pp[:, ft, t0 + 16 + 32 * i:t0 + 17 + 32 * i] if d == 0 else negr[:, ft, i:i + 1]
                                    S.op('act', lambda e: e.activation(out=ek[ft][:, i, lo:hi_], in_=Pp[:, ft, xo + lo:xo + hi_], func=AF.Exp, scale=-sgn, bias=bias), reads=[bP, bnr], writes=[bek[ft]])
                                bq_ = B0n if d == 0 else B1p
                                S.op('act', lambda e: e.activation(out=eq2[:], in_=X, func=AF.Exp, scale=sgn, bias=bq_), reads=[bP, bnr], writes=[beq2])
                                bk_ = B1p if d == 0 else B0n
                                S.op('act', lambda e: e.activation(out=dq2[:], in_=X, func=AF.Exp, scale=-sgn, bias=bk_), reads=[bP, bnr], writes=[bdq2])
                                S.op('act', lambda e: e.activation(out=dec[:, ft:ft + 1], in_=B1p, func=AF.Exp, scale=1.0, bias=B0n), reads=[bP, bnr], writes=[bdec])
                            for ft in (range(4) if part == 2 else []):
                                dq, bdq, eq, beq, dq2, bdq2, eq2, beq2, X, B0p, B1p, B0n, B1n = tiles(ft)
                                S.op('dve', lambda e: e.tensor_tensor(out=khT[:, ft, :], in0=dq2[:], in1=kT[:, ft, t0:t0 + 128], op=ALU.mult), reads=[bdq2, bkT], writes=[bkhT])
                                S.op('pe', lambda e: e.transpose(ptr[:, ft * 128:(ft + 1) * 128], khT[:, ft, :], k.ident[:]), reads=[bkhT, k.b_const], writes=[k.bps[5]])
                                S.op('dve', lambda e: e.tensor_tensor(out=qtl[0:64, 0, ft, :], in0=eq[0:64, :], in1=qT[0:64, ft, t0:t0 + 128], op=ALU.mult), reads=[beq, bqT], writes=[bqtl])
                                S.op('dve', lambda e: e.tensor_tensor(out=qtl[64:128, 1, ft, :], in0=eq[64:128, :], in1=qT[64:128, ft, t0:t0 + 128], op=ALU.mult), reads=[beq, bqT], writes=[bqtl])
                                S.op('dve', lambda e: e.tensor_tensor(out=ktl[:, ft, :, :], in0=ek[ft][:], in1=kT[:, ft, t0:t0 + 128].unsqueeze(1).broadcast_to([128, 4, 128]), op=ALU.mult), reads=[bek[ft], bkT], writes=[bktl])
                                S.op('dve', lambda e: e.tensor_tensor(out=qh[:, ft, :], in0=eq2[:], in1=qT[:, ft, t0:t0 + 128], op=ALU.mult), reads=[beq2, bqT], writes=[bqh])
                            if part == 2:
                                S.op('dve', lambda e: e.tensor_copy(khtm[:], ptr[:, 0:512]), reads=[k.bps[5]], writes=[bkhtm])

                        def stageF(t, pi):
                            t0 = t * 128
                            itm = itm2[pi]; bitm = bitm2[pi]; qtl = qtl2[pi]; bqtl = bqtl2[pi]; ktl = ktl2[pi]; bktl = bktl2[pi]
                            qh = qh2[pi]; bqh = bqh2[pi]; khtm = khtm2[pi]; bkhtm = bkhtm2[pi]; dec = dec2[pi]; bdec = bdec2[pi]
                            for hd in range(8):
                                cp = hd // 2
                                sbk = 3 + hd // 4
                                for i in range(4):
                                    dst = k.ps[sbk][:, (hd % 4) * 128 + 32 * i:(hd % 4) * 128 + 32 * i + 32]
                                    S.op('pe', lambda e: e.matmul(dst, lhsT=ktl[:, cp, i, :], rhs=qtl[:, hd % 2, cp, 32 * i:32 * i + 32], start=True, stop=True),
                                         reads=[bktl, bqtl], writes=[k.bps[sbk]], inc=(hd % 4 == 3 and i == 3))
                            for half in range(2):
                                S.op('dve', lambda e: e.copy_predicated(out=At[:, 4 * half:4 * half + 4, :], mask=mask[:].unsqueeze(1).broadcast_to([128, 4, 128]), data=k.ps[3 + half][:].rearrange("p (h t) -> p h t", h=4)), reads=[k.bps[3 + half], bc, bAt], writes=[bAt])
                            for cp in range(4):
                                for j in range(2):
                                    hd = 2 * cp + j
                                    S.op('pe', lambda e: e.matmul(k.ps[6][64 * j:64 * j + 64, cp * 128:(cp + 1) * 128], lhsT=itm[:, hd * 64:(hd + 1) * 64], rhs=At[:, hd, :], start=True, stop=False), reads=[bitm, bAt], writes=[k.bps[6]], inc=False)
                                S.op('pe', lambda e: e.matmul(k.ps[6][:, cp * 128:(cp + 1) * 128], lhsT=Sbf[:, cp, :], rhs=qh[:, cp, :], start=False, stop=True), reads=[bSbf, bqh], writes=[k.bps[6]], inc=(cp == 3))
                            for cp in range(4):
                                S.op('pe', lambda e: e.matmul(k.ps[7][:, cp * 128:(cp + 1) * 128], lhsT=khtm[:, cp * 128:(cp + 1) * 128], rhs=itm[:, cp * 128:(cp + 1) * 128], start=True, stop=True), reads=[bkhtm, bitm], writes=[k.bps[7]], inc=(cp == 3))
                            for cp in range(4):
                                for j in range(2):
                                    pb = 64 * j
                                    S.op('dve', lambda e: e.scalar_tensor_tensor(out=St[pb:pb + 64, d, cp, :], in0=St[pb:pb + 64, d, cp, :], scalar=dec[pb:pb + 64, cp:cp + 1], in1=k.ps[7][pb:pb + 64, cp * 128 + 64 * j:cp * 128 + 64 * j + 64], op0=ALU.mult, op1=ALU.add),
                                         reads=[bS[d], bdec, k.bps[7]], writes=[bS[d]])
                            S.op('act', lambda e: e.copy(Sbf[0:64, :, 0:64], St[0:64, d, :, :]), reads=[bS[d]], writes=[bSbf])
                            S.op('act', lambda e: e.copy(Sbf[64:128, :, 64:128], St[64:128, d, :, :]), reads=[bS[d]], writes=[bSbf])
                            if not final:
                                S.op('act', lambda e: e.copy(o_b[:, :, t0:t0 + 128], k.ps[6][:].rearrange("p (c t) -> p c t", c=4)), reads=[k.bps[6]], writes=[bo])
                            else:
                                S.op('dve', lambda e: e.tensor_tensor(out=o_b[:, :, t0:t0 + 128], in0=k.ps[6][:].rearrange("p (c t) -> p c t", c=4), in1=o_b[:, :, t0:t0 + 128], op=ALU.add), reads=[k.bps[6], bo], writes=[bo])

                        for it_, t in enumerate(tl):
                            if it_ == 0:
                                stageE(t, pcnt[0] % 2, 1)
                                stageE(t, pcnt[0] % 2, 2)
                            if it_ + 1 < len(tl):
                                stageE(tl[it_ + 1], (pcnt[0] + 1) % 2, 1)
                            stageF(t, pcnt[0] % 2)
                            if it_ + 1 < len(tl):
                                stageE(tl[it_ + 1], (pcnt[0] + 1) % 2, 2)
                            pcnt[0] += 1
                        if not final:
                            S.dma('pool', k.ofT[s, :, c0:c0 + n].rearrange("(j p) t -> p j t", p=128), o_b[:, :, :n], reads=[bo], writes=k.t_ofT[s].r(c0, c0 + n))
                        elif (not isctx) or with_ctx:
                            for cp in range(4):
                                S.op('act', lambda e, cp=cp, o_b=o_b: e.activation(out=sq[:, :n], in_=o_b[:, cp, :n], func=AF.Square), reads=[bo], writes=[bsq])
                                S.op('pe', lambda e: e.matmul(k.ps[0][:, :n], lhsT=bones[:], rhs=sq[:, :n], start=True, stop=True), reads=[bc, bsq], writes=[k.bps[0]])
                                S.op('act', lambda e: e.activation(out=rst[:, :n], in_=k.ps[0][:, :n], func=AF.Ln, scale=1.0 / 64.0, bias=EPS), reads=[k.bps[0]], writes=[brst])
                                S.op('act', lambda e: e.activation(out=rst[:, :n], in_=rst[:, :n], func=AF.Exp, scale=-0.5), reads=[brst], writes=[brst])
                                inproj_fm(k, 1, n, w, bw, 1536 + cp * 128, h, bh)
                                S.op('act', lambda e: e.activation(out=gl[:, :n], in_=k.ps[1][:, :n], func=AF.Silu), reads=[k.bps[1]], writes=[bgl])
                                S.op('dve', lambda e, cp=cp, o_b=o_b: e.scalar_tensor_tensor(out=rst[:, :n], in0=o_b[:, cp, :n], scalar=gn[:, cp:cp + 1], in1=rst[:, :n], op0=ALU.mult, op1=ALU.mult), reads=[bo, bc, brst], writes=[brst])
                                S.op('dve', lambda e, cp=cp, y_b=y_b: e.tensor_tensor(out=y_b[:, cp, :n], in0=rst[:, :n], in1=gl[:, :n], op=ALU.mult), reads=[brst, bgl], writes=[by_])
                            S.dma('pool', k.yT[s, 3, :, c0:c0 + n].rearrange("(j p) t -> p j t", p=128), y_b[:, :, :n], reads=[by_], writes=k.t_yT[s][3].r(c0, c0 + n))
        S.barrier()


def phase_m(k, l, with_ctx):
    S, nc, I = k.S, k.nc, k.I
    p = l % 2
    with contextlib.ExitStack() as st:
        sb = lambda n, sh, d: st.enter_context(nc.sbuf_tensor(_u(n), list(sh), d))
        wg = sb("m_wg", [128, 8, 4096], BF16); bw = Buf()
        wbr = sb("m_wbr", [128, 16, D], BF16)
        wo = sb("m_wo", [128, 8, D], BF16)
        bwc = Buf()
        hb = [sb("m_hb%d" % i, [128, 8, 256], BF16) for i in range(2)]; bhb = [Buf() for _ in range(2)]
        yb = [sb("m_yb%d" % i, [128, 16, 256], BF16) for i in range(2)]; byb = [Buf() for _ in range(2)]
        sgt = [sb("m_sg%d" % i, [128, 256], F32) for i in range(2)]; bsg = [Buf() for _ in range(2)]
        tmp = [sb("m_tmp%d" % i, [128, 256], F32) for i in range(2)]; btmp = [Buf() for _ in range(2)]
        macc = sb("m_acc", [128, 256], F32); bacc = Buf()
        mT = sb("m_mT", [128, 8, 256], BF16); bmT = Buf()
        xt = [sb("m_x%d" % i, [128, D], F32) for i in range(2)]; bx = [Buf() for _ in range(2)]
        tt = sb("m_tt", [128, D], F32); btt = Buf()
        junk = sb("m_junk", [128, 512], F32); bj = Buf()
        ms = sb("m_ms", [128, 2], F32); bms = Buf()
        load_w(k, l, wg[:], COL["gate"], 4096, bw)
        S.dma('sp', wbr[:], k.w_br_bf[p].rearrange("(a p) c -> p a c", p=128), reads=[k.t_w[p]], writes=[bwc])
        S.dma('sp', wo[:], k.w_out_bf[p].rearrange("(a p) c -> p a c", p=128), reads=[k.t_w[p]], writes=[bwc])
        bi = 0
        xi = 0
        gi = 0
        for s in range(2):
            blocks = [(c0, False) for c0 in range(0, NTOK, 256)] + ([(NTOK, True)] if with_ctx else [])
            for (c0, isctx) in blocks:
                mi = 2 if isctx else s
                h = hb[bi % 2]; bh = bhb[bi % 2]; y = yb[bi % 2]; by_ = byb[bi % 2]
                bi += 1
                load_h(k, s, h, c0, 256, bh)
                for r in range(4):
                    S.dma('sp', y[:, 4 * r:4 * r + 4, :], k.yT[s, r, :, c0:c0 + 256].rearrange("(j p) t -> p j t", p=128), reads=k.t_yT[s][r].r(c0, c0 + 256), writes=[by_])
                for fc in range(8):
                    for r in range(4):
                        gb = gi % 2; gi += 1
                        for kc in range(8):
                            S.op('pe', lambda e, kc=kc, r=r, fc=fc, gb=gb: e.matmul(k.ps[gb][:, :256], lhsT=wg[:, kc, r * 1024 + fc * 128:r * 1024 + (fc + 1) * 128], rhs=h[:, kc, :], start=(kc == 0), stop=(kc == 7)),
                                 reads=[bw, bh], writes=[k.bps[gb]], inc=(kc == 7))
                        S.op('act', lambda e, gb=gb: e.activation(out=sgt[gb][:], in_=k.ps[gb][:, :256], func=AF.Sigmoid), reads=[k.bps[gb]], writes=[bsg[gb]])
                        for kc in range(4):
                            S.op('pe', lambda e, kc=kc, r=r, fc=fc, gb=gb: e.matmul(k.ps[2 + gb][:, :256], lhsT=wbr[:, 4 * r + kc, fc * 128:(fc + 1) * 128], rhs=y[:, 4 * r + kc, :], start=(kc == 0), stop=(kc == 3)),
                                 reads=[bwc, by_], writes=[k.bps[2 + gb]], inc=(kc == 3))
                        if r == 0:
                            S.op('dve', lambda e, gb=gb: e.tensor_tensor(out=macc[:], in0=k.ps[2 + gb][:, :256], in1=sgt[gb][:], op=ALU.mult), reads=[k.bps[2 + gb], bsg[gb]], writes=[bacc])
                        else:
                            S.op('dve', lambda e, gb=gb: e.tensor_tensor(out=tmp[gb][:], in0=k.ps[2 + gb][:, :256], in1=sgt[gb][:], op=ALU.mult), reads=[k.bps[2 + gb], bsg[gb]], writes=[btmp[gb]])
                            S.op('dve', lambda e, gb=gb: e.tensor_tensor(out=macc[:], in0=macc[:], in1=tmp[gb][:], op=ALU.add), reads=[bacc, btmp[gb]], writes=[bacc])
                    S.op('act', lambda e, fc=fc: e.copy(mT[:, fc, :], macc[:]), reads=[bacc], writes=[bmT])
                for t in range(2):
                    tok0 = c0 + t * 128
                    x = xt[xi % 2]; bxx = bx[xi % 2]; xi += 1
                    if isctx:
                        src = (I["ctx"] if l == 0 else k.ctxcur)[s, t * 128:(t + 1) * 128, :]
                        dstd = k.ctxcur[s, t * 128:(t + 1) * 128, :]
                        tb = k.t_ctx[s].r(t * 128, t * 128 + 128)
                    else:
                        src = (I["x"] if l == 0 else k.OUT)[s, tok0:tok0 + 128, :]
                        dstd = k.OUT[s, tok0:tok0 + 128, :]
                        tb = k.t_x[s].r(tok0, tok0 + 128)
                    S.dma('sp', x[:], src, reads=tb, writes=[bxx])
                    for half in range(2):
                        for kc in range(8):
                            S.op('pe', lambda e, kc=kc, half=half, t=t: e.matmul(k.ps[4 + half][:], lhsT=mT[:, kc, t * 128:(t + 1) * 128], rhs=wo[:, kc, half * 512:(half + 1) * 512], start=(kc == 0), stop=(kc == 7)),
                                 reads=[bmT, bwc], writes=[k.bps[4 + half]], inc=(kc == 7))
                        S.op('act', lambda e, half=half: e.activation(out=junk[:], in_=k.ps[4 + half][:], func=AF.Square, scale=1.0 / 32.0, accum_out=ms[:, half:half + 1]), reads=[k.bps[4 + half]], writes=[bj, bms])
                    S.op('dve', lambda e: e.tensor_tensor(out=ms[:, 0:1], in0=ms[:, 0:1], in1=ms[:, 1:2], op=ALU.add), reads=[bms], writes=[bms])
                    rstd_from_ms(k, ms[:, 0:1], bms)
                    for half in range(2):
                        sl = slice(half * 512, (half + 1) * 512)
                        S.op('dve', lambda e, half=half, sl=sl, mi=mi: e.scalar_tensor_tensor(out=tt[:, sl], in0=k.ps[4 + half][:], scalar=ms[:, 0:1], in1=k.gtg[:, mi, sl], op0=ALU.mult, op1=ALU.mult), reads=[k.bps[4 + half], bms, k.b_gtg], writes=[btt])
                    S.op('dve', lambda e, x=x: e.tensor_tensor(out=x[:], in0=x[:], in1=tt[:], op=ALU.add), reads=[bxx, btt], writes=[bxx])
                    S.dma('pool', dstd, x[:], reads=[bxx], writes=tb)
        S.barrier()


_BF = ml_dtypes.bfloat16
_CONST = {}


def _constants():
    if _CONST:
        return _CONST
    c = {}
    c["ident"] = np.eye(128, dtype=np.float32).astype(_BF)
    c["eye8"] = (8.0 * np.eye(128, dtype=np.float32)).astype(_BF)
    rm = np.zeros((128, 128), np.float32)
    for dp in range(128):
        dd = dp % 64
        if (dd % 32) < 16:
            rm[dp, dp + 16] = -1.0
        else:
            rm[dp, dp - 16] = 1.0
    c["rsign"] = np.ascontiguousarray(rm.T).astype(_BF)
    t = np.arange(NTOK)
    pos = np.stack([t // 64, t % 64], 0).astype(np.float64)
    inv = 10000.0 ** (-np.arange(16, dtype=np.float64) * 2.0 / 32.0)
    d = np.arange(128) % 64
    ang = pos[d // 32, :] * inv[d % 16][:, None]
    c["cosT"] = np.cos(ang).astype(np.float32)
    c["sinT"] = np.sin(ang).astype(np.float32)
    s_, t_ = np.meshgrid(np.arange(128), np.arange(128), indexing="ij")
    c["trif"] = (s_ <= t_).astype(np.int32)
    c["trib"] = (s_ >= t_).astype(np.int32)
    c["bones"] = ((s_ // 64) == (t_ // 64)).astype(np.float32).astype(_BF)
    n = np.arange(NTOK, dtype=np.int64)
    m = (n[:, None] * n[None, :]) % NTOK
    sc = 1.0 / np.sqrt(NTOK * 128.0)
    angm = (2.0 * np.pi / NTOK) * m.astype(np.float32)
    cs = (np.cos(angm) * sc).astype(np.float32)
    sn = (np.sin(angm) * sc).astype(np.float32)
    dft = np.empty((16, NTOK, 512), dtype=_BF)
    for kb in range(16):
        dft[kb, :, 0:256] = cs[:, kb * 256:(kb + 1) * 256].astype(_BF)
        dft[kb, :, 256:512] = sn[:, kb * 256:(kb + 1) * 256].astype(_BF)
    c["dft"] = dft
    n2 = np.arange(LCTX, dtype=np.int64)
    a2 = (2.0 * np.pi / LCTX) * ((n2[:, None] * n2[None, :]) % LCTX)
    sc2 = 1.0 / np.sqrt(LCTX * 128.0)
    c["dft256"] = np.concatenate([np.cos(a2) * sc2, np.sin(a2) * sc2], 1).astype(np.float32).astype(_BF)
    n3 = np.arange(128, dtype=np.int64)
    a3 = (2.0 * np.pi / 128) * ((n3[:, None] * n3[None, :]) % 128)
    c["cc128"] = np.cos(a3).astype(np.float32).astype(_BF)
    c["ssn128"] = (-np.sin(a3)).astype(np.float32).astype(_BF)
    _CONST.update(c)
    return _CONST


def _zb_tables(rpb):
    L = rpb.shape[0]
    e = np.arange(2)[:, None, None, None]
    kc = np.arange(64)[None, :, None, None]
    w = np.arange(ZW)[None, None, :, None]
    qc = np.arange(64)[None, None, None, :]
    dr = 14 - w + e + 0 * kc + 0 * qc
    cs = np.clip(qc - 8, 0, 48)
    col_ok = (kc >= cs) & (kc < cs + 16)
    cidx = np.clip(kc - qc + 15, 0, 30) + 0 * dr
    out = np.empty((L, 2, 128, 8, ZW * 64), dtype=_BF)
    for kind in range(2):
        row_ok = (dr >= -7) & (dr <= 7)
        if kind == 1:
            row_ok = row_ok & (dr >= -4) & (dr < 4)
        ok = (row_ok & col_ok)
        ridx = np.clip(dr + 7, 0, 14)
        for l in range(L):
            g = rpb[l][:, ridx, cidx]
            g = np.where(ok[None], g, np.float32(NEG))
            out[l, kind] = g.transpose(1, 2, 0, 3, 4).reshape(128, 8, ZW * 64).astype(_BF)
    return out


def make_in_maps(inputs, nlayers=DEPTH, cores=range(8)):
    f = lambda a: np.ascontiguousarray(np.asarray(a, dtype=np.float32))
    c = _constants()
    L = nlayers
    shared = dict(c)
    shared["w_ada"] = f(inputs["w_ada"][:L])
    shared["b_ada"] = f(inputs["b_ada"][:L])
    shared["b_adaT"] = f(np.asarray(inputs["b_ada"][:L]).reshape(L, 24, 128).transpose(0, 2, 1))
    shared["g_preT"] = f(np.asarray(inputs["g_pre"][:L]).reshape(L, 8, 128).transpose(0, 2, 1))
    shared["g_post"] = f(inputs["g_post"][:L])
    shared["w_in"] = f(inputs["w_in"][:L])
    shared["zb"] = _zb_tables(np.asarray(inputs["na_rpb"][:L], dtype=np.float32))
    shared["fnet_w"] = f(np.asarray(inputs["fnet_w"][:L]).reshape(L, 512, 128))
    shared["gmlp_g"] = f(inputs["gmlp_norm_g"][:L])
    shared["gmlp_wsT"] = f(np.asarray(inputs["gmlp_ws"][:L]).transpose(0, 1, 3, 2).reshape(L, 1024, 128))
    shared["gmlp_bsT"] = f(np.asarray(inputs["gmlp_bs"][:L]).transpose(0, 2, 1))
    lg = np.asarray(inputs["hgrn_lb_logits"], dtype=np.float32)
    shared["lbT"] = f(lg.reshape(DEPTH, 2, 4, 128).transpose(3, 1, 2, 0).reshape(128, 8, DEPTH))
    shared["hgrn_gT"] = f(np.asarray(inputs["hgrn_norm_g"][:L]).reshape(L, 4, 128).transpose(0, 2, 1))
    shared["w_branch"] = f(np.asarray(inputs["w_branch"][:L]).reshape(L, 2048, D))
    shared["w_out"] = f(inputs["w_out"][:L])
    x = np.asarray(inputs["x"]); ctx = np.asarray(inputs["ctx"]); cc = np.asarray(inputs["c"]); c_ctx = np.asarray(inputs["c_ctx"])
    maps = []
    for ci in cores:
        m = dict(shared)
        m["x"] = f(x[2 * ci:2 * ci + 2])
        m["ctx"] = f(ctx[2 * ci:2 * ci + 2])
        c3 = np.stack([cc[2 * ci], cc[2 * ci + 1], c_ctx], 0)
        m["cT"] = f(c3.reshape(3, 8, 128).transpose(2, 1, 0))
        maps.append(m)
    return maps


_NC_CACHE = {}


def kernel(**inputs):
    if "nc" not in _NC_CACHE:
        _NC_CACHE["nc"] = build()
    nc = _NC_CACHE["nc"]
    maps = make_in_maps(inputs)
    res = run_bass_kernel_spmd(nc, maps, core_ids=list(range(8)))
    return np.concatenate([np.asarray(r["out"], dtype=np.float32) for r in res.results], axis=0)
```

```python
import contextlib
import numpy as np
import ml_dtypes
import concourse.bass as bass
import concourse.mybir as mybir
from concourse.bass_utils import run_bass_kernel_spmd

F32 = mybir.dt.float32
BF16 = mybir.dt.bfloat16
I32 = mybir.dt.int32
AF = mybir.ActivationFunctionType
ALU = mybir.AluOpType

D = 1024
NTOK = 4096
LCTX = 256
NT = NTOK + LCTX
NCOL = 11264
DEPTH = 4
EPS = 1e-6
COL = dict(a_q=0, a_k=512, a_v=1024, a_g=1536, b_x=2048, b_g=2560, c_u=3072, c_v=3584, c_g=4096,
           d_q=4608, d_ff=5120, d_fb=5632, d_i=6144, d_g=6656, gate=7168)
ZW = 26
NEG = -30000.0

SEM_WINDOW = 16000
DMA_RING = 8


class Buf:
    __slots__ = ("name", "lw", "rd", "excl")

    def __init__(self, name="", excl=False):
        self.name = name
        self.lw = None
        self.rd = {}
        self.excl = excl


class DTrack:
    def __init__(self, ncols, unit=128):
        self.unit = unit
        self.b = [Buf() for _ in range((ncols + unit - 1) // unit)]

    def r(self, c0, c1):
        return self.b[c0 // self.unit:(c1 + self.unit - 1) // self.unit]


class _Rec:
    def __init__(self):
        self.call = None

    def __getattr__(self, name):
        def f(*a, **kw):
            self.call = (name, a, kw)
            return self
        return f


def _freeze(fn):
    r = _Rec()
    fn(r)
    name, a, kw = r.call
    return lambda e: getattr(e, name)(*a, **kw)


class Sched:
    ENGS = ("pe", "act", "dve", "pool", "sp")

    def __init__(self, nc):
        self.nc = nc
        self.ops = {e: [] for e in self.ENGS}
        self.cnt = {e: 0 for e in self.ENGS}
        self.dcnt = {e: 0 for e in self.ENGS}
        self.known = {e: {} for e in self.ENGS}
        self.pending = {e: False for e in self.ENGS}

    def _tokwaits(self, eng, toks):
        waits = {}
        for t in toks:
            if t[0] == 'e':
                if t[1] == eng and eng == 'pe':
                    continue
                key = ('e', t[1], (t[2] - 1) // SEM_WINDOW)
                val = (t[2] - 1) % SEM_WINDOW + 1
            else:
                key = ('d', t[1], t[2] % DMA_RING)
                val = 16 * (t[2] // DMA_RING + 1)
            if waits.get(key, 0) < val:
                waits[key] = val
        out = []
        kn = self.known[eng]
        for key, val in waits.items():
            if kn.get(key, 0) >= val:
                continue
            kn[key] = val
            out.append((key, val))
        return out

    def _deps(self, eng, reads, writes):
        toks = []
        for b in reads:
            if b.lw is not None:
                toks.append(b.lw)
            if b.excl:
                toks.extend(v for kk, v in b.rd.items() if kk != eng)
        for b in writes:
            if b.lw is not None and not (b.lw[0] == 'e' and b.lw[1] == eng):
                toks.append(b.lw)
            toks.extend(v for v in b.rd.values() if not (v[0] == 'e' and v[1] == eng))
        return self._tokwaits(eng, toks)

    def op(self, eng, fn, reads=(), writes=(), inc=True):
        fn = _freeze(fn)
        waits = self._deps(eng, reads, writes)
        idx = self.cnt[eng] + 1
        tok = ('e', eng, idx)
        if inc:
            self.cnt[eng] = idx
            self.ops[eng].append((waits, fn, ('e', eng, (idx - 1) // SEM_WINDOW), 1))
            self.pending[eng] = False
        else:
            self.ops[eng].append((waits, fn, None, 0))
            self.pending[eng] = True
        for b in reads:
            b.rd[eng] = tok
        for b in writes:
            b.lw = tok
            b.rd = {}
        return tok

    def dma(self, q, out, in_, reads=(), writes=()):
        waits = self._deps(q, reads, writes)
        i = self.dcnt[q]
        self.dcnt[q] += 1
        if i >= DMA_RING:
            key = ('d', q, i % DMA_RING)
            val = 16 * (i // DMA_RING)
            kn = self.known[q]
            if kn.get(key, 0) < val:
                kn[key] = val
                waits.append((key, val))
        tok = ('d', q, i)
        fn = lambda e, out=out, in_=in_: e.dma_start(out=out, in_=in_)
        self.ops[q].append((waits, fn, ('d', q, i % DMA_RING), 16))
        qk = 'q' + q
        for b in reads:
            b.rd[qk] = tok
        for b in writes:
            b.lw = tok
            b.rd = {}
        return tok

    def barrier(self):
        toks = []
        for e in self.ENGS:
            assert not self.pending[e]
            if self.cnt[e] > 0:
                toks.append(('e', e, self.cnt[e]))
            n = self.dcnt[e]
            for i in range(max(0, n - DMA_RING), n):
                toks.append(('d', e, i))
        for e in self.ENGS:
            w = self._tokwaits(e, toks)
            if w:
                self.ops[e].append((w, None, None, 0))

    def emit(self):
        nc = self.nc
        self.barrier()
        sems = {}
        with contextlib.ExitStack() as st:
            def getsem(key):
                if key not in sems:
                    sems[key] = st.enter_context(nc.semaphore("s_%s_%s_%d" % key))
                return sems[key]
            for e in self.ENGS:
                for (waits, fn, inc, amt) in self.ops[e]:
                    for key, val in waits:
                        getsem(key)
                    if inc is not None:
                        getsem(inc)
            block = st.enter_context(nc.Block())
            handles = {"pe": block.tensor, "act": block.scalar, "dve": block.vector,
                       "pool": block.gpsimd, "sp": block.sync}
            for e in self.ENGS:
                ops = self.ops[e]
                if not ops:
                    continue

                def body(engine, ops=ops):
                    for (waits, fn, inc, amt) in ops:
                        for key, val in waits:
                            engine.wait_ge(sems[key], val)
                        if fn is not None:
                            ins = fn(engine)
                            if inc is not None:
                                ins.then_inc(sems[inc], amt)
                handles[e](body)


_UC = [0]
_DBG = {}


def _u(n):
    _UC[0] += 1
    return "%s_%d" % (n, _UC[0])


class K:
    pass


def _dram(nc, name, shape, dt, kind=None):
    if kind is None:
        return nc.dram_tensor(name, list(shape), dt).ap()
    return nc.dram_tensor(name, list(shape), dt, kind=kind).ap()


def build(nlayers=DEPTH, phases="PABCDM", dump=()):
    nc = bass.Bass("TRN2", target_bir_lowering=False)
    k = K()
    k.nc = nc
    S = Sched(nc)
    k.S = S
    IN = "ExternalInput"
    I = {}
    def inp(name, shape, dt=F32):
        I[name] = _dram(nc, name, shape, dt, IN)
        return I[name]
    inp("x", [2, NTOK, D]); inp("ctx", [2, LCTX, D]); inp("cT", [128, 8, 3])
    inp("w_ada", [nlayers, D, 3 * D]); inp("b_ada", [nlayers, 3 * D]); inp("b_adaT", [nlayers, 128, 24])
    inp("g_preT", [nlayers, 128, 8]); inp("g_post", [nlayers, D])
    inp("w_in", [nlayers, D, NCOL])
    inp("zb", [nlayers, 2, 128, 8, ZW * 64], BF16)
    inp("fnet_w", [nlayers, 512, 128]); inp("gmlp_g", [nlayers, 512]); inp("gmlp_wsT", [nlayers, 1024, 128])
    inp("gmlp_bsT", [nlayers, 128, 8]); inp("lbT", [128, 8, DEPTH]); inp("hgrn_gT", [nlayers, 128, 4])
    inp("w_branch", [nlayers, 2048, D]); inp("w_out", [nlayers, D, D])
    inp("ident", [128, 128], BF16); inp("rsign", [128, 128], BF16); inp("eye8", [128, 128], BF16)
    inp("cosT", [128, NTOK]); inp("sinT", [128, NTOK])
    inp("trif", [128, 128], I32); inp("trib", [128, 128], I32); inp("bones", [128, 128], BF16)
    inp("dft", [16, NTOK, 512], BF16); inp("dft256", [LCTX, 512], BF16)
    inp("cc128", [128, 128], BF16); inp("ssn128", [128, 128], BF16)
    OUT = _dram(nc, "out", [2, NTOK, D], F32, "ExternalOutput")
    def scr(name, shape, dt):
        return _dram(nc, name, shape, dt, "ExternalOutput" if name in dump else None)
    k.w_in_bf = scr("w_in_bf", [2, D, NCOL], BF16)
    k.w_br_bf = scr("w_br_bf", [2, 2048, D], BF16)
    k.w_out_bf = scr("w_out_bf", [2, D, D], BF16)
    k.fw_bf = scr("fw_bf", [2, 512, 128], BF16)
    k.ws_bf = scr("ws_bf", [2, 1024, 128], BF16)
    k.hT = scr("hT", [2, D, NT], BF16)
    k.yT = scr("yT", [2, 4, 512, NT], BF16)
    k.ofT = scr("ofT", [2, 512, NT], F32)
    k.ctxcur = scr("ctxcur", [2, LCTX, D], F32)
    k.I = I
    k.OUT = OUT
    k.t_w = [Buf() for _ in range(2)]
    k.t_hT = [DTrack(NT) for _ in range(2)]
    k.t_yT = [[DTrack(NT) for _ in range(4)] for _ in range(2)]
    k.t_ofT = [DTrack(NT) for _ in range(2)]
    k.t_x = [DTrack(NTOK) for _ in range(2)]
    k.t_ctx = [DTrack(LCTX) for _ in range(2)]

    with contextlib.ExitStack() as gst:
        k.gst = gst
        def gsb(name, shape, dt):
            return gst.enter_context(nc.sbuf_tensor(name, list(shape), dt))
        k.ps = [gst.enter_context(nc.psum_tensor("ps%d" % i, [128, 512], F32)) for i in range(8)]
        k.bps = [Buf("ps%d" % i, excl=True) for i in range(8)]
        k.ident = gsb("ident_sb", [128, 128], BF16)
        k.ones_bf = gsb("ones_bf", [128, 128], BF16)
        k.ones_f = gsb("ones_f", [128, 512], F32)
        k.scT = gsb("scT", [128, 8, 3], F32)
        k.modT = gsb("modT", [128, 24, 3], F32)
        k.gsT = gsb("gsT", [128, 8, 3], F32)
        k.gtg = gsb("gtg", [128, 3, D], F32)
        k.lb = gsb("lb", [128, 8, DEPTH], F32)
        k.oml = gsb("oml", [128, 8, DEPTH], F32)
        k.b_const = Buf(); k.b_scT = Buf(); k.b_mod = Buf(); k.b_gtg = Buf(); k.b_lb = Buf()
        S.dma('sp', k.ident[:], I["ident"], writes=[k.b_const])
        S.op('dve', lambda e: e.memset(k.ones_bf[:], 1.0), writes=[k.b_const])
        S.op('dve', lambda e: e.memset(k.ones_f[:], 1.0), writes=[k.b_const])
        prep_global(k)
        for l in range(nlayers):
            S.barrier()
            convert_weights(k, l)
            adaln(k, l)
            with_ctx = l < DEPTH - 1
            if "P" in phases:
                for s in range(2):
                    phase_pre(k, l, s)
            if "C" in phases:
                phase_c(k, l, with_ctx)
            if "B" in phases:
                phase_b(k, l, with_ctx)
            if "A" in phases:
                phase_a(k, l, with_ctx)
            if "D" in phases:
                phase_d(k, l, with_ctx)
            if "M" in phases:
                phase_m(k, l, with_ctx)
        S.emit()
    return nc


def prep_global(k):
    S, nc, I = k.S, k.nc, k.I
    with contextlib.ExitStack() as st:
        sb = lambda n, s, d: st.enter_context(nc.sbuf_tensor(_u(n), list(s), d))
        cT = sb("pg_cT", [128, 8, 3], F32)
        lg = sb("pg_lg", [128, 8, DEPTH], F32)
        ex = sb("pg_ex", [128, 8, DEPTH], F32)
        sm = sb("pg_sm", [128, 8], F32)
        b = Buf()
        S.dma('sp', cT[:], I["cT"], writes=[b])
        S.op('act', lambda e: e.activation(out=k.scT[:], in_=cT[:], func=AF.Silu), reads=[b], writes=[k.b_scT])
        b2 = Buf()
        S.dma('sp', lg[:], I["lbT"], writes=[b2])
        S.op('act', lambda e: e.activation(out=ex[:], in_=lg[:], func=AF.Exp), reads=[b2], writes=[b2])
        S.op('dve', lambda e: e.tensor_reduce(out=sm[:], in_=ex[:], axis=mybir.AxisListType.X, op=ALU.add), reads=[b2], writes=[b2])
        S.op('dve', lambda e: e.reciprocal(sm[:], sm[:]), reads=[b2], writes=[b2])
        S.op('dve', lambda e: e.tensor_tensor(out=ex[:], in0=ex[:], in1=sm[:].unsqueeze(2).broadcast_to([128, 8, DEPTH]), op=ALU.mult), reads=[b2], writes=[b2])
        S.op('dve', lambda e: e.memset(k.lb[:, :, 0:1], 0.0), reads=[b2], writes=[k.b_lb])
        for l in range(1, DEPTH):
            S.op('dve', lambda e, l=l: e.tensor_tensor(out=k.lb[:, :, l:l + 1], in0=k.lb[:, :, l - 1:l], in1=ex[:, :, l:l + 1], op=ALU.add), reads=[b2, k.b_lb], writes=[k.b_lb])
        S.op('dve', lambda e: e.tensor_scalar(out=k.lb[:], in0=k.lb[:], scalar1=0.0, scalar2=None, op0=ALU.max), reads=[k.b_lb], writes=[k.b_lb])
        S.op('dve', lambda e: e.tensor_scalar(out=k.oml[:], in0=k.lb[:], scalar1=-1.0, scalar2=1.0, op0=ALU.mult, op1=ALU.add), reads=[k.b_lb], writes=[k.b_lb])
        S.barrier()


def convert_weights(k, l):
    S, I = k.S, k.I
    p = l % 2
    w = [k.t_w[p]]
    for c0 in range(0, NCOL, 1408):
        S.dma('pool', k.w_in_bf[p, :, c0:c0 + 1408], I["w_in"][l, :, c0:c0 + 1408], writes=w)
    S.dma('pool', k.w_br_bf[p], I["w_branch"][l], writes=w)
    S.dma('pool', k.w_out_bf[p], I["w_out"][l], writes=w)
    S.dma('pool', k.fw_bf[p], I["fnet_w"][l], writes=w)
    S.dma('pool', k.ws_bf[p], I["gmlp_wsT"][l], writes=w)


def load_w(k, l, dst, c0, ncol, bdst):
    p = l % 2
    src = k.w_in_bf[p, :, c0:c0 + ncol].rearrange("(kc p) c -> p kc c", p=128)
    k.S.dma('sp', dst, src, reads=[k.t_w[p]], writes=[bdst])


def adaln(k, l):
    S, nc, I = k.S, k.nc, k.I
    with contextlib.ExitStack() as st:
        sb = lambda n, s, d: st.enter_context(nc.sbuf_tensor(_u(n), list(s), d))
        wa = [sb("ad_wa%d" % i, [128, 8, 512], F32) for i in range(2)]
        bwa = [Buf() for _ in range(2)]
        scbc = sb("ad_scbc", [128, 8, 3, 128], F32)
        bT = sb("ad_bT", [128, 24], F32)
        gpT = sb("ad_gpT", [128, 8], F32)
        brow = sb("ad_brow", [128, D], F32)
        grow = sb("ad_grow", [128, D], F32)
        tmp = sb("ad_tmp", [128, 8, 3], F32)
        b = Buf(); bsc = Buf()
        S.dma('sp', bT[:], I["b_adaT"][l], writes=[b])
        S.dma('sp', gpT[:], I["g_preT"][l], writes=[b])
        S.dma('sp', brow[:], I["b_ada"][l:l + 1, 2 * D:3 * D].partition_broadcast(128), writes=[b])
        S.dma('sp', grow[:], I["g_post"][l:l + 1, :].partition_broadcast(128), writes=[b])
        S.op('dve', lambda e: e.tensor_copy(scbc[:], k.scT[:].unsqueeze(3).broadcast_to([128, 8, 3, 128])), reads=[k.b_scT], writes=[bsc])
        pm = k.ps[0]
        for g in range(6):
            w = wa[g % 2]
            S.dma('sp', w[:], I["w_ada"][l, :, g * 512:(g + 1) * 512].rearrange("(kc p) c -> p kc c", p=128), writes=[bwa[g % 2]])
            for j in range(4):
                ch = 4 * g + j
                for kc in range(8):
                    S.op('pe', lambda e, w=w, kc=kc, j=j, ch=ch: e.matmul(pm[:, ch * 3:ch * 3 + 3], lhsT=w[:, kc, j * 128:(j + 1) * 128], rhs=k.scT[:, kc, :], start=(kc == 0), stop=(kc == 7)),
                         reads=[bwa[g % 2], k.b_scT], writes=[k.bps[0]], inc=(kc == 7))
            if g >= 4:
                half = g - 4
                for s in range(3):
                    pg = k.ps[1 + s]
                    for kc in range(8):
                        S.op('pe', lambda e, w=w, kc=kc, s=s, pg=pg: e.matmul(pg[:], lhsT=scbc[:, kc, s, :], rhs=w[:, kc, :], start=(kc == 0), stop=(kc == 7)),
                             reads=[bwa[g % 2], bsc], writes=[k.bps[1 + s]], inc=(kc == 7))
                    sl = slice(half * 512, (half + 1) * 512)
                    S.op('dve', lambda e, s=s, pg=pg, sl=sl: e.tensor_tensor(out=k.gtg[:, s, sl], in0=pg[:], in1=brow[:, sl], op=ALU.add), reads=[k.bps[1 + s], b], writes=[k.b_gtg])
                    S.op('dve', lambda e, s=s, sl=sl: e.tensor_tensor(out=k.gtg[:, s, sl], in0=k.gtg[:, s, sl], in1=grow[:, sl], op=ALU.mult), reads=[b, k.b_gtg], writes=[k.b_gtg])
        S.op('dve', lambda e: e.tensor_tensor(out=k.modT[:], in0=pm[:, 0:72].rearrange("p (c s) -> p c s", s=3), in1=bT[:].unsqueeze(2).broadcast_to([128, 24, 3]), op=ALU.add), reads=[k.bps[0], b], writes=[k.b_mod])
        S.op('dve', lambda e: e.tensor_scalar(out=tmp[:], in0=k.modT[:, 8:16, :], scalar1=1.0, scalar2=None, op0=ALU.add), reads=[k.b_mod], writes=[b])
        S.op('dve', lambda e: e.tensor_tensor(out=k.gsT[:], in0=tmp[:], in1=gpT[:].unsqueeze(2).broadcast_to([128, 8, 3]), op=ALU.mult), reads=[b], writes=[k.b_mod])
        S.barrier()


def rstd_from_ms(k, ms, bms):
    S = k.S
    S.op('act', lambda e: e.activation(out=ms, in_=ms, func=AF.Sqrt, bias=EPS, scale=1.0), reads=[bms], writes=[bms])
    S.op('dve', lambda e: e.reciprocal(ms, ms), reads=[bms], writes=[bms])


def phase_pre(k, l, s):
    S, nc, I = k.S, k.nc, k.I
    with contextlib.ExitStack() as st:
        sb = lambda n, sh, d: st.enter_context(nc.sbuf_tensor(_u(n), list(sh), d))
        xt = [sb("pr_x%d" % i, [128, D], F32) for i in range(8)]
        bx = [Buf() for _ in range(8)]
        xs = [sb("pr_xs%d" % i, [128, D], BF16) for i in range(2)]
        bxs = [Buf() for _ in range(2)]
        junk = sb("pr_junk", [128, D], F32)
        ms = [sb("pr_ms%d" % i, [128, 4], F32) for i in range(2)]
        bms = [Buf() for _ in range(2)]
        hb = [sb("pr_hb%d" % i, [128, 8, 512], BF16) for i in range(2)]
        bhb = [Buf() for _ in range(2)]
        bj = Buf()
        pT = [k.ps[i][:].bitcast(BF16) for i in range(4)]
        it = 0
        blocks = [(0, b0 * 512, 4) for b0 in range(8)] + [(1, NTOK, 2)]
        for bi, (isctx, c0, ntile) in enumerate(blocks):
            mi = 2 if isctx else s
            xis = []
            for t in range(ntile):
                xi = it % 8
                xis.append(xi)
                if isctx:
                    src = (I["ctx"] if l == 0 else k.ctxcur)[s, t * 128:(t + 1) * 128, :]
                    rb = k.t_ctx[s].r(t * 128, t * 128 + 128)
                else:
                    src = (I["x"] if l == 0 else k.OUT)[s, c0 + t * 128:c0 + (t + 1) * 128, :]
                    rb = k.t_x[s].r(c0 + t * 128, c0 + t * 128 + 128)
                S.dma('sp', xt[xi][:], src, reads=rb, writes=[bx[xi]])
                S.op('act', lambda e: e.activation(out=junk[:], in_=xt[xi][:], func=AF.Square, scale=1.0 / 32.0, accum_out=ms[bi % 2][:, t:t + 1]), reads=[bx[xi]], writes=[bj, bms[bi % 2]])
                it += 1
            rstd_from_ms(k, ms[bi % 2][:, :ntile], bms[bi % 2])
            for t in range(ntile):
                xi = xis[t]
                si = (bi * 4 + t) % 2
                if t % 2 == 0:
                    S.op('act', lambda e: e.activation(out=xs[si][:], in_=xt[xi][:], func=AF.Copy, scale=ms[bi % 2][:, t:t + 1]), reads=[bx[xi], bms[bi % 2]], writes=[bxs[si]])
                else:
                    S.op('dve', lambda e: e.tensor_scalar(out=xs[si][:], in0=xt[xi][:], scalar1=ms[bi % 2][:, t:t + 1], scalar2=None, op0=ALU.mult), reads=[bx[xi], bms[bi % 2]], writes=[bxs[si]])
                for kc in range(8):
                    dst = pT[kc // 2][:, (kc % 2) * 512 + t * 128:(kc % 2) * 512 + (t + 1) * 128]
                    S.op('pe', lambda e: e.transpose(dst, xs[si][:, kc * 128:(kc + 1) * 128], k.ident[:]), reads=[bxs[si], k.b_const], writes=[k.bps[kc // 2]], inc=(kc == 7))
            hi = bi % 2
            nt = ntile * 128
            for kc in range(8):
                src = pT[kc // 2][:, (kc % 2) * 512:(kc % 2) * 512 + nt]
                if kc % 2 == 0:
                    S.op('act', lambda e, src=src, kc=kc, hi=hi, nt=nt, mi=mi: e.activation(out=hb[hi][:, kc, :nt], in_=src, func=AF.Identity, bias=k.modT[:, kc, mi:mi + 1], scale=k.gsT[:, kc, mi:mi + 1]), reads=[k.bps[kc // 2], k.b_mod], writes=[bhb[hi]])
                else:
                    S.op('dve', lambda e, src=src, kc=kc, hi=hi, nt=nt, mi=mi: e.tensor_scalar(out=hb[hi][:, kc, :nt], in0=src, scalar1=k.gsT[:, kc, mi:mi + 1], scalar2=k.modT[:, kc, mi:mi + 1], op0=ALU.mult, op1=ALU.add), reads=[k.bps[kc // 2], k.b_mod], writes=[bhb[hi]])
            S.dma('sp', k.hT[s, :, c0:c0 + nt].rearrange("(kc p) t -> p kc t", p=128), hb[hi][:, :, :nt], reads=[bhb[hi]], writes=k.t_hT[s].r(c0, c0 + nt))
        S.barrier()


def load_h(k, s, dst, c0, n, bdst):
    k.S.dma('sp', dst[:, :, :n], k.hT[s, :, c0:c0 + n].rearrange("(kc p) t -> p kc t", p=128), reads=k.t_hT[s].r(c0, c0 + n), writes=[bdst])


def inproj_fm(k, pbank, n, w, bw, cw, h, bh, hc0=0):
    for kc in range(8):
        k.S.op('pe', lambda e, kc=kc: e.matmul(k.ps[pbank][:, :n], lhsT=w[:, kc, cw:cw + 128], rhs=h[:, kc, hc0:hc0 + n], start=(kc == 0), stop=(kc == 7)),
               reads=[bw, bh], writes=[k.bps[pbank]], inc=(kc == 7))


def inproj_tm(k, pbank, w, bw, cw, h, bh, t0):
    for kc in range(8):
        k.S.op('pe', lambda e, kc=kc: e.matmul(k.ps[pbank][:], lhsT=h[:, kc, t0:t0 + 128], rhs=w[:, kc, cw:cw + 512], start=(kc == 0), stop=(kc == 7)),
               reads=[bw, bh], writes=[k.bps[pbank]], inc=(kc == 7))


def seq_blocks(with_ctx_block=True):
    bl = [(b0 * 512, 4) for b0 in range(8)]
    if with_ctx_block:
        bl.append((NTOK, 2))
    return bl


def phase_c(k, l, with_ctx):
    S, nc, I = k.S, k.nc, k.I
    p = l % 2
    with contextlib.ExitStack() as st:
        sb = lambda n, sh, d: st.enter_context(nc.sbuf_tensor(_u(n), list(sh), d))
        w = sb("c_w", [128, 8, 1536], BF16); bw = Buf()
        wsT = sb("c_ws", [128, 8, 128], BF16)
        gn = sb("c_gn", [128, 512], F32)
        bsT = sb("c_bs", [128, 8], F32)
        bc = Buf()
        hb = [sb("c_hb%d" % i, [128, 8, 512], BF16) for i in range(2)]; bhb = [Buf() for _ in range(2)]
        junk = sb("c_junk", [128, 512], F32); bj = Buf()
        ms = sb("c_ms", [128, 1], F32); bms = Buf()
        vn = sb("c_vn", [128, 512], BF16); bvn = Buf()
        sg = sb("c_sg", [128, 512], F32); bsg = Buf()
        t1 = sb("c_t1", [128, 512], F32); bt1 = Buf()
        yt = sb("c_y", [128, 512], BF16); byt = Buf()
        yb = [sb("c_yb%d" % i, [128, 4, 512], BF16) for i in range(2)]; byb = [Buf() for _ in range(2)]
        load_w(k, l, w[:], COL["c_u"], 1536, bw)
        S.dma('sp', wsT[:], k.ws_bf[p].rearrange("(g s) t -> s g t", s=128), reads=[k.t_w[p]], writes=[bc])
        S.dma('sp', gn[:], I["gmlp_g"][l:l + 1, :].partition_broadcast(128), writes=[bc])
        S.dma('sp', bsT[:], I["gmlp_bsT"][l], writes=[bc])
        pT = [k.ps[4][:].bitcast(BF16), k.ps[5][:].bitcast(BF16)]
        bi = 0
        for s in range(2):
            blocks = seq_blocks(with_ctx)
            load_h(k, s, hb[bi % 2], blocks[0][0], blocks[0][1] * 128, bhb[bi % 2])
            for ib, (c0, ntile) in enumerate(blocks):
                h = hb[bi % 2]; bh = bhb[bi % 2]
                if ib + 1 < len(blocks):
                    load_h(k, s, hb[(bi + 1) % 2], blocks[ib + 1][0], blocks[ib + 1][1] * 128, bhb[(bi + 1) % 2])
                for t in range(ntile):
                    t0 = t * 128
                    inproj_tm(k, 0, w, bw, 512, h, bh, t0)
                    S.op('act', lambda e: e.activation(out=junk[:], in_=k.ps[0][:], func=AF.Square, scale=float(512 ** -0.5), accum_out=ms[:]), reads=[k.bps[0]], writes=[bj, bms])
                    rstd_from_ms(k, ms[:], bms)
                    S.op('dve', lambda e: e.scalar_tensor_tensor(out=vn[:], in0=k.ps[0][:], scalar=ms[:, 0:1], in1=gn[:], op0=ALU.mult, op1=ALU.mult), reads=[k.bps[0], bms, bc], writes=[bvn])
                    inproj_tm(k, 2, w, bw, 0, h, bh, t0)
                    inproj_tm(k, 3, w, bw, 1024, h, bh, t0)
                    for g in range(8):
                        S.op('pe', lambda e, g=g: e.matmul(k.ps[1][:, g * 64:(g + 1) * 64], lhsT=wsT[:, g, :], rhs=vn[:, g * 64:(g + 1) * 64], start=True, stop=True), reads=[bc, bvn], writes=[k.bps[1]], inc=(g == 7))
                    S.op('act', lambda e: e.activation(out=sg[:], in_=k.ps[3][:], func=AF.Silu), reads=[k.bps[3]], writes=[bsg])
                    S.op('dve', lambda e: e.tensor_tensor(out=t1[:].rearrange("p (g c) -> p g c", g=8), in0=k.ps[1][:].rearrange("p (g c) -> p g c", g=8), in1=bsT[:].unsqueeze(2).broadcast_to([128, 8, 64]), op=ALU.add), reads=[k.bps[1], bc], writes=[bt1])
                    S.op('dve', lambda e: e.tensor_tensor(out=t1[:], in0=k.ps[2][:], in1=t1[:], op=ALU.mult), reads=[k.bps[2], bt1], writes=[bt1])
                    S.op('dve', lambda e: e.tensor_tensor(out=yt[:], in0=t1[:], in1=sg[:], op=ALU.mult), reads=[bt1, bsg], writes=[byt])
                    for j in range(4):
                        dst = pT[j // 2][:, (j % 2) * 512 + t0:(j % 2) * 512 + t0 + 128]
                        S.op('pe', lambda e, dst=dst, j=j: e.transpose(dst, yt[:, j * 128:(j + 1) * 128], k.ident[:]), reads=[byt, k.b_const], writes=[k.bps[4 + j // 2]], inc=(j == 3))
                nt = ntile * 128
                yo = yb[bi % 2]
                for j in range(4):
                    src = pT[j // 2][:, (j % 2) * 512:(j % 2) * 512 + nt]
                    S.op('act', lambda e, src=src, j=j, yo=yo, nt=nt: e.copy(yo[:, j, :nt], src), reads=[k.bps[4 + j // 2]], writes=[byb[bi % 2]])
                S.dma('pool', k.yT[s, 2, :, c0:c0 + nt].rearrange("(j p) t -> p j t", p=128), yo[:, :, :nt], reads=[byb[bi % 2]], writes=k.t_yT[s][2].r(c0, c0 + nt))
                bi += 1
        S.barrier()


def phase_b(k, l, with_ctx):
    S, nc, I = k.S, k.nc, k.I
    p = l % 2
    with contextlib.ExitStack() as st:
        sb = lambda n, sh, d: st.enter_context(nc.sbuf_tensor(_u(n), list(sh), d))
        w = sb("b_w", [128, 8, 1024], BF16); bw = Buf()
        fw = sb("b_fw", [128, 4, 128], BF16)
        cc = sb("b_cc", [128, 128], BF16); ssn = sb("b_ss", [128, 128], BF16)
        m12 = sb("b_m12", [128, 2, 4, 128], BF16)
        bc = Buf(); bm = Buf()
        z = sb("b_z", [128, 32, 512], BF16); bz = Buf()
        hb = [sb("b_hb%d" % i, [128, 8, 512], BF16) for i in range(2)]; bhb = [Buf() for _ in range(2)]
        dft = [sb("b_dft%d" % i, [128, 32, 512], BF16) for i in range(2)]; bdft = [Buf() for _ in range(2)]
        Y = sb("b_Y", [128, 4, 512], BF16); bY = Buf()
        sg = sb("b_sg", [128, 4, 256], F32); bsg = Buf()
        yb = [sb("b_yb%d" % i, [128, 4, 256], BF16) for i in range(2)]; byb = [Buf() for _ in range(2)]
        load_w(k, l, w[:], COL["b_x"], 1024, bw)
        S.dma('sp', fw[:], k.fw_bf[p].rearrange("(g c) d -> c g d", c=128), reads=[k.t_w[p]], writes=[bc])
        S.dma('sp', cc[:], I["cc128"], writes=[bc])
        S.dma('sp', ssn[:], I["ssn128"], writes=[bc])
        for i, mat in enumerate((cc, ssn)):
            S.op('pe', lambda e, mat=mat, i=i: e.matmul(k.ps[i][:], lhsT=mat[:], rhs=fw[:].rearrange("c g d -> c (g d)"), start=True, stop=True), reads=[bc], writes=[k.bps[i]])
            S.op('act', lambda e, i=i: e.copy(m12[:, i, :, :].rearrange("c g d -> c (g d)"), k.ps[i][:]), reads=[k.bps[i]], writes=[bm])
        di = 0
        for s in range(2):
            for (tc0, ntile, nkb, isctx) in ([(0, 32, 16, False)] + ([(NTOK, 2, 1, True)] if with_ctx else [])):
                nblk = (ntile + 3) // 4
                for ib in range(nblk):
                    nt = min(4, ntile - ib * 4)
                    load_h(k, s, hb[ib % 2], tc0 + ib * 512, nt * 128, bhb[ib % 2])
                    for t in range(nt):
                        inproj_tm(k, 0, w, bw, 0, hb[ib % 2], bhb[ib % 2], t * 128)
                        S.op('act', lambda e, ib=ib, t=t: e.copy(z[:, ib * 4 + t, :], k.ps[0][:]), reads=[k.bps[0]], writes=[bz])
                for kb in range(nkb):
                    dt_ = dft[di % 2]; bd = bdft[di % 2]
                    if isctx:
                        S.dma('sp', dt_[:, :2, :], I["dft256"].rearrange("(t p) c -> p t c", p=128), writes=[bd])
                    else:
                        for hh in range(2):
                            S.dma('sp', dt_[:, hh * 16:(hh + 1) * 16, :], I["dft"][kb, hh * 2048:(hh + 1) * 2048, :].rearrange("(t p) c -> p t c", p=128), writes=[bd])
                    kc0 = tc0 + kb * 256
                    hi = (kb + 1) % 2
                    load_h(k, s, hb[hi], kc0, 256, bhb[hi])
                    for g in range(4):
                        for t in range(ntile):
                            S.op('pe', lambda e, g=g, t=t, dt_=dt_: e.matmul(k.ps[g][:], lhsT=z[:, t, g * 128:(g + 1) * 128], rhs=dt_[:, t, :], start=(t == 0), stop=(t == ntile - 1)),
                                 reads=[bz, bd], writes=[k.bps[g]], inc=(t == ntile - 1))
                        if g % 2 == 0:
                            S.op('act', lambda e, g=g: e.copy(Y[:, g, :], k.ps[g][:]), reads=[k.bps[g]], writes=[bY])
                        else:
                            S.op('dve', lambda e, g=g: e.tensor_copy(Y[:, g, :], k.ps[g][:]), reads=[k.bps[g]], writes=[bY])
                    for g in range(4):
                        pb_ = 4 + g // 2
                        dst = k.ps[pb_][:, (g % 2) * 256:(g % 2) * 256 + 256]
                        S.op('pe', lambda e, g=g, dst=dst: e.matmul(dst, lhsT=m12[:, 0, g, :], rhs=Y[:, g, 0:256], start=True, stop=False), reads=[bm, bY], writes=[k.bps[pb_]], inc=False)
                        S.op('pe', lambda e, g=g, dst=dst: e.matmul(dst, lhsT=m12[:, 1, g, :], rhs=Y[:, g, 256:512], start=False, stop=True), reads=[bm, bY], writes=[k.bps[pb_]])
                    for g in range(4):
                        pb_ = 6 + g // 2
                        dst = k.ps[pb_][:, (g % 2) * 256:(g % 2) * 256 + 256]
                        for kc in range(8):
                            S.op('pe', lambda e, g=g, dst=dst, kc=kc, hi=hi: e.matmul(dst, lhsT=w[:, kc, 512 + g * 128:512 + (g + 1) * 128], rhs=hb[hi][:, kc, 0:256], start=(kc == 0), stop=(kc == 7)),
                                 reads=[bw, bhb[hi]], writes=[k.bps[pb_]], inc=(kc == 7))
                    yo = yb[di % 2]
                    for gg in range(2):
                        S.op('act', lambda e, gg=gg: e.activation(out=sg[:, 2 * gg:2 * gg + 2, :].rearrange("p a t -> p (a t)"), in_=k.ps[6 + gg][:], func=AF.Silu), reads=[k.bps[6 + gg]], writes=[bsg])
                        S.op('dve', lambda e, gg=gg, yo=yo: e.tensor_tensor(out=yo[:, 2 * gg:2 * gg + 2, :].rearrange("p a t -> p (a t)"), in0=k.ps[4 + gg][:], in1=sg[:, 2 * gg:2 * gg + 2, :].rearrange("p a t -> p (a t)"), op=ALU.mult), reads=[k.bps[4 + gg], bsg], writes=[byb[di % 2]])
                    S.dma('pool', k.yT[s, 1, :, kc0:kc0 + 256].rearrange("(j p) t -> p j t", p=128), yo[:], reads=[byb[di % 2]], writes=k.t_yT[s][1].r(kc0, kc0 + 256))
                    di += 1
        S.barrier()


def na_qblocks(with_ctx):
    bl = [(0, 256, 0, list(range(0, 4)), 0, True), (60 * 64, 256, 0, list(range(28, 32)), 60, True),
          (4 * 64, 256, 1, list(range(0, 6)), 4, True), (56 * 64, 256, 1, list(range(26, 32)), 56, True)]
    for gq in range(1, 7):
        bl.append((gq * 512, 512, 1, list(range(4 * gq - 2, 4 * gq + 6)), 8 * gq, True))
    if with_ctx:
        bl.append((NTOK, 256, 1, [], 0, False))
    return bl


def phase_a(k, l, with_ctx):
    S, nc, I = k.S, k.nc, k.I
    with contextlib.ExitStack() as st:
        sb = lambda n, sh, d: st.enter_context(nc.sbuf_tensor(_u(n), list(sh), d))
        w = sb("a_w", [128, 8, 1024], BF16); bw = Buf()
        kT = sb("a_kT", [128, 4, NT], BF16); bkT = Buf()
        vt = sb("a_v", [128, 34, 512], BF16); bv = Buf()
        csb = [sb("a_cs%d" % i, [128, 2, 512], F32) for i in range(2)]; bcs = [Buf() for _ in range(2)]
        csi = [0]
        rs = sb("a_rs", [128, 128], BF16); eye8 = sb("a_e8", [128, 128], BF16)
        bc = Buf()
        zb = sb("a_zb", [128, 8, ZW * 64], BF16); bzb = Buf()
        hb = [sb("a_hb%d" % i, [128, 8, 512], BF16) for i in range(2)]; bhb = [Buf() for _ in range(2)]
        qp = sb("a_qp", [128, 4, 512], BF16); bqp = Buf()
        qp2 = sb("a_qp2", [128, 2, 4, 512], BF16); bqp2 = Buf()
        qr2 = sb("a_qr2", [128, 2, 4, 512], BF16); bqr = Buf()
        gs = sb("a_gs", [128, 4, 512], BF16); bgs = Buf()
        t1 = sb("a_t1", [128, 512], F32); bt1 = Buf()
        t2 = sb("a_t2", [128, 512], F32); bt2 = Buf()
        kp = sb("a_kp", [128, 512], BF16); bkp = Buf()
        kp_b = sb("a_kpb", [128, 512], BF16); bkp_b = Buf()
        es = [sb("a_es%d" % i, [128, 512], BF16) for i in range(4)]; bes = [Buf() for _ in range(4)]
        rd = sb("a_rd", [128, 512], F32); brd = Buf()
        yo = [sb("a_yo%d" % i, [128, 4, 512], BF16) for i in range(2)]; byo = [Buf() for _ in range(2)]
        S.dma('sp', rs[:], I["rsign"], writes=[bc]); S.dma('sp', eye8[:], I["eye8"], writes=[bc])
        S.op('pool', lambda e: e.memset(qp2[:], 0.0), writes=[bqp2])
        S.op('pool', lambda e: e.memset(qr2[:], 0.0), writes=[bqr])

        def load_cs(c0, n):
            csi[0] += 1
            i = csi[0] % 2
            S.dma('sp', csb[i][:, 0, :n], I["cosT"][:, c0:c0 + n], writes=[bcs[i]])
            S.dma('sp', csb[i][:, 1, :n], I["sinT"][:, c0:c0 + n], writes=[bcs[i]])

        def rope(psrc, plain_bf, bplain, dst, c0, n, rbank):
            i = csi[0] % 2
            if _DBG.get('nors'):
                rbank = psrc
            else:
                S.op('pe', lambda e: e.matmul(k.ps[rbank][:, :n], lhsT=rs[:], rhs=plain_bf, start=True, stop=True), reads=[bc, bplain], writes=[k.bps[rbank]])
            S.op('dve', lambda e: e.tensor_tensor(out=t1[:, :n], in0=k.ps[psrc][:, :n], in1=csb[i][:, 0, :n], op=ALU.mult), reads=[k.bps[psrc], bcs[i], bplain], writes=[bt1])
            S.op('dve', lambda e: e.tensor_tensor(out=t2[:, :n], in0=k.ps[rbank][:, :n], in1=csb[i][:, 1, :n], op=ALU.mult), reads=[k.bps[rbank], bcs[i]], writes=[bt2])

        hi = 0
        for s in range(2):
            load_w(k, l, w[:], COL["a_k"], 1024, bw)
            for (c0, ntile) in seq_blocks(True):
                n = ntile * 128
                h = hb[hi % 2]; bh = bhb[hi % 2]; hi += 1
                load_h(k, s, h, c0, n, bh)
                if c0 < NTOK:
                    load_cs(c0, n)
                for cp in range(4):
                    kb0, kb1 = (0, 1) if cp % 2 == 0 else (3, 4)
                    kpt, bkpt = (kp, bkp) if cp % 2 == 0 else (kp_b, bkp_b)
                    inproj_fm(k, kb0, n, w, bw, cp * 128, h, bh)
                    if c0 >= NTOK or _DBG.get('norope'):
                        S.op('act', lambda e, cp=cp: e.copy(kT[:, cp, c0:c0 + n], k.ps[kb0][:, :n]), reads=[k.bps[kb0]], writes=[bkT])
                    else:
                        S.op('act', lambda e: e.copy(kpt[:, :n], k.ps[kb0][:, :n]), reads=[k.bps[kb0]], writes=[bkpt])
                        rope(kb0, kpt[:, :n], bkpt, None, c0, n, kb1)
                        if _DBG.get('nopool'):
                            S.op('dve', lambda e, cp=cp: e.tensor_tensor(out=kT[:, cp, c0:c0 + n], in0=t1[:, :n], in1=t2[:, :n], op=ALU.add), reads=[bt1, bt2], writes=[bkT])
                        else:
                            S.op('dve', lambda e, cp=cp: e.tensor_tensor(out=kT[:, cp, c0:c0 + n], in0=t1[:, :n], in1=t2[:, :n], op=ALU.add), reads=[bt1, bt2], writes=[bkT])
                for t in range(ntile):
                    vb_ = 2 if t % 2 == 0 else 5
                    inproj_tm(k, vb_, w, bw, 512, h, bh, t * 128)
                    S.op('act', lambda e, t=t: e.copy(vt[:, c0 // 128 + t, :], k.ps[vb_][:]), reads=[k.bps[vb_]], writes=[bv])
            if _DBG.get('a1only'):
                continue
            load_w(k, l, w[:, :, 0:512], COL["a_q"], 512, bw)
            load_w(k, l, w[:, :, 512:1024], COL["a_g"], 512, bw)
            cur_kind = None
            for qi, (t0, nq, kind, chunks, q0, use_rope) in enumerate(na_qblocks(with_ctx)):
                if 'qsel' in _DBG and qi not in _DBG['qsel']:
                    continue
                if use_rope and kind != cur_kind:
                    S.dma('sp', zb[:], I["zb"][l, kind], writes=[bzb])
                    cur_kind = kind
                h = hb[hi % 2]; bh = bhb[hi % 2]; hi += 1
                load_h(k, s, h, t0, nq, bh)
                if use_rope:
                    load_cs(t0, nq)
                for cp in range(4):
                    qb0, qb1, qb2 = (0, 1, 2) if cp % 2 == 0 else (3, 4, 5)
                    inproj_fm(k, qb0, nq, w, bw, cp * 128, h, bh)
                    S.op('act', lambda e, cp=cp: e.copy(qp[:, cp, :nq], k.ps[qb0][:, :nq]), reads=[k.bps[qb0]], writes=[bqp])
                    for j in range(2):
                        S.op('act', lambda e, cp=cp, j=j: e.copy(qp2[64 * j:64 * j + 64, j, cp, :nq], k.ps[qb0][64 * j:64 * j + 64, :nq]), reads=[k.bps[qb0]], writes=[bqp2])
                    if use_rope:
                        rope(qb0, qp[:, cp, :nq], bqp, None, t0, nq, qb1)
                        for j in range(2):
                            S.op('dve', lambda e, cp=cp, j=j: e.tensor_tensor(out=qr2[64 * j:64 * j + 64, j, cp, :nq], in0=t1[64 * j:64 * j + 64, :nq], in1=t2[64 * j:64 * j + 64, :nq], op=ALU.add), reads=[bt1, bt2], writes=[bqr])
                    inproj_fm(k, qb2, nq, w, bw, 512 + cp * 128, h, bh)
                    S.op('act', lambda e, cp=cp: e.activation(out=gs[:, cp, :nq], in_=k.ps[qb2][:, :nq], func=AF.Silu), reads=[k.bps[qb2]], writes=[bgs])
                y = yo[qi % 2]
                for cp in range(4):
                    ob, db = (6, 7) if cp % 2 == 0 else (0, 1)
                    klist = [(c, True) for c in chunks] + [(32, False), (33, False)]
                    items = []
                    for j in range(2):
                        for ic, (c, band) in enumerate(klist):
                            items.append((j, c, band, ic == 0, ic == len(klist) - 1))

                    def stage1(idx, cp=cp):
                        j, c, band, first, last = items[idx]
                        pb = 64 * j; hd = 2 * cp + j
                        sbk = 3 + (idx % 3)
                        e_t = es[idx % 4]; be = bes[idx % 4]
                        if band:
                            woff = 14 - (2 * c - q0)
                            S.op('pe', lambda e: e.matmul(k.ps[sbk][:, :nq], lhsT=kT[:, cp, c * 128:(c + 1) * 128], rhs=qr2[:, j, cp, :nq], start=True, stop=False),
                                 reads=[bkT, bqr], writes=[k.bps[sbk]], inc=False)
                            S.op('pe', lambda e: e.matmul(k.ps[sbk][:, :nq], lhsT=eye8[:], rhs=zb[:, hd, woff * 64:woff * 64 + nq], start=False, stop=True),
                                 reads=[bc, bzb], writes=[k.bps[sbk]])
                        else:
                            S.op('pe', lambda e: e.matmul(k.ps[sbk][:, :nq], lhsT=kT[:, cp, c * 128:(c + 1) * 128], rhs=qp2[:, j, cp, :nq], start=True, stop=True),
                                 reads=[bkT, bqp2], writes=[k.bps[sbk]])
                        S.op('act', lambda e: e.activation(out=e_t[:, :nq], in_=k.ps[sbk][:, :nq], func=AF.Exp, scale=0.125), reads=[k.bps[sbk]], writes=[be])

                    def stage2(idx, cp=cp, ob=ob, db=db):
                        j, c, band, first, last = items[idx]
                        pb = 64 * j; hd = 2 * cp + j
                        e_t = es[idx % 4]; be = bes[idx % 4]
                        S.op('pe', lambda e: e.matmul(k.ps[ob][pb:pb + 64, :nq], lhsT=vt[:, c, hd * 64:(hd + 1) * 64], rhs=e_t[:, :nq], start=first, stop=last),
                             reads=[bv, be], writes=[k.bps[ob]], inc=False)
                        S.op('pe', lambda e: e.matmul(k.ps[db][pb:pb + 64, :nq], lhsT=k.ones_bf[:, 0:64], rhs=e_t[:, :nq], start=first, stop=last),
                             reads=[k.b_const, be], writes=[k.bps[db]])

                    LOOK = 2
                    for idx in range(len(items) + LOOK):
                        if idx < len(items):
                            stage1(idx)
                        if idx >= LOOK:
                            stage2(idx - LOOK)
                    S.op('act', lambda e: e.activation(out=rd[:, :nq], in_=k.ps[db][:, :nq], func=AF.Ln), reads=[k.bps[db]], writes=[brd])
                    S.op('act', lambda e: e.activation(out=rd[:, :nq], in_=rd[:, :nq], func=AF.Exp, scale=-1.0), reads=[brd], writes=[brd])
                    S.op('dve', lambda e: e.tensor_tensor(out=rd[:, :nq], in0=k.ps[ob][:, :nq], in1=rd[:, :nq], op=ALU.mult), reads=[k.bps[ob], brd], writes=[brd])
                    S.op('dve', lambda e, cp=cp, y=y: e.tensor_tensor(out=y[:, cp, :nq], in0=rd[:, :nq], in1=gs[:, cp, :nq], op=ALU.mult), reads=[brd, bgs], writes=[byo[qi % 2]])
                S.dma('pool', k.yT[s, 0, :, t0:t0 + nq].rearrange("(j p) t -> p j t", p=128), y[:, :, :nq], reads=[byo[qi % 2]], writes=k.t_yT[s][0].r(t0, t0 + nq))
        S.barrier()


def phase_d(k, l, with_ctx):
    S, nc, I = k.S, k.nc, k.I
    with contextlib.ExitStack() as st:
        sb = lambda n, sh, d: st.enter_context(nc.sbuf_tensor(_u(n), list(sh), d))
        w = sb("d_w", [128, 8, 2048], BF16); bw = Buf()
        gn = sb("d_gn", [128, 4], F32)
        trif = sb("d_trif", [128, 128], I32); trib = sb("d_trib", [128, 128], I32); bones = sb("d_bones", [128, 128], BF16)
        bc = Buf()
        hb = [sb("d_hb%d" % i, [128, 8, 512], BF16) for i in range(2)]; bhb = [Buf() for _ in range(2)]
        Pp = sb("d_P", [128, 4, 516], F32); bP = Buf()
        kT = sb("d_kT", [128, 4, 512], BF16); bkT = Buf()
        qT = sb("d_qT", [128, 4, 512], BF16); bqT = Buf()
        itm2 = [sb("d_itm%d" % i, [128, 512], BF16) for i in range(2)]; bitm2 = [Buf() for _ in range(2)]
        negr2 = [sb("d_negr%d" % i, [128, 4, 8], F32) for i in range(2)]; bnr2 = [Buf() for _ in range(2)]
        pcnt = [0]
        dqa = [sb("d_dq%d" % i, [128, 128], F32) for i in range(8)]; bdqa = [Buf() for _ in range(8)]
        eqa = [sb("d_eq%d" % i, [128, 128], F32) for i in range(8)]; beqa = [Buf() for _ in range(8)]
        sga = [sb("d_sg%d" % i, [128, 512], F32) for i in range(4)]; bsga = [Buf() for _ in range(4)]
        lfa = [sb("d_lf%d" % i, [128, 512], F32) for i in range(4)]; blfa = [Buf() for _ in range(4)]
        ek = [sb("d_ek%d" % i, [128, 4, 128], F32) for i in range(4)]; bek = [Buf() for _ in range(4)]
        qtl2 = [sb("d_qtl%d" % i, [128, 2, 4, 128], BF16) for i in range(2)]; bqtl2 = [Buf() for _ in range(2)]
        ktl2 = [sb("d_ktl%d" % i, [128, 4, 4, 128], BF16) for i in range(2)]; bktl2 = [Buf() for _ in range(2)]
        qh2 = [sb("d_qh%d" % i, [128, 4, 128], BF16) for i in range(2)]; bqh2 = [Buf() for _ in range(2)]
        khT = sb("d_khT", [128, 4, 128], BF16); bkhT = Buf()
        khtm2 = [sb("d_khtm%d" % i, [128, 512], BF16) for i in range(2)]; bkhtm2 = [Buf() for _ in range(2)]
        dec2 = [sb("d_dec%d" % i, [128, 4], F32) for i in range(2)]; bdec2 = [Buf() for _ in range(2)]
        At = sb("d_At", [128, 8, 128], BF16); bAt = Buf()
        St = sb("d_S", [128, 2, 4, 64], F32); bS = [Buf(), Buf()]
        Sbf = sb("d_Sbf", [128, 4, 128], BF16); bSbf = Buf()
        ob = [sb("d_ob%d" % i, [128, 4, 512], F32) for i in range(2)]; bob = [Buf() for _ in range(2)]
        sq = sb("d_sq", [128, 512], BF16); bsq = Buf()
        rst = sb("d_rst", [128, 512], F32); brst = Buf()
        gl = sb("d_gl", [128, 512], F32); bgl = Buf()
        yb = [sb("d_yb%d" % i, [128, 4, 512], BF16) for i in range(2)]; byb = [Buf() for _ in range(2)]
        S.dma('sp', gn[:], I["hgrn_gT"][l], writes=[bc])
        S.dma('sp', trif[:], I["trif"], writes=[bc]); S.dma('sp', trib[:], I["trib"], writes=[bc]); S.dma('sp', bones[:], I["bones"], writes=[bc])
        S.op('dve', lambda e: e.memset(Pp[:], 0.0), writes=[bP])
        for i in range(2):
            S.op('pool', lambda e, i=i: e.memset(qtl2[i][:], 0.0), writes=[bqtl2[i]])
        S.op('pool', lambda e: e.memset(Sbf[:], 0.0), writes=[bSbf])
        ptr = k.ps[5][:].bitcast(BF16)
        hi = 0
        for s in range(2):
            for d in range(2):
                S.op('dve', lambda e, d=d: e.memset(St[:, d, :, :], 0.0), writes=[bS[d]])
            for (base, nblk_tiles) in ((NTOK, [2]), (0, [4] * 8)):
                isctx = base >= NTOK
                for d in range(2):
                    _DBG['dpass'] = _DBG.get('dpass', 0) + 1
                    if 'dmax' in _DBG and _DBG['dpass'] > _DBG['dmax']:
                        continue
                    sgn = 1.0 if d == 0 else -1.0
                    final = (d == 1)
                    load_w(k, l, w[:, :, 0:512], COL["d_q"], 512, bw)
                    load_w(k, l, w[:, :, 512:1024], COL["d_ff"] if d == 0 else COL["d_fb"], 512, bw)
                    load_w(k, l, w[:, :, 1024:1536], COL["d_i"], 512, bw)
                    if final:
                        load_w(k, l, w[:, :, 1536:2048], COL["d_g"], 512, bw)
                    for i in range(4):
                        S.op('pool', lambda e, i=i: e.memset(ek[i][:], 0.0), writes=[bek[i]])
                    S.op('pool', lambda e: e.memset(At[:], 0.0), writes=[bAt])
                    S.op('act', lambda e, d=d: e.copy(Sbf[0:64, :, 0:64], St[0:64, d, :, :]), reads=[bS[d]], writes=[bSbf]); S.op('act', lambda e, d=d: e.copy(Sbf[64:128, :, 64:128], St[64:128, d, :, :]), reads=[bS[d]], writes=[bSbf])
                    mask = trif if d == 0 else trib
                    blist = list(range(len(nblk_tiles)))
                    if d == 1:
                        blist = blist[::-1]
                    for ib in blist:
                        ntile = nblk_tiles[ib]
                        n = ntile * 128
                        c0 = base + ib * 512
                        h = hb[hi % 2]; bh = bhb[hi % 2]
                        o_b = ob[hi % 2]; bo = bob[hi % 2]; y_b = yb[hi % 2]; by_ = byb[hi % 2]
                        hi += 1
                        load_h(k, s, h, c0, n, bh)
                        if final:
                            S.dma('sp', o_b[:, :, :n], k.ofT[s, :, c0:c0 + n].rearrange("(j p) t -> p j t", p=128), reads=k.t_ofT[s].r(c0, c0 + n), writes=[bo])
                        for ft in range(4):
                            zb_ = ft % 2
                            inproj_fm(k, zb_, n, w, bw, 512 + ft * 128, h, bh)
                            S.op('act', lambda e: e.activation(out=sga[ft][:, :n], in_=k.ps[zb_][:, :n], func=AF.Sigmoid), reads=[k.bps[zb_]], writes=[bsga[ft]])
                            S.op('dve', lambda e: e.tensor_scalar(out=sga[ft][:, :n], in0=sga[ft][:, :n], scalar1=k.oml[:, d * 4 + ft, l:l + 1], scalar2=k.lb[:, d * 4 + ft, l:l + 1], op0=ALU.mult, op1=ALU.add), reads=[bsga[ft], k.b_lb], writes=[bsga[ft]])
                            S.op('dve', lambda e: e.tensor_scalar(out=sga[ft][:, :n], in0=sga[ft][:, :n], scalar1=1e-30, scalar2=None, op0=ALU.max), reads=[bsga[ft]], writes=[bsga[ft]])
                        for ft in range(4):
                            S.op('act', lambda e: e.activation(out=lfa[ft][:, :n], in_=sga[ft][:, :n], func=AF.Ln), reads=[bsga[ft]], writes=[blfa[ft]])
                            S.op('dve', lambda e: e.tensor_scalar(out=kT[:, ft, :n], in0=sga[ft][:, :n], scalar1=-1.0, scalar2=1.0, op0=ALU.mult, op1=ALU.add), reads=[bsga[ft]], writes=[bkT])
                            S.op('dve', lambda e: e.tensor_tensor_scan(out=Pp[:, ft, 1:1 + n], data0=k.ones_f[:, :n], data1=lfa[ft][:, :n], initial=0.0, op0=ALU.mult, op1=ALU.add), reads=[blfa[ft], k.b_const], writes=[bP])
                        for ft in range(4):
                            zb_ = ft % 2
                            inproj_fm(k, zb_, n, w, bw, ft * 128, h, bh)
                            S.op('act', lambda e: e.copy(qT[:, ft, :n], k.ps[zb_][:, :n]), reads=[k.bps[zb_]], writes=[bqT])
                        tl = list(range(ntile))
                        if d == 1:
                            tl = tl[::-1]

                        def stageE(t, pi, part):
                            t0 = t * 128
                            xo = t0 + 1 if d == 0 else t0
                            itm = itm2[pi]; bitm = bitm2[pi]; qtl = qtl2[pi]; bqtl = bqtl2[pi]; ktl = ktl2[pi]; bktl = bktl2[pi]
                            qh = qh2[pi]; bqh = bqh2[pi]; khtm = khtm2[pi]; bkhtm = bkhtm2[pi]; dec = dec2[pi]; bdec = bdec2[pi]
                            negr = negr2[pi]; bnr = bnr2[pi]
                            if part == 1:
                              inproj_tm(k, 2, w, bw, 1024, h, bh, t0)
                              S.op('act', lambda e: e.copy(itm[:], k.ps[2][:]), reads=[k.bps[2]], writes=[bitm])
                            if part == 1:
                              S.op('dve', lambda e: e.tensor_scalar(out=negr[:, :, 0:4], in0=Pp[:, :, t0 + 16:t0 + 113:32], scalar1=-1.0, scalar2=None, op0=ALU.mult), reads=[bP], writes=[bnr])
                              S.op('dve', lambda e: e.tensor_scalar(out=negr[:, :, 4:6], in0=Pp[:, :, t0:t0 + 129:128], scalar1=-1.0, scalar2=None, op0=ALU.mult), reads=[bP], writes=[bnr])
                            def tiles(ft):
                                return (dqa[ft], bdqa[ft], eqa[ft], beqa[ft], dqa[4 + ft], bdqa[4 + ft], eqa[4 + ft], beqa[4 + ft],
                                        Pp[:, ft, xo:xo + 128], Pp[:, ft, t0:t0 + 1], Pp[:, ft, t0 + 128:t0 + 129], negr[:, ft, 4:5], negr[:, ft, 5:6])
                            for ft in (range(4) if part == 1 else []):
                                dq, bdq, eq, beq, dq2, bdq2, eq2, beq2, X, B0p, B1p, B0n, B1n = tiles(ft)
                                S.op('dve', lambda e: e.tensor_tensor(out=dq[:].rearrange("p (i c) -> p i c", i=4), in0=X.rearrange("p (i c) -> p i c", i=4), in1=Pp[:, ft, t0 + 16:t0 + 113:32].unsqueeze(2).broadcast_to([128, 4, 32]), op=ALU.subtract), reads=[bP], writes=[bdq])
                            for ft in (range(4) if part == 1 else []):
                                dq, bdq, eq, beq, dq2, bdq2, eq2, beq2, X, B0p, B1p, B0n, B1n = tiles(ft)
                                S.op('act', lambda e: e.activation(out=eq[:], in_=dq[:], func=AF.Exp, scale=sgn), reads=[bdq], writes=[beq])
                                for i in range(4):
                                    lo, hi_ = (0, 32 * (i + 1)) if d == 0 else (32 * i, 128)
                                    bias = Pp[:, ft, t0 + 16 + 32 * i:t0 + 17 + 32 * i] if d == 0 else negr[:, ft, i:i + 1]
                                    S.op('act', lambda e: e.activation(out=ek[ft][:, i, lo:hi_], in_=Pp[:, ft, xo + lo:xo + hi_], func=AF.Exp, scale=-sgn, bias=bias), reads=[bP, bnr], writes=[bek[ft]])
                                bq_ = B0n if d == 0 else B1p
                                S.op('act', lambda e: e.activation(out=eq2[:], in_=X, func=AF.Exp, scale=sgn, bias=bq_), reads=[bP, bnr], writes=[beq2])
                                bk_ = B1p if d == 0 else B0n
                                S.op('act', lambda e: e.activation(out=dq2[:], in_=X, func=AF.Exp, scale=-sgn, bias=bk_), reads=[bP, bnr], writes=[bdq2])
                                S.op('act', lambda e: e.activation(out=dec[:, ft:ft + 1], in_=B1p, func=AF.Exp, scale=1.0, bias=B0n), reads=[bP, bnr], writes=[bdec])
                            for ft in (range(4) if part == 2 else []):
                                dq, bdq, eq, beq, dq2, bdq2, eq2, beq2, X, B0p, B1p, B0n, B1n = tiles(ft)
                                S.op('dve', lambda e: e.tensor_tensor(out=khT[:, ft, :], in0=dq2[:], in1=kT[:, ft, t0:t0 + 128], op=ALU.mult), reads=[bdq2, bkT], writes=[bkhT])
                                S.op('pe', lambda e: e.transpose(ptr[:, ft * 128:(ft + 1) * 128], khT[:, ft, :], k.ident[:]), reads=[bkhT, k.b_const], writes=[k.bps[5]])
                                S.op('dve', lambda e: e.tensor_tensor(out=qtl[0:64, 0, ft, :], in0=eq[0:64, :], in1=qT[0:64, ft, t0:t0 + 128], op=ALU.mult), reads=[beq, bqT], writes=[bqtl])
                                S.op('dve', lambda e: e.tensor_tensor(out=qtl[64:128, 1, ft, :], in0=eq[64:128, :], in1=qT[64:128, ft, t0:t0 + 128], op=ALU.mult), reads=[beq, bqT], writes=[bqtl])
                                S.op('dve', lambda e: e.tensor_tensor(out=ktl[:, ft, :, :], in0=ek[ft][:], in1=kT[:, ft, t0:t0 + 128].unsqueeze(1).broadcast_to([128, 4, 128]), op=ALU.mult), reads=[bek[ft], bkT], writes=[bktl])
                                S.op('dve', lambda e: e.tensor_tensor(out=qh[:, ft, :], in0=eq2[:], in1=qT[:, ft, t0:t0 + 128], op=ALU.mult), reads=[beq2, bqT], writes=[bqh])
                            if part == 2:
                                S.op('dve', lambda e: e.tensor_copy(khtm[:], ptr[:, 0:512]), reads=[k.bps[5]], writes=[bkhtm])

                        def stageF(t, pi):
                            t0 = t * 128
                            itm = itm2[pi]; bitm = bitm2[pi]; qtl = qtl2[pi]; bqtl = bqtl2[pi]; ktl = ktl2[pi]; bktl = bktl2[pi]
                            qh = qh2[pi]; bqh = bqh2[pi]; khtm = khtm2[pi]; bkhtm = bkhtm2[pi]; dec = dec2[pi]; bdec = bdec2[pi]
                            for hd in range(8):
                                cp = hd // 2
                                sbk = 3 + hd // 4
                                for i in range(4):
                                    dst = k.ps[sbk][:, (hd % 4) * 128 + 32 * i:(hd % 4) * 128 + 32 * i + 32]
                                    S.op('pe', lambda e: e.matmul(dst, lhsT=ktl[:, cp, i, :], rhs=qtl[:, hd % 2, cp, 32 * i:32 * i + 32], start=True, stop=True),
                                         reads=[bktl, bqtl], writes=[k.bps[sbk]], inc=(hd % 4 == 3 and i == 3))
                            for half in range(2):
                                S.op('dve', lambda e: e.copy_predicated(out=At[:, 4 * half:4 * half + 4, :], mask=mask[:].unsqueeze(1).broadcast_to([128, 4, 128]), data=k.ps[3 + half][:].rearrange("p (h t) -> p h t", h=4)), reads=[k.bps[3 + half], bc, bAt], writes=[bAt])
                            for cp in range(4):
                                for j in range(2):
                                    hd = 2 * cp + j
                                    S.op('pe', lambda e: e.matmul(k.ps[6][64 * j:64 * j + 64, cp * 128:(cp + 1) * 128], lhsT=itm[:, hd * 64:(hd + 1) * 64], rhs=At[:, hd, :], start=True, stop=False), reads=[bitm, bAt], writes=[k.bps[6]], inc=False)
                                S.op('pe', lambda e: e.matmul(k.ps[6][:, cp * 128:(cp + 1) * 128], lhsT=Sbf[:, cp, :], rhs=qh[:, cp, :], start=False, stop=True), reads=[bSbf, bqh], writes=[k.bps[6]], inc=(cp == 3))
                            for cp in range(4):
                                S.op('pe', lambda e: e.matmul(k.ps[7][:, cp * 128:(cp + 1) * 128], lhsT=khtm[:, cp * 128:(cp + 1) * 128], rhs=itm[:, cp * 128:(cp + 1) * 128], start=True, stop=True), reads=[bkhtm, bitm], writes=[k.bps[7]], inc=(cp == 3))
                            for cp in range(4):
                                for j in range(2):
                                    pb = 64 * j
                                    S.op('dve', lambda e: e.scalar_tensor_tensor(out=St[pb:pb + 64, d, cp, :], in0=St[pb:pb + 64, d, cp, :], scalar=dec[pb:pb + 64, cp:cp + 1], in1=k.ps[7][pb:pb + 64, cp * 128 + 64 * j:cp * 128 + 64 * j + 64], op0=ALU.mult, op1=ALU.add),
                                         reads=[bS[d], bdec, k.bps[7]], writes=[bS[d]])
                            S.op('act', lambda e: e.copy(Sbf[0:64, :, 0:64], St[0:64, d, :, :]), reads=[bS[d]], writes=[bSbf])
                            S.op('act', lambda e: e.copy(Sbf[64:128, :, 64:128], St[64:128, d, :, :]), reads=[bS[d]], writes=[bSbf])
                            if not final:
                                S.op('act', lambda e: e.copy(o_b[:, :, t0:t0 + 128], k.ps[6][:].rearrange("p (c t) -> p c t", c=4)), reads=[k.bps[6]], writes=[bo])
                            else:
                                S.op('dve', lambda e: e.tensor_tensor(out=o_b[:, :, t0:t0 + 128], in0=k.ps[6][:].rearrange("p (c t) -> p c t", c=4), in1=o_b[:, :, t0:t0 + 128], op=ALU.add), reads=[k.bps[6], bo], writes=[bo])

                        for it_, t in enumerate(tl):
                            if it_ == 0:
                                stageE(t, pcnt[0] % 2, 1)
                                stageE(t, pcnt[0] % 2, 2)
                            if it_ + 1 < len(tl):
                                stageE(tl[it_ + 1], (pcnt[0] + 1) % 2, 1)
                            stageF(t, pcnt[0] % 2)
                            if it_ + 1 < len(tl):
                                stageE(tl[it_ + 1], (pcnt[0] + 1) % 2, 2)
                            pcnt[0] += 1
                        if not final:
                            S.dma('pool', k.ofT[s, :, c0:c0 + n].rearrange("(j p) t -> p j t", p=128), o_b[:, :, :n], reads=[bo], writes=k.t_ofT[s].r(c0, c0 + n))
                        elif (not isctx) or with_ctx:
                            for cp in range(4):
                                S.op('act', lambda e, cp=cp, o_b=o_b: e.activation(out=sq[:, :n], in_=o_b[:, cp, :n], func=AF.Square), reads=[bo], writes=[bsq])
                                S.op('pe', lambda e: e.matmul(k.ps[0][:, :n], lhsT=bones[:], rhs=sq[:, :n], start=True, stop=True), reads=[bc, bsq], writes=[k.bps[0]])
                                S.op('act', lambda e: e.activation(out=rst[:, :n], in_=k.ps[0][:, :n], func=AF.Ln, scale=1.0 / 64.0, bias=EPS), reads=[k.bps[0]], writes=[brst])
                                S.op('act', lambda e: e.activation(out=rst[:, :n], in_=rst[:, :n], func=AF.Exp, scale=-0.5), reads=[brst], writes=[brst])
                                inproj_fm(k, 1, n, w, bw, 1536 + cp * 128, h, bh)
                                S.op('act', lambda e: e.activation(out=gl[:, :n], in_=k.ps[1][:, :n], func=AF.Silu), reads=[k.bps[1]], writes=[bgl])
                                S.op('dve', lambda e, cp=cp, o_b=o_b: e.scalar_tensor_tensor(out=rst[:, :n], in0=o_b[:, cp, :n], scalar=gn[:, cp:cp + 1], in1=rst[:, :n], op0=ALU.mult, op1=ALU.mult), reads=[bo, bc, brst], writes=[brst])
                                S.op('dve', lambda e, cp=cp, y_b=y_b: e.tensor_tensor(out=y_b[:, cp, :n], in0=rst[:, :n], in1=gl[:, :n], op=ALU.mult), reads=[brst, bgl], writes=[by_])
                            S.dma('pool', k.yT[s, 3, :, c0:c0 + n].rearrange("(j p) t -> p j t", p=128), y_b[:, :, :n], reads=[by_], writes=k.t_yT[s][3].r(c0, c0 + n))
        S.barrier()


def phase_m(k, l, with_ctx):
    S, nc, I = k.S, k.nc, k.I
    p = l % 2
    with contextlib.ExitStack() as st:
        sb = lambda n, sh, d: st.enter_context(nc.sbuf_tensor(_u(n), list(sh), d))
        wg = sb("m_wg", [128, 8, 4096], BF16); bw = Buf()
        wbr = sb("m_wbr", [128, 16, D], BF16)
        wo = sb("m_wo", [128, 8, D], BF16)
        bwc = Buf()
        hb = [sb("m_hb%d" % i, [128, 8, 256], BF16) for i in range(2)]; bhb = [Buf() for _ in range(2)]
        yb = [sb("m_yb%d" % i, [128, 16, 256], BF16) for i in range(2)]; byb = [Buf() for _ in range(2)]
        sgt = [sb("m_sg%d" % i, [128, 256], F32) for i in range(2)]; bsg = [Buf() for _ in range(2)]
        tmp = [sb("m_tmp%d" % i, [128, 256], F32) for i in range(2)]; btmp = [Buf() for _ in range(2)]
        macc = sb("m_acc", [128, 256], F32); bacc = Buf()
        mT = sb("m_mT", [128, 8, 256], BF16); bmT = Buf()
        xt = [sb("m_x%d" % i, [128, D], F32) for i in range(2)]; bx = [Buf() for _ in range(2)]
        tt = sb("m_tt", [128, D], F32); btt = Buf()
        junk = sb("m_junk", [128, 512], F32); bj = Buf()
        ms = sb("m_ms", [128, 2], F32); bms = Buf()
        load_w(k, l, wg[:], COL["gate"], 4096, bw)
        S.dma('sp', wbr[:], k.w_br_bf[p].rearrange("(a p) c -> p a c", p=128), reads=[k.t_w[p]], writes=[bwc])
        S.dma('sp', wo[:], k.w_out_bf[p].rearrange("(a p) c -> p a c", p=128), reads=[k.t_w[p]], writes=[bwc])
        bi = 0
        xi = 0
        gi = 0
        for s in range(2):
            blocks = [(c0, False) for c0 in range(0, NTOK, 256)] + ([(NTOK, True)] if with_ctx else [])
            for (c0, isctx) in blocks:
                mi = 2 if isctx else s
                h = hb[bi % 2]; bh = bhb[bi % 2]; y = yb[bi % 2]; by_ = byb[bi % 2]
                bi += 1
                load_h(k, s, h, c0, 256, bh)
                for r in range(4):
                    S.dma('sp', y[:, 4 * r:4 * r + 4, :], k.yT[s, r, :, c0:c0 + 256].rearrange("(j p) t -> p j t", p=128), reads=k.t_yT[s][r].r(c0, c0 + 256), writes=[by_])
                for fc in range(8):
                    for r in range(4):
                        gb = gi % 2; gi += 1
                        for kc in range(8):
                            S.op('pe', lambda e, kc=kc, r=r, fc=fc, gb=gb: e.matmul(k.ps[gb][:, :256], lhsT=wg[:, kc, r * 1024 + fc * 128:r * 1024 + (fc + 1) * 128], rhs=h[:, kc, :], start=(kc == 0), stop=(kc == 7)),
                                 reads=[bw, bh], writes=[k.bps[gb]], inc=(kc == 7))
                        S.op('act', lambda e, gb=gb: e.activation(out=sgt[gb][:], in_=k.ps[gb][:, :256], func=AF.Sigmoid), reads=[k.bps[gb]], writes=[bsg[gb]])
                        for kc in range(4):
                            S.op('pe', lambda e, kc=kc, r=r, fc=fc, gb=gb: e.matmul(k.ps[2 + gb][:, :256], lhsT=wbr[:, 4 * r + kc, fc * 128:(fc + 1) * 128], rhs=y[:, 4 * r + kc, :], start=(kc == 0), stop=(kc == 3)),
                                 reads=[bwc, by_], writes=[k.bps[2 + gb]], inc=(kc == 3))
                        if r == 0:
                            S.op('dve', lambda e, gb=gb: e.tensor_tensor(out=macc[:], in0=k.ps[2 + gb][:, :256], in1=sgt[gb][:], op=ALU.mult), reads=[k.bps[2 + gb], bsg[gb]], writes=[bacc])
                        else:
                            S.op('dve', lambda e, gb=gb: e.tensor_tensor(out=tmp[gb][:], in0=k.ps[2 + gb][:, :256], in1=sgt[gb][:], op=ALU.mult), reads=[k.bps[2 + gb], bsg[gb]], writes=[btmp[gb]])
                            S.op('dve', lambda e, gb=gb: e.tensor_tensor(out=macc[:], in0=macc[:], in1=tmp[gb][:], op=ALU.add), reads=[bacc, btmp[gb]], writes=[bacc])
                    S.op('act', lambda e, fc=fc: e.copy(mT[:, fc, :], macc[:]), reads=[bacc], writes=[bmT])
                for t in range(2):
                    tok0 = c0 + t * 128
                    x = xt[xi % 2]; bxx = bx[xi % 2]; xi += 1
                    if isctx:
                        src = (I["ctx"] if l == 0 else k.ctxcur)[s, t * 128:(t + 1) * 128, :]
                        dstd = k.ctxcur[s, t * 128:(t + 1) * 128, :]
                        tb = k.t_ctx[s].r(t * 128, t * 128 + 128)
                    else:
                        src = (I["x"] if l == 0 else k.OUT)[s, tok0:tok0 + 128, :]
                        dstd = k.OUT[s, tok0:tok0 + 128, :]
                        tb = k.t_x[s].r(tok0, tok0 + 128)
                    S.dma('sp', x[:], src, reads=tb, writes=[bxx])
                    for half in range(2):
                        for kc in range(8):
                            S.op('pe', lambda e, kc=kc, half=half, t=t: e.matmul(k.ps[4 + half][:], lhsT=mT[:, kc, t * 128:(t + 1) * 128], rhs=wo[:, kc, half * 512:(half + 1) * 512], start=(kc == 0), stop=(kc == 7)),
                                 reads=[bmT, bwc], writes=[k.bps[4 + half]], inc=(kc == 7))
                        S.op('act', lambda e, half=half: e.activation(out=junk[:], in_=k.ps[4 + half][:], func=AF.Square, scale=1.0 / 32.0, accum_out=ms[:, half:half + 1]), reads=[k.bps[4 + half]], writes=[bj, bms])
                    S.op('dve', lambda e: e.tensor_tensor(out=ms[:, 0:1], in0=ms[:, 0:1], in1=ms[:, 1:2], op=ALU.add), reads=[bms], writes=[bms])
                    rstd_from_ms(k, ms[:, 0:1], bms)
                    for half in range(2):
                        sl = slice(half * 512, (half + 1) * 512)
                        S.op('dve', lambda e, half=half, sl=sl, mi=mi: e.scalar_tensor_tensor(out=tt[:, sl], in0=k.ps[4 + half][:], scalar=ms[:, 0:1], in1=k.gtg[:, mi, sl], op0=ALU.mult, op1=ALU.mult), reads=[k.bps[4 + half], bms, k.b_gtg], writes=[btt])
                    S.op('dve', lambda e, x=x: e.tensor_tensor(out=x[:], in0=x[:], in1=tt[:], op=ALU.add), reads=[bxx, btt], writes=[bxx])
                    S.dma('pool', dstd, x[:], reads=[bxx], writes=tb)
        S.barrier()


_BF = ml_dtypes.bfloat16
_CONST = {}


def _constants():
    if _CONST:
        return _CONST
    c = {}
    c["ident"] = np.eye(128, dtype=np.float32).astype(_BF)
    c["eye8"] = (8.0 * np.eye(128, dtype=np.float32)).astype(_BF)
    rm = np.zeros((128, 128), np.float32)
    for dp in range(128):
        dd = dp % 64
        if (dd % 32) < 16:
            rm[dp, dp + 16] = -1.0
        else:
            rm[dp, dp - 16] = 1.0
    c["rsign"] = np.ascontiguousarray(rm.T).astype(_BF)
    t = np.arange(NTOK)
    pos = np.stack([t // 64, t % 64], 0).astype(np.float64)
    inv = 10000.0 ** (-np.arange(16, dtype=np.float64) * 2.0 / 32.0)
    d = np.arange(128) % 64
    ang = pos[d // 32, :] * inv[d % 16][:, None]
    c["cosT"] = np.cos(ang).astype(np.float32)
    c["sinT"] = np.sin(ang).astype(np.float32)
    s_, t_ = np.meshgrid(np.arange(128), np.arange(128), indexing="ij")
    c["trif"] = (s_ <= t_).astype(np.int32)
    c["trib"] = (s_ >= t_).astype(np.int32)
    c["bones"] = ((s_ // 64) == (t_ // 64)).astype(np.float32).astype(_BF)
    n = np.arange(NTOK, dtype=np.int64)
    m = (n[:, None] * n[None, :]) % NTOK
    sc = 1.0 / np.sqrt(NTOK * 128.0)
    angm = (2.0 * np.pi / NTOK) * m.astype(np.float32)
    cs = (np.cos(angm) * sc).astype(np.float32)
    sn = (np.sin(angm) * sc).astype(np.float32)
    dft = np.empty((16, NTOK, 512), dtype=_BF)
    for kb in range(16):
        dft[kb, :, 0:256] = cs[:, kb * 256:(kb + 1) * 256].astype(_BF)
        dft[kb, :, 256:512] = sn[:, kb * 256:(kb + 1) * 256].astype(_BF)
    c["dft"] = dft
    n2 = np.arange(LCTX, dtype=np.int64)
    a2 = (2.0 * np.pi / LCTX) * ((n2[:, None] * n2[None, :]) % LCTX)
    sc2 = 1.0 / np.sqrt(LCTX * 128.0)
    c["dft256"] = np.concatenate([np.cos(a2) * sc2, np.sin(a2) * sc2], 1).astype(np.float32).astype(_BF)
    n3 = np.arange(128, dtype=np.int64)
    a3 = (2.0 * np.pi / 128) * ((n3[:, None] * n3[None, :]) % 128)
    c["cc128"] = np.cos(a3).astype(np.float32).astype(_BF)
    c["ssn128"] = (-np.sin(a3)).astype(np.float32).astype(_BF)
    _CONST.update(c)
    return _CONST


def _zb_tables(rpb):
    L = rpb.shape[0]
    e = np.arange(2)[:, None, None, None]
    kc = np.arange(64)[None, :, None, None]
    w = np.arange(ZW)[None, None, :, None]
    qc = np.arange(64)[None, None, None, :]
    dr = 14 - w + e + 0 * kc + 0 * qc
    cs = np.clip(qc - 8, 0, 48)
    col_ok = (kc >= cs) & (kc < cs + 16)
    cidx = np.clip(kc - qc + 15, 0, 30) + 0 * dr
    out = np.empty((L, 2, 128, 8, ZW * 64), dtype=_BF)
    for kind in range(2):
        row_ok = (dr >= -7) & (dr <= 7)
        if kind == 1:
            row_ok = row_ok & (dr >= -4) & (dr < 4)
        ok = (row_ok & col_ok)
        ridx = np.clip(dr + 7, 0, 14)
        for l in range(L):
            g = rpb[l][:, ridx, cidx]
            g = np.where(ok[None], g, np.float32(NEG))
            out[l, kind] = g.transpose(1, 2, 0, 3, 4).reshape(128, 8, ZW * 64).astype(_BF)
    return out


def make_in_maps(inputs, nlayers=DEPTH, cores=range(8)):
    f = lambda a: np.ascontiguousarray(np.asarray(a, dtype=np.float32))
    c = _constants()
    L = nlayers
    shared = dict(c)
    shared["w_ada"] = f(inputs["w_ada"][:L])
    shared["b_ada"] = f(inputs["b_ada"][:L])
    shared["b_adaT"] = f(np.asarray(inputs["b_ada"][:L]).reshape(L, 24, 128).transpose(0, 2, 1))
    shared["g_preT"] = f(np.asarray(inputs["g_pre"][:L]).reshape(L, 8, 128).transpose(0, 2, 1))
    shared["g_post"] = f(inputs["g_post"][:L])
    shared["w_in"] = f(inputs["w_in"][:L])
    shared["zb"] = _zb_tables(np.asarray(inputs["na_rpb"][:L], dtype=np.float32))
    shared["fnet_w"] = f(np.asarray(inputs["fnet_w"][:L]).reshape(L, 512, 128))
    shared["gmlp_g"] = f(inputs["gmlp_norm_g"][:L])
    shared["gmlp_wsT"] = f(np.asarray(inputs["gmlp_ws"][:L]).transpose(0, 1, 3, 2).reshape(L, 1024, 128))
    shared["gmlp_bsT"] = f(np.asarray(inputs["gmlp_bs"][:L]).transpose(0, 2, 1))
    lg = np.asarray(inputs["hgrn_lb_logits"], dtype=np.float32)
    shared["lbT"] = f(lg.reshape(DEPTH, 2, 4, 128).transpose(3, 1, 2, 0).reshape(128, 8, DEPTH))
    shared["hgrn_gT"] = f(np.asarray(inputs["hgrn_norm_g"][:L]).reshape(L, 4, 128).transpose(0, 2, 1))
    shared["w_branch"] = f(np.asarray(inputs["w_branch"][:L]).reshape(L, 2048, D))
    shared["w_out"] = f(inputs["w_out"][:L])
    x = np.asarray(inputs["x"]); ctx = np.asarray(inputs["ctx"]); cc = np.asarray(inputs["c"]); c_ctx = np.asarray(inputs["c_ctx"])
    maps = []
    for ci in cores:
        m = dict(shared)
        m["x"] = f(x[2 * ci:2 * ci + 2])
        m["ctx"] = f(ctx[2 * ci:2 * ci + 2])
        c3 = np.stack([cc[2 * ci], cc[2 * ci + 1], c_ctx], 0)
        m["cT"] = f(c3.reshape(3, 8, 128).transpose(2, 1, 0))
        maps.append(m)
    return maps


_NC_CACHE = {}


def kernel(**inputs):
    if "nc" not in _NC_CACHE:
        _NC_CACHE["nc"] = build()
    nc = _NC_CACHE["nc"]
    maps = make_in_maps(inputs)
    res = run_bass_kernel_spmd(nc, maps, core_ids=list(range(8)))
    return np.concatenate([np.asarray(r["out"], dtype=np.float32) for r in res.results], axis=0)
```

```python
import contextlib
import numpy as np
import ml_dtypes
import concourse.bass as bass
import concourse.mybir as mybir
from concourse.bass_utils import run_bass_kernel_spmd

F32 = mybir.dt.float32
BF16 = mybir.dt.bfloat16
I32 = mybir.dt.int32
AF = mybir.ActivationFunctionType
ALU = mybir.AluOpType

D = 1024
NTOK = 4096
LCTX = 256
NT = NTOK + LCTX
NCOL = 11264
DEPTH = 4
EPS = 1e-6
COL = dict(a_q=0, a_k=512, a_v=1024, a_g=1536, b_x=2048, b_g=2560, c_u=3072, c_v=3584, c_g=4096,
           d_q=4608, d_ff=5120, d_fb=5632, d_i=6144, d_g=6656, gate=7168)
ZW = 26
NEG = -30000.0

SEM_WINDOW = 16000
DMA_RING = 8


class Buf:
    __slots__ = ("name", "lw", "rd", "excl")

    def __init__(self, name="", excl=False):
        self.name = name
        self.lw = None
        self.rd = {}
        self.excl = excl


class DTrack:
    def __init__(self, ncols, unit=128):
        self.unit = unit
        self.b = [Buf() for _ in range((ncols + unit - 1) // unit)]

    def r(self, c0, c1):
        return self.b[c0 // self.unit:(c1 + self.unit - 1) // self.unit]


class _Rec:
    def __init__(self):
        self.call = None

    def __getattr__(self, name):
        def f(*a, **kw):
            self.call = (name, a, kw)
            return self
        return f


def _freeze(fn):
    r = _Rec()
    fn(r)
    name, a, kw = r.call
    return lambda e: getattr(e, name)(*a, **kw)


class Sched:
    ENGS = ("pe", "act", "dve", "pool", "sp")

    def __init__(self, nc):
        self.nc = nc
        self.ops = {e: [] for e in self.ENGS}
        self.cnt = {e: 0 for e in self.ENGS}
        self.dcnt = {e: 0 for e in self.ENGS}
        self.known = {e: {} for e in self.ENGS}
        self.pending = {e: False for e in self.ENGS}

    def _tokwaits(self, eng, toks):
        waits = {}
        for t in toks:
            if t[0] == 'e':
                if t[1] == eng and eng == 'pe':
                    continue
                key = ('e', t[1], (t[2] - 1) // SEM_WINDOW)
                val = (t[2] - 1) % SEM_WINDOW + 1
            else:
                key = ('d', t[1], t[2] % DMA_RING)
                val = 16 * (t[2] // DMA_RING + 1)
            if waits.get(key, 0) < val:
                waits[key] = val
        out = []
        kn = self.known[eng]
        for key, val in waits.items():
            if kn.get(key, 0) >= val:
                continue
            kn[key] = val
            out.append((key, val))
        return out

    def _deps(self, eng, reads, writes):
        toks = []
        for b in reads:
            if b.lw is not None:
                toks.append(b.lw)
            if b.excl:
                toks.extend(v for kk, v in b.rd.items() if kk != eng)
        for b in writes:
            if b.lw is not None and not (b.lw[0] == 'e' and b.lw[1] == eng):
                toks.append(b.lw)
            toks.extend(v for v in b.rd.values() if not (v[0] == 'e' and v[1] == eng))
        return self._tokwaits(eng, toks)

    def op(self, eng, fn, reads=(), writes=(), inc=True):
        fn = _freeze(fn)
        waits = self._deps(eng, reads, writes)
        idx = self.cnt[eng] + 1
        tok = ('e', eng, idx)
        if inc:
            self.cnt[eng] = idx
            self.ops[eng].append((waits, fn, ('e', eng, (idx - 1) // SEM_WINDOW), 1))
            self.pending[eng] = False
        else:
            self.ops[eng].append((waits, fn, None, 0))
            self.pending[eng] = True
        for b in reads:
            b.rd[eng] = tok
        for b in writes:
            b.lw = tok
            b.rd = {}
        return tok

    def dma(self, q, out, in_, reads=(), writes=()):
        waits = self._deps(q, reads, writes)
        i = self.dcnt[q]
        self.dcnt[q] += 1
        if i >= DMA_RING:
            key = ('d', q, i % DMA_RING)
            val = 16 * (i // DMA_RING)
            kn = self.known[q]
            if kn.get(key, 0) < val:
                kn[key] = val
                waits.append((key, val))
        tok = ('d', q, i)
        fn = lambda e, out=out, in_=in_: e.dma_start(out=out, in_=in_)
        self.ops[q].append((waits, fn, ('d', q, i % DMA_RING), 16))
        qk = 'q' + q
        for b in reads:
            b.rd[qk] = tok
        for b in writes:
            b.lw = tok
            b.rd = {}
        return tok

    def barrier(self):
        toks = []
        for e in self.ENGS:
            assert not self.pending[e]
            if self.cnt[e] > 0:
                toks.append(('e', e, self.cnt[e]))
            n = self.dcnt[e]
            for i in range(max(0, n - DMA_RING), n):
                toks.append(('d', e, i))
        for e in self.ENGS:
            w = self._tokwaits(e, toks)
            if w:
                self.ops[e].append((w, None, None, 0))

    def emit(self):
        nc = self.nc
        self.barrier()
        sems = {}
        with contextlib.ExitStack() as st:
            def getsem(key):
                if key not in sems:
                    sems[key] = st.enter_context(nc.semaphore("s_%s_%s_%d" % key))
                return sems[key]
            for e in self.ENGS:
                for (waits, fn, inc, amt) in self.ops[e]:
                    for key, val in waits:
                        getsem(key)
                    if inc is not None:
                        getsem(inc)
            block = st.enter_context(nc.Block())
            handles = {"pe": block.tensor, "act": block.scalar, "dve": block.vector,
                       "pool": block.gpsimd, "sp": block.sync}
            for e in self.ENGS:
                ops = self.ops[e]
                if not ops:
                    continue

                def body(engine, ops=ops):
                    for (waits, fn, inc, amt) in ops:
                        for key, val in waits:
                            engine.wait_ge(sems[key], val)
                        if fn is not None:
                            ins = fn(engine)
                            if inc is not None:
                                ins.then_inc(sems[inc], amt)
                handles[e](body)


_UC = [0]
_DBG = {}


def _u(n):
    _UC[0] += 1
    return "%s_%d" % (n, _UC[0])


class K:
    pass


def _dram(nc, name, shape, dt, kind=None):
    if kind is None:
        return nc.dram_tensor(name, list(shape), dt).ap()
    return nc.dram_tensor(name, list(shape), dt, kind=kind).ap()


def build(nlayers=DEPTH, phases="PABCDM", dump=()):
    nc = bass.Bass("TRN2", target_bir_lowering=False)
    k = K()
    k.nc = nc
    S = Sched(nc)
    k.S = S
    IN = "ExternalInput"
    I = {}
    def inp(name, shape, dt=F32):
        I[name] = _dram(nc, name, shape, dt, IN)
        return I[name]
    inp("x", [2, NTOK, D]); inp("ctx", [2, LCTX, D]); inp("cT", [128, 8, 3])
    inp("w_ada", [nlayers, D, 3 * D]); inp("b_ada", [nlayers, 3 * D]); inp("b_adaT", [nlayers, 128, 24])
    inp("g_preT", [nlayers, 128, 8]); inp("g_post", [nlayers, D])
    inp("w_in", [nlayers, D, NCOL])
    inp("zb", [nlayers, 2, 128, 8, ZW * 64], BF16)
    inp("fnet_w", [nlayers, 512, 128]); inp("gmlp_g", [nlayers, 512]); inp("gmlp_wsT", [nlayers, 1024, 128])
    inp("gmlp_bsT", [nlayers, 128, 8]); inp("lbT", [128, 8, DEPTH]); inp("hgrn_gT", [nlayers, 128, 4])
    inp("w_branch", [nlayers, 2048, D]); inp("w_out", [nlayers, D, D])
    inp("ident", [128, 128], BF16); inp("rsign", [128, 128], BF16); inp("eye8", [128, 128], BF16)
    inp("cosT", [128, NTOK]); inp("sinT", [128, NTOK])
    inp("trif", [128, 128], I32); inp("trib", [128, 128], I32); inp("bones", [128, 128], BF16)
    inp("dft", [16, NTOK, 512], BF16); inp("dft256", [LCTX, 512], BF16)
    inp("cc128", [128, 128], BF16); inp("ssn128", [128, 128], BF16)
    OUT = _dram(nc, "out", [2, NTOK, D], F32, "ExternalOutput")
    def scr(name, shape, dt):
        return _dram(nc, name, shape, dt, "ExternalOutput" if name in dump else None)
    k.w_in_bf = scr("w_in_bf", [2, D, NCOL], BF16)
    k.w_br_bf = scr("w_br_bf", [2, 2048, D], BF16)
    k.w_out_bf = scr("w_out_bf", [2, D, D], BF16)
    k.fw_bf = scr("fw_bf", [2, 512, 128], BF16)
    k.ws_bf = scr("ws_bf", [2, 1024, 128], BF16)
    k.hT = scr("hT", [2, D, NT], BF16)
    k.yT = scr("yT", [2, 4, 512, NT], BF16)
    k.ofT = scr("ofT", [2, 512, NT], F32)
    k.ctxcur = scr("ctxcur", [2, LCTX, D], F32)
    k.I = I
    k.OUT = OUT
    k.t_w = [Buf() for _ in range(2)]
    k.t_hT = [DTrack(NT) for _ in range(2)]
    k.t_yT = [[DTrack(NT) for _ in range(4)] for _ in range(2)]
    k.t_ofT = [DTrack(NT) for _ in range(2)]
    k.t_x = [DTrack(NTOK) for _ in range(2)]
    k.t_ctx = [DTrack(LCTX) for _ in range(2)]

    with contextlib.ExitStack() as gst:
        k.gst = gst
        def gsb(name, shape, dt):
            return gst.enter_context(nc.sbuf_tensor(name, list(shape), dt))
        k.ps = [gst.enter_context(nc.psum_tensor("ps%d" % i, [128, 512], F32)) for i in range(8)]
        k.bps = [Buf("ps%d" % i, excl=True) for i in range(8)]
        k.ident = gsb("ident_sb", [128, 128], BF16)
        k.ones_bf = gsb("ones_bf", [128, 128], BF16)
        k.ones_f = gsb("ones_f", [128, 512], F32)
        k.scT = gsb("scT", [128, 8, 3], F32)
        k.modT = gsb("modT", [128, 24, 3], F32)
        k.gsT = gsb("gsT", [128, 8, 3], F32)
        k.gtg = gsb("gtg", [128, 3, D], F32)
        k.lb = gsb("lb", [128, 8, DEPTH], F32)
        k.oml = gsb("oml", [128, 8, DEPTH], F32)
        k.b_const = Buf(); k.b_scT = Buf(); k.b_mod = Buf(); k.b_gtg = Buf(); k.b_lb = Buf()
        S.dma('sp', k.ident[:], I["ident"], writes=[k.b_const])
        S.op('dve', lambda e: e.memset(k.ones_bf[:], 1.0), writes=[k.b_const])
        S.op('dve', lambda e: e.memset(k.ones_f[:], 1.0), writes=[k.b_const])
        prep_global(k)
        for l in range(nlayers):
            S.barrier()
            convert_weights(k, l)
            adaln(k, l)
            with_ctx = l < DEPTH - 1
            if "P" in phases:
                for s in range(2):
                    phase_pre(k, l, s)
            if "C" in phases:
                phase_c(k, l, with_ctx)
            if "B" in phases:
                phase_b(k, l, with_ctx)
            if "A" in phases:
                phase_a(k, l, with_ctx)
            if "D" in phases:
                phase_d(k, l, with_ctx)
            if "M" in phases:
                phase_m(k, l, with_ctx)
        S.emit()
    return nc


def prep_global(k):
    S, nc, I = k.S, k.nc, k.I
    with contextlib.ExitStack() as st:
        sb = lambda n, s, d: st.enter_context(nc.sbuf_tensor(_u(n), list(s), d))
        cT = sb("pg_cT", [128, 8, 3], F32)
        lg = sb("pg_lg", [128, 8, DEPTH], F32)
        ex = sb("pg_ex", [128, 8, DEPTH], F32)
        sm = sb("pg_sm", [128, 8], F32)
        b = Buf()
        S.dma('sp', cT[:], I["cT"], writes=[b])
        S.op('act', lambda e: e.activation(out=k.scT[:], in_=cT[:], func=AF.Silu), reads=[b], writes=[k.b_scT])
        b2 = Buf()
        S.dma('sp', lg[:], I["lbT"], writes=[b2])
        S.op('act', lambda e: e.activation(out=ex[:], in_=lg[:], func=AF.Exp), reads=[b2], writes=[b2])
        S.op('dve', lambda e: e.tensor_reduce(out=sm[:], in_=ex[:], axis=mybir.AxisListType.X, op=ALU.add), reads=[b2], writes=[b2])
        S.op('dve', lambda e: e.reciprocal(sm[:], sm[:]), reads=[b2], writes=[b2])
        S.op('dve', lambda e: e.tensor_tensor(out=ex[:], in0=ex[:], in1=sm[:].unsqueeze(2).broadcast_to([128, 8, DEPTH]), op=ALU.mult), reads=[b2], writes=[b2])
        S.op('dve', lambda e: e.memset(k.lb[:, :, 0:1], 0.0), reads=[b2], writes=[k.b_lb])
        for l in range(1, DEPTH):
            S.op('dve', lambda e, l=l: e.tensor_tensor(out=k.lb[:, :, l:l + 1], in0=k.lb[:, :, l - 1:l], in1=ex[:, :, l:l + 1], op=ALU.add), reads=[b2, k.b_lb], writes=[k.b_lb])
        S.op('dve', lambda e: e.tensor_scalar(out=k.lb[:], in0=k.lb[:], scalar1=0.0, scalar2=None, op0=ALU.max), reads=[k.b_lb], writes=[k.b_lb])
        S.op('dve', lambda e: e.tensor_scalar(out=k.oml[:], in0=k.lb[:], scalar1=-1.0, scalar2=1.0, op0=ALU.mult, op1=ALU.add), reads=[k.b_lb], writes=[k.b_lb])
        S.barrier()


def convert_weights(k, l):
    S, I = k.S, k.I
    p = l % 2
    w = [k.t_w[p]]
    for c0 in range(0, NCOL, 1408):
        S.dma('pool', k.w_in_bf[p, :, c0:c0 + 1408], I["w_in"][l, :, c0:c0 + 1408], writes=w)
    S.dma('pool', k.w_br_bf[p], I["w_branch"][l], writes=w)
    S.dma('pool', k.w_out_bf[p], I["w_out"][l], writes=w)
    S.dma('pool', k.fw_bf[p], I["fnet_w"][l], writes=w)
    S.dma('pool', k.ws_bf[p], I["gmlp_wsT"][l], writes=w)


def load_w(k, l, dst, c0, ncol, bdst):
    p = l % 2
    src = k.w_in_bf[p, :, c0:c0 + ncol].rearrange("(kc p) c -> p kc c", p=128)
    k.S.dma('sp', dst, src, reads=[k.t_w[p]], writes=[bdst])


def adaln(k, l):
    S, nc, I = k.S, k.nc, k.I
    with contextlib.ExitStack() as st:
        sb = lambda n, s, d: st.enter_context(nc.sbuf_tensor(_u(n), list(s), d))
        wa = [sb("ad_wa%d" % i, [128, 8, 512], F32) for i in range(2)]
        bwa = [Buf() for _ in range(2)]
        scbc = sb("ad_scbc", [128, 8, 3, 128], F32)
        bT = sb("ad_bT", [128, 24], F32)
        gpT = sb("ad_gpT", [128, 8], F32)
        brow = sb("ad_brow", [128, D], F32)
        grow = sb("ad_grow", [128, D], F32)
        tmp = sb("ad_tmp", [128, 8, 3], F32)
        b = Buf(); bsc = Buf()
        S.dma('sp', bT[:], I["b_adaT"][l], writes=[b])
        S.dma('sp', gpT[:], I["g_preT"][l], writes=[b])
        S.dma('sp', brow[:], I["b_ada"][l:l + 1, 2 * D:3 * D].partition_broadcast(128), writes=[b])
        S.dma('sp', grow[:], I["g_post"][l:l + 1, :].partition_broadcast(128), writes=[b])
        S.op('dve', lambda e: e.tensor_copy(scbc[:], k.scT[:].unsqueeze(3).broadcast_to([128, 8, 3, 128])), reads=[k.b_scT], writes=[bsc])
        pm = k.ps[0]
        for g in range(6):
            w = wa[g % 2]
            S.dma('sp', w[:], I["w_ada"][l, :, g * 512:(g + 1) * 512].rearrange("(kc p) c -> p kc c", p=128), writes=[bwa[g % 2]])
            for j in range(4):
                ch = 4 * g + j
                for kc in range(8):
                    S.op('pe', lambda e, w=w, kc=kc, j=j, ch=ch: e.matmul(pm[:, ch * 3:ch * 3 + 3], lhsT=w[:, kc, j * 128:(j + 1) * 128], rhs=k.scT[:, kc, :], start=(kc == 0), stop=(kc == 7)),
                         reads=[bwa[g % 2], k.b_scT], writes=[k.bps[0]], inc=(kc == 7))
            if g >= 4:
                half = g - 4
                for s in range(3):
                    pg = k.ps[1 + s]
                    for kc in range(8):
                        S.op('pe', lambda e, w=w, kc=kc, s=s, pg=pg: e.matmul(pg[:], lhsT=scbc[:, kc, s, :], rhs=w[:, kc, :], start=(kc == 0), stop=(kc == 7)),
                             reads=[bwa[g % 2], bsc], writes=[k.bps[1 + s]], inc=(kc == 7))
                    sl = slice(half * 512, (half + 1) * 512)
                    S.op('dve', lambda e, s=s, pg=pg, sl=sl: e.tensor_tensor(out=k.gtg[:, s, sl], in0=pg[:], in1=brow[:, sl], op=ALU.add), reads=[k.bps[1 + s], b], writes=[k.b_gtg])
                    S.op('dve', lambda e, s=s, sl=sl: e.tensor_tensor(out=k.gtg[:, s, sl], in0=k.gtg[:, s, sl], in1=grow[:, sl], op=ALU.mult), reads=[b, k.b_gtg], writes=[k.b_gtg])
        S.op('dve', lambda e: e.tensor_tensor(out=k.modT[:], in0=pm[:, 0:72].rearrange("p (c s) -> p c s", s=3), in1=bT[:].unsqueeze(2).broadcast_to([128, 24, 3]), op=ALU.add), reads=[k.bps[0], b], writes=[k.b_mod])
        S.op('dve', lambda e: e.tensor_scalar(out=tmp[:], in0=k.modT[:, 8:16, :], scalar1=1.0, scalar2=None, op0=ALU.add), reads=[k.b_mod], writes=[b])
        S.op('dve', lambda e: e.tensor_tensor(out=k.gsT[:], in0=tmp[:], in1=gpT[:].unsqueeze(2).broadcast_to([128, 8, 3]), op=ALU.mult), reads=[b], writes=[k.b_mod])
        S.barrier()


def rstd_from_ms(k, ms, bms):
    S = k.S
    S.op('act', lambda e: e.activation(out=ms, in_=ms, func=AF.Sqrt, bias=EPS, scale=1.0), reads=[bms], writes=[bms])
    S.op('dve', lambda e: e.reciprocal(ms, ms), reads=[bms], writes=[bms])


def phase_pre(k, l, s):
    S, nc, I = k.S, k.nc, k.I
    with contextlib.ExitStack() as st:
        sb = lambda n, sh, d: st.enter_context(nc.sbuf_tensor(_u(n), list(sh), d))
        xt = [sb("pr_x%d" % i, [128, D], F32) for i in range(8)]
        bx = [Buf() for _ in range(8)]
        xs = [sb("pr_xs%d" % i, [128, D], BF16) for i in range(2)]
        bxs = [Buf() for _ in range(2)]
        junk = sb("pr_junk", [128, D], F32)
        ms = [sb("pr_ms%d" % i, [128, 4], F32) for i in range(2)]
        bms = [Buf() for _ in range(2)]
        hb = [sb("pr_hb%d" % i, [128, 8, 512], BF16) for i in range(2)]
        bhb = [Buf() for _ in range(2)]
        bj = Buf()
        pT = [k.ps[i][:].bitcast(BF16) for i in range(4)]
        it = 0
        blocks = [(0, b0 * 512, 4) for b0 in range(8)] + [(1, NTOK, 2)]
        for bi, (isctx, c0, ntile) in enumerate(blocks):
            mi = 2 if isctx else s
            xis = []
            for t in range(ntile):
                xi = it % 8
                xis.append(xi)
                if isctx:
                    src = (I["ctx"] if l == 0 else k.ctxcur)[s, t * 128:(t + 1) * 128, :]
                    rb = k.t_ctx[s].r(t * 128, t * 128 + 128)
                else:
                    src = (I["x"] if l == 0 else k.OUT)[s, c0 + t * 128:c0 + (t + 1) * 128, :]
                    rb = k.t_x[s].r(c0 + t * 128, c0 + t * 128 + 128)
                S.dma('sp', xt[xi][:], src, reads=rb, writes=[bx[xi]])
                S.op('act', lambda e: e.activation(out=junk[:], in_=xt[xi][:], func=AF.Square, scale=1.0 / 32.0, accum_out=ms[bi % 2][:, t:t + 1]), reads=[bx[xi]], writes=[bj, bms[bi % 2]])
                it += 1
            rstd_from_ms(k, ms[bi % 2][:, :ntile], bms[bi % 2])
            for t in range(ntile):
                xi = xis[t]
                si = (bi * 4 + t) % 2
                if t % 2 == 0:
                    S.op('act', lambda e: e.activation(out=xs[si][:], in_=xt[xi][:], func=AF.Copy, scale=ms[bi % 2][:, t:t + 1]), reads=[bx[xi], bms[bi % 2]], writes=[bxs[si]])
                else:
                    S.op('dve', lambda e: e.tensor_scalar(out=xs[si][:], in0=xt[xi][:], scalar1=ms[bi % 2][:, t:t + 1], scalar2=None, op0=ALU.mult), reads=[bx[xi], bms[bi % 2]], writes=[bxs[si]])
                for kc in range(8):
                    dst = pT[kc // 2][:, (kc % 2) * 512 + t * 128:(kc % 2) * 512 + (t + 1) * 128]
                    S.op('pe', lambda e: e.transpose(dst, xs[si][:, kc * 128:(kc + 1) * 128], k.ident[:]), reads=[bxs[si], k.b_const], writes=[k.bps[kc // 2]], inc=(kc == 7))
            hi = bi % 2
            nt = ntile * 128
            for kc in range(8):
                src = pT[kc // 2][:, (kc % 2) * 512:(kc % 2) * 512 + nt]
                if kc % 2 == 0:
                    S.op('act', lambda e, src=src, kc=kc, hi=hi, nt=nt, mi=mi: e.activation(out=hb[hi][:, kc, :nt], in_=src, func=AF.Identity, bias=k.modT[:, kc, mi:mi + 1], scale=k.gsT[:, kc, mi:mi + 1]), reads=[k.bps[kc // 2], k.b_mod], writes=[bhb[hi]])
                else:
                    S.op('dve', lambda e, src=src, kc=kc, hi=hi, nt=nt, mi=mi: e.tensor_scalar(out=hb[hi][:, kc, :nt], in0=src, scalar1=k.gsT[:, kc, mi:mi + 1], scalar2=k.modT[:, kc, mi:mi + 1], op0=ALU.mult, op1=ALU.add), reads=[k.bps[kc // 2], k.b_mod], writes=[bhb[hi]])
            S.dma('sp', k.hT[s, :, c0:c0 + nt].rearrange("(kc p) t -> p kc t", p=128), hb[hi][:, :, :nt], reads=[bhb[hi]], writes=k.t_hT[s].r(c0, c0 + nt))
        S.barrier()


def load_h(k, s, dst, c0, n, bdst):
    k.S.dma('sp', dst[:, :, :n], k.hT[s, :, c0:c0 + n].rearrange("(kc p) t -> p kc t", p=128), reads=k.t_hT[s].r(c0, c0 + n), writes=[bdst])


def inproj_fm(k, pbank, n, w, bw, cw, h, bh, hc0=0):
    for kc in range(8):
        k.S.op('pe', lambda e, kc=kc: e.matmul(k.ps[pbank][:, :n], lhsT=w[:, kc, cw:cw + 128], rhs=h[:, kc, hc0:hc0 + n], start=(kc == 0), stop=(kc == 7)),
               reads=[bw, bh], writes=[k.bps[pbank]], inc=(kc == 7))


def inproj_tm(k, pbank, w, bw, cw, h, bh, t0):
    for kc in range(8):
        k.S.op('pe', lambda e, kc=kc: e.matmul(k.ps[pbank][:], lhsT=h[:, kc, t0:t0 + 128], rhs=w[:, kc, cw:cw + 512], start=(kc == 0), stop=(kc == 7)),
               reads=[bw, bh], writes=[k.bps[pbank]], inc=(kc == 7))


def seq_blocks(with_ctx_block=True):
    bl = [(b0 * 512, 4) for b0 in range(8)]
    if with_ctx_block:
        bl.append((NTOK, 2))
    return bl


def phase_c(k, l, with_ctx):
    S, nc, I = k.S, k.nc, k.I
    p = l % 2
    with contextlib.ExitStack() as st:
        sb = lambda n, sh, d: st.enter_context(nc.sbuf_tensor(_u(n), list(sh), d))
        w = sb("c_w", [128, 8, 1536], BF16); bw = Buf()
        wsT = sb("c_ws", [128, 8, 128], BF16)
        gn = sb("c_gn", [128, 512], F32)
        bsT = sb("c_bs", [128, 8], F32)
        bc = Buf()
        hb = [sb("c_hb%d" % i, [128, 8, 512], BF16) for i in range(2)]; bhb = [Buf() for _ in range(2)]
        junk = sb("c_junk", [128, 512], F32); bj = Buf()
        ms = sb("c_ms", [128, 1], F32); bms = Buf()
        vn = sb("c_vn", [128, 512], BF16); bvn = Buf()
        sg = sb("c_sg", [128, 512], F32); bsg = Buf()
        t1 = sb("c_t1", [128, 512], F32); bt1 = Buf()
        yt = sb("c_y", [128, 512], BF16); byt = Buf()
        yb = [sb("c_yb%d" % i, [128, 4, 512], BF16) for i in range(2)]; byb = [Buf() for _ in range(2)]
        load_w(k, l, w[:], COL["c_u"], 1536, bw)
        S.dma('sp', wsT[:], k.ws_bf[p].rearrange("(g s) t -> s g t", s=128), reads=[k.t_w[p]], writes=[bc])
        S.dma('sp', gn[:], I["gmlp_g"][l:l + 1, :].partition_broadcast(128), writes=[bc])
        S.dma('sp', bsT[:], I["gmlp_bsT"][l], writes=[bc])
        pT = [k.ps[4][:].bitcast(BF16), k.ps[5][:].bitcast(BF16)]
        bi = 0
        for s in range(2):
            blocks = seq_blocks(with_ctx)
            load_h(k, s, hb[bi % 2], blocks[0][0], blocks[0][1] * 128, bhb[bi % 2])
            for ib, (c0, ntile) in enumerate(blocks):
                h = hb[bi % 2]; bh = bhb[bi % 2]
                if ib + 1 < len(blocks):
                    load_h(k, s, hb[(bi + 1) % 2], blocks[ib + 1][0], blocks[ib + 1][1] * 128, bhb[(bi + 1) % 2])
                for t in range(ntile):
                    t0 = t * 128
                    inproj_tm(k, 0, w, bw, 512, h, bh, t0)
                    S.op('act', lambda e: e.activation(out=junk[:], in_=k.ps[0][:], func=AF.Square, scale=float(512 ** -0.5), accum_out=ms[:]), reads=[k.bps[0]], writes=[bj, bms])
                    rstd_from_ms(k, ms[:], bms)
                    S.op('dve', lambda e: e.scalar_tensor_tensor(out=vn[:], in0=k.ps[0][:], scalar=ms[:, 0:1], in1=gn[:], op0=ALU.mult, op1=ALU.mult), reads=[k.bps[0], bms, bc], writes=[bvn])
                    inproj_tm(k, 2, w, bw, 0, h, bh, t0)
                    inproj_tm(k, 3, w, bw, 1024, h, bh, t0)
                    for g in range(8):
                        S.op('pe', lambda e, g=g: e.matmul(k.ps[1][:, g * 64:(g + 1) * 64], lhsT=wsT[:, g, :], rhs=vn[:, g * 64:(g + 1) * 64], start=True, stop=True), reads=[bc, bvn], writes=[k.bps[1]], inc=(g == 7))
                    S.op('act', lambda e: e.activation(out=sg[:], in_=k.ps[3][:], func=AF.Silu), reads=[k.bps[3]], writes=[bsg])
                    S.op('dve', lambda e: e.tensor_tensor(out=t1[:].rearrange("p (g c) -> p g c", g=8), in0=k.ps[1][:].rearrange("p (g c) -> p g c", g=8), in1=bsT[:].unsqueeze(2).broadcast_to([128, 8, 64]), op=ALU.add), reads=[k.bps[1], bc], writes=[bt1])
                    S.op('dve', lambda e: e.tensor_tensor(out=t1[:], in0=k.ps[2][:], in1=t1[:], op=ALU.mult), reads=[k.bps[2], bt1], writes=[bt1])
                    S.op('dve', lambda e: e.tensor_tensor(out=yt[:], in0=t1[:], in1=sg[:], op=ALU.mult), reads=[bt1, bsg], writes=[byt])
                    for j in range(4):
                        dst = pT[j // 2][:, (j % 2) * 512 + t0:(j % 2) * 512 + t0 + 128]
                        S.op('pe', lambda e, dst=dst, j=j: e.transpose(dst, yt[:, j * 128:(j + 1) * 128], k.ident[:]), reads=[byt, k.b_const], writes=[k.bps[4 + j // 2]], inc=(j == 3))
                nt = ntile * 128
                yo = yb[bi % 2]
                for j in range(4):
                    src = pT[j // 2][:, (j % 2) * 512:(j % 2) * 512 + nt]
                    S.op('act', lambda e, src=src, j=j, yo=yo, nt=nt: e.copy(yo[:, j, :nt], src), reads=[k.bps[4 + j // 2]], writes=[byb[bi % 2]])
                S.dma('pool', k.yT[s, 2, :, c0:c0 + nt].rearrange("(j p) t -> p j t", p=128), yo[:, :, :nt], reads=[byb[bi % 2]], writes=k.t_yT[s][2].r(c0, c0 + nt))
                bi += 1
        S.barrier()


def phase_b(k, l, with_ctx):
    S, nc, I = k.S, k.nc, k.I
    p = l % 2
    with contextlib.ExitStack() as st:
        sb = lambda n, sh, d: st.enter_context(nc.sbuf_tensor(_u(n), list(sh), d))
        w = sb("b_w", [128, 8, 1024], BF16); bw = Buf()
        fw = sb("b_fw", [128, 4, 128], BF16)
        cc = sb("b_cc", [128, 128], BF16); ssn = sb("b_ss", [128, 128], BF16)
        m12 = sb("b_m12", [128, 2, 4, 128], BF16)
        bc = Buf(); bm = Buf()
        z = sb("b_z", [128, 32, 512], BF16); bz = Buf()
        hb = [sb("b_hb%d" % i, [128, 8, 512], BF16) for i in range(2)]; bhb = [Buf() for _ in range(2)]
        dft = [sb("b_dft%d" % i, [128, 32, 512], BF16) for i in range(2)]; bdft = [Buf() for _ in range(2)]
        Y = sb("b_Y", [128, 4, 512], BF16); bY = Buf()
        sg = sb("b_sg", [128, 4, 256], F32); bsg = Buf()
        yb = [sb("b_yb%d" % i, [128, 4, 256], BF16) for i in range(2)]; byb = [Buf() for _ in range(2)]
        load_w(k, l, w[:], COL["b_x"], 1024, bw)
        S.dma('sp', fw[:], k.fw_bf[p].rearrange("(g c) d -> c g d", c=128), reads=[k.t_w[p]], writes=[bc])
        S.dma('sp', cc[:], I["cc128"], writes=[bc])
        S.dma('sp', ssn[:], I["ssn128"], writes=[bc])
        for i, mat in enumerate((cc, ssn)):
            S.op('pe', lambda e, mat=mat, i=i: e.matmul(k.ps[i][:], lhsT=mat[:], rhs=fw[:].rearrange("c g d -> c (g d)"), start=True, stop=True), reads=[bc], writes=[k.bps[i]])
            S.op('act', lambda e, i=i: e.copy(m12[:, i, :, :].rearrange("c g d -> c (g d)"), k.ps[i][:]), reads=[k.bps[i]], writes=[bm])
        di = 0
        for s in range(2):
            for (tc0, ntile, nkb, isctx) in ([(0, 32, 16, False)] + ([(NTOK, 2, 1, True)] if with_ctx else [])):
                nblk = (ntile + 3) // 4
                for ib in range(nblk):
                    nt = min(4, ntile - ib * 4)
                    load_h(k, s, hb[ib % 2], tc0 + ib * 512, nt * 128, bhb[ib % 2])
                    for t in range(nt):
                        inproj_tm(k, 0, w, bw, 0, hb[ib % 2], bhb[ib % 2], t * 128)
                        S.op('act', lambda e, ib=ib, t=t: e.copy(z[:, ib * 4 + t, :], k.ps[0][:]), reads=[k.bps[0]], writes=[bz])
                for kb in range(nkb):
                    dt_ = dft[di % 2]; bd = bdft[di % 2]
                    if isctx:
                        S.dma('sp', dt_[:, :2, :], I["dft256"].rearrange("(t p) c -> p t c", p=128), writes=[bd])
                    else:
                        for hh in range(2):
                            S.dma('sp', dt_[:, hh * 16:(hh + 1) * 16, :], I["dft"][kb, hh * 2048:(hh + 1) * 2048, :].rearrange("(t p) c -> p t c", p=128), writes=[bd])
                    kc0 = tc0 + kb * 256
                    hi = (kb + 1) % 2
                    load_h(k, s, hb[hi], kc0, 256, bhb[hi])
                    for g in range(4):
                        for t in range(ntile):
                            S.op('pe', lambda e, g=g, t=t, dt_=dt_: e.matmul(k.ps[g][:], lhsT=z[:, t, g * 128:(g + 1) * 128], rhs=dt_[:, t, :], start=(t == 0), stop=(t == ntile - 1)),
                                 reads=[bz, bd], writes=[k.bps[g]], inc=(t == ntile - 1))
                        if g % 2 == 0:
                            S.op('act', lambda e, g=g: e.copy(Y[:, g, :], k.ps[g][:]), reads=[k.bps[g]], writes=[bY])
                        else:
                            S.op('dve', lambda e, g=g: e.tensor_copy(Y[:, g, :], k.ps[g][:]), reads=[k.bps[g]], writes=[bY])
                    for g in range(4):
                        pb_ = 4 + g // 2
                        dst = k.ps[pb_][:, (g % 2) * 256:(g % 2) * 256 + 256]
                        S.op('pe', lambda e, g=g, dst=dst: e.matmul(dst, lhsT=m12[:, 0, g, :], rhs=Y[:, g, 0:256], start=True, stop=False), reads=[bm, bY], writes=[k.bps[pb_]], inc=False)
                        S.op('pe', lambda e, g=g, dst=dst: e.matmul(dst, lhsT=m12[:, 1, g, :], rhs=Y[:, g, 256:512], start=False, stop=True), reads=[bm, bY], writes=[k.bps[pb_]])
                    for g in range(4):
                        pb_ = 6 + g // 2
                        dst = k.ps[pb_][:, (g % 2) * 256:(g % 2) * 256 + 256]
                        for kc in range(8):
                            S.op('pe', lambda e, g=g, dst=dst, kc=kc, hi=hi: e.matmul(dst, lhsT=w[:, kc, 512 + g * 128:512 + (g + 1) * 128], rhs=hb[hi][:, kc, 0:256], start=(kc == 0), stop=(kc == 7)),
                                 reads=[bw, bhb[hi]], writes=[k.bps[pb_]], inc=(kc == 7))
                    yo = yb[di % 2]
                    for gg in range(2):
                        S.op('act', lambda e, gg=gg: e.activation(out=sg[:, 2 * gg:2 * gg + 2, :].rearrange("p a t -> p (a t)"), in_=k.ps[6 + gg][:], func=AF.Silu), reads=[k.bps[6 + gg]], writes=[bsg])
                        S.op('dve', lambda e, gg=gg, yo=yo: e.tensor_tensor(out=yo[:, 2 * gg:2 * gg + 2, :].rearrange("p a t -> p (a t)"), in0=k.ps[4 + gg][:], in1=sg[:, 2 * gg:2 * gg + 2, :].rearrange("p a t -> p (a t)"), op=ALU.mult), reads=[k.bps[4 + gg], bsg], writes=[byb[di % 2]])
                    S.dma('pool', k.yT[s, 1, :, kc0:kc0 + 256].rearrange("(j p) t -> p j t", p=128), yo[:], reads=[byb[di % 2]], writes=k.t_yT[s][1].r(kc0, kc0 + 256))
                    di += 1
        S.barrier()


def na_qblocks(with_ctx):
    bl = [(0, 256, 0, list(range(0, 4)), 0, True), (60 * 64, 256, 0, list(range(28, 32)), 60, True),
          (4 * 64, 256, 1, list(range(0, 6)), 4, True), (56 * 64, 256, 1, list(range(26, 32)), 56, True)]
    for gq in range(1, 7):
        bl.append((gq * 512, 512, 1, list(range(4 * gq - 2, 4 * gq + 6)), 8 * gq, True))
    if with_ctx:
        bl.append((NTOK, 256, 1, [], 0, False))
    return bl


def phase_a(k, l, with_ctx):
    S, nc, I = k.S, k.nc, k.I
    with contextlib.ExitStack() as st:
        sb = lambda n, sh, d: st.enter_context(nc.sbuf_tensor(_u(n), list(sh), d))
        w = sb("a_w", [128, 8, 1024], BF16); bw = Buf()
        kT = sb("a_kT", [128, 4, NT], BF16); bkT = Buf()
        vt = sb("a_v", [128, 34, 512], BF16); bv = Buf()
        csb = [sb("a_cs%d" % i, [128, 2, 512], F32) for i in range(2)]; bcs = [Buf() for _ in range(2)]
        csi = [0]
        rs = sb("a_rs", [128, 128], BF16); eye8 = sb("a_e8", [128, 128], BF16)
        bc = Buf()
        zb = sb("a_zb", [128, 8, ZW * 64], BF16); bzb = Buf()
        hb = [sb("a_hb%d" % i, [128, 8, 512], BF16) for i in range(2)]; bhb = [Buf() for _ in range(2)]
        qp = sb("a_qp", [128, 4, 512], BF16); bqp = Buf()
        qp2 = sb("a_qp2", [128, 2, 4, 512], BF16); bqp2 = Buf()
        qr2 = sb("a_qr2", [128, 2, 4, 512], BF16); bqr = Buf()
        gs = sb("a_gs", [128, 4, 512], BF16); bgs = Buf()
        t1 = sb("a_t1", [128, 512], F32); bt1 = Buf()
        t2 = sb("a_t2", [128, 512], F32); bt2 = Buf()
        kp = sb("a_kp", [128, 512], BF16); bkp = Buf()
        kp_b = sb("a_kpb", [128, 512], BF16); bkp_b = Buf()
        es = [sb("a_es%d" % i, [128, 512], BF16) for i in range(4)]; bes = [Buf() for _ in range(4)]
        rd = sb("a_rd", [128, 512], F32); brd = Buf()
        yo = [sb("a_yo%d" % i, [128, 4, 512], BF16) for i in range(2)]; byo = [Buf() for _ in range(2)]
        S.dma('sp', rs[:], I["rsign"], writes=[bc]); S.dma('sp', eye8[:], I["eye8"], writes=[bc])
        S.op('pool', lambda e: e.memset(qp2[:], 0.0), writes=[bqp2])
        S.op('pool', lambda e: e.memset(qr2[:], 0.0), writes=[bqr])

        def load_cs(c0, n):
            csi[0] += 1
            i = csi[0] % 2
            S.dma('sp', csb[i][:, 0, :n], I["cosT"][:, c0:c0 + n], writes=[bcs[i]])
            S.dma('sp', csb[i][:, 1, :n], I["sinT"][:, c0:c0 + n], writes=[bcs[i]])

        def rope(psrc, plain_bf, bplain, dst, c0, n, rbank):
            i = csi[0] % 2
            if _DBG.get('nors'):
                rbank = psrc
            else:
                S.op('pe', lambda e: e.matmul(k.ps[rbank][:, :n], lhsT=rs[:], rhs=plain_bf, start=True, stop=True), reads=[bc, bplain], writes=[k.bps[rbank]])
            S.op('dve', lambda e: e.tensor_tensor(out=t1[:, :n], in0=k.ps[psrc][:, :n], in1=csb[i][:, 0, :n], op=ALU.mult), reads=[k.bps[psrc], bcs[i], bplain], writes=[bt1])
            S.op('dve', lambda e: e.tensor_tensor(out=t2[:, :n], in0=k.ps[rbank][:, :n], in1=csb[i][:, 1, :n], op=ALU.mult), reads=[k.bps[rbank], bcs[i]], writes=[bt2])

        hi = 0
        for s in range(2):
            load_w(k, l, w[:], COL["a_k"], 1024, bw)
            for (c0, ntile) in seq_blocks(True):
                n = ntile * 128
                h = hb[hi % 2]; bh = bhb[hi % 2]; hi += 1
                load_h(k, s, h, c0, n, bh)
                if c0 < NTOK:
                    load_cs(c0, n)
                for cp in range(4):
                    kb0, kb1 = (0, 1) if cp % 2 == 0 else (3, 4)
                    kpt, bkpt = (kp, bkp) if cp % 2 == 0 else (kp_b, bkp_b)
                    inproj_fm(k, kb0, n, w, bw, cp * 128, h, bh)
                    if c0 >= NTOK or _DBG.get('norope'):
                        S.op('act', lambda e, cp=cp: e.copy(kT[:, cp, c0:c0 + n], k.ps[kb0][:, :n]), reads=[k.bps[kb0]], writes=[bkT])
                    else:
                        S.op('act', lambda e: e.copy(kpt[:, :n], k.ps[kb0][:, :n]), reads=[k.bps[kb0]], writes=[bkpt])
                        rope(kb0, kpt[:, :n], bkpt, None, c0, n, kb1)
                        if _DBG.get('nopool'):
                            S.op('dve', lambda e, cp=cp: e.tensor_tensor(out=kT[:, cp, c0:c0 + n], in0=t1[:, :n], in1=t2[:, :n], op=ALU.add), reads=[bt1, bt2], writes=[bkT])
                        else:
                            S.op('dve', lambda e, cp=cp: e.tensor_tensor(out=kT[:, cp, c0:c0 + n], in0=t1[:, :n], in1=t2[:, :n], op=ALU.add), reads=[bt1, bt2], writes=[bkT])
                for t in range(ntile):
                    vb_ = 2 if t % 2 == 0 else 5
                    inproj_tm(k, vb_, w, bw, 512, h, bh, t * 128)
                    S.op('act', lambda e, t=t: e.copy(vt[:, c0 // 128 + t, :], k.ps[vb_][:]), reads=[k.bps[vb_]], writes=[bv])
            if _DBG.get('a1only'):
                continue
            load_w(k, l, w[:, :, 0:512], COL["a_q"], 512, bw)
            load_w(k, l, w[:, :, 512:1024], COL["a_g"], 512, bw)
            cur_kind = None
            for qi, (t0, nq, kind, chunks, q0, use_rope) in enumerate(na_qblocks(with_ctx)):
                if 'qsel' in _DBG and qi not in _DBG['qsel']:
                    continue
                if use_rope and kind != cur_kind:
                    S.dma('sp', zb[:], I["zb"][l, kind], writes=[bzb])
                    cur_kind = kind
                h = hb[hi % 2]; bh = bhb[hi % 2]; hi += 1
                load_h(k, s, h, t0, nq, bh)
                if use_rope:
                    load_cs(t0, nq)
                for cp in range(4):
                    qb0, qb1, qb2 = (0, 1, 2) if cp % 2 == 0 else (3, 4, 5)
                    inproj_fm(k, qb0, nq, w, bw, cp * 128, h, bh)
                    S.op('act', lambda e, cp=cp: e.copy(qp[:, cp, :nq], k.ps[qb0][:, :nq]), reads=[k.bps[qb0]], writes=[bqp])
                    for j in range(2):
                        S.op('act', lambda e, cp=cp, j=j: e.copy(qp2[64 * j:64 * j + 64, j, cp, :nq], k.ps[qb0][64 * j:64 * j + 64, :nq]), reads=[k.bps[qb0]], writes=[bqp2])
                    if use_rope:
                        rope(qb0, qp[:, cp, :nq], bqp, None, t0, nq, qb1)
                        for j in range(2):
                            S.op('dve', lambda e, cp=cp, j=j: e.tensor_tensor(out=qr2[64 * j:64 * j + 64, j, cp, :nq], in0=t1[64 * j:64 * j + 64, :nq], in1=t2[64 * j:64 * j + 64, :nq], op=ALU.add), reads=[bt1, bt2], writes=[bqr])
                    inproj_fm(k, qb2, nq, w, bw, 512 + cp * 128, h, bh)
                    S.op('act', lambda e, cp=cp: e.activation(out=gs[:, cp, :nq], in_=k.ps[qb2][:, :nq], func=AF.Silu), reads=[k.bps[qb2]], writes=[bgs])
                y = yo[qi % 2]
                for cp in range(4):
                    ob, db = (6, 7) if cp % 2 == 0 else (0, 1)
                    klist = [(c, True) for c in chunks] + [(32, False), (33, False)]
                    items = []
                    for j in range(2):
                        for ic, (c, band) in enumerate(klist):
                            items.append((j, c, band, ic == 0, ic == len(klist) - 1))

                    def stage1(idx, cp=cp):
                        j, c, band, first, last = items[idx]
                        pb = 64 * j; hd = 2 * cp + j
                        sbk = 3 + (idx % 3)
                        e_t = es[idx % 4]; be = bes[idx % 4]
                        if band:
                            woff = 14 - (2 * c - q0)
                            S.op('pe', lambda e: e.matmul(k.ps[sbk][:, :nq], lhsT=kT[:, cp, c * 128:(c + 1) * 128], rhs=qr2[:, j, cp, :nq], start=True, stop=False),
                                 reads=[bkT, bqr], writes=[k.bps[sbk]], inc=False)
                            S.op('pe', lambda e: e.matmul(k.ps[sbk][:, :nq], lhsT=eye8[:], rhs=zb[:, hd, woff * 64:woff * 64 + nq], start=False, stop=True),
                                 reads=[bc, bzb], writes=[k.bps[sbk]])
                        else:
                            S.op('pe', lambda e: e.matmul(k.ps[sbk][:, :nq], lhsT=kT[:, cp, c * 128:(c + 1) * 128], rhs=qp2[:, j, cp, :nq], start=True, stop=True),
                                 reads=[bkT, bqp2], writes=[k.bps[sbk]])
                        S.op('act', lambda e: e.activation(out=e_t[:, :nq], in_=k.ps[sbk][:, :nq], func=AF.Exp, scale=0.125), reads=[k.bps[sbk]], writes=[be])

                    def stage2(idx, cp=cp, ob=ob, db=db):
                        j, c, band, first, last = items[idx]
                        pb = 64 * j; hd = 2 * cp + j
                        e_t = es[idx % 4]; be = bes[idx % 4]
                        S.op('pe', lambda e: e.matmul(k.ps[ob][pb:pb + 64, :nq], lhsT=vt[:, c, hd * 64:(hd + 1) * 64], rhs=e_t[:, :nq], start=first, stop=last),
                             reads=[bv, be], writes=[k.bps[ob]], inc=False)
                        S.op('pe', lambda e: e.matmul(k.ps[db][pb:pb + 64, :nq], lhsT=k.ones_bf[:, 0:64], rhs=e_t[:, :nq], start=first, stop=last),
                             reads=[k.b_const, be], writes=[k.bps[db]])

                    LOOK = 3
                    for idx in range(len(items) + LOOK):
                        if idx < len(items):
                            stage1(idx)
                        if idx >= LOOK:
                            stage2(idx - LOOK)
                    S.op('act', lambda e: e.activation(out=rd[:, :nq], in_=k.ps[db][:, :nq], func=AF.Ln), reads=[k.bps[db]], writes=[brd])
                    S.op('act', lambda e: e.activation(out=rd[:, :nq], in_=rd[:, :nq], func=AF.Exp, scale=-1.0), reads=[brd], writes=[brd])
                    S.op('dve', lambda e: e.tensor_tensor(out=rd[:, :nq], in0=k.ps[ob][:, :nq], in1=rd[:, :nq], op=ALU.mult), reads=[k.bps[ob], brd], writes=[brd])
                    S.op('dve', lambda e, cp=cp, y=y: e.tensor_tensor(out=y[:, cp, :nq], in0=rd[:, :nq], in1=gs[:, cp, :nq], op=ALU.mult), reads=[brd, bgs], writes=[byo[qi % 2]])
                S.dma('pool', k.yT[s, 0, :, t0:t0 + nq].rearrange("(j p) t -> p j t", p=128), y[:, :, :nq], reads=[byo[qi % 2]], writes=k.t_yT[s][0].r(t0, t0 + nq))
        S.barrier()


def phase_d(k, l, with_ctx):
    S, nc, I = k.S, k.nc, k.I
    with contextlib.ExitStack() as st:
        sb = lambda n, sh, d: st.enter_context(nc.sbuf_tensor(_u(n), list(sh), d))
        w = sb("d_w", [128, 8, 2048], BF16); bw = Buf()
        gn = sb("d_gn", [128, 4], F32)
        trif = sb("d_trif", [128, 128], I32); trib = sb("d_trib", [128, 128], I32); bones = sb("d_bones", [128, 128], BF16)
        bc = Buf()
        hb = [sb("d_hb%d" % i, [128, 8, 512], BF16) for i in range(2)]; bhb = [Buf() for _ in range(2)]
        Pp = sb("d_P", [128, 4, 516], F32); bP = Buf()
        kT = sb("d_kT", [128, 4, 512], BF16); bkT = Buf()
        qT = sb("d_qT", [128, 4, 512], BF16); bqT = Buf()
        itm2 = [sb("d_itm%d" % i, [128, 512], BF16) for i in range(2)]; bitm2 = [Buf() for _ in range(2)]
        negr2 = [sb("d_negr%d" % i, [128, 4, 8], F32) for i in range(2)]; bnr2 = [Buf() for _ in range(2)]
        pcnt = [0]
        dqa = [sb("d_dq%d" % i, [128, 128], F32) for i in range(8)]; bdqa = [Buf() for _ in range(8)]
        eqa = [sb("d_eq%d" % i, [128, 128], F32) for i in range(8)]; beqa = [Buf() for _ in range(8)]
        sga = [sb("d_sg%d" % i, [128, 512], F32) for i in range(4)]; bsga = [Buf() for _ in range(4)]
        lfa = [sb("d_lf%d" % i, [128, 512], F32) for i in range(4)]; blfa = [Buf() for _ in range(4)]
        ek = [sb("d_ek%d" % i, [128, 4, 128], F32) for i in range(4)]; bek = [Buf() for _ in range(4)]
        qtl2 = [sb("d_qtl%d" % i, [128, 2, 4, 128], BF16) for i in range(2)]; bqtl2 = [Buf() for _ in range(2)]
        ktl2 = [sb("d_ktl%d" % i, [128, 4, 4, 128], BF16) for i in range(2)]; bktl2 = [Buf() for _ in range(2)]
        qh2 = [sb("d_qh%d" % i, [128, 4, 128], BF16) for i in range(2)]; bqh2 = [Buf() for _ in range(2)]
        khT = sb("d_khT", [128, 4, 128], BF16); bkhT = Buf()
        khtm2 = [sb("d_khtm%d" % i, [128, 512], BF16) for i in range(2)]; bkhtm2 = [Buf() for _ in range(2)]
        dec2 = [sb("d_dec%d" % i, [128, 4], F32) for i in range(2)]; bdec2 = [Buf() for _ in range(2)]
        At = sb("d_At", [128, 8, 128], BF16); bAt = Buf()
        St = sb("d_S", [128, 2, 4, 64], F32); bS = [Buf(), Buf()]
        Sbf = sb("d_Sbf", [128, 4, 128], BF16); bSbf = Buf()
        ob = [sb("d_ob%d" % i, [128, 4, 512], F32) for i in range(2)]; bob = [Buf() for _ in range(2)]
        sq = sb("d_sq", [128, 512], BF16); bsq = Buf()
        rst = sb("d_rst", [128, 512], F32); brst = Buf()
        gl = sb("d_gl", [128, 512], F32); bgl = Buf()
        yb = [sb("d_yb%d" % i, [128, 4, 512], BF16) for i in range(2)]; byb = [Buf() for _ in range(2)]
        S.dma('sp', gn[:], I["hgrn_gT"][l], writes=[bc])
        S.dma('sp', trif[:], I["trif"], writes=[bc]); S.dma('sp', trib[:], I["trib"], writes=[bc]); S.dma('sp', bones[:], I["bones"], writes=[bc])
        S.op('dve', lambda e: e.memset(Pp[:], 0.0), writes=[bP])
        for i in range(2):
            S.op('pool', lambda e, i=i: e.memset(qtl2[i][:], 0.0), writes=[bqtl2[i]])
        S.op('pool', lambda e: e.memset(Sbf[:], 0.0), writes=[bSbf])
        ptr = k.ps[5][:].bitcast(BF16)
        hi = 0
        for s in range(2):
            for d in range(2):
                S.op('dve', lambda e, d=d: e.memset(St[:, d, :, :], 0.0), writes=[bS[d]])
            for (base, nblk_tiles) in ((NTOK, [2]), (0, [4] * 8)):
                isctx = base >= NTOK
                for d in range(2):
                    _DBG['dpass'] = _DBG.get('dpass', 0) + 1
                    if 'dmax' in _DBG and _DBG['dpass'] > _DBG['dmax']:
                        continue
                    sgn = 1.0 if d == 0 else -1.0
                    final = (d == 1)
                    load_w(k, l, w[:, :, 0:512], COL["d_q"], 512, bw)
                    load_w(k, l, w[:, :, 512:1024], COL["d_ff"] if d == 0 else COL["d_fb"], 512, bw)
                    load_w(k, l, w[:, :, 1024:1536], COL["d_i"], 512, bw)
                    if final:
                        load_w(k, l, w[:, :, 1536:2048], COL["d_g"], 512, bw)
                    for i in range(4):
                        S.op('pool', lambda e, i=i: e.memset(ek[i][:], 0.0), writes=[bek[i]])
                    S.op('pool', lambda e: e.memset(At[:], 0.0), writes=[bAt])
                    S.op('act', lambda e, d=d: e.copy(Sbf[0:64, :, 0:64], St[0:64, d, :, :]), reads=[bS[d]], writes=[bSbf]); S.op('act', lambda e, d=d: e.copy(Sbf[64:128, :, 64:128], St[64:128, d, :, :]), reads=[bS[d]], writes=[bSbf])
                    mask = trif if d == 0 else trib
                    blist = list(range(len(nblk_tiles)))
                    if d == 1:
                        blist = blist[::-1]
                    for ib in blist:
                        ntile = nblk_tiles[ib]
                        n = ntile * 128
                        c0 = base + ib * 512
                        h = hb[hi % 2]; bh = bhb[hi % 2]
                        o_b = ob[hi % 2]; bo = bob[hi % 2]; y_b = yb[hi % 2]; by_ = byb[hi % 2]
                        hi += 1
                        load_h(k, s, h, c0, n, bh)
                        if final:
                            S.dma('sp', o_b[:, :, :n], k.ofT[s, :, c0:c0 + n].rearrange("(j p) t -> p j t", p=128), reads=k.t_ofT[s].r(c0, c0 + n), writes=[bo])
                        for ft in range(4):
                            zb_ = ft % 2
                            inproj_fm(k, zb_, n, w, bw, 512 + ft * 128, h, bh)
                            S.op('act', lambda e: e.activation(out=sga[ft][:, :n], in_=k.ps[zb_][:, :n], func=AF.Sigmoid), reads=[k.bps[zb_]], writes=[bsga[ft]])
                            S.op('dve', lambda e: e.tensor_scalar(out=sga[ft][:, :n], in0=sga[ft][:, :n], scalar1=k.oml[:, d * 4 + ft, l:l + 1], scalar2=k.lb[:, d * 4 + ft, l:l + 1], op0=ALU.mult, op1=ALU.add), reads=[bsga[ft], k.b_lb], writes=[bsga[ft]])
                            S.op('dve', lambda e: e.tensor_scalar(out=sga[ft][:, :n], in0=sga[ft][:, :n], scalar1=1e-30, scalar2=None, op0=ALU.max), reads=[bsga[ft]], writes=[bsga[ft]])
                        for ft in range(4):
                            S.op('act', lambda e: e.activation(out=lfa[ft][:, :n], in_=sga[ft][:, :n], func=AF.Ln), reads=[bsga[ft]], writes=[blfa[ft]])
                            S.op('dve', lambda e: e.tensor_scalar(out=kT[:, ft, :n], in0=sga[ft][:, :n], scalar1=-1.0, scalar2=1.0, op0=ALU.mult, op1=ALU.add), reads=[bsga[ft]], writes=[bkT])
                            S.op('dve', lambda e: e.tensor_tensor_scan(out=Pp[:, ft, 1:1 + n], data0=k.ones_f[:, :n], data1=lfa[ft][:, :n], initial=0.0, op0=ALU.mult, op1=ALU.add), reads=[blfa[ft], k.b_const], writes=[bP])
                        for ft in range(4):
                            zb_ = ft % 2
                            inproj_fm(k, zb_, n, w, bw, ft * 128, h, bh)
                            S.op('act', lambda e: e.copy(qT[:, ft, :n], k.ps[zb_][:, :n]), reads=[k.bps[zb_]], writes=[bqT])
                        tl = list(range(ntile))
                        if d == 1:
                            tl = tl[::-1]

                        def stageE(t, pi, part):
                            t0 = t * 128
                            xo = t0 + 1 if d == 0 else t0
                            itm = itm2[pi]; bitm = bitm2[pi]; qtl = qtl2[pi]; bqtl = bqtl2[pi]; ktl = ktl2[pi]; bktl = bktl2[pi]
                            qh = qh2[pi]; bqh = bqh2[pi]; khtm = khtm2[pi]; bkhtm = bkhtm2[pi]; dec = dec2[pi]; bdec = bdec2[pi]
                            negr = negr2[pi]; bnr = bnr2[pi]
                            if part == 1:
                              inproj_tm(k, 2, w, bw, 1024, h, bh, t0)
                              S.op('act', lambda e: e.copy(itm[:], k.ps[2][:]), reads=[k.bps[2]], writes=[bitm])
                            if part == 1:
                              S.op('dve', lambda e: e.tensor_scalar(out=negr[:, :, 0:4], in0=Pp[:, :, t0 + 16:t0 + 113:32], scalar1=-1.0, scalar2=None, op0=ALU.mult), reads=[bP], writes=[bnr])
                              S.op('dve', lambda e: e.tensor_scalar(out=negr[:, :, 4:6], in0=Pp[:, :, t0:t0 + 129:128], scalar1=-1.0, scalar2=None, op0=ALU.mult), reads=[bP], writes=[bnr])
                            def tiles(ft):
                                return (dqa[ft], bdqa[ft], eqa[ft], beqa[ft], dqa[4 + ft], bdqa[4 + ft], eqa[4 + ft], beqa[4 + ft],
                                        Pp[:, ft, xo:xo + 128], Pp[:, ft, t0:t0 + 1], Pp[:, ft, t0 + 128:t0 + 129], negr[:, ft, 4:5], negr[:, ft, 5:6])
                            for ft in (range(4) if part == 1 else []):
                                dq, bdq, eq, beq, dq2, bdq2, eq2, beq2, X, B0p, B1p, B0n, B1n = tiles(ft)
                                S.op('dve', lambda e: e.tensor_tensor(out=dq[:].rearrange("p (i c) -> p i c", i=4), in0=X.rearrange("p (i c) -> p i c", i=4), in1=Pp[:, ft, t0 + 16:t0 + 113:32].unsqueeze(2).broadcast_to([128, 4, 32]), op=ALU.subtract), reads=[bP], writes=[bdq])
                            for ft in (range(4) if part == 1 else []):
                                dq, bdq, eq, beq, dq2, bdq2, eq2, beq2, X, B0p, B1p, B0n, B1n = tiles(ft)
                                S.op('act', lambda e: e.activation(out=eq[:], in_=dq[:], func=AF.Exp, scale=sgn), reads=[bdq], writes=[beq])
                                for i in range(4):
                                    lo, hi_ = (0, 32 * (i + 1)) if d == 0 else (32 * i, 128)
                                    bias = Pp[:, ft, t0 + 16 + 32 * i:t0 + 17 + 32 * i] if d == 0 else negr[:, ft, i:i + 1]
                                    S.op('act', lambda e: e.activation(out=ek[ft][:, i, lo:hi_], in_=Pp[:, ft, xo + lo:xo + hi_], func=AF.Exp, scale=-sgn, bias=bias), reads=[bP, bnr], writes=[bek[ft]])
                                bq_ = B0n if d == 0 else B1p
                                S.op('act', lambda e: e.activation(out=eq2[:], in_=X, func=AF.Exp, scale=sgn, bias=bq_), reads=[bP, bnr], writes=[beq2])
                                bk_ = B1p if d == 0 else B0n
                                S.op('act', lambda e: e.activation(out=dq2[:], in_=X, func=AF.Exp, scale=-sgn, bias=bk_), reads=[bP, bnr], writes=[bdq2])
                                S.op('act', lambda e: e.activation(out=dec[:, ft:ft + 1], in_=B1p, func=AF.Exp, scale=1.0, bias=B0n), reads=[bP, bnr], writes=[bdec])
                            for ft in (range(4) if part == 2 else []):
                                dq, bdq, eq, beq, dq2, bdq2, eq2, beq2, X, B0p, B1p, B0n, B1n = tiles(ft)
                                S.op('dve', lambda e: e.tensor_tensor(out=khT[:, ft, :], in0=dq2[:], in1=kT[:, ft, t0:t0 + 128], op=ALU.mult), reads=[bdq2, bkT], writes=[bkhT])
                                S.op('pe', lambda e: e.transpose(ptr[:, ft * 128:(ft + 1) * 128], khT[:, ft, :], k.ident[:]), reads=[bkhT, k.b_const], writes=[k.bps[5]])
                                S.op('dve', lambda e: e.tensor_tensor(out=qtl[0:64, 0, ft, :], in0=eq[0:64, :], in1=qT[0:64, ft, t0:t0 + 128], op=ALU.mult), reads=[beq, bqT], writes=[bqtl])
                                S.op('dve', lambda e: e.tensor_tensor(out=qtl[64:128, 1, ft, :], in0=eq[64:128, :], in1=qT[64:128, ft, t0:t0 + 128], op=ALU.mult), reads=[beq, bqT], writes=[bqtl])
                                S.op('dve', lambda e: e.tensor_tensor(out=ktl[:, ft, :, :], in0=ek[ft][:], in1=kT[:, ft, t0:t0 + 128].unsqueeze(1).broadcast_to([128, 4, 128]), op=ALU.mult), reads=[bek[ft], bkT], writes=[bktl])
                                S.op('dve', lambda e: e.tensor_tensor(out=qh[:, ft, :], in0=eq2[:], in1=qT[:, ft, t0:t0 + 128], op=ALU.mult), reads=[beq2, bqT], writes=[bqh])
                            if part == 2:
                                S.op('dve', lambda e: e.tensor_copy(khtm[:], ptr[:, 0:512]), reads=[k.bps[5]], writes=[bkhtm])

                        def stageF(t, pi):
                            t0 = t * 128
                            itm = itm2[pi]; bitm = bitm2[pi]; qtl = qtl2[pi]; bqtl = bqtl2[pi]; ktl = ktl2[pi]; bktl = bktl2[pi]
                            qh = qh2[pi]; bqh = bqh2[pi]; khtm = khtm2[pi]; bkhtm = bkhtm2[pi]; dec = dec2[pi]; bdec = bdec2[pi]
                            for hd in range(8):
                                cp = hd // 2
                                sbk = 3 + hd // 4
                                for i in range(4):
                                    dst = k.ps[sbk][:, (hd % 4) * 128 + 32 * i:(hd % 4) * 128 + 32 * i + 32]
                                    S.op('pe', lambda e: e.matmul(dst, lhsT=ktl[:, cp, i, :], rhs=qtl[:, hd % 2, cp, 32 * i:32 * i + 32], start=True, stop=True),
                                         reads=[bktl, bqtl], writes=[k.bps[sbk]], inc=(hd % 4 == 3 and i == 3))
                            for half in range(2):
                                S.op('dve', lambda e: e.copy_predicated(out=At[:, 4 * half:4 * half + 4, :], mask=mask[:].unsqueeze(1).broadcast_to([128, 4, 128]), data=k.ps[3 + half][:].rearrange("p (h t) -> p h t", h=4)), reads=[k.bps[3 + half], bc, bAt], writes=[bAt])
                            for cp in range(4):
                                for j in range(2):
                                    hd = 2 * cp + j
                                    S.op('pe', lambda e: e.matmul(k.ps[6][64 * j:64 * j + 64, cp * 128:(cp + 1) * 128], lhsT=itm[:, hd * 64:(hd + 1) * 64], rhs=At[:, hd, :], start=True, stop=False), reads=[bitm, bAt], writes=[k.bps[6]], inc=False)
                                S.op('pe', lambda e: e.matmul(k.ps[6][:, cp * 128:(cp + 1) * 128], lhsT=Sbf[:, cp, :], rhs=qh[:, cp, :], start=False, stop=True), reads=[bSbf, bqh], writes=[k.bps[6]], inc=(cp == 3))
                            for cp in range(4):
                                S.op('pe', lambda e: e.matmul(k.ps[7][:, cp * 128:(cp + 1) * 128], lhsT=khtm[:, cp * 128:(cp + 1) * 128], rhs=itm[:, cp * 128:(cp + 1) * 128], start=True, stop=True), reads=[bkhtm, bitm], writes=[k.bps[7]], inc=(cp == 3))
                            for cp in range(4):
                                for j in range(2):
                                    pb = 64 * j
                                    S.op('dve', lambda e: e.scalar_tensor_tensor(out=St[pb:pb + 64, d, cp, :], in0=St[pb:pb + 64, d, cp, :], scalar=dec[pb:pb + 64, cp:cp + 1], in1=k.ps[7][pb:pb + 64, cp * 128 + 64 * j:cp * 128 + 64 * j + 64], op0=ALU.mult, op1=ALU.add),
                                         reads=[bS[d], bdec, k.bps[7]], writes=[bS[d]])
                            S.op('act', lambda e: e.copy(Sbf[0:64, :, 0:64], St[0:64, d, :, :]), reads=[bS[d]], writes=[bSbf])
                            S.op('act', lambda e: e.copy(Sbf[64:128, :, 64:128], St[64:128, d, :, :]), reads=[bS[d]], writes=[bSbf])
                            if not final:
                                S.op('act', lambda e: e.copy(o_b[:, :, t0:t0 + 128], k.ps[6][:].rearrange("p (c t) -> p c t", c=4)), reads=[k.bps[6]], writes=[bo])
                            else:
                                S.op('dve', lambda e: e.tensor_tensor(out=o_b[:, :, t0:t0 + 128], in0=k.ps[6][:].rearrange("p (c t) -> p c t", c=4), in1=o_b[:, :, t0:t0 + 128], op=ALU.add), reads=[k.bps[6], bo], writes=[bo])

                        for it_, t in enumerate(tl):
                            if it_ == 0:
                                stageE(t, pcnt[0] % 2, 1)
                                stageE(t, pcnt[0] % 2, 2)
                            if it_ + 1 < len(tl):
                                stageE(tl[it_ + 1], (pcnt[0] + 1) % 2, 1)
                            stageF(t, pcnt[0] % 2)
                            if it_ + 1 < len(tl):
                                stageE(tl[it_ + 1], (pcnt[0] + 1) % 2, 2)
                            pcnt[0] += 1
                        if not final:
                            S.dma('pool', k.ofT[s, :, c0:c0 + n].rearrange("(j p) t -> p j t", p=128), o_b[:, :, :n], reads=[bo], writes=k.t_ofT[s].r(c0, c0 + n))
                        elif (not isctx) or with_ctx:
                            for cp in range(4):
                                S.op('act', lambda e, cp=cp, o_b=o_b: e.activation(out=sq[:, :n], in_=o_b[:, cp, :n], func=AF.Square), reads=[bo], writes=[bsq])
                                S.op('pe', lambda e: e.matmul(k.ps[0][:, :n], lhsT=bones[:], rhs=sq[:, :n], start=True, stop=True), reads=[bc, bsq], writes=[k.bps[0]])
                                S.op('act', lambda e: e.activation(out=rst[:, :n], in_=k.ps[0][:, :n], func=AF.Ln, scale=1.0 / 64.0, bias=EPS), reads=[k.bps[0]], writes=[brst])
                                S.op('act', lambda e: e.activation(out=rst[:, :n], in_=rst[:, :n], func=AF.Exp, scale=-0.5), reads=[brst], writes=[brst])
                                inproj_fm(k, 1, n, w, bw, 1536 + cp * 128, h, bh)
                                S.op('act', lambda e: e.activation(out=gl[:, :n], in_=k.ps[1][:, :n], func=AF.Silu), reads=[k.bps[1]], writes=[bgl])
                                S.op('dve', lambda e, cp=cp, o_b=o_b: e.scalar_tensor_tensor(out=rst[:, :n], in0=o_b[:, cp, :n], scalar=gn[:, cp:cp + 1], in1=rst[:, :n], op0=ALU.mult, op1=ALU.mult), reads=[bo, bc, brst], writes=[brst])
                                S.op('dve', lambda e, cp=cp, y_b=y_b: e.tensor_tensor(out=y_b[:, cp, :n], in0=rst[:, :n], in1=gl[:, :n], op=ALU.mult), reads=[brst, bgl], writes=[by_])
                            S.dma('pool', k.yT[s, 3, :, c0:c0 + n].rearrange("(j p) t -> p j t", p=128), y_b[:, :, :n], reads=[by_], writes=k.t_yT[s][3].r(c0, c0 + n))
        S.barrier()


def phase_m(k, l, with_ctx):
    S, nc, I = k.S, k.nc, k.I
    p = l % 2
    with contextlib.ExitStack() as st:
        sb = lambda n, sh, d: st.enter_context(nc.sbuf_tensor(_u(n), list(sh), d))
        wg = sb("m_wg", [128, 8, 4096], BF16); bw = Buf()
        wbr = sb("m_wbr", [128, 16, D], BF16)
        wo = sb("m_wo", [128, 8, D], BF16)
        bwc = Buf()
        hb = [sb("m_hb%d" % i, [128, 8, 256], BF16) for i in range(2)]; bhb = [Buf() for _ in range(2)]
        yb = [sb("m_yb%d" % i, [128, 16, 256], BF16) for i in range(2)]; byb = [Buf() for _ in range(2)]
        sgt = [sb("m_sg%d" % i, [128, 256], F32) for i in range(2)]; bsg = [Buf() for _ in range(2)]
        tmp = [sb("m_tmp%d" % i, [128, 256], F32) for i in range(2)]; btmp = [Buf() for _ in range(2)]
        macc = sb("m_acc", [128, 256], F32); bacc = Buf()
        mT = sb("m_mT", [128, 8, 256], BF16); bmT = Buf()
        xt = [sb("m_x%d" % i, [128, D], F32) for i in range(2)]; bx = [Buf() for _ in range(2)]
        tt = sb("m_tt", [128, D], F32); btt = Buf()
        junk = sb("m_junk", [128, 512], F32); bj = Buf()
        ms = sb("m_ms", [128, 2], F32); bms = Buf()
        load_w(k, l, wg[:], COL["gate"], 4096, bw)
        S.dma('sp', wbr[:], k.w_br_bf[p].rearrange("(a p) c -> p a c", p=128), reads=[k.t_w[p]], writes=[bwc])
        S.dma('sp', wo[:], k.w_out_bf[p].rearrange("(a p) c -> p a c", p=128), reads=[k.t_w[p]], writes=[bwc])
        bi = 0
        xi = 0
        gi = 0
        for s in range(2):
            blocks = [(c0, False) for c0 in range(0, NTOK, 256)] + ([(NTOK, True)] if with_ctx else [])
            for (c0, isctx) in blocks:
                mi = 2 if isctx else s
                h = hb[bi % 2]; bh = bhb[bi % 2]; y = yb[bi % 2]; by_ = byb[bi % 2]
                bi += 1
                load_h(k, s, h, c0, 256, bh)
                for r in range(4):
                    S.dma('sp', y[:, 4 * r:4 * r + 4, :], k.yT[s, r, :, c0:c0 + 256].rearrange("(j p) t -> p j t", p=128), reads=k.t_yT[s][r].r(c0, c0 + 256), writes=[by_])
                for fc in range(8):
                    for r in range(4):
                        gb = gi % 2; gi += 1
                        for kc in range(8):
                            S.op('pe', lambda e, kc=kc, r=r, fc=fc, gb=gb: e.matmul(k.ps[gb][:, :256], lhsT=wg[:, kc, r * 1024 + fc * 128:r * 1024 + (fc + 1) * 128], rhs=h[:, kc, :], start=(kc == 0), stop=(kc == 7)),
                                 reads=[bw, bh], writes=[k.bps[gb]], inc=(kc == 7))
                        S.op('act', lambda e, gb=gb: e.activation(out=sgt[gb][:], in_=k.ps[gb][:, :256], func=AF.Sigmoid), reads=[k.bps[gb]], writes=[bsg[gb]])
                        for kc in range(4):
                            S.op('pe', lambda e, kc=kc, r=r, fc=fc, gb=gb: e.matmul(k.ps[2 + gb][:, :256], lhsT=wbr[:, 4 * r + kc, fc * 128:(fc + 1) * 128], rhs=y[:, 4 * r + kc, :], start=(kc == 0), stop=(kc == 3)),
                                 reads=[bwc, by_], writes=[k.bps[2 + gb]], inc=(kc == 3))
                        if r == 0:
                            S.op('dve', lambda e, gb=gb: e.tensor_tensor(out=macc[:], in0=k.ps[2 + gb][:, :256], in1=sgt[gb][:], op=ALU.mult), reads=[k.bps[2 + gb], bsg[gb]], writes=[bacc])
                        else:
                            S.op('dve', lambda e, gb=gb: e.tensor_tensor(out=tmp[gb][:], in0=k.ps[2 + gb][:, :256], in1=sgt[gb][:], op=ALU.mult), reads=[k.bps[2 + gb], bsg[gb]], writes=[btmp[gb]])
                            S.op('dve', lambda e, gb=gb: e.tensor_tensor(out=macc[:], in0=macc[:], in1=tmp[gb][:], op=ALU.add), reads=[bacc, btmp[gb]], writes=[bacc])
                    S.op('act', lambda e, fc=fc: e.copy(mT[:, fc, :], macc[:]), reads=[bacc], writes=[bmT])
                for t in range(2):
                    tok0 = c0 + t * 128
                    x = xt[xi % 2]; bxx = bx[xi % 2]; xi += 1
                    if isctx:
                        src = (I["ctx"] if l == 0 else k.ctxcur)[s, t * 128:(t + 1) * 128, :]
                        dstd = k.ctxcur[s, t * 128:(t + 1) * 128, :]
                        tb = k.t_ctx[s].r(t * 128, t * 128 + 128)
                    else:
                        src = (I["x"] if l == 0 else k.OUT)[s, tok0:tok0 + 128, :]
                        dstd = k.OUT[s, tok0:tok0 + 128, :]
                        tb = k.t_x[s].r(tok0, tok0 + 128)
                    S.dma('sp', x[:], src, reads=tb, writes=[bxx])
                    for half in range(2):
                        for kc in range(8):
                            S.op('pe', lambda e, kc=kc, half=half, t=t: e.matmul(k.ps[4 + half][:], lhsT=mT[:, kc, t * 128:(t + 1) * 128], rhs=wo[:, kc, half * 512:(half + 1) * 512], start=(kc == 0), stop=(kc == 7)),
                                 reads=[bmT, bwc], writes=[k.bps[4 + half]], inc=(kc == 7))
                        S.op('act', lambda e, half=half: e.activation(out=junk[:], in_=k.ps[4 + half][:], func=AF.Square, scale=1.0 / 32.0, accum_out=ms[:, half:half + 1]), reads=[k.bps[4 + half]], writes=[bj, bms])
                    S.op('dve', lambda e: e.tensor_tensor(out=ms[:, 0:1], in0=ms[:, 0:1], in1=ms[:, 1:2], op=ALU.add), reads=[bms], writes=[bms])
                    rstd_from_ms(k, ms[:, 0:1], bms)
                    for half in range(2):
                        sl = slice(half * 512, (half + 1) * 512)
                        S.op('dve', lambda e, half=half, sl=sl, mi=mi: e.scalar_tensor_tensor(out=tt[:, sl], in0=k.ps[4 + half][:], scalar=ms[:, 0:1], in1=k.gtg[:, mi, sl], op0=ALU.mult, op1=ALU.mult), reads=[k.bps[4 + half], bms, k.b_gtg], writes=[btt])
                    S.op('dve', lambda e, x=x: e.tensor_tensor(out=x[:], in0=x[:], in1=tt[:], op=ALU.add), reads=[bxx, btt], writes=[bxx])
                    S.dma('pool', dstd, x[:], reads=[bxx], writes=tb)
        S.barrier()


_BF = ml_dtypes.bfloat16
_CONST = {}


def _constants():
    if _CONST:
        return _CONST
    c = {}
    c["ident"] = np.eye(128, dtype=np.float32).astype(_BF)
    c["eye8"] = (8.0 * np.eye(128, dtype=np.float32)).astype(_BF)
    rm = np.zeros((128, 128), np.float32)
    for dp in range(128):
        dd = dp % 64
        if (dd % 32) < 16:
            rm[dp, dp + 16] = -1.0
        else:
            rm[dp, dp - 16] = 1.0
    c["rsign"] = np.ascontiguousarray(rm.T).astype(_BF)
    t = np.arange(NTOK)
    pos = np.stack([t // 64, t % 64], 0).astype(np.float64)
    inv = 10000.0 ** (-np.arange(16, dtype=np.float64) * 2.0 / 32.0)
    d = np.arange(128) % 64
    ang = pos[d // 32, :] * inv[d % 16][:, None]
    c["cosT"] = np.cos(ang).astype(np.float32)
    c["sinT"] = np.sin(ang).astype(np.float32)
    s_, t_ = np.meshgrid(np.arange(128), np.arange(128), indexing="ij")
    c["trif"] = (s_ <= t_).astype(np.int32)
    c["trib"] = (s_ >= t_).astype(np.int32)
    c["bones"] = ((s_ // 64) == (t_ // 64)).astype(np.float32).astype(_BF)
    n = np.arange(NTOK, dtype=np.int64)
    m = (n[:, None] * n[None, :]) % NTOK
    sc = 1.0 / np.sqrt(NTOK * 128.0)
    angm = (2.0 * np.pi / NTOK) * m.astype(np.float32)
    cs = (np.cos(angm) * sc).astype(np.float32)
    sn = (np.sin(angm) * sc).astype(np.float32)
    dft = np.empty((16, NTOK, 512), dtype=_BF)
    for kb in range(16):
        dft[kb, :, 0:256] = cs[:, kb * 256:(kb + 1) * 256].astype(_BF)
        dft[kb, :, 256:512] = sn[:, kb * 256:(kb + 1) * 256].astype(_BF)
    c["dft"] = dft
    n2 = np.arange(LCTX, dtype=np.int64)
    a2 = (2.0 * np.pi / LCTX) * ((n2[:, None] * n2[None, :]) % LCTX)
    sc2 = 1.0 / np.sqrt(LCTX * 128.0)
    c["dft256"] = np.concatenate([np.cos(a2) * sc2, np.sin(a2) * sc2], 1).astype(np.float32).astype(_BF)
    n3 = np.arange(128, dtype=np.int64)
    a3 = (2.0 * np.pi / 128) * ((n3[:, None] * n3[None, :]) % 128)
    c["cc128"] = np.cos(a3).astype(np.float32).astype(_BF)
    c["ssn128"] = (-np.sin(a3)).astype(np.float32).astype(_BF)
    _CONST.update(c)
    return _CONST


def _zb_tables(rpb):
    L = rpb.shape[0]
    e = np.arange(2)[:, None, None, None]
    kc = np.arange(64)[None, :, None, None]
    w = np.arange(ZW)[None, None, :, None]
    qc = np.arange(64)[None, None, None, :]
    dr = 14 - w + e + 0 * kc + 0 * qc
    cs = np.clip(qc - 8, 0, 48)
    col_ok = (kc >= cs) & (kc < cs + 16)
    cidx = np.clip(kc - qc + 15, 0, 30) + 0 * dr
    out = np.empty((L, 2, 128, 8, ZW * 64), dtype=_BF)
    for kind in range(2):
        row_ok = (dr >= -7) & (dr <= 7)
        if kind == 1:
            row_ok = row_ok & (dr >= -4) & (dr < 4)
        ok = (row_ok & col_ok)
        ridx = np.clip(dr + 7, 0, 14)
        for l in range(L):
            g = rpb[l][:, ridx, cidx]
            g = np.where(ok[None], g, np.float32(NEG))
            out[l, kind] = g.transpose(1, 2, 0, 3, 4).reshape(128, 8, ZW * 64).astype(_BF)
    return out


def make_in_maps(inputs, nlayers=DEPTH, cores=range(8)):
    f = lambda a: np.ascontiguousarray(np.asarray(a, dtype=np.float32))
    c = _constants()
    L = nlayers
    shared = dict(c)
    shared["w_ada"] = f(inputs["w_ada"][:L])
    shared["b_ada"] = f(inputs["b_ada"][:L])
    shared["b_adaT"] = f(np.asarray(inputs["b_ada"][:L]).reshape(L, 24, 128).transpose(0, 2, 1))
    shared["g_preT"] = f(np.asarray(inputs["g_pre"][:L]).reshape(L, 8, 128).transpose(0, 2, 1))
    shared["g_post"] = f(inputs["g_post"][:L])
    shared["w_in"] = f(inputs["w_in"][:L])
    shared["zb"] = _zb_tables(np.asarray(inputs["na_rpb"][:L], dtype=np.float32))
    shared["fnet_w"] = f(np.asarray(inputs["fnet_w"][:L]).reshape(L, 512, 128))
    shared["gmlp_g"] = f(inputs["gmlp_norm_g"][:L])
    shared["gmlp_wsT"] = f(np.asarray(inputs["gmlp_ws"][:L]).transpose(0, 1, 3, 2).reshape(L, 1024, 128))
    shared["gmlp_bsT"] = f(np.asarray(inputs["gmlp_bs"][:L]).transpose(0, 2, 1))
    lg = np.asarray(inputs["hgrn_lb_logits"], dtype=np.float32)
    shared["lbT"] = f(lg.reshape(DEPTH, 2, 4, 128).transpose(3, 1, 2, 0).reshape(128, 8, DEPTH))
    shared["hgrn_gT"] = f(np.asarray(inputs["hgrn_norm_g"][:L]).reshape(L, 4, 128).transpose(0, 2, 1))
    shared["w_branch"] = f(np.asarray(inputs["w_branch"][:L]).reshape(L, 2048, D))
    shared["w_out"] = f(inputs["w_out"][:L])
    x = np.asarray(inputs["x"]); ctx = np.asarray(inputs["ctx"]); cc = np.asarray(inputs["c"]); c_ctx = np.asarray(inputs["c_ctx"])
    maps = []
    for ci in cores:
        m = dict(shared)
        m["x"] = f(x[2 * ci:2 * ci + 2])
        m["ctx"] = f(ctx[2 * ci:2 * ci + 2])
        c3 = np.stack([cc[2 * ci], cc[2 * ci + 1], c_ctx], 0)
        m["cT"] = f(c3.reshape(3, 8, 128).transpose(2, 1, 0))
        maps.append(m)
    return maps


_NC_CACHE = {}


def kernel(**inputs):
    if "nc" not in _NC_CACHE:
        _NC_CACHE["nc"] = build()
    nc = _NC_CACHE["nc"]
    maps = make_in_maps(inputs)
    res = run_bass_kernel_spmd(nc, maps, core_ids=list(range(8)))
    return np.concatenate([np.asarray(r["out"], dtype=np.float32) for r in res.results], axis=0)
```
